# Optimizing a Trainium2 kernel written in Bass

```python
import jax, jax.numpy as jnp
from jax import lax
import numpy as np

D_MODEL = 1024
BATCH = 16
SEQ = 256
DEPTH = 4
DEC_BATCH = 2
DEC_SEQ = 4096
PAST_LEN = 512

GRID_W = 64
MLA_HEADS = 4
QK_NOPE_DIM = 128
QK_ROPE_DIM = 64
V_HEAD_DIM = 128
Q_LORA_RANK = 384
KV_LORA_RANK = 256
ROPE_THETA = 10000.0
AXIS_ROPE_DIM = QK_ROPE_DIM // 2
ROPE_HALF = QK_ROPE_DIM // 2
ATTN_SCALE = (QK_NOPE_DIM + QK_ROPE_DIM) ** -0.5
Q_BLOCK = 128
MLA_WIDTH = MLA_HEADS * V_HEAD_DIM
POOL_WINDOWS = (2, 4, 8, 16)
N_POOL_GROUPS = 4
POOL_GROUP = 64
POOL_WIDTH = N_POOL_GROUPS * POOL_GROUP
CONV_WIDTH = 256
CONV_K = 3
MIX_WIDTH = MLA_WIDTH + POOL_WIDTH + CONV_WIDTH
IN_SIZES = (Q_LORA_RANK, KV_LORA_RANK, QK_ROPE_DIM, POOL_WIDTH, CONV_WIDTH, CONV_WIDTH, CONV_WIDTH)
D_IN = Q_LORA_RANK + KV_LORA_RANK + QK_ROPE_DIM + POOL_WIDTH + 3 * CONV_WIDTH
N_EXPERTS = 16
EXPERT_DIM = 512
CAPACITY_FACTOR = 2
ALPHA = (2 * DEPTH) ** 0.25
BETA = (8 * DEPTH) ** -0.25
RMS_EPS = 1e-6
LN_EPS = 1e-5

kernel_name = 'hybrid_mla_pool_conv_expert_choice_dit_step'


def rms_norm(x, g):
    xf = x.astype(jnp.float32)
    y = xf * lax.rsqrt(jnp.mean(xf * xf, axis=-1, keepdims=True) + RMS_EPS)
    return (y * g.astype(jnp.float32)).astype(x.dtype)


def layer_norm(x, g, b):
    xf = x.astype(jnp.float32)
    mu = jnp.mean(xf, axis=-1, keepdims=True)
    var = jnp.mean(jnp.square(xf - mu), axis=-1, keepdims=True)
    y = (xf - mu) * lax.rsqrt(var + LN_EPS)
    return (y * g.astype(jnp.float32) + b.astype(jnp.float32)).astype(x.dtype)


def split_cols(x, sizes):
    out, off = [], 0
    for s in sizes:
        out.append(x[..., off:off + s])
        off += s
    return out


def axial_rope_tables(n_tokens):
    rows_n = n_tokens // GRID_W
    r, cl = jnp.meshgrid(jnp.arange(rows_n, dtype=jnp.float32),
                         jnp.arange(GRID_W, dtype=jnp.float32), indexing='ij')
    inv = ROPE_THETA ** (-jnp.arange(0, AXIS_ROPE_DIM, 2, dtype=jnp.float32) / AXIS_ROPE_DIM)
    ang = jnp.concatenate([r.reshape(-1)[:, None] * inv, cl.reshape(-1)[:, None] * inv], axis=-1)
    return jnp.cos(ang), jnp.sin(ang)


def apply_rope(x, cos, sin):
    xf = x.astype(jnp.float32)
    x1, x2 = xf[..., :ROPE_HALF], xf[..., ROPE_HALF:]
    return jnp.concatenate([x1 * cos - x2 * sin, x2 * cos + x1 * sin], axis=-1).astype(x.dtype)


def blocked_attention(q_nope, q_rope, k_nope, k_rope, v):
    B, T, H, _ = q_nope.shape
    nb = T // Q_BLOCK

    def one_block(args):
        qn, qr = args
        s = (jnp.einsum('bqhd,bkhd->bhqk', qn, k_nope, preferred_element_type=jnp.float32)
             + jnp.einsum('bqhr,bkr->bhqk', qr, k_rope, preferred_element_type=jnp.float32))
        p = jax.nn.softmax(s * ATTN_SCALE, axis=-1).astype(v.dtype)
        return jnp.einsum('bhqk,bkhd->bqhd', p, v)

    qn_b = q_nope.reshape(B, nb, Q_BLOCK, H, QK_NOPE_DIM).swapaxes(0, 1)
    qr_b = q_rope.reshape(B, nb, Q_BLOCK, H, QK_ROPE_DIM).swapaxes(0, 1)
    o = lax.map(one_block, (qn_b, qr_b))
    return o.swapaxes(0, 1).reshape(B, T, H * V_HEAD_DIM)


def mla_mixer(q_c, kv_c, k_r, q_norm, w_uq, kv_norm, w_uk, w_uv, rope, ctx):
    B, T, _ = q_c.shape
    c_kv = rms_norm(kv_c, kv_norm)
    q = (rms_norm(q_c, q_norm) @ w_uq).reshape(B, T, MLA_HEADS, QK_NOPE_DIM + QK_ROPE_DIM)
    q_nope, q_rope = q[..., :QK_NOPE_DIM], q[..., QK_NOPE_DIM:]
    k_rope = k_r
    if rope is not None:
        cos, sin = rope
        q_rope = apply_rope(q_rope, cos[:, None, :], sin[:, None, :])
        k_rope = apply_rope(k_r, cos, sin)
    kv_all, kr_all = c_kv, k_rope
    if ctx is not None:
        ckv_ctx, kr_ctx = ctx
        kv_all = jnp.concatenate([ckv_ctx.astype(c_kv.dtype), c_kv], axis=1)
        kr_all = jnp.concatenate([kr_ctx.astype(k_rope.dtype), k_rope], axis=1)
    S = kv_all.shape[1]
    k_nope = (kv_all @ w_uk).reshape(B, S, MLA_HEADS, QK_NOPE_DIM)
    v = (kv_all @ w_uv).reshape(B, S, MLA_HEADS, V_HEAD_DIM)
    return blocked_attention(q_nope, q_rope, k_nope, kr_all, v), c_kv


def pool_mixer(p, pool_w, pool_scale):
    B, T, _ = p.shape
    pg = p.reshape(B, T, N_POOL_GROUPS, POOL_GROUP)
    cs = jnp.concatenate([jnp.zeros((B, 1, N_POOL_GROUPS, POOL_GROUP), jnp.float32),
                          jnp.cumsum(pg.astype(jnp.float32), axis=1)], axis=1)
    t = jnp.arange(T)
    means = []
    for g, w in enumerate(POOL_WINDOWS):
        lo = jnp.clip(t - w // 2, 0, T)
        hi = jnp.clip(t + w - w // 2, 0, T)
        cnt = (hi - lo).astype(jnp.float32)[None, :, None]
        means.append((cs[:, hi, g] - cs[:, lo, g]) / cnt)
    pooled = jnp.stack(means, axis=2).astype(p.dtype) - pg
    mixed = jnp.einsum('btgc,gcd->btgd', pooled, pool_w).reshape(B, T, POOL_WIDTH)
    return mixed * pool_scale


def short_conv(y, conv_w):
    return lax.conv_general_dilated(y, conv_w[:, None, :].astype(y.dtype), window_strides=(1,),
                                    padding='SAME', dimension_numbers=('NWC', 'WIO', 'NWC'),
                                    feature_group_count=y.shape[-1])


def expert_choice_ffn(u, w_router, w_gate, w_up, w_down):
    B, T, D = u.shape
    cap = CAPACITY_FACTOR * T // N_EXPERTS
    aff = jax.nn.softmax((u @ w_router).astype(jnp.float32), axis=-1)
    g, idx = lax.top_k(aff.swapaxes(1, 2), cap)
    xs = jax.vmap(lambda ub, ib: ub[ib])(u, idx)
    hdn = (jax.nn.silu(jnp.einsum('becd,edf->becf', xs, w_gate))
           * jnp.einsum('becd,edf->becf', xs, w_up))
    ye = jnp.einsum('becf,efd->becd', hdn, w_down) * g[..., None].astype(u.dtype)

    def scatter_one(ib, yb):
        return jnp.zeros((T, D), yb.dtype).at[ib.reshape(-1)].add(yb.reshape(-1, D))

    return jax.vmap(scatter_one)(idx, ye)


def trunk_layer(x, mod, w_in, q_norm, w_uq, kv_norm, w_uk, w_uv, pool_w, pool_scale, conv_w,
                w_out, ln1_g, ln1_b, ln2_g, ln2_b, w_router, w_gate, w_up, w_down, rope, ctx):
    shift1, scale1, gate1, shift2, scale2, gate2 = jnp.split(mod.astype(x.dtype), 6, axis=-1)
    u = x * (1 + scale1) + shift1
    q_c, kv_c, k_r, p_in, g_b, g_c, h_in = split_cols(u @ w_in, IN_SIZES)
    a_out, c_kv = mla_mixer(q_c, kv_c, k_r, q_norm, w_uq, kv_norm, w_uk, w_uv, rope, ctx)
    b_out = pool_mixer(p_in, pool_w, pool_scale)
    c_out = g_b * short_conv(g_c * h_in, conv_w)
    mix = jnp.concatenate([a_out, b_out, c_out], axis=-1) @ w_out
    x = layer_norm(ALPHA * x + gate1 * mix, ln1_g, ln1_b)
    u = x * (1 + scale2) + shift2
    x = layer_norm(ALPHA * x + gate2 * expert_choice_ffn(u, w_router, w_gate, w_up, w_down), ln2_g, ln2_b)
    return x, c_kv, k_r


def setup_inputs(seed: int = 0) -> dict:
    key = jax.random.key(seed)
    ks = jax.random.split(key, 32)

    def nrm(k, shape, scale):
        return jax.random.normal(k, shape, jnp.float32) * scale

    return {
        'x_prompt': nrm(ks[0], (BATCH, SEQ, D_MODEL), 1.0),
        'x_sample': nrm(ks[1], (DEC_BATCH, DEC_SEQ, D_MODEL), 1.0),
        'cache_ckv': nrm(ks[2], (DEC_BATCH, DEPTH, PAST_LEN, KV_LORA_RANK), 1.0),
        'cache_krope': nrm(ks[3], (DEC_BATCH, DEPTH, PAST_LEN, QK_ROPE_DIM), 1.0),
        'c': nrm(ks[4], (DEC_BATCH, D_MODEL), 1.0),
        'c_ctx': nrm(ks[5], (D_MODEL,), 1.0),
        'w_in': nrm(ks[6], (DEPTH, D_MODEL, D_IN), D_MODEL ** -0.5),
        'q_norm': 1.0 + nrm(ks[7], (DEPTH, Q_LORA_RANK), 0.02),
        'w_uq': nrm(ks[8], (DEPTH, Q_LORA_RANK, MLA_HEADS * (QK_NOPE_DIM + QK_ROPE_DIM)), Q_LORA_RANK ** -0.5),
        'kv_norm': 1.0 + nrm(ks[9], (DEPTH, KV_LORA_RANK), 0.02),
        'w_uk': nrm(ks[10], (DEPTH, KV_LORA_RANK, MLA_HEADS * QK_NOPE_DIM), KV_LORA_RANK ** -0.5),
        'w_uv': nrm(ks[11], (DEPTH, KV_LORA_RANK, MLA_HEADS * V_HEAD_DIM), BETA * KV_LORA_RANK ** -0.5),
        'pool_w': nrm(ks[12], (DEPTH, N_POOL_GROUPS, POOL_GROUP, POOL_GROUP), POOL_GROUP ** -0.5),
        'pool_scale': 1.0 + nrm(ks[13], (DEPTH, POOL_WIDTH), 0.1),
        'conv_w': nrm(ks[14], (DEPTH, CONV_K, CONV_WIDTH), CONV_K ** -0.5),
        'w_out': nrm(ks[15], (DEPTH, MIX_WIDTH, D_MODEL), BETA * MIX_WIDTH ** -0.5),
        'w_ada': nrm(ks[16], (DEPTH, D_MODEL, 6 * D_MODEL), 0.5 * D_MODEL ** -0.5),
        'b_ada': nrm(ks[17], (DEPTH, 6 * D_MODEL), 0.02),
        'ln1_g': 1.0 + nrm(ks[18], (DEPTH, D_MODEL), 0.02),
        'ln1_b': nrm(ks[19], (DEPTH, D_MODEL), 0.02),
        'ln2_g': 1.0 + nrm(ks[20], (DEPTH, D_MODEL), 0.02),
        'ln2_b': nrm(ks[21], (DEPTH, D_MODEL), 0.02),
        'w_router': nrm(ks[22], (DEPTH, D_MODEL, N_EXPERTS), D_MODEL ** -0.5),
        'w_gate': nrm(ks[23], (DEPTH, N_EXPERTS, D_MODEL, EXPERT_DIM), D_MODEL ** -0.5),
        'w_up': nrm(ks[24], (DEPTH, N_EXPERTS, D_MODEL, EXPERT_DIM), D_MODEL ** -0.5),
        'w_down': nrm(ks[25], (DEPTH, N_EXPERTS, EXPERT_DIM, D_MODEL), BETA * EXPERT_DIM ** -0.5),
    }


def reference(x_prompt, x_sample, cache_ckv, cache_krope, c, c_ctx, w_in, q_norm, w_uq, kv_norm,
              w_uk, w_uv, pool_w, pool_scale, conv_w, w_out, w_ada, b_ada, ln1_g, ln1_b, ln2_g,
              ln2_b, w_router, w_gate, w_up, w_down):
    def layer_params(l):
        return (w_in[l], q_norm[l], w_uq[l], kv_norm[l], w_uk[l], w_uv[l], pool_w[l], pool_scale[l],
                conv_w[l], w_out[l], ln1_g[l], ln1_b[l], ln2_g[l], ln2_b[l], w_router[l], w_gate[l],
                w_up[l], w_down[l])

    x = x_prompt
    ckv_list, kr_list = [], []
    for l in range(DEPTH):
        mod = (jax.nn.silu(c_ctx) @ w_ada[l] + b_ada[l])[None, None, :]
        x, ckv, kr = trunk_layer(x, mod, *layer_params(l), rope=None, ctx=None)
        ckv_list.append(ckv)
        kr_list.append(kr)
    y_prompt = x
    new_ckv = jnp.stack(ckv_list, axis=1)
    new_krope = jnp.stack(kr_list, axis=1)

    cos, sin = axial_rope_tables(x_sample.shape[1])
    x = x_sample
    for l in range(DEPTH):
        mod = (jax.nn.silu(c) @ w_ada[l] + b_ada[l])[:, None, :]
        x, _, _ = trunk_layer(x, mod, *layer_params(l), rope=(cos, sin),
                              ctx=(cache_ckv[:, l], cache_krope[:, l]))
    y_sample = x
    return (y_prompt, y_sample, new_ckv, new_krope)
```

```python
import numpy as np
import ml_dtypes
from contextlib import ExitStack
import concourse.bass as bass
import concourse.mybir as mybir
from concourse.bass_utils import run_bass_kernel_spmd

F32 = mybir.dt.float32
BF16 = mybir.dt.bfloat16
I32 = mybir.dt.int32
AF = mybir.ActivationFunctionType
ALU = mybir.AluOpType
AX = mybir.AxisListType

D = 1024
DEPTH = 4
T_S = 4096
T_P = 256
PAST = 512
NH = 4
DN = 128
DR = 64
DV = 128
QL = 384
KVL = 256
NE = 16
EF = 512
ALPHA = (2 * DEPTH) ** 0.25
RMS_EPS = 1e-6
LN_EPS = 1e-5
ATTN_SCALE = (DN + DR) ** -0.5
CH = 256
HAL = 8
CW = CH + 2 * HAL
VA = 130
NCOL = 1792
C_Q, C_KV, C_KR, C_P, C_GB, C_GC, C_H, C_KRS = 0, 384, 640, 704, 960, 1216, 1472, 1728
RB = [0, T_S, T_S + T_P]
TTOT = T_S + 2 * T_P
REQ_T = [T_S, T_P, T_P]
REQ_CAP = [2 * T_S // NE, 2 * T_P // NE, 2 * T_P // NE]
ROWW = 1060
NV = 93
V_BADA, V_L1G, V_L1B, V_L2G, V_L2B, V_QN, V_KVN, V_PS, V_CW = 0, 48, 56, 64, 72, 80, 83, 85, 87
BIGIDX = float(1 << 20)


class Tok:
    __slots__ = ("sem", "val")

    def __init__(self, sem=None, val=0):
        self.sem = sem
        self.val = val


class Buf:
    __slots__ = ("name", "w", "r")

    def __init__(self, name):
        self.name = name
        self.w = None
        self.r = []


class Eng:
    def __init__(self, K, eng, name):
        self.K = K
        self.e = eng
        self.name = name
        self.sem = None
        self.cnt = 0
        self.nsem = 0
        self.waited = {}
        self.pending = []
        self.last = None
        self.nins = 0
        self._newsem()

    def _newsem(self):
        self.sem = self.K.es.enter_context(self.K.nc.semaphore(f"p_{self.name}{self.nsem}"))
        self.nsem += 1
        self.cnt = 0

    def wait(self, tok):
        if tok is None:
            return
        assert tok.sem is not None, f"wait on unsignalled token ({self.name})"
        key = id(tok.sem)
        if self.waited.get(key, (None, 0))[1] >= tok.val:
            return
        self.e.wait_ge(tok.sem, tok.val)
        self.waited[key] = (tok.sem, tok.val)

    def begin(self, R, W):
        for b in R:
            if b.w is not None:
                self.wait(b.w[1])
        for b in W:
            if b.w is not None and b.w[0] != self.name:
                self.wait(b.w[1])
            for (en, tk) in b.r:
                if en != self.name:
                    self.wait(tk)

    def end(self, ins, R, W, sig=True):
        tok = Tok()
        self.pending.append(tok)
        for b in R:
            b.r.append((self.name, tok))
        for b in W:
            b.w = (self.name, tok)
            b.r = []
        if sig:
            if self.cnt >= 30000:
                self._newsem()
            self.cnt += 1
            ins.then_inc(self.sem, 1)
            for t in self.pending:
                t.sem = self.sem
                t.val = self.cnt
            self.pending = []
            self.last = tok
        return tok

    def op(self, fn, R=(), W=(), sig=True):
        self.begin(R, W)
        ins = fn(self.e)
        self.nins += 1
        return self.end(ins, R, W, sig)


class DmaQ:
    def __init__(self, K, E, nsem):
        self.K = K
        self.E = E
        self.sems = [K.es.enter_context(K.nc.semaphore(f"d_{E.name}{i}")) for i in range(nsem)]
        self.vals = [0] * nsem
        self.last = [None] * nsem
        self.i = 0

    def issue(self, fn, R=(), W=()):
        E = self.E
        E.begin(R, W)
        j = self.i
        self.i = (self.i + 1) % len(self.sems)
        if self.vals[j] >= 30000:
            E.wait(self.last[j])
            self.sems[j] = self.K.es.enter_context(self.K.nc.semaphore(f"d_{E.name}{j}_{self.K.uid()}"))
            self.vals[j] = 0
            self.last[j] = None
        E.wait(self.last[j])
        ins = fn(E.e)
        self.vals[j] += 16
        ins.then_inc(self.sems[j], 16)
        tok = Tok(self.sems[j], self.vals[j])
        self.last[j] = tok
        for b in R:
            b.r.append(("dma", tok))
        for b in W:
            b.w = ("dma", tok)
            b.r = []
        return tok


class Ring:
    def __init__(self, items):
        self.items = items
        self.i = 0

    def get(self):
        it = self.items[self.i]
        self.i = (self.i + 1) % len(self.items)
        return it


class T:
    __slots__ = ("t", "b")

    def __init__(self, t, b):
        self.t = t
        self.b = b


class KB:
    def __init__(self, depth=DEPTH, debug=None):
        self.depth = depth
        self.debug = debug
        self.es = ExitStack()
        self.nc = bass.Bass("TRN2", target_bir_lowering=False)
        self._uid = 0
        nc = self.nc
        self.pe = Eng(self, nc.tensor, "pe")
        self.act = Eng(self, nc.scalar, "act")
        self.dve = Eng(self, nc.vector, "dve")
        self.pool = Eng(self, nc.gpsimd, "pool")
        self.sp = Eng(self, nc.sync, "sp")
        self.engs = [self.pe, self.act, self.dve, self.pool, self.sp]
        self.qsp = DmaQ(self, self.sp, 12)
        self.qpl = DmaQ(self, self.pool, 12)
        self.dbg_outs = []
        self.marks = []

    def mark(self, label):
        self.marks.append((label, self.pe.nins))

    def uid(self):
        self._uid += 1
        return self._uid

    def dram_in(self, name, shape, dt=F32):
        return self.nc.dram_tensor(name, list(shape), dt, kind="ExternalInput").ap()

    def dram_out(self, name, shape, dt=F32):
        return self.nc.dram_tensor(name, list(shape), dt, kind="ExternalOutput").ap()

    def dram_scr(self, name, shape, dt=F32):
        return self.nc.dram_tensor(name, list(shape), dt, kind="Internal").ap()

    def sbt(self, name, shape, dt):
        h = self.es.enter_context(self.nc.sbuf_tensor("s_" + name, list(shape), dt))
        return T(h, Buf(name))

    def carve_reset(self):
        self.aoff = 0

    def av(self, name, shape, dt):
        n = 1
        for s in shape[1:]:
            n *= s
        words = (n * (2 if dt == BF16 else 4) + 3) // 4
        words = (words + 7) // 8 * 8
        assert self.aoff + words <= self.arena_words, f"arena overflow at {name}: {self.aoff + words}"
        v = self.arena[:, self.aoff:self.aoff + words]
        self.aoff += words
        if dt != F32:
            v = v.bitcast(dt)
        v = v[:, 0:n]
        if len(shape) == 3:
            v = v.rearrange("p (a b) -> p a b", a=shape[1])
        elif len(shape) == 4:
            v = v.rearrange("p (a b c) -> p a b c", a=shape[1], b=shape[2])
        return T(v, Buf(name))

    def barrier(self):
        toks = [e.last for e in self.engs if e.last is not None]
        for q in (self.qsp, self.qpl):
            toks += [t for t in q.last if t is not None]
        for e in self.engs:
            for t in toks:
                e.wait(t)

    def dma(self, q, out, in_, R=(), W=(), **kw):
        return q.issue(lambda e: e.dma_start(out=out, in_=in_, **kw), R=R, W=W)

    def ps_get(self):
        return self.psring.get()

    def mm(self, out, lhsT, rhs, start, stop, R, W, sig=None):
        if sig is None:
            sig = stop
        return self.pe.op(lambda e: e.matmul(out, lhsT, rhs, start=start, stop=stop), R=R, W=W, sig=sig)

    def tr(self, out, in_, ident, R, W, sig=True):
        return self.pe.op(lambda e: e.transpose(out, in_, ident), R=R, W=W, sig=sig)

    def A(self, out, in_, func, R, W, **kw):
        return self.act.op(lambda e: e.activation(out=out, in_=in_, func=func, **kw), R=R, W=W)

    def V_tt(self, out, in0, in1, op, R, W, eng=None):
        eng = eng or self.dve
        return eng.op(lambda e: e.tensor_tensor(out=out, in0=in0, in1=in1, op=op), R=R, W=W)

    def V_ts(self, out, in0, s1, s2, op0, op1=None, R=(), W=(), eng=None, accum_out=None):
        eng = eng or self.dve
        kw = {}
        if op1 is not None:
            kw["op1"] = op1
        if accum_out is not None:
            kw["accum_out"] = accum_out
        return eng.op(lambda e: e.tensor_scalar(out=out, in0=in0, scalar1=s1, scalar2=s2, op0=op0, **kw), R=R, W=W)

    def V_stt(self, out, in0, scalar, in1, op0, op1, R, W):
        return self.dve.op(lambda e: e.scalar_tensor_tensor(out=out, in0=in0, scalar=scalar, in1=in1, op0=op0, op1=op1), R=R, W=W)

    def V_cp(self, out, in_, R, W, eng=None):
        eng = eng or self.dve
        return eng.op(lambda e: e.tensor_copy(out=out, in_=in_), R=R, W=W)

    def V_rcp(self, out, in_, R, W):
        return self.dve.op(lambda e: e.reciprocal(out=out, in_=in_), R=R, W=W)

    def memset(self, eng, ap, val, W):
        return eng.op(lambda e: e.memset(ap, val), R=(), W=W)

    def setup(self):
        nc = self.nc
        L = DEPTH
        di = self.dram_in
        self.xs_in = di("xs_in", [T_S, D])
        self.xp_in = di("xp_in", [2 * T_P, D])
        self.cckv = di("cckv", [L, PAST, KVL])
        self.ckr = di("ckr", [L, PAST, DR])
        self.condT = di("condT", [128, 8, 2])
        self.w_in_d = di("w_in_x", [L, 128, 8, NCOL])
        self.w_uq_d = di("w_uq_x", [L, 128, 3, 1024])
        self.w_uk_d = di("w_uk_x", [L, 128, 2, 512])
        self.w_uv_d = di("w_uv_x", [L, 128, 2, 512])
        self.poolw_d = di("poolw_x", [L, 128, 2, 256])
        self.w_out_d = di("w_out_x", [L, 128, 8, D])
        self.w_r_d = di("w_r_x", [L, 128, 8, NE])
        self.w_ada_d = di("w_ada", [L, D, 6 * D])
        self.vecs_d = di("vecs", [L, 128, NV])
        self.w_gate_d = di("w_gate", [L, NE, D, EF])
        self.w_up_d = di("w_up", [L, NE, D, EF])
        self.w_down_d = di("w_down", [L, NE, EF, D])
        self.identF_d = di("identF", [128, 128])
        self.gsum_d = di("gsum", [128, 128])
        self.lstrict_d = di("lstrict", [128, 128])
        self.iota1_d = di("iota1", [128, 512])
        self.tidhl_d = di("tidhl", [128, 36, 2])
        self.edges_d = di("edges", [128, 2, 16])
        self.misc_d = di("misc", [128, 16])
        self.cosT_d = di("cosT", [DR, T_S])
        self.sinT_d = di("sinT", [DR, T_S])
        self.tidcol_d = di("tidcol", [128, 36])
        self.y_s = self.dram_out("y_s", [T_S, D])
        self.y_p = self.dram_out("y_p", [2 * T_P, D])
        self.nckv = self.dram_out("nckv", [2, L, T_P, KVL])
        self.nkr = self.dram_out("nkr", [2, L, T_P, DR])
        self.xT_d = self.dram_scr("xT_d", [D, TTOT])
        self.x1T_d = self.dram_scr("x1T_d", [D, TTOT])
        self.u2rows_d = self.dram_scr("u2rows_d", [TTOT, ROWW], BF16)
        self.ffn_d = self.dram_scr("ffn_d", [TTOT, D])
        self.xT_b = [Buf(f"xT_d{c}") for c in range(TTOT // CH)]
        self.x1T_b = [Buf(f"x1T_d{c}") for c in range(TTOT // CH)]
        self.u2rows_b = Buf("u2rows_d")
        self.ffn_b = Buf("ffn_d")
        self.out_b = Buf("outs")

        s = self.sbt
        self.w_in = s("w_in", [128, 8, NCOL], BF16)
        self.w_uq = s("w_uq", [128, 3, 1024], BF16)
        self.w_uk = s("w_uk", [128, 2, 512], BF16)
        self.w_uv = s("w_uv", [128, 2, 512], BF16)
        self.poolw = s("poolw", [128, 2, 256], BF16)
        self.w_out = s("w_out", [128, 8, D], BF16)
        self.w_r = s("w_r", [128, 8, NE], F32)
        self.vec = s("vec", [128, NV], F32)
        self.modT = s("modT", [128, 48, 2], F32)
        self.s1p = s("s1p", [128, 8, 2], F32)
        self.g1a = s("g1a", [128, 8, 2], F32)
        self.G2 = s("G2", [128, 8, 2], F32)
        self.B2 = s("B2", [128, 8, 2], F32)
        self.g2a = s("g2a", [128, 8, 2], F32)
        self.wr2 = s("wr2", [128, 2, 8, NE], F32)
        self.rconst = s("rconst", [1, 2, NE], F32)
        self.scT = s("scT", [128, 8, 2], F32)
        self.identF = s("identF", [128, 128], F32)
        self.identB = s("identB", [128, 128], BF16)
        self.onesB = s("onesB", [128, 128], BF16)
        self.onesrow = s("onesrow", [1, 128], F32)
        self.tidhl = s("tidhl", [128, 36, 2], BF16)
        self.tidcol = s("tidcol", [128, 36], F32)
        self.edges = s("edges", [128, 2, 16], F32)
        self.misc = s("misc", [128, 16], F32)
        self.affS = s("affS", [128, 36, NE], F32)
        self.xt = s("xt", [128, 8, CW], F32)
        self.ubuf = s("ubuf", [128, 8, CW], BF16)
        self.st = [s(f"st{i}", [128, CH], F32) for i in range(3)]
        self.stf = [s(f"stf{i}", [128, CH], F32) for i in range(2)]
        self.u2b = [Buf(f"u2b{k}") for k in range(8)]
        self.zbk = Ring([s(f"zbk{i}", [128, CH], BF16) for i in range(2)])
        self.zsqk = Ring([s(f"zsqk{i}", [128, CH], BF16) for i in range(2)])
        self.xtb = [Buf(f"xt{k}") for k in range(8)]
        self.ubb = [Buf(f"ub{k}") for k in range(8)]
        self.affSb = [Buf(f"affS{g}") for g in range(TTOT // 128)]
        self.u2rows_bt = [Buf(f"u2r{g}") for g in range(TTOT // 128)]
        self.ffn_bt = [Buf(f"ffn{g}") for g in range(TTOT // 128)]
        self.vTM = [s("vTM0", [128, 4, 128], F32), s("vTM1", [128, 2, NE], F32), s("vTM2", [128, 2, NE], F32)]
        self.arena_words = 118 * 256
        self.arena = self.es.enter_context(nc.sbuf_tensor("arena", [128, self.arena_words], F32))
        banks = [T(self.es.enter_context(nc.psum_tensor(f"ps{i}", [128, 512], F32)), Buf(f"ps{i}")) for i in range(8)]
        self.psring = Ring(banks[:6])
        self.psS = Ring(banks[:3])
        self.psB = Ring(banks[3:6])
        self.psacc = Ring([(banks[6], banks[7])])
        self.bcreg = None

    def load_consts(self):
        q = self.qsp
        for (dst, src) in [(self.identF, self.identF_d), (self.edges, self.edges_d),
                           (self.misc, self.misc_d), (self.scT, self.condT), (self.tidcol, self.tidcol_d)]:
            self.dma(q, dst.t[:], src, W=[dst.b])
        self.V_cp(self.identB.t[:], self.identF.t[:], R=[self.identF.b], W=[self.identB.b])
        self.carve_reset()
        tf = self.av("tidhlF", [128, 36, 2], F32)
        self.dma(q, tf.t[:], self.tidhl_d, W=[tf.b])
        self.V_cp(self.tidhl.t[:], tf.t[:], R=[tf.b], W=[self.tidhl.b])
        self.barrier()
        self.memset(self.dve, self.onesB.t[:], 1.0, W=[self.onesB.b])
        self.memset(self.dve, self.onesrow.t[:], 1.0, W=[self.onesrow.b])
        self.A(self.scT.t[:], self.scT.t[:], AF.Silu, R=[self.scT.b], W=[self.scT.b])

    def transpose_in(self):
        self.carve_reset()
        tin = [self.av(f"tin{i}", [128, D], F32) for i in range(2)]
        tout = [self.av(f"tout{i}", [128, 8, 128], F32) for i in range(2)]
        xTv = self.xT_d.rearrange("(k p) t -> p k t", p=128)
        for g in range(TTOT // 128):
            src = self.xs_in[g * 128:(g + 1) * 128, :] if g < T_S // 128 else self.xp_in[(g - T_S // 128) * 128:(g - T_S // 128 + 1) * 128, :]
            ti = tin[g % 2]
            to = tout[g % 2]
            self.dma(self.qsp, ti.t[:], src, W=[ti.b])
            for hf in range(2):
                ps = self.ps_get()
                for kk in range(4):
                    k = hf * 4 + kk
                    self.tr(ps.t[:, kk * 128:(kk + 1) * 128], ti.t[:, k * 128:(k + 1) * 128], self.identF.t[:],
                            R=[ti.b, self.identF.b], W=[ps.b], sig=(kk == 3))
                if hf == 0:
                    self.A(to.t[:, 0:4, :], ps.t[:, :].rearrange("p (a b) -> p a b", a=4), AF.Copy, R=[ps.b], W=[to.b])
                else:
                    self.V_cp(to.t[:, 4:8, :], ps.t[:, :].rearrange("p (a b) -> p a b", a=4), R=[ps.b], W=[to.b])
            cb = self.xT_b[g // 2]
            self.dma(self.qsp, xTv[:, :, g * 128:(g + 1) * 128], to.t[:], R=[to.b], W=[cb])

    def load_layer_weights(self, l):
        q = self.qpl
        for k in range(8):
            self.dma(q, self.w_in.t[:, k, :], self.w_in_d[l][:, k, :], W=[self.w_in.b], max_dma_last_dim=4096)
        for k in range(3):
            self.dma(q, self.w_uq.t[:, k, :], self.w_uq_d[l][:, k, :], W=[self.w_uq.b], max_dma_last_dim=4096)
        self.dma(q, self.w_uk.t[:], self.w_uk_d[l], W=[self.w_uk.b], max_dma_last_dim=2048)
        self.dma(q, self.w_uv.t[:], self.w_uv_d[l], W=[self.w_uv.b], max_dma_last_dim=2048)
        self.dma(q, self.poolw.t[:], self.poolw_d[l], W=[self.poolw.b], max_dma_last_dim=1024)
        for k in range(8):
            self.dma(q, self.w_out.t[:, k, :], self.w_out_d[l][:, k, :], W=[self.w_out.b], max_dma_last_dim=4096)
        self.dma(self.qsp, self.w_r.t[:], self.w_r_d[l], W=[self.w_r.b])

    def mod_phase(self, l):
        self.carve_reset()
        wst = [self.av(f"wst{i}", [128, 8, 512], F32) for i in range(2)]
        tmp = self.av("modtmp", [128, 8, 2], F32)
        vec, modT = self.vec, self.modT
        self.dma(self.qsp, vec.t[:], self.vecs_d[l], W=[vec.b])
        for blk in range(12):
            w = wst[blk % 2]
            self.dma(self.qsp, w.t[:], self.w_ada_d[l][:, blk * 512:(blk + 1) * 512].rearrange("(k p) n -> p k n", p=128), W=[w.b])
            ps = self.ps_get()
            for j in range(4):
                for k in range(8):
                    self.mm(ps.t[:, j * 2:(j + 1) * 2], w.t[:, k, j * 128:(j + 1) * 128], self.scT.t[:, k, :],
                            start=(k == 0), stop=(k == 7), R=[w.b, self.scT.b], W=[ps.b])
            for j in range(4):
                jj = blk * 4 + j
                self.V_ts(modT.t[:, jj, :], ps.t[:, j * 2:(j + 1) * 2], vec.t[:, V_BADA + jj:V_BADA + jj + 1], None, ALU.add,
                          R=[ps.b, vec.b], W=[modT.b])
        sh1, sc1, gt1 = modT.t[:, 0:8, :], modT.t[:, 8:16, :], modT.t[:, 16:24, :]
        sh2, sc2, gt2 = modT.t[:, 24:32, :], modT.t[:, 32:40, :], modT.t[:, 40:48, :]
        mb = [modT.b]
        self.V_ts(self.s1p.t[:], sc1, 1.0, None, ALU.add, R=mb, W=[self.s1p.b])
        self.V_ts(self.g1a.t[:], gt1, 1.0 / ALPHA, None, ALU.mult, R=mb, W=[self.g1a.b])
        self.V_ts(self.g2a.t[:], gt2, 1.0 / ALPHA, None, ALU.mult, R=mb, W=[self.g2a.b])
        self.V_ts(tmp.t[:], sc2, 1.0, None, ALU.add, R=mb, W=[tmp.b])
        for c in range(2):
            self.V_tt(self.G2.t[:, :, c], tmp.t[:, :, c], vec.t[:, V_L1G:V_L1G + 8], ALU.mult, R=[tmp.b, vec.b], W=[self.G2.b])
            self.V_tt(self.B2.t[:, :, c], tmp.t[:, :, c], vec.t[:, V_L1B:V_L1B + 8], ALU.mult, R=[tmp.b, vec.b], W=[self.B2.b])
        self.V_tt(self.B2.t[:], self.B2.t[:], sh2, ALU.add, R=[self.B2.b] + mb, W=[self.B2.b])
        for c in range(2):
            for k in range(8):
                self.V_ts(self.wr2.t[:, c, k, :], self.w_r.t[:, k, :], self.G2.t[:, k, c:c + 1], None, ALU.mult,
                          R=[self.w_r.b, self.G2.b], W=[self.wr2.b])
            ps = self.ps_get()
            for k in range(8):
                self.mm(ps.t[0:1, 0:NE], self.B2.t[:, k, c:c + 1], self.w_r.t[:, k, :], start=(k == 0), stop=(k == 7),
                        R=[self.B2.b, self.w_r.b], W=[ps.b])
            self.V_cp(self.rconst.t[0:1, c, :], ps.t[0:1, 0:NE], R=[ps.b], W=[self.rconst.b])

    def carve_ab(self):
        self.carve_reset()
        a = self.av
        self.KT = a("KT", [128, NH, PAST + T_S], BF16)
        self.krT = a("krT", [128, PAST + T_S], BF16)
        self.Vt = a("Vt", [128, (PAST + T_S) // 128, NH * VA], BF16)
        nkc = (PAST + T_S) // CH
        self.KTb = [Buf(f"KT{i}") for i in range(nkc)]
        self.krTb = [Buf(f"krT{i}") for i in range(nkc)]
        self.Vb = [Buf(f"V{i}") for i in range(nkc)]
        self.qn = a("qn", [128, 3, CH], BF16)
        self.qnope = a("qnope", [128, NH, CH], BF16)
        self.qrope = a("qrope", [128, NH, CH], BF16)
        self.sqb = a("sqb", [128, 3, CH], BF16)
        self.PT = Ring([a(f"PT{i}", [128, 2, CH], BF16) for i in range(2)])
        pb_off = self.aoff
        self.pb = [a(f"pb{i}", [128, 2, CW], F32) for i in range(4)]
        assert self.aoff - pb_off == 8 * CW
        self.xnext = self.arena[:, pb_off:pb_off + 8 * CW].rearrange("p (a b) -> p a b", a=8)
        self.gbS = a("gbS", [128, 2, CH], F32)
        self.pooledB = a("pooledB", [128, 2, CH], BF16)
        self.ckvB = a("ckvB", [128, 2, CH], BF16)
        self.rowbuf = Ring([a(f"rowbuf{i}", [128, ROWW], BF16) for i in range(1)])
        self.u2buf = a("u2buf", [128, 8, CH], BF16)
        for rw in self.rowbuf.items:
            self.memset(self.pool, rw.t[:, D + 2 * NE:ROWW], 0.0, W=[rw.b])
        self.rt = a("rt", [128, 2, CH], F32)
        self.otm = Ring([a(f"otm{i}", [128, DV], BF16) for i in range(2)])
        self.smx = a("smx", [128, 8], F32)
        self.ex = a("ex", [128, NE], F32)
        self.edt = a("edt", [128, 2, 8], F32)
        save = self.aoff
        self.ckvF = a("ckvF", [128, 2, CH], F32)
        self.krF = a("krF", [128, CH], F32)
        self.outT = a("outT", [128, 2, KVL], F32)
        self.krout = a("krout", [128, 2, DR], F32)
        self.aoff = save
        self.ctxF = [a(f"ctxF{i}", [128, KVL], F32) for i in range(2)]
        self.ctxkr = [a(f"ctxkr{i}", [128, DR], F32) for i in range(2)]
        self.memset(self.pool, self.krT.t[DR:128, :], 0.0, W=self.krTb)
        self.memset(self.pool, self.Vt.t[:], 1.0, W=self.Vb)
        self.memset(self.pool, self.qrope.t[DR:128, :, :], 0.0, W=[self.qrope.b])

    def kv_up(self, kc):
        koff = kc * CH
        for h in range(NH):
            ps = self.ps_get()
            for m in range(2):
                self.mm(ps.t[:, 0:CH], self.w_uk.t[:, m, h * 128:(h + 1) * 128], self.ckvB.t[:, m, :],
                        start=(m == 0), stop=(m == 1), R=[self.w_uk.b, self.ckvB.b], W=[ps.b])
            if h % 2 == 0:
                self.A(self.KT.t[:, h, koff:koff + CH], ps.t[:, 0:CH], AF.Copy, R=[ps.b], W=[self.KTb[kc]])
            else:
                self.V_cp(self.KT.t[:, h, koff:koff + CH], ps.t[:, 0:CH], R=[ps.b], W=[self.KTb[kc]])
        for j in range(2):
            ps = self.ps_get()
            for m in range(2):
                self.mm(ps.t[:, :], self.ckvB.t[:, m, j * 128:(j + 1) * 128], self.w_uv.t[:, m, :],
                        start=(m == 0), stop=(m == 1), R=[self.w_uv.b, self.ckvB.b], W=[ps.b])
            kt = koff // 128 + j
            dst = self.Vt.t[:, kt, :].rearrange("p (h v) -> p h v", h=NH)[:, :, 0:DV]
            src = ps.t[:, :].rearrange("p (h v) -> p h v", h=NH)
            if j == 0:
                self.A(dst, src, AF.Copy, R=[ps.b], W=[self.Vb[kc]])
            else:
                self.V_cp(dst, src, R=[ps.b], W=[self.Vb[kc]])

    def ctx_phase(self, l):
        for jj in range(PAST // CH):
            for j in range(2):
                tile = jj * 2 + j
                cf, ck = self.ctxF[tile % 2], self.ctxkr[tile % 2]
                self.dma(self.qsp, cf.t[:], self.cckv[l][tile * 128:(tile + 1) * 128, :], W=[cf.b])
                self.dma(self.qsp, ck.t[:], self.ckr[l][tile * 128:(tile + 1) * 128, :], W=[ck.b])
                ps = self.ps_get()
                for m in range(2):
                    self.tr(ps.t[:, m * 128:(m + 1) * 128], cf.t[:, m * 128:(m + 1) * 128], self.identF.t[:],
                            R=[cf.b, self.identF.b], W=[ps.b], sig=(m == 1))
                self.V_cp(self.ckvB.t[:, :, j * 128:(j + 1) * 128], ps.t[:, 0:256].rearrange("p (m k) -> p m k", m=2),
                          R=[ps.b], W=[self.ckvB.b])
                ps2 = self.ps_get()
                self.tr(ps2.t[0:DR, 0:128], ck.t[:, :], self.identF.t[:], R=[ck.b, self.identF.b], W=[ps2.b])
                self.A(self.krT.t[0:DR, tile * 128:(tile + 1) * 128], ps2.t[0:DR, 0:128], AF.Copy, R=[ps2.b], W=[self.krTb[jj]])
            self.kv_up(jj)

    def u1_gen(self, cond, c0, c1):
        xt, ub = self.xt, self.ubuf
        for k in range(8):
            sc = self.s1p.t[:, k, cond:cond + 1]
            sh = self.modT.t[:, k, cond:cond + 1]
            if k % 2 == 0:
                self.V_ts(ub.t[:, k, c0:c1], xt.t[:, k, c0:c1], sc, sh, ALU.mult, ALU.add,
                          R=[self.xtb[k], self.s1p.b, self.modT.b], W=[self.ubb[k]])
            else:
                self.A(ub.t[:, k, c0:c1], xt.t[:, k, c0:c1], AF.Identity, scale=sc, bias=sh,
                       R=[self.xtb[k], self.s1p.b, self.modT.b], W=[self.ubb[k]])

    def load_rope(self, t0):
        self.dma(self.qsp, self.rt.t[0:DR, 0, :], self.cosT_d[:, t0:t0 + CH], W=[self.rt.b])
        self.dma(self.qsp, self.rt.t[0:DR, 1, :], self.sinT_d[:, t0:t0 + CH], W=[self.rt.b])

    def phase_a_chunk(self, l, r, c):
        cond = 1 if r == 0 else 0
        t0 = c * CH
        g0 = RB[r] + t0
        gc = g0 // CH
        kc = (PAST // CH if r == 0 else 0) + c
        koff = kc * CH
        xTv = self.xT_d.rearrange("(k p) t -> p k t", p=128)
        xt, ub = self.xt, self.ubuf
        self.dma(self.qsp, xt.t[:, :, HAL:HAL + CH], xTv[:, :, g0:g0 + CH], R=[self.xT_b[gc]], W=self.xtb)
        if r == 0:
            self.load_rope(t0)
        self.u1_gen(cond, HAL, HAL + CH)
        pkv = [self.ps_get(), self.ps_get()]
        for m in range(2):
            for k in range(8):
                self.mm(pkv[m].t[:, 0:CH], self.w_in.t[:, k, C_KV + m * 128:C_KV + (m + 1) * 128], ub.t[:, k, HAL:HAL + CH],
                        start=(k == 0), stop=(k == 7), R=[self.w_in.b, self.ubb[k]], W=[pkv[m].b])
        pkr = self.ps_get()
        for k in range(8):
            self.mm(pkr.t[0:DR, 0:CH], self.w_in.t[:, k, C_KR:C_KR + DR], ub.t[:, k, HAL:HAL + CH],
                    start=(k == 0), stop=(k == 7), R=[self.w_in.b, self.ubb[k]], W=[pkr.b])
        if r == 0:
            for k in range(8):
                self.mm(pkr.t[0:DR, CH:2 * CH], self.w_in.t[:, k, C_KRS:C_KRS + DR], ub.t[:, k, HAL:HAL + CH],
                        start=(k == 0), stop=(k == 7), R=[self.w_in.b, self.ubb[k]], W=[pkr.b])
        for m in range(2):
            self.A(self.sqb.t[:, m, :], pkv[m].t[:, 0:CH], AF.Square, R=[pkv[m].b], W=[self.sqb.b])
        pss = self.ps_get()
        for m in range(2):
            self.mm(pss.t[:, 0:CH], self.onesB.t[:], self.sqb.t[:, m, :], start=(m == 0), stop=(m == 1),
                    R=[self.onesB.b, self.sqb.b], W=[pss.b])
        st0 = self.st[0]
        self.A(st0.t[:], pss.t[:, 0:CH], AF.Ln, scale=1.0 / KVL, bias=self.misc.t[:, 0:1], R=[pss.b, self.misc.b], W=[st0.b])
        self.A(st0.t[:], st0.t[:], AF.Exp, scale=-0.5, R=[st0.b], W=[st0.b])
        for m in range(2):
            nrm = self.vec.t[:, V_KVN + m:V_KVN + m + 1]
            if r == 0:
                self.V_stt(self.ckvB.t[:, m, :], pkv[m].t[:, 0:CH], nrm, st0.t[:], ALU.mult, ALU.mult,
                           R=[pkv[m].b, self.vec.b, st0.b], W=[self.ckvB.b])
            else:
                self.V_stt(self.ckvF.t[:, m, :], pkv[m].t[:, 0:CH], nrm, st0.t[:], ALU.mult, ALU.mult,
                           R=[pkv[m].b, self.vec.b, st0.b], W=[self.ckvF.b])
        if r != 0:
            self.V_cp(self.ckvB.t[:], self.ckvF.t[:], R=[self.ckvF.b], W=[self.ckvB.b])
        if r == 0:
            s1, s2 = self.st[1], self.st[2]
            self.V_tt(s1.t[0:DR, :], pkr.t[0:DR, 0:CH], self.rt.t[0:DR, 0, :], ALU.mult, R=[pkr.b, self.rt.b], W=[s1.b])
            self.V_tt(s2.t[0:DR, :], pkr.t[0:DR, CH:2 * CH], self.rt.t[0:DR, 1, :], ALU.mult, R=[pkr.b, self.rt.b], W=[s2.b])
            self.V_tt(self.krT.t[0:DR, koff:koff + CH], s1.t[0:DR, :], s2.t[0:DR, :], ALU.add, R=[s1.b, s2.b], W=[self.krTb[kc]])
        else:
            self.A(self.krF.t[0:DR, :], pkr.t[0:DR, 0:CH], AF.Copy, R=[pkr.b], W=[self.krF.b])
            self.V_cp(self.krT.t[0:DR, koff:koff + CH], self.krF.t[0:DR, :], R=[self.krF.b], W=[self.krTb[kc]])
        self.kv_up(kc)
        if r != 0:
            for j in range(2):
                ps = self.ps_get()
                for m in range(2):
                    self.tr(ps.t[:, m * 128:(m + 1) * 128], self.ckvF.t[:, m, j * 128:(j + 1) * 128], self.identF.t[:],
                            R=[self.ckvF.b, self.identF.b], W=[ps.b], sig=(m == 1))
                self.A(self.outT.t[:, j, :], ps.t[:, 0:KVL], AF.Copy, R=[ps.b], W=[self.outT.b])
            self.dma(self.qsp, self.nckv[r - 1, l].rearrange("(j p) f -> p j f", p=128), self.outT.t[:], R=[self.outT.b], W=[self.out_b])
            ps = self.ps_get()
            for j in range(2):
                self.tr(ps.t[:, j * DR:(j + 1) * DR], self.krF.t[0:DR, j * 128:(j + 1) * 128], self.identF.t[0:DR, 0:DR],
                        R=[self.krF.b, self.identF.b], W=[ps.b], sig=(j == 1))
            self.V_cp(self.krout.t[:], ps.t[:, 0:2 * DR].rearrange("p (j f) -> p j f", j=2), R=[ps.b], W=[self.krout.b])
            self.dma(self.qsp, self.nkr[r - 1, l].rearrange("(j p) f -> p j f", p=128), self.krout.t[:], R=[self.krout.b], W=[self.out_b])

    def ln_core(self, ring=None, gen=False):
        g = self._ln_core(ring)
        if gen:
            return g
        for _ in g:
            pass

    def _ln_core(self, ring):
        xt = self.xt
        ring = ring or self.psring
        psm, psq = ring.get(), ring.get()
        cs = slice(HAL, HAL + CH)
        for j in range(8):
            zb, zs = self.zbk.get(), self.zsqk.get()
            self.A(zb.t[:], xt.t[:, j, cs], AF.Copy, R=[self.xtb[j]], W=[zb.b])
            self.V_tt(zs.t[:], xt.t[:, j, cs], xt.t[:, j, cs], ALU.mult, R=[self.xtb[j]], W=[zs.b])
            yield
            self.mm(psm.t[:, 0:CH], self.onesB.t[:], zb.t[:], start=(j == 0), stop=(j == 7), R=[self.onesB.b, zb.b], W=[psm.b], sig=True)
            self.mm(psq.t[:, 0:CH], self.onesB.t[:], zs.t[:], start=(j == 0), stop=(j == 7), R=[self.onesB.b, zs.b], W=[psq.b], sig=True)
            yield
        s0, s1, s2 = self.st[0], self.st[1], self.st[2]
        self.A(s0.t[:], psm.t[:, 0:CH], AF.Copy, scale=1.0 / D, R=[psm.b], W=[s0.b])
        self.V_tt(s1.t[:], s0.t[:], s0.t[:], ALU.mult, R=[s0.b], W=[s1.b])
        yield
        self.V_stt(s1.t[:], psq.t[:, 0:CH], 1.0 / D, s1.t[:], ALU.mult, ALU.subtract, R=[psq.b, s1.b], W=[s1.b])
        yield
        self.A(s1.t[:], s1.t[:], AF.Ln, bias=self.misc.t[:, 1:2], scale=1.0, R=[s1.b, self.misc.b], W=[s1.b])
        yield
        self.A(s1.t[:], s1.t[:], AF.Exp, scale=-0.5, R=[s1.b], W=[s1.b])
        self.V_stt(s2.t[:], s0.t[:], -1.0, s1.t[:], ALU.mult, ALU.mult, R=[s0.b, s1.b], W=[s2.b])
        yield
        for j in range(8):
            self.V_tt(xt.t[:, j, cs], xt.t[:, j, cs], s1.t[:], ALU.mult, R=[self.xtb[j], s1.b], W=[self.xtb[j]])
            self.V_tt(xt.t[:, j, cs], xt.t[:, j, cs], s2.t[:], ALU.add, R=[self.xtb[j], s2.b], W=[self.xtb[j]], eng=self.pool)
            yield

    def affine_xt(self, gcol0, bcol0, gen=False):
        g = self._affine_xt(gcol0, bcol0)
        if gen:
            return g
        for _ in g:
            pass

    def _affine_xt(self, gcol0, bcol0):
        xt = self.xt
        cs = slice(HAL, HAL + CH)
        for k in range(8):
            g = self.vec.t[:, gcol0 + k:gcol0 + k + 1]
            b = self.vec.t[:, bcol0 + k:bcol0 + k + 1]
            if k % 2 == 0:
                self.V_ts(xt.t[:, k, cs], xt.t[:, k, cs], g, b, ALU.mult, ALU.add, R=[self.xtb[k], self.vec.b], W=[self.xtb[k]])
            else:
                self.A(xt.t[:, k, cs], xt.t[:, k, cs], AF.Identity, scale=g, bias=b, R=[self.xtb[k], self.vec.b], W=[self.xtb[k]])
            yield

    def load_xnext(self, r, c):
        nch = REQ_T[r] // CH
        g0 = RB[r] + c * CH
        gc = g0 // CH
        first, last = (c == 0), (c == nch - 1)
        lo = 0 if first else -HAL
        hi = CH if last else CH + HAL
        xTv = self.xT_d.rearrange("(k p) t -> p k t", p=128)
        rb = [self.xT_b[gc]] + ([] if first else [self.xT_b[gc - 1]]) + ([] if last else [self.xT_b[gc + 1]])
        self.dma(self.qsp, self.xnext[:, :, HAL + lo:HAL + hi], xTv[:, :, g0 + lo:g0 + hi], R=rb, W=[q.b for q in self.pb])

    def pb_ctx(self, r, c):
        Tr = REQ_T[r]
        nch = Tr // CH
        t0 = c * CH
        g0 = RB[r] + t0
        first, last = (c == 0), (c == nch - 1)
        return dict(cond=1 if r == 0 else 0, t0=t0, g0=g0, gc=g0 // CH, first=first, last=last,
                    lo=0 if first else -HAL, hi=CH if last else CH + HAL)

    def pb_front(self, l, r, c):
        X = self.pb_ctx(r, c)
        cond, first, last, lo, hi = X["cond"], X["first"], X["last"], X["lo"], X["hi"]
        ub = self.ubuf
        cs = slice(HAL, HAL + CH)
        pbb = [q.b for q in self.pb]
        xn = self.xnext
        if first:
            self.load_xnext(r, c)
        if r == 0:
            self.load_rope(X["t0"])
        if first:
            self.memset(self.pool, ub.t[:, :, 0:HAL], 0.0, W=self.ubb)
        if last:
            self.memset(self.pool, ub.t[:, :, HAL + CH:CW], 0.0, W=self.ubb)
        c0, c1 = HAL + lo, HAL + hi
        for k in range(8):
            sc = self.s1p.t[:, k, cond:cond + 1]
            sh = self.modT.t[:, k, cond:cond + 1]
            if k % 2 == 0:
                self.V_ts(ub.t[:, k, c0:c1], xn[:, k, c0:c1], sc, sh, ALU.mult, ALU.add,
                          R=[pbb[k // 2], self.s1p.b, self.modT.b], W=[self.ubb[k]])
            else:
                self.A(ub.t[:, k, c0:c1], xn[:, k, c0:c1], AF.Identity, scale=sc, bias=sh,
                       R=[pbb[k // 2], self.s1p.b, self.modT.b], W=[self.ubb[k]])
        wb = self.w_in.b
        ring = self.psS

        def win(ps, col, ncols, c0, c1, prow=128):
            for k in range(8):
                self.mm(ps.t[0:prow, 0:c1 - c0], self.w_in.t[:, k, col:col + ncols], ub.t[:, k, c0:c1],
                        start=(k == 0), stop=(k == 7), R=[wb, self.ubb[k]], W=[ps.b])

        pq = self.psB.items
        for m in range(3):
            win(pq[m], C_Q + m * 128, 128, HAL, HAL + CH)
            self.A(self.sqb.t[:, m, :], pq[m].t[:, 0:CH], AF.Square, R=[pq[m].b], W=[self.sqb.b])
        P, Bq, Cq, Vv = self.pb
        for m in range(2):
            ps = ring.get()
            win(ps, C_P + m * 128, 128, 0, CW)
            self.A(P.t[:, m, :], ps.t[:, 0:CW], AF.Copy, R=[ps.b], W=[P.b])
        for m in range(2):
            psc = ring.get()
            win(psc, C_GC + m * 128, 128, 0, CW)
            self.A(Vv.t[:, m, :], psc.t[:, 0:CW], AF.Copy, R=[psc.b], W=[Vv.b])
            psh = ring.get()
            win(psh, C_H + m * 128, 128, 0, CW)
            self.V_tt(Vv.t[:, m, :], psh.t[:, 0:CW], Vv.t[:, m, :], ALU.mult, R=[psh.b, Vv.b], W=[Vv.b])
        for m in range(2):
            ps = ring.get()
            win(ps, C_GB + m * 128, 128, HAL, HAL + CH)
            self.A(self.gbS.t[:, m, :], ps.t[:, 0:CH], AF.Copy, R=[ps.b], W=[self.gbS.b])
        st0 = self.stf[0]
        pss = ring.get()
        for m in range(3):
            self.mm(pss.t[:, 0:CH], self.onesB.t[:], self.sqb.t[:, m, :], start=(m == 0), stop=(m == 2),
                    R=[self.onesB.b, self.sqb.b], W=[pss.b])
        self.A(st0.t[:], pss.t[:, 0:CH], AF.Ln, scale=1.0 / QL, bias=self.misc.t[:, 0:1], R=[pss.b, self.misc.b], W=[st0.b])
        self.A(st0.t[:], st0.t[:], AF.Exp, scale=-0.5, R=[st0.b], W=[st0.b])
        for m in range(3):
            self.V_stt(self.qn.t[:, m, :], pq[m].t[:, 0:CH], self.vec.t[:, V_QN + m:V_QN + m + 1], st0.t[:], ALU.mult, ALU.mult,
                       R=[pq[m].b, self.vec.b, st0.b], W=[self.qn.b])
        for h in range(NH):
            ps = ring.get()
            for m in range(3):
                self.mm(ps.t[:, 0:CH], self.w_uq.t[:, m, h * 192:h * 192 + 128], self.qn.t[:, m, :], start=(m == 0), stop=(m == 2),
                        R=[self.w_uq.b, self.qn.b], W=[ps.b])
            self.A(self.qnope.t[:, h, :], ps.t[:, 0:CH], AF.Copy, scale=ATTN_SCALE, R=[ps.b], W=[self.qnope.b])
            ps2 = ring.get()
            for m in range(3):
                self.mm(ps2.t[0:DR, 0:CH], self.w_uq.t[:, m, h * 192 + 128:h * 192 + 192], self.qn.t[:, m, :], start=(m == 0), stop=(m == 2),
                        R=[self.w_uq.b, self.qn.b], W=[ps2.b])
            if r == 0:
                for m in range(3):
                    self.mm(ps2.t[0:DR, CH:2 * CH], self.w_uq.t[:, m, 768 + h * DR:768 + (h + 1) * DR], self.qn.t[:, m, :],
                            start=(m == 0), stop=(m == 2), R=[self.w_uq.b, self.qn.b], W=[ps2.b])
                s1 = self.stf[1]
                self.V_tt(s1.t[0:DR, :], ps2.t[0:DR, 0:CH], self.rt.t[0:DR, 0, :], ALU.mult, R=[ps2.b, self.rt.b], W=[s1.b])
                self.V_tt(ps2.t[0:DR, CH:2 * CH], ps2.t[0:DR, CH:2 * CH], self.rt.t[0:DR, 1, :], ALU.mult, R=[ps2.b, self.rt.b], W=[ps2.b])
                self.V_tt(s1.t[0:DR, :], ps2.t[0:DR, CH:2 * CH], s1.t[0:DR, :], ALU.add, R=[s1.b, ps2.b], W=[s1.b])
                self.A(self.qrope.t[0:DR, h, :], s1.t[0:DR, :], AF.Copy, scale=ATTN_SCALE, R=[s1.b], W=[self.qrope.b])
            else:
                self.A(self.qrope.t[0:DR, h, :], ps2.t[0:DR, 0:CH], AF.Copy, scale=ATTN_SCALE, R=[ps2.b], W=[self.qrope.b])
        pl = self.pool
        add = ALU.add
        self.V_tt(Bq.t[:, :, 1:CW], P.t[:, :, 0:CW - 1], P.t[:, :, 1:CW], add, R=[P.b], W=[Bq.b], eng=pl)
        self.V_tt(Cq.t[64:128, 0, 2:CW - 1], Bq.t[64:128, 0, 1:CW - 2], Bq.t[64:128, 0, 3:CW], add, R=[Bq.b], W=[Cq.b], eng=pl)
        self.V_tt(Cq.t[:, 1, 2:CW - 1], Bq.t[:, 1, 1:CW - 2], Bq.t[:, 1, 3:CW], add, R=[Bq.b], W=[Cq.b], eng=pl)
        self.V_tt(Bq.t[:, 1, 4:CW - 3], Cq.t[:, 1, 2:CW - 5], Cq.t[:, 1, 6:CW - 1], add, R=[Cq.b], W=[Bq.b], eng=pl)
        self.V_tt(Cq.t[64:128, 1, 8:CW - 7], Bq.t[64:128, 1, 4:CW - 11], Bq.t[64:128, 1, 12:CW - 3], add, R=[Bq.b], W=[Cq.b], eng=pl)
        for m in range(2):
            for (p0, p1, Wb) in ((0, 64, Bq), (64, 128, Cq)):
                self.V_stt(self.pooledB.t[p0:p1, m, :], Wb.t[p0:p1, m, cs], self.misc.t[p0:p1, 2 + m:3 + m], P.t[p0:p1, m, cs],
                           ALU.mult, ALU.subtract, R=[Wb.b, P.b, self.misc.b], W=[self.pooledB.b])
                for (is_edge, ec0, wc0, oc0) in ((first, 0, HAL, 0), (last, 8, HAL + CH - 8, CH - 8)):
                    if is_edge:
                        self.V_tt(self.edt.t[p0:p1, m, :], Wb.t[p0:p1, m, wc0:wc0 + 8], self.edges.t[p0:p1, m, ec0:ec0 + 8], ALU.mult,
                                  R=[Wb.b, self.edges.b], W=[self.edt.b])
                        self.V_tt(self.pooledB.t[p0:p1, m, oc0:oc0 + 8], self.edt.t[p0:p1, m, :], P.t[p0:p1, m, wc0:wc0 + 8], ALU.subtract,
                                  R=[self.edt.b, P.b], W=[self.pooledB.b])
        cacc = Bq
        for m in range(2):
            w0 = self.vec.t[:, V_CW + m * 3 + 0:V_CW + m * 3 + 1]
            w1 = self.vec.t[:, V_CW + m * 3 + 1:V_CW + m * 3 + 2]
            w2 = self.vec.t[:, V_CW + m * 3 + 2:V_CW + m * 3 + 3]
            self.V_ts(cacc.t[:, m, cs], Vv.t[:, m, HAL - 1:HAL + CH - 1], w0, None, ALU.mult, R=[Vv.b, self.vec.b, self.pooledB.b], W=[cacc.b])
            self.V_stt(cacc.t[:, m, cs], Vv.t[:, m, cs], w1, cacc.t[:, m, cs], ALU.mult, ALU.add, R=[Vv.b, cacc.b, self.vec.b], W=[cacc.b])
            self.V_stt(cacc.t[:, m, cs], Vv.t[:, m, HAL + 1:HAL + CH + 1], w2, cacc.t[:, m, cs], ALU.mult, ALU.add, R=[Vv.b, cacc.b, self.vec.b], W=[cacc.b])
        for mo in range(2):
            ps = ring.get()
            for mi in range(2):
                self.mm(ps.t[:, 0:CH], self.poolw.t[:, mi, mo * 128:(mo + 1) * 128], self.pooledB.t[:, mi, :], start=(mi == 0), stop=(mi == 1),
                        R=[self.poolw.b, self.pooledB.b], W=[ps.b])
            self.A(ub.t[:, 4 + mo, cs], ps.t[:, 0:CH], AF.Copy, scale=self.vec.t[:, V_PS + mo:V_PS + mo + 1], R=[ps.b, self.vec.b], W=[self.ubb[4 + mo]])
            self.V_tt(ub.t[:, 6 + mo, cs], cacc.t[:, mo, cs], self.gbS.t[:, mo, :], ALU.mult, R=[cacc.b, self.gbS.b], W=[self.ubb[6 + mo]])
        if not last:
            self.load_xnext(r, c + 1)

    def pb_attention(self, l, r, c, side):
        ub = self.ubuf
        nkt = (PAST + T_S) // 128 if r == 0 else T_P // 128
        npair = nkt // 2
        ring = self.psS

        def emit_S(h, kp):
            ps = ring.get()
            for j in range(2):
                kt = kp * 2 + j
                self.mm(ps.t[:, j * CH:(j + 1) * CH], self.KT.t[:, h, kt * 128:(kt + 1) * 128], self.qnope.t[:, h, :],
                        start=True, stop=False, R=[self.KTb[kp], self.qnope.b], W=[ps.b], sig=False)
                self.mm(ps.t[:, j * CH:(j + 1) * CH], self.krT.t[:, kt * 128:(kt + 1) * 128], self.qrope.t[:, h, :],
                        start=False, stop=True, R=[self.krTb[kp], self.qrope.b], W=[ps.b], sig=(j == 1))
            return ps

        seq = [(h, kp) for h in range(NH) for kp in range(npair)]
        nside = max(1, (130 + len(seq) - 1) // len(seq))
        ps_next = emit_S(*seq[0])
        acc = self.psacc.items[0]
        for i, (h, kp) in enumerate(seq):
            ps = ps_next
            pt = self.PT.get()
            self.A(pt.t[:].rearrange("p a b -> p (a b)"), ps.t[:, :], AF.Exp, R=[ps.b], W=[pt.b])
            if i + 1 < len(seq):
                ps_next = emit_S(*seq[i + 1])
            for j in range(2):
                kt = kp * 2 + j
                for qt in range(2):
                    self.mm(acc[qt].t[:, 0:VA], pt.t[:, j, qt * 128:(qt + 1) * 128], self.Vt.t[:, kt, h * VA:(h + 1) * VA],
                            start=(kt == 0), stop=(kt == nkt - 1), R=[self.Vb[kp], pt.b], W=[acc[qt].b], sig=(j == 1 and qt == 1))
            if kp == npair - 1:
                sm = self.smx
                for qt in range(2):
                    self.V_rcp(sm.t[:, 4 + qt:5 + qt], acc[qt].t[:, DV:DV + 1], R=[acc[qt].b], W=[sm.b])
                    otm = self.otm.get()
                    self.V_ts(otm.t[:], acc[qt].t[:, 0:DV], sm.t[:, 4 + qt:5 + qt], None, ALU.mult, R=[acc[qt].b, sm.b], W=[otm.b])
                    pst = ring.get()
                    pstb = pst.t[:, :].bitcast(BF16)
                    self.tr(pstb[:, 0:128], otm.t[:], self.identB.t[:], R=[otm.b, self.identB.b], W=[pst.b])
                    self.A(ub.t[:, h, HAL + qt * 128:HAL + (qt + 1) * 128], pstb[:, 0:128], AF.Copy, R=[pst.b], W=[self.ubb[h]])
            if side is not None:
                for _ in range(nside):
                    if next(side, "done") == "done":
                        side = None
                        break
        if side is not None:
            for _ in side:
                pass

    def pb_wout(self, l, r, c):
        X = self.pb_ctx(r, c)
        cond, g0, gc = X["cond"], X["g0"], X["gc"]
        xTv = self.xT_d.rearrange("(k p) t -> p k t", p=128)
        xt, ub = self.xt, self.ubuf
        cs = slice(HAL, HAL + CH)
        self.dma(self.qsp, xt.t[:, :, cs], xTv[:, :, g0:g0 + CH], R=[self.xT_b[gc]], W=self.xtb)
        for j in range(8):
            ps = self.psB.get()
            for k in range(8):
                self.mm(ps.t[:, 0:CH], self.w_out.t[:, k, j * 128:(j + 1) * 128], ub.t[:, k, cs], start=(k == 0), stop=(k == 7),
                        R=[self.w_out.b, self.ubb[k]], W=[ps.b])
            self.V_stt(xt.t[:, j, cs], ps.t[:, 0:CH], self.g1a.t[:, j, cond:cond + 1], xt.t[:, j, cs], ALU.mult, ALU.add,
                       R=[ps.b, self.g1a.b, self.xtb[j]], W=[self.xtb[j]])

    def pb_late(self, l, r, c):
        X = self.pb_ctx(r, c)
        cond, g0, gc = X["cond"], X["g0"], X["gc"]
        x1Tv = self.x1T_d.rearrange("(k p) t -> p k t", p=128)
        xt, u2 = self.xt, self.u2buf
        cs = slice(HAL, HAL + CH)
        yield from self.ln_core(ring=self.psB, gen=True)
        for jt in range(2):
            gt = g0 // 128 + jt
            ps = self.psB.get()
            for k in range(8):
                self.mm(ps.t[:, 0:NE], xt.t[:, k, HAL + jt * 128:HAL + (jt + 1) * 128], self.wr2.t[:, cond, k, :], start=(k == 0), stop=False,
                        R=[self.xtb[k], self.wr2.b], W=[ps.b], sig=False)
            self.mm(ps.t[:, 0:NE], self.onesrow.t[0:1, :], self.rconst.t[0:1, cond, :], start=False, stop=True,
                    R=[self.onesrow.b, self.rconst.b], W=[ps.b], sig=True)
            yield
            sm = self.smx
            self.dve.op(lambda e: e.tensor_reduce(out=sm.t[:, 0:1], in_=ps.t[:, 0:NE], axis=AX.X, op=ALU.max), R=[ps.b], W=[sm.b])
            self.V_ts(sm.t[:, 1:2], sm.t[:, 0:1], -1.0, None, ALU.mult, R=[sm.b], W=[sm.b])
            yield
            self.A(self.ex.t[:], ps.t[:, 0:NE], AF.Exp, bias=sm.t[:, 1:2], scale=1.0, accum_out=sm.t[:, 2:3], R=[ps.b, sm.b], W=[self.ex.b, sm.b])
            yield
            self.V_rcp(sm.t[:, 3:4], sm.t[:, 2:3], R=[sm.b], W=[sm.b])
            self.V_ts(self.affS.t[:, gt, :], self.ex.t[:], sm.t[:, 3:4], None, ALU.mult, R=[self.ex.b, sm.b], W=[self.affSb[gt]])
            yield
        for k in range(8):
            g = self.G2.t[:, k, cond:cond + 1]
            b = self.B2.t[:, k, cond:cond + 1]
            if k % 2 == 1:
                self.V_ts(u2.t[:, k, :], xt.t[:, k, cs], g, b, ALU.mult, ALU.add, R=[self.xtb[k], self.G2.b, self.B2.b], W=[self.u2b[k]])
            else:
                self.A(u2.t[:, k, :], xt.t[:, k, cs], AF.Identity, scale=g, bias=b, R=[self.xtb[k], self.G2.b, self.B2.b], W=[self.u2b[k]])
            yield
        yield from self.affine_xt(V_L1G, V_L1B, gen=True)
        self.dma(self.qsp, x1Tv[:, :, g0:g0 + CH], xt.t[:, :, cs], R=self.xtb, W=[self.x1T_b[gc]])
        for jt in range(2):
            gt = g0 // 128 + jt
            ps = self.psB.get()
            psb = ps.t[:, :].bitcast(BF16)
            for k in range(8):
                self.tr(psb[:, k * 128:(k + 1) * 128], u2.t[:, k, jt * 128:(jt + 1) * 128], self.identB.t[:],
                        R=[self.u2b[k], self.identB.b], W=[ps.b], sig=(k == 7))
                if k % 4 == 3:
                    yield
            rw = self.rowbuf.get()
            self.A(rw.t[:, 0:D], psb[:, :], AF.Copy, R=[ps.b], W=[rw.b])
            self.V_cp(rw.t[:, D:D + 2 * NE].bitcast(F32), self.affS.t[:, gt, :], R=[self.affSb[gt], rw.b], W=[rw.b])
            self.V_cp(rw.t[:, D + 2 * NE:D + 2 * NE + 2].bitcast(I32), self.tidcol.t[:, gt:gt + 1], R=[self.tidcol.b, rw.b], W=[rw.b])
            self.dma(self.qsp, self.u2rows_d[gt * 128:(gt + 1) * 128, :], rw.t[:], R=[rw.b], W=[self.u2rows_bt[gt]])
            yield

    def phase_b(self, l, r):
        nch = REQ_T[r] // CH
        side = None
        for c in range(nch):
            self.pb_front(l, r, c)
            self.pb_attention(l, r, c, side)
            self.pb_wout(l, r, c)
            side = self.pb_late(l, r, c)
        for _ in side:
            pass

    def routing(self, r):
        self.carve_reset()
        a = self.av
        sample = (r == 0)
        Pn = 128 if sample else NE
        Fn = 512 if sample else T_P
        nblk = Fn // 128
        Kcap = float(REQ_CAP[r])
        A_sb = a("A_sb", [128, 512], F32)
        junk = a("junk", [128, 512], F32)
        msk = a("msk", [128, 512], F32)
        csb = a("csb", [128, 512], F32)
        affX = [a(f"affX{i}", [128, NE, 8], F32) for i in range(2)]
        bis = a("bis", [128, 8], I32)
        cnt = a("cnt", [128, 4], F32)
        self.gsum = a("gsum", [128, 128], F32)
        self.lstrict = a("lstrict", [128, 128], F32)
        if sample:
            self.dma(self.qsp, self.gsum.t[:], self.gsum_d, W=[self.gsum.b])
            self.dma(self.qsp, self.lstrict.t[:], self.lstrict_d, W=[self.lstrict.b])
        g0t = RB[r] // 128
        for b in range(nblk):
            ps = self.ps_get()
            if sample:
                for seg in range(8):
                    gt = seg * 4 + b
                    ax = affX[seg % 2]
                    self.memset(self.pool, ax.t[:], 0.0, W=[ax.b])
                    self.V_cp(ax.t[:, :, seg], self.affS.t[:, gt, :], R=[self.affSb[gt], ax.b], W=[ax.b])
                    self.mm(ps.t[:, 0:128], ax.t[:].rearrange("p e s -> p (e s)"), self.identF.t[:], start=(seg == 0), stop=(seg == 7),
                            R=[ax.b, self.identF.b], W=[ps.b], sig=True)
            else:
                gt = g0t + b
                self.mm(ps.t[0:NE, 0:128], self.affS.t[:, gt, :], self.identF.t[:], start=True, stop=True,
                        R=[self.affSb[gt], self.identF.b], W=[ps.b])
            self.V_cp(A_sb.t[0:Pn, b * 128:(b + 1) * 128], ps.t[0:Pn, 0:128], R=[ps.b], W=[A_sb.b])
        lo, mid, ge = (bis.t[0:Pn, i:i + 1] for i in range(3))
        bb = [bis.b]
        self.memset(self.dve, bis.t[:, 0:1], 0, W=bb)
        self.memset(self.dve, cnt.t[:], 0.0, W=[cnt.b])
        for bit in range(29, -1, -1):
            self.V_ts(mid, lo, 1 << bit, None, ALU.bitwise_or, R=bb, W=bb)
            self.V_ts(junk.t[0:Pn, 0:Fn], A_sb.t[0:Pn, 0:Fn], mid.bitcast(F32), None, ALU.is_ge, ALU.add,
                      R=[A_sb.b] + bb, W=[junk.b, cnt.b], accum_out=cnt.t[0:Pn, 0:1])
            if sample:
                ps = self.ps_get()
                self.mm(ps.t[:, 0:2], self.gsum.t[:], cnt.t[:, 0:2], start=True, stop=True, R=[self.gsum.b, cnt.b], W=[ps.b])
                src, sb_ = ps.t[0:Pn, 0:1], [ps.b]
            else:
                src, sb_ = cnt.t[0:Pn, 0:1], [cnt.b]
            self.V_ts(ge, src, Kcap - 0.5, None, ALU.is_ge, R=sb_, W=bb)
            self.dve.op(lambda e: e.copy_predicated(out=lo, mask=ge, data=mid), R=bb, W=bb)
        self.V_ts(msk.t[0:Pn, 0:Fn], A_sb.t[0:Pn, 0:Fn], lo.bitcast(F32), None, ALU.is_ge, R=[A_sb.b] + bb, W=[msk.b])
        self.memset(self.dve, junk.t[:], 1.0, W=[junk.b])
        self.dve.op(lambda e: e.tensor_tensor_scan(out=csb.t[0:Pn, 0:Fn], data0=junk.t[0:Pn, 0:Fn], data1=msk.t[0:Pn, 0:Fn],
                                                   initial=0.0, op0=ALU.mult, op1=ALU.add), R=[junk.b, msk.b], W=[csb.b])
        if sample:
            self.V_cp(cnt.t[:, 0:1], csb.t[:, Fn - 1:Fn], R=[csb.b], W=[cnt.b])
            ps = self.ps_get()
            self.mm(ps.t[:, 0:2], self.lstrict.t[:], cnt.t[:, 0:2], start=True, stop=True, R=[self.lstrict.b, cnt.b], W=[ps.b])
            self.V_cp(cnt.t[:, 2:3], ps.t[:, 0:1], R=[ps.b], W=[cnt.b])
            self.V_ts(csb.t[:, 0:Fn], csb.t[:, 0:Fn], cnt.t[:, 2:3], None, ALU.add, R=[csb.b, cnt.b], W=[csb.b])
        self.V_stt(junk.t[0:Pn, 0:Fn], csb.t[0:Pn, 0:Fn], Kcap + 0.5, msk.t[0:Pn, 0:Fn], ALU.is_le, ALU.mult, R=[csb.b, msk.b], W=[junk.b])
        self.V_tt(csb.t[0:Pn, 0:Fn], csb.t[0:Pn, 0:Fn], junk.t[0:Pn, 0:Fn], ALU.mult, R=[csb.b, junk.b], W=[csb.b])
        vt = self.vTM[r]
        for b in range(nblk):
            ps = self.ps_get()
            self.tr(ps.t[:, 0:Pn], csb.t[0:Pn, b * 128:(b + 1) * 128], self.identF.t[0:Pn, 0:Pn], R=[csb.b, self.identF.b], W=[ps.b])
            self.V_cp(vt.t[:, b, 0:Pn], ps.t[:, 0:Pn], R=[ps.b], W=[vt.b])

    def moe_phase(self, l):
        self.carve_reset()
        a = self.av
        wg = [a(f"wg{i}", [128, 8, EF], BF16) for i in range(2)]
        wu = [a(f"wu{i}", [128, 8, EF], BF16) for i in range(2)]
        wd = [a(f"wd{i}", [128, 4, D], BF16) for i in range(2)]
        xs = [a(f"xs{i}", [128, ROWW], BF16) for i in range(10)]
        NS = 576
        xsT = a("xsT", [128, 8, NS], BF16)
        hdn = a("hdn", [128, 4, NS], BF16)
        sil = a("sil", [128, NS], F32)
        ye = Ring([a(f"ye{i}", [128, D], F32) for i in range(3)])
        Sr = Ring([a(f"S{i}", [128, 512], BF16) for i in range(3)])
        Sp = Ring([a(f"Sp{i}", [128, 32], BF16) for i in range(2)])
        idxrow = a("idxrow", [2, NS], F32)
        idxI = Ring([a(f"idxI{i}", [128, 8], I32) for i in range(2)])
        idxF = a("idxF", [128, 5, 2], F32)
        zt = a("zt", [128, D], F32)
        self.iota1 = a("iota1", [128, 512], F32)
        self.dma(self.qsp, self.iota1.t[:], self.iota1_d, W=[self.iota1.b])
        self.memset(self.pool, zt.t[:], 0.0, W=[zt.b])
        for gt in range(TTOT // 128):
            self.dma(self.qsp, self.ffn_d[gt * 128:(gt + 1) * 128, :], zt.t[:], R=[zt.b], W=[self.ffn_bt[gt]])
        if self.bcreg is None:
            self.bcreg = self.nc.gpsimd.to_reg(TTOT - 1)

        def load_w(e):
            i = e % 2
            self.dma(self.qpl, wg[i].t[:], self.w_gate_d[l, e].rearrange("(k p) f -> p k f", p=128), W=[wg[i].b])
            self.dma(self.qpl, wu[i].t[:], self.w_up_d[l, e].rearrange("(k p) f -> p k f", p=128), W=[wu[i].b])
            self.dma(self.qpl, wd[i].t[:], self.w_down_d[l, e].rearrange("(k p) d -> p k d", p=128), W=[wd[i].b])

        load_w(0)
        tiles = [(c, 128) for c in range(4)] + [(4, 64)]
        scat = {"prev": [], "cur": []}

        def stage1(e):
            xse = xs[(e % 2) * 5:(e % 2) * 5 + 5]
            psI = self.psacc.items[0][0]
            psI2 = self.psacc.items[0][1]
            vt = self.vTM[0]
            ntile = T_S // 128
            Sl = [None] * ntile

            def onehot(gt):
                seg, b = gt // 4, gt % 4
                S = Sr.get()
                col = vt.t[:, b, e * 8 + seg:e * 8 + seg + 1]
                self.V_ts(S.t[:], self.iota1.t[:], col, None, ALU.is_equal, R=[self.iota1.b, vt.b], W=[S.b])
                Sl[gt] = S

            for gt in range(3):
                onehot(gt)
            yield
            for gt in range(ntile):
                S = Sl[gt]
                self.mm(psI.t[0:2, :], self.tidhl.t[:, gt, :], S.t[:], start=(gt == 0), stop=(gt == ntile - 1),
                        R=[self.tidhl.b, S.b], W=[psI.b], sig=True)
                if gt + 3 < ntile:
                    onehot(gt + 3)
                yield
            self.A(idxrow.t[0:2, 0:512], psI.t[0:2, :], AF.Copy, R=[psI.b], W=[idxrow.b])
            psJ = psI2
            for r in (1, 2):
                vtp = self.vTM[r]
                for b in range(2):
                    gt = RB[r] // 128 + b
                    S = Sp.get()
                    self.V_ts(S.t[:], self.iota1.t[:, 0:32], vtp.t[:, b, e:e + 1], None, ALU.is_equal, R=[self.iota1.b, vtp.b], W=[S.b])
                    self.mm(psJ.t[0:2, (r - 1) * 32:r * 32], self.tidhl.t[:, gt, :], S.t[:], start=(b == 0), stop=(b == 1),
                            R=[self.tidhl.b, S.b], W=[psJ.b], sig=True)
            yield
            self.A(idxrow.t[0:2, 512:NS], psJ.t[0:2, 0:64], AF.Copy, R=[psJ.b], W=[idxrow.b])
            psT = psI
            for (c, nr) in tiles:
                self.tr(psT.t[0:nr, c * 2:c * 2 + 2], idxrow.t[0:2, c * 128:c * 128 + nr], self.identF.t[0:2, 0:2],
                        R=[idxrow.b, self.identF.b], W=[psT.b], sig=(c == 4))
            yield
            self.V_cp(idxF.t[:].rearrange("p a b -> p (a b)"), psT.t[:, 0:10], R=[psT.b], W=[idxF.b])
            ii = idxI.get()
            self.V_stt(ii.t[:, 0:5], idxF.t[:, :, 0], 64.0, idxF.t[:, :, 1], ALU.mult, ALU.add, R=[idxF.b], W=[ii.b])
            for (c, nr) in tiles:
                self.qpl.issue(lambda g, c=c, nr=nr: g.indirect_dma_start(
                    out=xse[c].t[0:nr, :], out_offset=None, in_=self.u2rows_d,
                    in_offset=bass.IndirectOffsetOnAxis(ap=ii.t[0:nr, c:c + 1], axis=0),
                    bounds_check=self.bcreg, oob_is_err=False), R=[ii.b] + self.u2rows_bt, W=[xse[c].b])
            yield

        def stage2(e):
            i = e % 2
            xse = xs[(e % 2) * 5:(e % 2) * 5 + 5]
            for (c, nr) in tiles:
                ps = self.ps_get()
                psb = ps.t[:, :].bitcast(BF16)
                for k in range(8):
                    self.tr(psb[:, k * 128:k * 128 + nr], xse[c].t[0:nr, k * 128:(k + 1) * 128], self.identB.t[0:nr, 0:nr],
                            R=[xse[c].b, self.identB.b], W=[ps.b], sig=(k == 7))
                yield
                src = psb[:, :].rearrange("p (k n) -> p k n", k=8)[:, :, 0:nr]
                if c % 2 == 0:
                    self.A(xsT.t[:, :, c * 128:c * 128 + nr], src, AF.Copy, R=[ps.b], W=[xsT.b])
                else:
                    self.V_cp(xsT.t[:, :, c * 128:c * 128 + nr], src, R=[ps.b], W=[xsT.b])
            for f in range(4):
                for (c0, c1) in ((0, 512), (512, NS)):
                    n = c1 - c0
                    psg, psu = self.ps_get(), self.ps_get()
                    for k in range(8):
                        self.mm(psg.t[:, 0:n], wg[i].t[:, k, f * 128:(f + 1) * 128], xsT.t[:, k, c0:c1], start=(k == 0), stop=(k == 7),
                                R=[wg[i].b, xsT.b], W=[psg.b])
                    for k in range(8):
                        self.mm(psu.t[:, 0:n], wu[i].t[:, k, f * 128:(f + 1) * 128], xsT.t[:, k, c0:c1], start=(k == 0), stop=(k == 7),
                                R=[wu[i].b, xsT.b], W=[psu.b])
                    yield
                    self.A(sil.t[:, c0:c1], psg.t[:, 0:n], AF.Silu, R=[psg.b], W=[sil.b])
                    self.V_tt(hdn.t[:, f, c0:c1], psu.t[:, 0:n], sil.t[:, c0:c1], ALU.mult, R=[psu.b, sil.b], W=[hdn.b])
            for (c, nr) in tiles:
                y = ye.get()
                gcol = xse[c].t[0:nr, D + 2 * e:D + 2 * e + 2].bitcast(F32)
                for half in range(2):
                    ps = self.ps_get()
                    for f in range(4):
                        self.mm(ps.t[0:nr, :], hdn.t[:, f, c * 128:c * 128 + nr], wd[i].t[:, f, half * 512:(half + 1) * 512],
                                start=(f == 0), stop=(f == 3), R=[hdn.b, wd[i].b], W=[ps.b])
                    yield
                    if half == 0:
                        self.A(y.t[0:nr, 0:512], ps.t[0:nr, :], AF.Copy, scale=gcol, R=[ps.b, xse[c].b], W=[y.b])
                    else:
                        self.V_ts(y.t[0:nr, 512:D], ps.t[0:nr, :], gcol, None, ALU.mult, R=[ps.b, xse[c].b], W=[y.b])
                tid = xse[c].t[0:nr, D + 2 * NE:D + 2 * NE + 2].bitcast(I32)
                for t_prev in scat["prev"]:
                    self.pool.wait(t_prev)
                last_e = (e == NE - 1)
                tk = self.qpl.issue(lambda g, y=y, nr=nr, tid=tid: g.indirect_dma_start(
                    out=self.ffn_d, out_offset=bass.IndirectOffsetOnAxis(ap=tid, axis=0), in_=y.t[0:nr, :], in_offset=None,
                    compute_op=ALU.add), R=[y.b, xse[c].b] + (self.ffn_bt if e == 0 else []), W=(self.ffn_bt if last_e else []))
                scat["cur"].append(tk)
            scat["prev"], scat["cur"] = scat["cur"], []

        for _ in stage1(0):
            pass
        for e in range(NE):
            side = None
            if e + 1 < NE:
                load_w(e + 1)
                side = stage1(e + 1)
            if e == 1 and l + 1 < self.depth:
                self.load_layer_weights(l + 1)
            for _ in stage2(e):
                if side is not None:
                    for _k in range(4):
                        if next(side, "done") == "done":
                            side = None
                            break
            if side is not None:
                for _ in side:
                    pass

    def ln2_chunk(self, l, r, c, ft, ot):
        cond = 1 if r == 0 else 0
        g0 = RB[r] + c * CH
        gc = g0 // CH
        xTv = self.xT_d.rearrange("(k p) t -> p k t", p=128)
        x1Tv = self.x1T_d.rearrange("(k p) t -> p k t", p=128)
        xt = self.xt
        cs = slice(HAL, HAL + CH)
        self.dma(self.qsp, xt.t[:, :, cs], x1Tv[:, :, g0:g0 + CH], R=[self.x1T_b[gc]], W=self.xtb)
        for jt in range(2):
            gt = g0 // 128 + jt
            self.dma(self.qsp, ft[jt].t[:], self.ffn_d[gt * 128:(gt + 1) * 128, :], R=[self.ffn_bt[gt]], W=[ft[jt].b])
        for k in range(8):
            ps = self.ps_get()
            for jt in range(2):
                self.tr(ps.t[:, jt * 128:(jt + 1) * 128], ft[jt].t[:, k * 128:(k + 1) * 128], self.identF.t[:],
                        R=[ft[jt].b, self.identF.b], W=[ps.b], sig=(jt == 1))
            self.V_stt(xt.t[:, k, cs], ps.t[:, 0:CH], self.g2a.t[:, k, cond:cond + 1], xt.t[:, k, cs], ALU.mult, ALU.add,
                       R=[ps.b, self.g2a.b, self.xtb[k]], W=[self.xtb[k]])
        self.ln_core()
        self.affine_xt(V_L2G, V_L2B)
        if l < self.depth - 1:
            self.dma(self.qsp, xTv[:, :, g0:g0 + CH], xt.t[:, :, cs], R=self.xtb, W=[self.xT_b[gc]])
        else:
            for jt in range(2):
                o = ot[jt]
                for hf in range(2):
                    ps = self.ps_get()
                    for kk in range(4):
                        k = hf * 4 + kk
                        self.tr(ps.t[:, kk * 128:(kk + 1) * 128], xt.t[:, k, HAL + jt * 128:HAL + (jt + 1) * 128], self.identF.t[:],
                                R=[self.xtb[k], self.identF.b], W=[ps.b], sig=(kk == 3))
                    if hf == 0:
                        self.A(o.t[:, 0:512], ps.t[:, :], AF.Copy, R=[ps.b], W=[o.b])
                    else:
                        self.V_cp(o.t[:, 512:D], ps.t[:, :], R=[ps.b], W=[o.b])
                row = c * CH + jt * 128
                dst = self.y_s[row:row + 128, :] if r == 0 else self.y_p[(r - 1) * T_P + row:(r - 1) * T_P + row + 128, :]
                self.dma(self.qsp, dst, o.t[:], R=[o.b], W=[Buf("o")])

    def ln2_phase(self, l):
        self.carve_reset()
        ft = [self.av(f"ft{i}", [128, D], F32) for i in range(2)]
        ot = [self.av(f"ot{i}", [128, D], F32) for i in range(2)]
        for r in range(3):
            for c in range(REQ_T[r] // CH):
                self.ln2_chunk(l, r, c, ft, ot)

    def build(self):
        self.setup()
        self.load_consts()
        self.transpose_in()
        self.load_layer_weights(0)
        self.barrier()
        for l in range(self.depth):
            self.mark(f"L{l} mod")
            self.mod_phase(l)
            self.barrier()
            for r in range(3):
                self.carve_ab()
                self.mark(f"L{l} r{r} A")
                if r == 0:
                    self.ctx_phase(l)
                nch = REQ_T[r] // CH
                for c in range(nch):
                    self.phase_a_chunk(l, r, c)
                self.mark(f"L{l} r{r} B")
                self.phase_b(l, r)
                self.barrier()
                self.mark(f"L{l} r{r} route")
                self.routing(r)
                self.barrier()
            self.mark(f"L{l} moe")
            self.moe_phase(l)
            self.barrier()
            self.mark(f"L{l} ln2")
            self.ln2_phase(l)
            self.barrier()
        self.barrier()
        self.mark("end")
        return self.nc


def _rope_tables():
    rows_n = T_S // 64
    r, cl = np.meshgrid(np.arange(rows_n, dtype=np.float32), np.arange(64, dtype=np.float32), indexing="ij")
    inv = (np.float32(10000.0) ** (-np.arange(0, 32, 2, dtype=np.float32) / np.float32(32))).astype(np.float32)
    ang = np.concatenate([r.reshape(-1)[:, None] * inv, cl.reshape(-1)[:, None] * inv], axis=-1).astype(np.float32)
    cos, sin = np.cos(ang).astype(np.float32), np.sin(ang).astype(np.float32)
    cosT = np.concatenate([cos.T, cos.T], axis=0)
    sinT = np.concatenate([-sin.T, sin.T], axis=0)
    return np.ascontiguousarray(cosT), np.ascontiguousarray(sinT)


def _consts():
    c = {}
    c["identF"] = np.eye(128, dtype=np.float32)
    p = np.arange(128)
    c["gsum"] = (p[:, None] // 8 == p[None, :] // 8).astype(np.float32)
    c["lstrict"] = ((p[:, None] // 8 == p[None, :] // 8) & (p[:, None] % 8 < p[None, :] % 8)).astype(np.float32)
    c["iota1"] = np.broadcast_to(np.arange(1, 513, dtype=np.float32)[None, :], (128, 512)).copy()
    rows = np.arange(36)[None, :] * 128 + p[:, None]
    c["tidhl"] = np.stack([rows // 64, rows % 64], axis=-1).astype(np.float32)
    c["tidcol"] = rows.astype(np.float32)
    edges = np.zeros((128, 2, 16), np.float32)
    invw = np.zeros((128, 2), np.float32)
    for m in range(2):
        for half in range(2):
            w = (2, 4, 8, 16)[m * 2 + half]
            sl = slice(half * 64, half * 64 + 64)
            invw[sl, m] = 1.0 / w
            for i in range(8):
                t = i
                cnt = min(t + w // 2, 10 ** 9) - max(0, t - w // 2)
                edges[sl, m, i] = 1.0 / cnt
                j = 7 - i
                cnt = min(w // 2, j + 1) + min(w // 2, 10 ** 9)
                edges[sl, m, 8 + i] = 1.0 / cnt
    c["edges"] = edges
    misc = np.zeros((128, 16), np.float32)
    misc[:, 0] = RMS_EPS
    misc[:, 1] = LN_EPS / (ALPHA * ALPHA)
    misc[:, 2:4] = invw
    c["misc"] = misc
    c["cosT"], c["sinT"] = _rope_tables()
    return c


def _prep_shared(inp):
    f = lambda a: np.ascontiguousarray(np.asarray(a, dtype=np.float32))
    w_in = f(inp["w_in"])
    L = DEPTH
    krs = np.concatenate([np.arange(C_KR + 32, C_KR + 64), np.arange(C_KR, C_KR + 32)])
    w_in_x = np.concatenate([w_in, w_in[:, :, krs]], axis=2)
    sh = {}
    sh["w_in_x"] = np.ascontiguousarray(w_in_x.reshape(L, 8, 128, NCOL).transpose(0, 2, 1, 3))
    w_uq = f(inp["w_uq"])
    sw = np.concatenate([np.concatenate([np.arange(h * 192 + 160, h * 192 + 192), np.arange(h * 192 + 128, h * 192 + 160)]) for h in range(NH)])
    w_uq_x = np.concatenate([w_uq, w_uq[:, :, sw]], axis=2)
    sh["w_uq_x"] = np.ascontiguousarray(w_uq_x.reshape(L, 3, 128, 1024).transpose(0, 2, 1, 3))
    sh["w_uk_x"] = np.ascontiguousarray(f(inp["w_uk"]).reshape(L, 2, 128, 512).transpose(0, 2, 1, 3))
    sh["w_uv_x"] = np.ascontiguousarray(f(inp["w_uv"]).reshape(L, 2, 128, 512).transpose(0, 2, 1, 3))
    pw = f(inp["pool_w"])
    bd = np.zeros((L, 256, 256), np.float32)
    for g in range(4):
        bd[:, g * 64:(g + 1) * 64, g * 64:(g + 1) * 64] = pw[:, g]
    sh["poolw_x"] = np.ascontiguousarray(bd.reshape(L, 2, 128, 256).transpose(0, 2, 1, 3))
    sh["w_out_x"] = np.ascontiguousarray(f(inp["w_out"]).reshape(L, 8, 128, D).transpose(0, 2, 1, 3))
    sh["w_r_x"] = np.ascontiguousarray(f(inp["w_router"]).reshape(L, 8, 128, NE).transpose(0, 2, 1, 3))
    sh["w_ada"] = f(inp["w_ada"])
    vecs = np.zeros((L, 128, NV), np.float32)
    col = lambda v, n: v.reshape(L, n, 128).transpose(0, 2, 1)
    vecs[:, :, V_BADA:V_BADA + 48] = col(f(inp["b_ada"]), 48)
    vecs[:, :, V_L1G:V_L1G + 8] = col(f(inp["ln1_g"]), 8)
    vecs[:, :, V_L1B:V_L1B + 8] = col(f(inp["ln1_b"]), 8)
    vecs[:, :, V_L2G:V_L2G + 8] = col(f(inp["ln2_g"]), 8)
    vecs[:, :, V_L2B:V_L2B + 8] = col(f(inp["ln2_b"]), 8)
    vecs[:, :, V_QN:V_QN + 3] = col(f(inp["q_norm"]), 3)
    vecs[:, :, V_KVN:V_KVN + 2] = col(f(inp["kv_norm"]), 2)
    vecs[:, :, V_PS:V_PS + 2] = col(f(inp["pool_scale"]), 2)
    cw = f(inp["conv_w"])
    for m in range(2):
        for t in range(3):
            vecs[:, :, V_CW + m * 3 + t] = cw[:, t, m * 128:(m + 1) * 128]
    sh["vecs"] = vecs
    sh["w_gate"] = f(inp["w_gate"])
    sh["w_up"] = f(inp["w_up"])
    sh["w_down"] = f(inp["w_down"])
    sh.update(_consts())
    return sh


_NC_CACHE = {}


def kernel(**inp):
    sh = _prep_shared(inp)
    xp = np.asarray(inp["x_prompt"], dtype=np.float32)
    xsmp = np.asarray(inp["x_sample"], dtype=np.float32)
    cckv = np.asarray(inp["cache_ckv"], dtype=np.float32)
    ckr = np.asarray(inp["cache_krope"], dtype=np.float32)
    cvec = np.asarray(inp["c"], dtype=np.float32)
    cctx = np.asarray(inp["c_ctx"], dtype=np.float32)
    in_maps = []
    for core in range(8):
        s = core // 4
        m = dict(sh)
        m["xs_in"] = np.ascontiguousarray(xsmp[s])
        m["xp_in"] = np.ascontiguousarray(xp[2 * core:2 * core + 2].reshape(2 * T_P, D))
        m["cckv"] = np.ascontiguousarray(cckv[s])
        m["ckr"] = np.ascontiguousarray(ckr[s])
        cond = np.stack([cctx, cvec[s]], axis=-1)
        m["condT"] = np.ascontiguousarray(cond.reshape(8, 128, 2).transpose(1, 0, 2))
        in_maps.append(m)
    if "nc" not in _NC_CACHE:
        _NC_CACHE["nc"] = KB().build()
    nc = _NC_CACHE["nc"]
    res = run_bass_kernel_spmd(nc, in_maps, core_ids=list(range(8)))
    rs = res.results
    y_prompt = np.concatenate([np.asarray(rs[c]["y_p"]).reshape(2, T_P, D) for c in range(8)], axis=0)
    y_sample = np.stack([np.asarray(rs[0]["y_s"]), np.asarray(rs[4]["y_s"])], axis=0)
    new_ckv = np.concatenate([np.asarray(rs[c]["nckv"]) for c in range(8)], axis=0)
    new_krope = np.concatenate([np.asarray(rs[c]["nkr"]) for c in range(8)], axis=0)
    return (y_prompt.astype(np.float32), y_sample.astype(np.float32), new_ckv.astype(np.float32), new_krope.astype(np.float32))
```

```python
import numpy as np
import ml_dtypes
from contextlib import ExitStack
import concourse.bass as bass
import concourse.mybir as mybir
from concourse.bass_utils import run_bass_kernel_spmd

F32 = mybir.dt.float32
BF16 = mybir.dt.bfloat16
I32 = mybir.dt.int32
AF = mybir.ActivationFunctionType
ALU = mybir.AluOpType
AX = mybir.AxisListType

D = 1024
DEPTH = 4
T_S = 4096
T_P = 256
PAST = 512
NH = 4
DN = 128
DR = 64
DV = 128
QL = 384
KVL = 256
NE = 16
EF = 512
ALPHA = (2 * DEPTH) ** 0.25
RMS_EPS = 1e-6
LN_EPS = 1e-5
ATTN_SCALE = (DN + DR) ** -0.5
CH = 256
HAL = 8
CW = CH + 2 * HAL
VA = 130
NCOL = 1792
C_Q, C_KV, C_KR, C_P, C_GB, C_GC, C_H, C_KRS = 0, 384, 640, 704, 960, 1216, 1472, 1728
RB = [0, T_S, T_S + T_P]
TTOT = T_S + 2 * T_P
REQ_T = [T_S, T_P, T_P]
REQ_CAP = [2 * T_S // NE, 2 * T_P // NE, 2 * T_P // NE]
ROWW = 1060
NV = 93
V_BADA, V_L1G, V_L1B, V_L2G, V_L2B, V_QN, V_KVN, V_PS, V_CW = 0, 48, 56, 64, 72, 80, 83, 85, 87
BIGIDX = float(1 << 20)


class Tok:
    __slots__ = ("sem", "val")

    def __init__(self, sem=None, val=0):
        self.sem = sem
        self.val = val


class Buf:
    __slots__ = ("name", "w", "r")

    def __init__(self, name):
        self.name = name
        self.w = None
        self.r = []


class Eng:
    def __init__(self, K, eng, name):
        self.K = K
        self.e = eng
        self.name = name
        self.sem = None
        self.cnt = 0
        self.nsem = 0
        self.waited = {}
        self.pending = []
        self.last = None
        self.nins = 0
        self._newsem()

    def _newsem(self):
        self.sem = self.K.es.enter_context(self.K.nc.semaphore(f"p_{self.name}{self.nsem}"))
        self.nsem += 1
        self.cnt = 0

    def wait(self, tok):
        if tok is None:
            return
        assert tok.sem is not None, f"wait on unsignalled token ({self.name})"
        key = id(tok.sem)
        if self.waited.get(key, (None, 0))[1] >= tok.val:
            return
        self.e.wait_ge(tok.sem, tok.val)
        self.waited[key] = (tok.sem, tok.val)

    def begin(self, R, W):
        for b in R:
            if b.w is not None:
                self.wait(b.w[1])
        for b in W:
            if b.w is not None and b.w[0] != self.name:
                self.wait(b.w[1])
            for (en, tk) in b.r:
                if en != self.name:
                    self.wait(tk)

    def end(self, ins, R, W, sig=True):
        tok = Tok()
        self.pending.append(tok)
        for b in R:
            b.r.append((self.name, tok))
        for b in W:
            b.w = (self.name, tok)
            b.r = []
        if sig:
            if self.cnt >= 30000:
                self._newsem()
            self.cnt += 1
            ins.then_inc(self.sem, 1)
            for t in self.pending:
                t.sem = self.sem
                t.val = self.cnt
            self.pending = []
            self.last = tok
        return tok

    def op(self, fn, R=(), W=(), sig=True):
        self.begin(R, W)
        ins = fn(self.e)
        self.nins += 1
        return self.end(ins, R, W, sig)


class DmaQ:
    def __init__(self, K, E, nsem):
        self.K = K
        self.E = E
        self.sems = [K.es.enter_context(K.nc.semaphore(f"d_{E.name}{i}")) for i in range(nsem)]
        self.vals = [0] * nsem
        self.last = [None] * nsem
        self.i = 0

    def issue(self, fn, R=(), W=()):
        E = self.E
        E.begin(R, W)
        j = self.i
        self.i = (self.i + 1) % len(self.sems)
        if self.vals[j] >= 30000:
            E.wait(self.last[j])
            self.sems[j] = self.K.es.enter_context(self.K.nc.semaphore(f"d_{E.name}{j}_{self.K.uid()}"))
            self.vals[j] = 0
            self.last[j] = None
        E.wait(self.last[j])
        ins = fn(E.e)
        self.vals[j] += 16
        ins.then_inc(self.sems[j], 16)
        tok = Tok(self.sems[j], self.vals[j])
        self.last[j] = tok
        for b in R:
            b.r.append(("dma", tok))
        for b in W:
            b.w = ("dma", tok)
            b.r = []
        return tok


class Ring:
    def __init__(self, items):
        self.items = items
        self.i = 0

    def get(self):
        it = self.items[self.i]
        self.i = (self.i + 1) % len(self.items)
        return it


class T:
    __slots__ = ("t", "b")

    def __init__(self, t, b):
        self.t = t
        self.b = b


class KB:
    def __init__(self, depth=DEPTH, debug=None):
        self.depth = depth
        self.debug = debug
        self.es = ExitStack()
        self.nc = bass.Bass("TRN2", target_bir_lowering=False)
        self._uid = 0
        nc = self.nc
        self.pe = Eng(self, nc.tensor, "pe")
        self.act = Eng(self, nc.scalar, "act")
        self.dve = Eng(self, nc.vector, "dve")
        self.pool = Eng(self, nc.gpsimd, "pool")
        self.sp = Eng(self, nc.sync, "sp")
        self.engs = [self.pe, self.act, self.dve, self.pool, self.sp]
        self.qsp = DmaQ(self, self.sp, 12)
        self.qpl = DmaQ(self, self.pool, 12)
        self.dbg_outs = []
        self.marks = []

    def mark(self, label):
        self.marks.append((label, self.pe.nins))
        if self.debug == "marks" and hasattr(self, "identB"):
            ps = self.ps_get()
            self.mm(ps.t[0:3, 0:6], self.identB.t[:, 0:3], self.identB.t[:, 0:6], start=True, stop=True, R=[self.identB.b], W=[ps.b])

    def uid(self):
        self._uid += 1
        return self._uid

    def dram_in(self, name, shape, dt=F32):
        return self.nc.dram_tensor(name, list(shape), dt, kind="ExternalInput").ap()

    def dram_out(self, name, shape, dt=F32):
        return self.nc.dram_tensor(name, list(shape), dt, kind="ExternalOutput").ap()

    def dram_scr(self, name, shape, dt=F32):
        return self.nc.dram_tensor(name, list(shape), dt, kind="Internal").ap()

    def sbt(self, name, shape, dt):
        h = self.es.enter_context(self.nc.sbuf_tensor("s_" + name, list(shape), dt))
        return T(h, Buf(name))

    def carve_reset(self):
        self.aoff = 0

    def av(self, name, shape, dt):
        n = 1
        for s in shape[1:]:
            n *= s
        words = (n * (2 if dt == BF16 else 4) + 3) // 4
        words = (words + 7) // 8 * 8
        assert self.aoff + words <= self.arena_words, f"arena overflow at {name}: {self.aoff + words}"
        v = self.arena[:, self.aoff:self.aoff + words]
        self.aoff += words
        if dt != F32:
            v = v.bitcast(dt)
        v = v[:, 0:n]
        if len(shape) == 3:
            v = v.rearrange("p (a b) -> p a b", a=shape[1])
        elif len(shape) == 4:
            v = v.rearrange("p (a b c) -> p a b c", a=shape[1], b=shape[2])
        return T(v, Buf(name))

    def barrier(self):
        toks = [e.last for e in self.engs if e.last is not None]
        for q in (self.qsp, self.qpl):
            toks += [t for t in q.last if t is not None]
        for e in self.engs:
            for t in toks:
                e.wait(t)

    def dma(self, q, out, in_, R=(), W=(), **kw):
        return q.issue(lambda e: e.dma_start(out=out, in_=in_, **kw), R=R, W=W)

    def ps_get(self):
        return self.psring.get()

    def mm(self, out, lhsT, rhs, start, stop, R, W, sig=None):
        if sig is None:
            sig = stop
        return self.pe.op(lambda e: e.matmul(out, lhsT, rhs, start=start, stop=stop), R=R, W=W, sig=sig)

    def tr(self, out, in_, ident, R, W, sig=True):
        return self.pe.op(lambda e: e.transpose(out, in_, ident), R=R, W=W, sig=sig)

    def A(self, out, in_, func, R, W, **kw):
        return self.act.op(lambda e: e.activation(out=out, in_=in_, func=func, **kw), R=R, W=W)

    def V_tt(self, out, in0, in1, op, R, W, eng=None):
        eng = eng or self.dve
        return eng.op(lambda e: e.tensor_tensor(out=out, in0=in0, in1=in1, op=op), R=R, W=W)

    def V_ts(self, out, in0, s1, s2, op0, op1=None, R=(), W=(), eng=None, accum_out=None):
        eng = eng or self.dve
        kw = {}
        if op1 is not None:
            kw["op1"] = op1
        if accum_out is not None:
            kw["accum_out"] = accum_out
        return eng.op(lambda e: e.tensor_scalar(out=out, in0=in0, scalar1=s1, scalar2=s2, op0=op0, **kw), R=R, W=W)

    def V_stt(self, out, in0, scalar, in1, op0, op1, R, W):
        return self.dve.op(lambda e: e.scalar_tensor_tensor(out=out, in0=in0, scalar=scalar, in1=in1, op0=op0, op1=op1), R=R, W=W)

    def V_cp(self, out, in_, R, W, eng=None):
        eng = eng or self.dve
        return eng.op(lambda e: e.tensor_copy(out=out, in_=in_), R=R, W=W)

    def V_rcp(self, out, in_, R, W):
        return self.dve.op(lambda e: e.reciprocal(out=out, in_=in_), R=R, W=W)

    def memset(self, eng, ap, val, W):
        return eng.op(lambda e: e.memset(ap, val), R=(), W=W)

    def setup(self):
        nc = self.nc
        L = DEPTH
        di = self.dram_in
        self.xs_in = di("xs_in", [T_S, D])
        self.xp_in = di("xp_in", [2 * T_P, D])
        self.cckv = di("cckv", [L, PAST, KVL])
        self.ckr = di("ckr", [L, PAST, DR])
        self.condT = di("condT", [128, 8, 2])
        self.w_in_d = di("w_in_x", [L, 128, 8, NCOL])
        self.w_uq_d = di("w_uq_x", [L, 128, 3, 1024])
        self.w_uk_d = di("w_uk_x", [L, 128, 2, 512])
        self.w_uv_d = di("w_uv_x", [L, 128, 2, 512])
        self.poolw_d = di("poolw_x", [L, 128, 2, 256])
        self.w_out_d = di("w_out_x", [L, 128, 8, D])
        self.w_r_d = di("w_r_x", [L, 128, 8, NE])
        self.w_ada_d = di("w_ada", [L, D, 6 * D])
        self.vecs_d = di("vecs", [L, 128, NV])
        self.w_gate_d = di("w_gate", [L, NE, D, EF])
        self.w_up_d = di("w_up", [L, NE, D, EF])
        self.w_down_d = di("w_down", [L, NE, EF, D])
        self.identF_d = di("identF", [128, 128])
        self.gsum_d = di("gsum", [128, 128])
        self.lstrict_d = di("lstrict", [128, 128])
        self.iota1_d = di("iota1", [128, 512])
        self.tidhl_d = di("tidhl", [128, 36, 2])
        self.edges_d = di("edges", [128, 2, 16])
        self.misc_d = di("misc", [128, 16])
        self.cosT_d = di("cosT", [DR, T_S])
        self.sinT_d = di("sinT", [DR, T_S])
        self.tidcol_d = di("tidcol", [128, 36])
        self.y_s = self.dram_out("y_s", [T_S, D])
        self.y_p = self.dram_out("y_p", [2 * T_P, D])
        self.nckv = self.dram_out("nckv", [2, L, T_P, KVL])
        self.nkr = self.dram_out("nkr", [2, L, T_P, DR])
        self.xT_d = self.dram_scr("xT_d", [D, TTOT])
        self.x1T_d = self.dram_scr("x1T_d", [D, TTOT])
        self.u2rows_d = self.dram_scr("u2rows_d", [TTOT, ROWW], BF16)
        self.ffn_d = self.dram_scr("ffn_d", [TTOT, D])
        self.xT_b = [Buf(f"xT_d{c}") for c in range(TTOT // CH)]
        self.x1T_b = [Buf(f"x1T_d{c}") for c in range(TTOT // CH)]
        self.u2rows_b = Buf("u2rows_d")
        self.ffn_b = Buf("ffn_d")
        self.out_b = Buf("outs")

        s = self.sbt
        self.w_in = s("w_in", [128, 8, NCOL], BF16)
        self.w_uq = s("w_uq", [128, 3, 1024], BF16)
        self.w_uk = s("w_uk", [128, 2, 512], BF16)
        self.w_uv = s("w_uv", [128, 2, 512], BF16)
        self.poolw = s("poolw", [128, 2, 256], BF16)
        self.w_out = s("w_out", [128, 8, D], BF16)
        self.w_r = s("w_r", [128, 8, NE], F32)
        self.vec = s("vec", [128, NV], F32)
        self.modT = s("modT", [128, 48, 2], F32)
        self.s1p = s("s1p", [128, 8, 2], F32)
        self.g1a = s("g1a", [128, 8, 2], F32)
        self.G2 = s("G2", [128, 8, 2], F32)
        self.B2 = s("B2", [128, 8, 2], F32)
        self.g2a = s("g2a", [128, 8, 2], F32)
        self.wr2 = s("wr2", [128, 2, 8, NE], F32)
        self.rconst = s("rconst", [1, 2, NE], F32)
        self.scT = s("scT", [128, 8, 2], F32)
        self.identF = s("identF", [128, 128], F32)
        self.identB = s("identB", [128, 128], BF16)
        self.onesB = s("onesB", [128, 128], BF16)
        self.onesrow = s("onesrow", [1, 128], F32)
        self.tidhl = s("tidhl", [128, 36, 2], BF16)
        self.tidcol = s("tidcol", [128, 36], F32)
        self.edges = s("edges", [128, 2, 16], F32)
        self.misc = s("misc", [128, 16], F32)
        self.affS = s("affS", [128, 36, NE], F32)
        self.xt = s("xt", [128, 8, CW], F32)
        self.ubuf = s("ubuf", [128, 8, CW], BF16)
        self.st = [s(f"st{i}", [128, CH], F32) for i in range(3)]
        self.stf = [s(f"stf{i}", [128, CH], F32) for i in range(2)]
        self.u2b = [Buf(f"u2b{k}") for k in range(8)]
        self.zbk = Ring([s(f"zbk{i}", [128, CH], BF16) for i in range(2)])
        self.zsqk = Ring([s(f"zsqk{i}", [128, CH], BF16) for i in range(2)])
        self.xtb = [Buf(f"xt{k}") for k in range(8)]
        self.ubb = [Buf(f"ub{k}") for k in range(8)]
        self.affSb = [Buf(f"affS{g}") for g in range(TTOT // 128)]
        self.u2rows_bt = [Buf(f"u2r{g}") for g in range(TTOT // 128)]
        self.ffn_bt = [Buf(f"ffn{g}") for g in range(TTOT // 128)]
        self.vTM = [s("vTM0", [128, 4, 128], F32), s("vTM1", [128, 2, NE], F32), s("vTM2", [128, 2, NE], F32)]
        self.arena_words = 118 * 256
        self.arena = self.es.enter_context(nc.sbuf_tensor("arena", [128, self.arena_words], F32))
        banks = [T(self.es.enter_context(nc.psum_tensor(f"ps{i}", [128, 512], F32)), Buf(f"ps{i}")) for i in range(8)]
        self.psring = Ring(banks[:6])
        self.psS = Ring(banks[:3])
        self.psB = Ring(banks[3:6])
        self.psacc = Ring([(banks[6], banks[7])])
        self.bcreg = None

    def load_consts(self):
        q = self.qsp
        for (dst, src) in [(self.identF, self.identF_d), (self.edges, self.edges_d),
                           (self.misc, self.misc_d), (self.scT, self.condT), (self.tidcol, self.tidcol_d)]:
            self.dma(q, dst.t[:], src, W=[dst.b])
        self.V_cp(self.identB.t[:], self.identF.t[:], R=[self.identF.b], W=[self.identB.b])
        self.carve_reset()
        tf = self.av("tidhlF", [128, 36, 2], F32)
        self.dma(q, tf.t[:], self.tidhl_d, W=[tf.b])
        self.V_cp(self.tidhl.t[:], tf.t[:], R=[tf.b], W=[self.tidhl.b])
        self.barrier()
        self.memset(self.dve, self.onesB.t[:], 1.0, W=[self.onesB.b])
        self.memset(self.dve, self.onesrow.t[:], 1.0, W=[self.onesrow.b])
        self.A(self.scT.t[:], self.scT.t[:], AF.Silu, R=[self.scT.b], W=[self.scT.b])

    def transpose_in(self):
        self.carve_reset()
        tin = [self.av(f"tin{i}", [128, D], F32) for i in range(2)]
        tout = [self.av(f"tout{i}", [128, 8, 128], F32) for i in range(2)]
        xTv = self.xT_d.rearrange("(k p) t -> p k t", p=128)
        for g in range(TTOT // 128):
            src = self.xs_in[g * 128:(g + 1) * 128, :] if g < T_S // 128 else self.xp_in[(g - T_S // 128) * 128:(g - T_S // 128 + 1) * 128, :]
            ti = tin[g % 2]
            to = tout[g % 2]
            self.dma(self.qsp, ti.t[:], src, W=[ti.b])
            for hf in range(2):
                ps = self.ps_get()
                for kk in range(4):
                    k = hf * 4 + kk
                    self.tr(ps.t[:, kk * 128:(kk + 1) * 128], ti.t[:, k * 128:(k + 1) * 128], self.identF.t[:],
                            R=[ti.b, self.identF.b], W=[ps.b], sig=(kk == 3))
                if hf == 0:
                    self.A(to.t[:, 0:4, :], ps.t[:, :].rearrange("p (a b) -> p a b", a=4), AF.Copy, R=[ps.b], W=[to.b])
                else:
                    self.V_cp(to.t[:, 4:8, :], ps.t[:, :].rearrange("p (a b) -> p a b", a=4), R=[ps.b], W=[to.b])
            cb = self.xT_b[g // 2]
            self.dma(self.qsp, xTv[:, :, g * 128:(g + 1) * 128], to.t[:], R=[to.b], W=[cb])

    def load_layer_weights(self, l):
        q = self.qpl
        for k in range(8):
            self.dma(q, self.w_in.t[:, k, :], self.w_in_d[l][:, k, :], W=[self.w_in.b], max_dma_last_dim=4096)
        for k in range(3):
            self.dma(q, self.w_uq.t[:, k, :], self.w_uq_d[l][:, k, :], W=[self.w_uq.b], max_dma_last_dim=4096)
        self.dma(q, self.w_uk.t[:], self.w_uk_d[l], W=[self.w_uk.b], max_dma_last_dim=2048)
        self.dma(q, self.w_uv.t[:], self.w_uv_d[l], W=[self.w_uv.b], max_dma_last_dim=2048)
        self.dma(q, self.poolw.t[:], self.poolw_d[l], W=[self.poolw.b], max_dma_last_dim=1024)
        for k in range(8):
            self.dma(q, self.w_out.t[:, k, :], self.w_out_d[l][:, k, :], W=[self.w_out.b], max_dma_last_dim=4096)
        self.dma(self.qsp, self.w_r.t[:], self.w_r_d[l], W=[self.w_r.b])

    def mod_phase(self, l):
        self.carve_reset()
        wst = [self.av(f"wst{i}", [128, 8, 512], F32) for i in range(2)]
        tmp = self.av("modtmp", [128, 8, 2], F32)
        vec, modT = self.vec, self.modT
        self.dma(self.qsp, vec.t[:], self.vecs_d[l], W=[vec.b])
        for blk in range(12):
            w = wst[blk % 2]
            self.dma(self.qsp, w.t[:], self.w_ada_d[l][:, blk * 512:(blk + 1) * 512].rearrange("(k p) n -> p k n", p=128), W=[w.b])
            ps = self.ps_get()
            for j in range(4):
                for k in range(8):
                    self.mm(ps.t[:, j * 2:(j + 1) * 2], w.t[:, k, j * 128:(j + 1) * 128], self.scT.t[:, k, :],
                            start=(k == 0), stop=(k == 7), R=[w.b, self.scT.b], W=[ps.b])
            for j in range(4):
                jj = blk * 4 + j
                self.V_ts(modT.t[:, jj, :], ps.t[:, j * 2:(j + 1) * 2], vec.t[:, V_BADA + jj:V_BADA + jj + 1], None, ALU.add,
                          R=[ps.b, vec.b], W=[modT.b])
        sh1, sc1, gt1 = modT.t[:, 0:8, :], modT.t[:, 8:16, :], modT.t[:, 16:24, :]
        sh2, sc2, gt2 = modT.t[:, 24:32, :], modT.t[:, 32:40, :], modT.t[:, 40:48, :]
        mb = [modT.b]
        self.V_ts(self.s1p.t[:], sc1, 1.0, None, ALU.add, R=mb, W=[self.s1p.b])
        self.V_ts(self.g1a.t[:], gt1, 1.0 / ALPHA, None, ALU.mult, R=mb, W=[self.g1a.b])
        self.V_ts(self.g2a.t[:], gt2, 1.0 / ALPHA, None, ALU.mult, R=mb, W=[self.g2a.b])
        self.V_ts(tmp.t[:], sc2, 1.0, None, ALU.add, R=mb, W=[tmp.b])
        for c in range(2):
            self.V_tt(self.G2.t[:, :, c], tmp.t[:, :, c], vec.t[:, V_L1G:V_L1G + 8], ALU.mult, R=[tmp.b, vec.b], W=[self.G2.b])
            self.V_tt(self.B2.t[:, :, c], tmp.t[:, :, c], vec.t[:, V_L1B:V_L1B + 8], ALU.mult, R=[tmp.b, vec.b], W=[self.B2.b])
        self.V_tt(self.B2.t[:], self.B2.t[:], sh2, ALU.add, R=[self.B2.b] + mb, W=[self.B2.b])
        for c in range(2):
            for k in range(8):
                self.V_ts(self.wr2.t[:, c, k, :], self.w_r.t[:, k, :], self.G2.t[:, k, c:c + 1], None, ALU.mult,
                          R=[self.w_r.b, self.G2.b], W=[self.wr2.b])
            ps = self.ps_get()
            for k in range(8):
                self.mm(ps.t[0:1, 0:NE], self.B2.t[:, k, c:c + 1], self.w_r.t[:, k, :], start=(k == 0), stop=(k == 7),
                        R=[self.B2.b, self.w_r.b], W=[ps.b])
            self.V_cp(self.rconst.t[0:1, c, :], ps.t[0:1, 0:NE], R=[ps.b], W=[self.rconst.b])

    def carve_ab(self):
        self.carve_reset()
        a = self.av
        self.KT = a("KT", [128, NH, PAST + T_S], BF16)
        self.krT = a("krT", [128, PAST + T_S], BF16)
        self.Vt = a("Vt", [128, (PAST + T_S) // 128, NH * VA], BF16)
        nkc = (PAST + T_S) // CH
        self.KTb = [Buf(f"KT{i}") for i in range(nkc)]
        self.krTb = [Buf(f"krT{i}") for i in range(nkc)]
        self.Vb = [Buf(f"V{i}") for i in range(nkc)]
        self.qn = a("qn", [128, 3, CH], BF16)
        self.qnope = a("qnope", [128, NH, CH], BF16)
        self.qrope = a("qrope", [128, NH, CH], BF16)
        self.sqb = a("sqb", [128, 3, CH], BF16)
        self.PT = Ring([a(f"PT{i}", [128, 2, CH], BF16) for i in range(2)])
        pb_off = self.aoff
        self.pb = [a(f"pb{i}", [128, 2, CW], F32) for i in range(4)]
        assert self.aoff - pb_off == 8 * CW
        self.xnext = self.arena[:, pb_off:pb_off + 8 * CW].rearrange("p (a b) -> p a b", a=8)
        self.gbS = a("gbS", [128, 2, CH], F32)
        self.pooledB = a("pooledB", [128, 2, CH], BF16)
        self.ckvB = a("ckvB", [128, 2, CH], BF16)
        self.rowbuf = Ring([a(f"rowbuf{i}", [128, ROWW], BF16) for i in range(1)])
        self.u2buf = a("u2buf", [128, 8, CH], BF16)
        for rw in self.rowbuf.items:
            self.memset(self.pool, rw.t[:, D + 2 * NE:ROWW], 0.0, W=[rw.b])
        self.rt = a("rt", [128, 2, CH], F32)
        self.otm = Ring([a(f"otm{i}", [128, DV], BF16) for i in range(2)])
        self.smx = a("smx", [128, 8], F32)
        self.ex = a("ex", [128, NE], F32)
        self.edt = a("edt", [128, 2, 8], F32)
        save = self.aoff
        self.ckvF = a("ckvF", [128, 2, CH], F32)
        self.krF = a("krF", [128, CH], F32)
        self.outT = a("outT", [128, 2, KVL], F32)
        self.krout = a("krout", [128, 2, DR], F32)
        self.aoff = save
        self.ctxF = [a(f"ctxF{i}", [128, KVL], F32) for i in range(2)]
        self.ctxkr = [a(f"ctxkr{i}", [128, DR], F32) for i in range(2)]
        self.memset(self.pool, self.krT.t[DR:128, :], 0.0, W=self.krTb)
        self.memset(self.pool, self.Vt.t[:], 1.0, W=self.Vb)
        self.memset(self.pool, self.qrope.t[DR:128, :, :], 0.0, W=[self.qrope.b])

    def kv_up(self, kc):
        koff = kc * CH
        for h in range(NH):
            ps = self.ps_get()
            for m in range(2):
                self.mm(ps.t[:, 0:CH], self.w_uk.t[:, m, h * 128:(h + 1) * 128], self.ckvB.t[:, m, :],
                        start=(m == 0), stop=(m == 1), R=[self.w_uk.b, self.ckvB.b], W=[ps.b])
            if h % 2 == 0:
                self.A(self.KT.t[:, h, koff:koff + CH], ps.t[:, 0:CH], AF.Copy, R=[ps.b], W=[self.KTb[kc]])
            else:
                self.V_cp(self.KT.t[:, h, koff:koff + CH], ps.t[:, 0:CH], R=[ps.b], W=[self.KTb[kc]])
        for j in range(2):
            ps = self.ps_get()
            for m in range(2):
                self.mm(ps.t[:, :], self.ckvB.t[:, m, j * 128:(j + 1) * 128], self.w_uv.t[:, m, :],
                        start=(m == 0), stop=(m == 1), R=[self.w_uv.b, self.ckvB.b], W=[ps.b])
            kt = koff // 128 + j
            dst = self.Vt.t[:, kt, :].rearrange("p (h v) -> p h v", h=NH)[:, :, 0:DV]
            src = ps.t[:, :].rearrange("p (h v) -> p h v", h=NH)
            if j == 0:
                self.A(dst, src, AF.Copy, R=[ps.b], W=[self.Vb[kc]])
            else:
                self.V_cp(dst, src, R=[ps.b], W=[self.Vb[kc]])

    def ctx_phase(self, l):
        for jj in range(PAST // CH):
            for j in range(2):
                tile = jj * 2 + j
                cf, ck = self.ctxF[tile % 2], self.ctxkr[tile % 2]
                self.dma(self.qsp, cf.t[:], self.cckv[l][tile * 128:(tile + 1) * 128, :], W=[cf.b])
                self.dma(self.qsp, ck.t[:], self.ckr[l][tile * 128:(tile + 1) * 128, :], W=[ck.b])
                ps = self.ps_get()
                for m in range(2):
                    self.tr(ps.t[:, m * 128:(m + 1) * 128], cf.t[:, m * 128:(m + 1) * 128], self.identF.t[:],
                            R=[cf.b, self.identF.b], W=[ps.b], sig=(m == 1))
                self.V_cp(self.ckvB.t[:, :, j * 128:(j + 1) * 128], ps.t[:, 0:256].rearrange("p (m k) -> p m k", m=2),
                          R=[ps.b], W=[self.ckvB.b])
                ps2 = self.ps_get()
                self.tr(ps2.t[0:DR, 0:128], ck.t[:, :], self.identF.t[:], R=[ck.b, self.identF.b], W=[ps2.b])
                self.A(self.krT.t[0:DR, tile * 128:(tile + 1) * 128], ps2.t[0:DR, 0:128], AF.Copy, R=[ps2.b], W=[self.krTb[jj]])
            self.kv_up(jj)

    def u1_gen(self, cond, c0, c1):
        xt, ub = self.xt, self.ubuf
        for k in range(8):
            sc = self.s1p.t[:, k, cond:cond + 1]
            sh = self.modT.t[:, k, cond:cond + 1]
            if k % 2 == 0:
                self.V_ts(ub.t[:, k, c0:c1], xt.t[:, k, c0:c1], sc, sh, ALU.mult, ALU.add,
                          R=[self.xtb[k], self.s1p.b, self.modT.b], W=[self.ubb[k]])
            else:
                self.A(ub.t[:, k, c0:c1], xt.t[:, k, c0:c1], AF.Identity, scale=sc, bias=sh,
                       R=[self.xtb[k], self.s1p.b, self.modT.b], W=[self.ubb[k]])

    def load_rope(self, t0):
        self.dma(self.qsp, self.rt.t[0:DR, 0, :], self.cosT_d[:, t0:t0 + CH], W=[self.rt.b])
        self.dma(self.qsp, self.rt.t[0:DR, 1, :], self.sinT_d[:, t0:t0 + CH], W=[self.rt.b])

    def phase_a_chunk(self, l, r, c):
        cond = 1 if r == 0 else 0
        t0 = c * CH
        g0 = RB[r] + t0
        gc = g0 // CH
        kc = (PAST // CH if r == 0 else 0) + c
        koff = kc * CH
        xTv = self.xT_d.rearrange("(k p) t -> p k t", p=128)
        xt, ub = self.xt, self.ubuf
        self.dma(self.qsp, xt.t[:, :, HAL:HAL + CH], xTv[:, :, g0:g0 + CH], R=[self.xT_b[gc]], W=self.xtb)
        if r == 0:
            self.load_rope(t0)
        self.u1_gen(cond, HAL, HAL + CH)
        pkv = [self.ps_get(), self.ps_get()]
        for m in range(2):
            for k in range(8):
                self.mm(pkv[m].t[:, 0:CH], self.w_in.t[:, k, C_KV + m * 128:C_KV + (m + 1) * 128], ub.t[:, k, HAL:HAL + CH],
                        start=(k == 0), stop=(k == 7), R=[self.w_in.b, self.ubb[k]], W=[pkv[m].b])
        pkr = self.ps_get()
        for k in range(8):
            self.mm(pkr.t[0:DR, 0:CH], self.w_in.t[:, k, C_KR:C_KR + DR], ub.t[:, k, HAL:HAL + CH],
                    start=(k == 0), stop=(k == 7), R=[self.w_in.b, self.ubb[k]], W=[pkr.b])
        if r == 0:
            for k in range(8):
                self.mm(pkr.t[0:DR, CH:2 * CH], self.w_in.t[:, k, C_KRS:C_KRS + DR], ub.t[:, k, HAL:HAL + CH],
                        start=(k == 0), stop=(k == 7), R=[self.w_in.b, self.ubb[k]], W=[pkr.b])
        for m in range(2):
            self.A(self.sqb.t[:, m, :], pkv[m].t[:, 0:CH], AF.Square, R=[pkv[m].b], W=[self.sqb.b])
        pss = self.ps_get()
        for m in range(2):
            self.mm(pss.t[:, 0:CH], self.onesB.t[:], self.sqb.t[:, m, :], start=(m == 0), stop=(m == 1),
                    R=[self.onesB.b, self.sqb.b], W=[pss.b])
        st0 = self.st[0]
        self.A(st0.t[:], pss.t[:, 0:CH], AF.Ln, scale=1.0 / KVL, bias=self.misc.t[:, 0:1], R=[pss.b, self.misc.b], W=[st0.b])
        self.A(st0.t[:], st0.t[:], AF.Exp, scale=-0.5, R=[st0.b], W=[st0.b])
        for m in range(2):
            nrm = self.vec.t[:, V_KVN + m:V_KVN + m + 1]
            if r == 0:
                self.V_stt(self.ckvB.t[:, m, :], pkv[m].t[:, 0:CH], nrm, st0.t[:], ALU.mult, ALU.mult,
                           R=[pkv[m].b, self.vec.b, st0.b], W=[self.ckvB.b])
            else:
                self.V_stt(self.ckvF.t[:, m, :], pkv[m].t[:, 0:CH], nrm, st0.t[:], ALU.mult, ALU.mult,
                           R=[pkv[m].b, self.vec.b, st0.b], W=[self.ckvF.b])
        if r != 0:
            self.V_cp(self.ckvB.t[:], self.ckvF.t[:], R=[self.ckvF.b], W=[self.ckvB.b])
        if r == 0:
            s1, s2 = self.st[1], self.st[2]
            self.V_tt(s1.t[0:DR, :], pkr.t[0:DR, 0:CH], self.rt.t[0:DR, 0, :], ALU.mult, R=[pkr.b, self.rt.b], W=[s1.b])
            self.V_tt(s2.t[0:DR, :], pkr.t[0:DR, CH:2 * CH], self.rt.t[0:DR, 1, :], ALU.mult, R=[pkr.b, self.rt.b], W=[s2.b])
            self.V_tt(self.krT.t[0:DR, koff:koff + CH], s1.t[0:DR, :], s2.t[0:DR, :], ALU.add, R=[s1.b, s2.b], W=[self.krTb[kc]])
        else:
            self.A(self.krF.t[0:DR, :], pkr.t[0:DR, 0:CH], AF.Copy, R=[pkr.b], W=[self.krF.b])
            self.V_cp(self.krT.t[0:DR, koff:koff + CH], self.krF.t[0:DR, :], R=[self.krF.b], W=[self.krTb[kc]])
        self.kv_up(kc)
        if r != 0:
            for j in range(2):
                ps = self.ps_get()
                for m in range(2):
                    self.tr(ps.t[:, m * 128:(m + 1) * 128], self.ckvF.t[:, m, j * 128:(j + 1) * 128], self.identF.t[:],
                            R=[self.ckvF.b, self.identF.b], W=[ps.b], sig=(m == 1))
                self.A(self.outT.t[:, j, :], ps.t[:, 0:KVL], AF.Copy, R=[ps.b], W=[self.outT.b])
            self.dma(self.qsp, self.nckv[r - 1, l].rearrange("(j p) f -> p j f", p=128), self.outT.t[:], R=[self.outT.b], W=[self.out_b])
            ps = self.ps_get()
            for j in range(2):
                self.tr(ps.t[:, j * DR:(j + 1) * DR], self.krF.t[0:DR, j * 128:(j + 1) * 128], self.identF.t[0:DR, 0:DR],
                        R=[self.krF.b, self.identF.b], W=[ps.b], sig=(j == 1))
            self.V_cp(self.krout.t[:], ps.t[:, 0:2 * DR].rearrange("p (j f) -> p j f", j=2), R=[ps.b], W=[self.krout.b])
            self.dma(self.qsp, self.nkr[r - 1, l].rearrange("(j p) f -> p j f", p=128), self.krout.t[:], R=[self.krout.b], W=[self.out_b])

    def ln_core(self, ring=None, gen=False, X=None):
        g = self._ln_core(ring, X)
        if gen:
            return g
        for _ in g:
            pass

    def _ln_core(self, ring, X=None):
        xt = X["xt"] if X else self.xt
        xtb = X["xtb"] if X else self.xtb
        stl = X["st"] if X else self.st
        ring = ring or self.psring
        psm, psq = ring.get(), ring.get()
        cs = slice(HAL, HAL + CH)
        for j in range(8):
            zb, zs = self.zbk.get(), self.zsqk.get()
            self.A(zb.t[:], xt.t[:, j, cs], AF.Copy, R=[xtb[j]], W=[zb.b])
            self.V_tt(zs.t[:], xt.t[:, j, cs], xt.t[:, j, cs], ALU.mult, R=[xtb[j]], W=[zs.b])
            yield
            self.mm(psm.t[:, 0:CH], self.onesB.t[:], zb.t[:], start=(j == 0), stop=(j == 7), R=[self.onesB.b, zb.b], W=[psm.b], sig=True)
            self.mm(psq.t[:, 0:CH], self.onesB.t[:], zs.t[:], start=(j == 0), stop=(j == 7), R=[self.onesB.b, zs.b], W=[psq.b], sig=True)
            yield
        s0, s1, s2 = stl[0], stl[1], stl[2]
        self.A(s0.t[:], psm.t[:, 0:CH], AF.Copy, scale=1.0 / D, R=[psm.b], W=[s0.b])
        self.V_tt(s1.t[:], s0.t[:], s0.t[:], ALU.mult, R=[s0.b], W=[s1.b])
        yield
        self.V_stt(s1.t[:], psq.t[:, 0:CH], 1.0 / D, s1.t[:], ALU.mult, ALU.subtract, R=[psq.b, s1.b], W=[s1.b])
        yield
        self.A(s1.t[:], s1.t[:], AF.Ln, bias=self.misc.t[:, 1:2], scale=1.0, R=[s1.b, self.misc.b], W=[s1.b])
        yield
        self.A(s1.t[:], s1.t[:], AF.Exp, scale=-0.5, R=[s1.b], W=[s1.b])
        self.V_stt(s2.t[:], s0.t[:], -1.0, s1.t[:], ALU.mult, ALU.mult, R=[s0.b, s1.b], W=[s2.b])
        yield
        for j in range(8):
            self.V_tt(xt.t[:, j, cs], xt.t[:, j, cs], s1.t[:], ALU.mult, R=[xtb[j], s1.b], W=[xtb[j]])
            self.V_tt(xt.t[:, j, cs], xt.t[:, j, cs], s2.t[:], ALU.add, R=[xtb[j], s2.b], W=[xtb[j]], eng=self.pool)
            yield

    def affine_xt(self, gcol0, bcol0, gen=False, X=None):
        g = self._affine_xt(gcol0, bcol0, X)
        if gen:
            return g
        for _ in g:
            pass

    def _affine_xt(self, gcol0, bcol0, X=None):
        xt = X["xt"] if X else self.xt
        xtb = X["xtb"] if X else self.xtb
        cs = slice(HAL, HAL + CH)
        for k in range(8):
            g = self.vec.t[:, gcol0 + k:gcol0 + k + 1]
            b = self.vec.t[:, bcol0 + k:bcol0 + k + 1]
            if k % 2 == 0:
                self.V_ts(xt.t[:, k, cs], xt.t[:, k, cs], g, b, ALU.mult, ALU.add, R=[xtb[k], self.vec.b], W=[xtb[k]])
            else:
                self.A(xt.t[:, k, cs], xt.t[:, k, cs], AF.Identity, scale=g, bias=b, R=[xtb[k], self.vec.b], W=[xtb[k]])
            yield

    def load_xnext(self, r, c):
        nch = REQ_T[r] // CH
        g0 = RB[r] + c * CH
        gc = g0 // CH
        first, last = (c == 0), (c == nch - 1)
        lo = 0 if first else -HAL
        hi = CH if last else CH + HAL
        xTv = self.xT_d.rearrange("(k p) t -> p k t", p=128)
        rb = [self.xT_b[gc]] + ([] if first else [self.xT_b[gc - 1]]) + ([] if last else [self.xT_b[gc + 1]])
        self.dma(self.qsp, self.xnext[:, :, HAL + lo:HAL + hi], xTv[:, :, g0 + lo:g0 + hi], R=rb, W=[q.b for q in self.pb])

    def pb_ctx(self, r, c):
        Tr = REQ_T[r]
        nch = Tr // CH
        t0 = c * CH
        g0 = RB[r] + t0
        first, last = (c == 0), (c == nch - 1)
        return dict(cond=1 if r == 0 else 0, t0=t0, g0=g0, gc=g0 // CH, first=first, last=last,
                    lo=0 if first else -HAL, hi=CH if last else CH + HAL)

    def pb_front(self, l, r, c):
        X = self.pb_ctx(r, c)
        cond, first, last, lo, hi = X["cond"], X["first"], X["last"], X["lo"], X["hi"]
        ub = self.ubuf
        cs = slice(HAL, HAL + CH)
        pbb = [q.b for q in self.pb]
        xn = self.xnext
        if first:
            self.load_xnext(r, c)
        if r == 0:
            self.load_rope(X["t0"])
        if first:
            self.memset(self.pool, ub.t[:, :, 0:HAL], 0.0, W=self.ubb)
        if last:
            self.memset(self.pool, ub.t[:, :, HAL + CH:CW], 0.0, W=self.ubb)
        c0, c1 = HAL + lo, HAL + hi
        for k in range(8):
            sc = self.s1p.t[:, k, cond:cond + 1]
            sh = self.modT.t[:, k, cond:cond + 1]
            if k % 2 == 0:
                self.V_ts(ub.t[:, k, c0:c1], xn[:, k, c0:c1], sc, sh, ALU.mult, ALU.add,
                          R=[pbb[k // 2], self.s1p.b, self.modT.b], W=[self.ubb[k]])
            else:
                self.A(ub.t[:, k, c0:c1], xn[:, k, c0:c1], AF.Identity, scale=sc, bias=sh,
                       R=[pbb[k // 2], self.s1p.b, self.modT.b], W=[self.ubb[k]])
        wb = self.w_in.b
        ring = self.psS

        def win(ps, col, ncols, c0, c1, prow=128):
            for k in range(8):
                self.mm(ps.t[0:prow, 0:c1 - c0], self.w_in.t[:, k, col:col + ncols], ub.t[:, k, c0:c1],
                        start=(k == 0), stop=(k == 7), R=[wb, self.ubb[k]], W=[ps.b])

        pq = self.psB.items
        for m in range(3):
            win(pq[m], C_Q + m * 128, 128, HAL, HAL + CH)
            self.A(self.sqb.t[:, m, :], pq[m].t[:, 0:CH], AF.Square, R=[pq[m].b], W=[self.sqb.b])
        P, Bq, Cq, Vv = self.pb
        for m in range(2):
            ps = ring.get()
            win(ps, C_P + m * 128, 128, 0, CW)
            self.A(P.t[:, m, :], ps.t[:, 0:CW], AF.Copy, R=[ps.b], W=[P.b])
        for m in range(2):
            psc = ring.get()
            win(psc, C_GC + m * 128, 128, 0, CW)
            self.A(Vv.t[:, m, :], psc.t[:, 0:CW], AF.Copy, R=[psc.b], W=[Vv.b])
            psh = ring.get()
            win(psh, C_H + m * 128, 128, 0, CW)
            self.V_tt(Vv.t[:, m, :], psh.t[:, 0:CW], Vv.t[:, m, :], ALU.mult, R=[psh.b, Vv.b], W=[Vv.b])
        for m in range(2):
            ps = ring.get()
            win(ps, C_GB + m * 128, 128, HAL, HAL + CH)
            self.A(self.gbS.t[:, m, :], ps.t[:, 0:CH], AF.Copy, R=[ps.b], W=[self.gbS.b])
        st0 = self.stf[0]
        pss = ring.get()
        for m in range(3):
            self.mm(pss.t[:, 0:CH], self.onesB.t[:], self.sqb.t[:, m, :], start=(m == 0), stop=(m == 2),
                    R=[self.onesB.b, self.sqb.b], W=[pss.b])
        self.A(st0.t[:], pss.t[:, 0:CH], AF.Ln, scale=1.0 / QL, bias=self.misc.t[:, 0:1], R=[pss.b, self.misc.b], W=[st0.b])
        self.A(st0.t[:], st0.t[:], AF.Exp, scale=-0.5, R=[st0.b], W=[st0.b])
        for m in range(3):
            self.V_stt(self.qn.t[:, m, :], pq[m].t[:, 0:CH], self.vec.t[:, V_QN + m:V_QN + m + 1], st0.t[:], ALU.mult, ALU.mult,
                       R=[pq[m].b, self.vec.b, st0.b], W=[self.qn.b])
        for h in range(NH):
            ps = ring.get()
            for m in range(3):
                self.mm(ps.t[:, 0:CH], self.w_uq.t[:, m, h * 192:h * 192 + 128], self.qn.t[:, m, :], start=(m == 0), stop=(m == 2),
                        R=[self.w_uq.b, self.qn.b], W=[ps.b])
            self.A(self.qnope.t[:, h, :], ps.t[:, 0:CH], AF.Copy, scale=ATTN_SCALE, R=[ps.b], W=[self.qnope.b])
            ps2 = ring.get()
            for m in range(3):
                self.mm(ps2.t[0:DR, 0:CH], self.w_uq.t[:, m, h * 192 + 128:h * 192 + 192], self.qn.t[:, m, :], start=(m == 0), stop=(m == 2),
                        R=[self.w_uq.b, self.qn.b], W=[ps2.b])
            if r == 0:
                for m in range(3):
                    self.mm(ps2.t[0:DR, CH:2 * CH], self.w_uq.t[:, m, 768 + h * DR:768 + (h + 1) * DR], self.qn.t[:, m, :],
                            start=(m == 0), stop=(m == 2), R=[self.w_uq.b, self.qn.b], W=[ps2.b])
                s1 = self.stf[1]
                self.V_tt(s1.t[0:DR, :], ps2.t[0:DR, 0:CH], self.rt.t[0:DR, 0, :], ALU.mult, R=[ps2.b, self.rt.b], W=[s1.b])
                self.V_tt(ps2.t[0:DR, CH:2 * CH], ps2.t[0:DR, CH:2 * CH], self.rt.t[0:DR, 1, :], ALU.mult, R=[ps2.b, self.rt.b], W=[ps2.b])
                self.V_tt(s1.t[0:DR, :], ps2.t[0:DR, CH:2 * CH], s1.t[0:DR, :], ALU.add, R=[s1.b, ps2.b], W=[s1.b])
                self.A(self.qrope.t[0:DR, h, :], s1.t[0:DR, :], AF.Copy, scale=ATTN_SCALE, R=[s1.b], W=[self.qrope.b])
            else:
                self.A(self.qrope.t[0:DR, h, :], ps2.t[0:DR, 0:CH], AF.Copy, scale=ATTN_SCALE, R=[ps2.b], W=[self.qrope.b])
        pl = self.pool
        add = ALU.add
        self.V_tt(Bq.t[:, :, 1:CW], P.t[:, :, 0:CW - 1], P.t[:, :, 1:CW], add, R=[P.b], W=[Bq.b], eng=pl)
        self.V_tt(Cq.t[64:128, 0, 2:CW - 1], Bq.t[64:128, 0, 1:CW - 2], Bq.t[64:128, 0, 3:CW], add, R=[Bq.b], W=[Cq.b], eng=pl)
        self.V_tt(Cq.t[:, 1, 2:CW - 1], Bq.t[:, 1, 1:CW - 2], Bq.t[:, 1, 3:CW], add, R=[Bq.b], W=[Cq.b], eng=pl)
        self.V_tt(Bq.t[:, 1, 4:CW - 3], Cq.t[:, 1, 2:CW - 5], Cq.t[:, 1, 6:CW - 1], add, R=[Cq.b], W=[Bq.b], eng=pl)
        self.V_tt(Cq.t[64:128, 1, 8:CW - 7], Bq.t[64:128, 1, 4:CW - 11], Bq.t[64:128, 1, 12:CW - 3], add, R=[Bq.b], W=[Cq.b], eng=pl)
        for m in range(2):
            for (p0, p1, Wb) in ((0, 64, Bq), (64, 128, Cq)):
                self.V_stt(self.pooledB.t[p0:p1, m, :], Wb.t[p0:p1, m, cs], self.misc.t[p0:p1, 2 + m:3 + m], P.t[p0:p1, m, cs],
                           ALU.mult, ALU.subtract, R=[Wb.b, P.b, self.misc.b], W=[self.pooledB.b])
                for (is_edge, ec0, wc0, oc0) in ((first, 0, HAL, 0), (last, 8, HAL + CH - 8, CH - 8)):
                    if is_edge:
                        self.V_tt(self.edt.t[p0:p1, m, :], Wb.t[p0:p1, m, wc0:wc0 + 8], self.edges.t[p0:p1, m, ec0:ec0 + 8], ALU.mult,
                                  R=[Wb.b, self.edges.b], W=[self.edt.b])
                        self.V_tt(self.pooledB.t[p0:p1, m, oc0:oc0 + 8], self.edt.t[p0:p1, m, :], P.t[p0:p1, m, wc0:wc0 + 8], ALU.subtract,
                                  R=[self.edt.b, P.b], W=[self.pooledB.b])
        cacc = Bq
        for m in range(2):
            w0 = self.vec.t[:, V_CW + m * 3 + 0:V_CW + m * 3 + 1]
            w1 = self.vec.t[:, V_CW + m * 3 + 1:V_CW + m * 3 + 2]
            w2 = self.vec.t[:, V_CW + m * 3 + 2:V_CW + m * 3 + 3]
            self.V_ts(cacc.t[:, m, cs], Vv.t[:, m, HAL - 1:HAL + CH - 1], w0, None, ALU.mult, R=[Vv.b, self.vec.b, self.pooledB.b], W=[cacc.b])
            self.V_stt(cacc.t[:, m, cs], Vv.t[:, m, cs], w1, cacc.t[:, m, cs], ALU.mult, ALU.add, R=[Vv.b, cacc.b, self.vec.b], W=[cacc.b])
            self.V_stt(cacc.t[:, m, cs], Vv.t[:, m, HAL + 1:HAL + CH + 1], w2, cacc.t[:, m, cs], ALU.mult, ALU.add, R=[Vv.b, cacc.b, self.vec.b], W=[cacc.b])
        for mo in range(2):
            ps = ring.get()
            for mi in range(2):
                self.mm(ps.t[:, 0:CH], self.poolw.t[:, mi, mo * 128:(mo + 1) * 128], self.pooledB.t[:, mi, :], start=(mi == 0), stop=(mi == 1),
                        R=[self.poolw.b, self.pooledB.b], W=[ps.b])
            self.A(ub.t[:, 4 + mo, cs], ps.t[:, 0:CH], AF.Copy, scale=self.vec.t[:, V_PS + mo:V_PS + mo + 1], R=[ps.b, self.vec.b], W=[self.ubb[4 + mo]])
            self.V_tt(ub.t[:, 6 + mo, cs], cacc.t[:, mo, cs], self.gbS.t[:, mo, :], ALU.mult, R=[cacc.b, self.gbS.b], W=[self.ubb[6 + mo]])
        if not last:
            self.load_xnext(r, c + 1)

    def pb_attention(self, l, r, c, side):
        ub = self.ubuf
        nkt = (PAST + T_S) // 128 if r == 0 else T_P // 128
        npair = nkt // 2
        ring = self.psS

        def emit_S(h, kp):
            ps = ring.get()
            for j in range(2):
                kt = kp * 2 + j
                self.mm(ps.t[:, j * CH:(j + 1) * CH], self.KT.t[:, h, kt * 128:(kt + 1) * 128], self.qnope.t[:, h, :],
                        start=True, stop=False, R=[self.KTb[kp], self.qnope.b], W=[ps.b], sig=False)
                self.mm(ps.t[:, j * CH:(j + 1) * CH], self.krT.t[:, kt * 128:(kt + 1) * 128], self.qrope.t[:, h, :],
                        start=False, stop=True, R=[self.krTb[kp], self.qrope.b], W=[ps.b], sig=(j == 1))
            return ps

        seq = [(h, kp) for h in range(NH) for kp in range(npair)]
        nside = max(1, (130 + len(seq) - 1) // len(seq))
        ps_next = emit_S(*seq[0])
        acc = self.psacc.items[0]
        for i, (h, kp) in enumerate(seq):
            ps = ps_next
            pt = self.PT.get()
            self.A(pt.t[:].rearrange("p a b -> p (a b)"), ps.t[:, :], AF.Exp, R=[ps.b], W=[pt.b])
            if i + 1 < len(seq):
                ps_next = emit_S(*seq[i + 1])
            for j in range(2):
                kt = kp * 2 + j
                for qt in range(2):
                    self.mm(acc[qt].t[:, 0:VA], pt.t[:, j, qt * 128:(qt + 1) * 128], self.Vt.t[:, kt, h * VA:(h + 1) * VA],
                            start=(kt == 0), stop=(kt == nkt - 1), R=[self.Vb[kp], pt.b], W=[acc[qt].b], sig=(j == 1 and qt == 1))
            if kp == npair - 1:
                sm = self.smx
                for qt in range(2):
                    self.V_rcp(sm.t[:, 4 + qt:5 + qt], acc[qt].t[:, DV:DV + 1], R=[acc[qt].b], W=[sm.b])
                    otm = self.otm.get()
                    self.V_ts(otm.t[:], acc[qt].t[:, 0:DV], sm.t[:, 4 + qt:5 + qt], None, ALU.mult, R=[acc[qt].b, sm.b], W=[otm.b])
                    pst = ring.get()
                    pstb = pst.t[:, :].bitcast(BF16)
                    self.tr(pstb[:, 0:128], otm.t[:], self.identB.t[:], R=[otm.b, self.identB.b], W=[pst.b])
                    self.A(ub.t[:, h, HAL + qt * 128:HAL + (qt + 1) * 128], pstb[:, 0:128], AF.Copy, R=[pst.b], W=[self.ubb[h]])
            if side is not None:
                for _ in range(nside):
                    if next(side, "done") == "done":
                        side = None
                        break
        if side is not None:
            for _ in side:
                pass

    def pb_wout(self, l, r, c):
        X = self.pb_ctx(r, c)
        cond, g0, gc = X["cond"], X["g0"], X["gc"]
        xTv = self.xT_d.rearrange("(k p) t -> p k t", p=128)
        xt, ub = self.xt, self.ubuf
        cs = slice(HAL, HAL + CH)
        self.dma(self.qsp, xt.t[:, :, cs], xTv[:, :, g0:g0 + CH], R=[self.xT_b[gc]], W=self.xtb)
        for j in range(8):
            ps = self.psB.get()
            for k in range(8):
                self.mm(ps.t[:, 0:CH], self.w_out.t[:, k, j * 128:(j + 1) * 128], ub.t[:, k, cs], start=(k == 0), stop=(k == 7),
                        R=[self.w_out.b, self.ubb[k]], W=[ps.b])
            self.V_stt(xt.t[:, j, cs], ps.t[:, 0:CH], self.g1a.t[:, j, cond:cond + 1], xt.t[:, j, cs], ALU.mult, ALU.add,
                       R=[ps.b, self.g1a.b, self.xtb[j]], W=[self.xtb[j]])

    def pb_late(self, l, r, c):
        X = self.pb_ctx(r, c)
        cond, g0, gc = X["cond"], X["g0"], X["gc"]
        x1Tv = self.x1T_d.rearrange("(k p) t -> p k t", p=128)
        xt, u2 = self.xt, self.u2buf
        cs = slice(HAL, HAL + CH)
        yield from self.ln_core(ring=self.psB, gen=True)
        for jt in range(2):
            gt = g0 // 128 + jt
            ps = self.psB.get()
            for k in range(8):
                self.mm(ps.t[:, 0:NE], xt.t[:, k, HAL + jt * 128:HAL + (jt + 1) * 128], self.wr2.t[:, cond, k, :], start=(k == 0), stop=False,
                        R=[self.xtb[k], self.wr2.b], W=[ps.b], sig=False)
            self.mm(ps.t[:, 0:NE], self.onesrow.t[0:1, :], self.rconst.t[0:1, cond, :], start=False, stop=True,
                    R=[self.onesrow.b, self.rconst.b], W=[ps.b], sig=True)
            yield
            sm = self.smx
            self.dve.op(lambda e: e.tensor_reduce(out=sm.t[:, 0:1], in_=ps.t[:, 0:NE], axis=AX.X, op=ALU.max), R=[ps.b], W=[sm.b])
            self.V_ts(sm.t[:, 1:2], sm.t[:, 0:1], -1.0, None, ALU.mult, R=[sm.b], W=[sm.b])
            yield
            self.A(self.ex.t[:], ps.t[:, 0:NE], AF.Exp, bias=sm.t[:, 1:2], scale=1.0, accum_out=sm.t[:, 2:3], R=[ps.b, sm.b], W=[self.ex.b, sm.b])
            yield
            self.V_rcp(sm.t[:, 3:4], sm.t[:, 2:3], R=[sm.b], W=[sm.b])
            self.V_ts(self.affS.t[:, gt, :], self.ex.t[:], sm.t[:, 3:4], None, ALU.mult, R=[self.ex.b, sm.b], W=[self.affSb[gt]])
            yield
        for k in range(8):
            g = self.G2.t[:, k, cond:cond + 1]
            b = self.B2.t[:, k, cond:cond + 1]
            if k % 2 == 1:
                self.V_ts(u2.t[:, k, :], xt.t[:, k, cs], g, b, ALU.mult, ALU.add, R=[self.xtb[k], self.G2.b, self.B2.b], W=[self.u2b[k]])
            else:
                self.A(u2.t[:, k, :], xt.t[:, k, cs], AF.Identity, scale=g, bias=b, R=[self.xtb[k], self.G2.b, self.B2.b], W=[self.u2b[k]])
            yield
        yield from self.affine_xt(V_L1G, V_L1B, gen=True)
        self.dma(self.qsp, x1Tv[:, :, g0:g0 + CH], xt.t[:, :, cs], R=self.xtb, W=[self.x1T_b[gc]])
        for jt in range(2):
            gt = g0 // 128 + jt
            ps = self.psB.get()
            psb = ps.t[:, :].bitcast(BF16)
            for k in range(8):
                self.tr(psb[:, k * 128:(k + 1) * 128], u2.t[:, k, jt * 128:(jt + 1) * 128], self.identB.t[:],
                        R=[self.u2b[k], self.identB.b], W=[ps.b], sig=(k == 7))
                if k % 4 == 3:
                    yield
            rw = self.rowbuf.get()
            self.A(rw.t[:, 0:D], psb[:, :], AF.Copy, R=[ps.b], W=[rw.b])
            self.V_cp(rw.t[:, D:D + 2 * NE].bitcast(F32), self.affS.t[:, gt, :], R=[self.affSb[gt], rw.b], W=[rw.b])
            self.V_cp(rw.t[:, D + 2 * NE:D + 2 * NE + 2].bitcast(I32), self.tidcol.t[:, gt:gt + 1], R=[self.tidcol.b, rw.b], W=[rw.b])
            self.dma(self.qsp, self.u2rows_d[gt * 128:(gt + 1) * 128, :], rw.t[:], R=[rw.b], W=[self.u2rows_bt[gt]])
            yield

    def phase_b(self, l, r):
        nch = REQ_T[r] // CH
        side = None
        for c in range(nch):
            self.pb_front(l, r, c)
            self.pb_attention(l, r, c, side)
            self.pb_wout(l, r, c)
            side = self.pb_late(l, r, c)
        for _ in side:
            pass

    def routing(self, r):
        self.carve_reset()
        a = self.av
        sample = (r == 0)
        Pn = 128 if sample else NE
        Fn = 512 if sample else T_P
        nblk = Fn // 128
        Kcap = float(REQ_CAP[r])
        A_sb = a("A_sb", [128, 512], F32)
        junk = a("junk", [128, 512], F32)
        msk = a("msk", [128, 512], F32)
        csb = a("csb", [128, 512], F32)
        affX = [a(f"affX{i}", [128, NE, 8], F32) for i in range(2)]
        bis = a("bis", [128, 8], I32)
        cnt = a("cnt", [128, 4], F32)
        self.gsum = a("gsum", [128, 128], F32)
        self.lstrict = a("lstrict", [128, 128], F32)
        if sample:
            self.dma(self.qsp, self.gsum.t[:], self.gsum_d, W=[self.gsum.b])
            self.dma(self.qsp, self.lstrict.t[:], self.lstrict_d, W=[self.lstrict.b])
        g0t = RB[r] // 128
        for b in range(nblk):
            ps = self.ps_get()
            if sample:
                for seg in range(8):
                    gt = seg * 4 + b
                    ax = affX[seg % 2]
                    self.memset(self.pool, ax.t[:], 0.0, W=[ax.b])
                    self.V_cp(ax.t[:, :, seg], self.affS.t[:, gt, :], R=[self.affSb[gt], ax.b], W=[ax.b])
                    self.mm(ps.t[:, 0:128], ax.t[:].rearrange("p e s -> p (e s)"), self.identF.t[:], start=(seg == 0), stop=(seg == 7),
                            R=[ax.b, self.identF.b], W=[ps.b], sig=True)
            else:
                gt = g0t + b
                self.mm(ps.t[0:NE, 0:128], self.affS.t[:, gt, :], self.identF.t[:], start=True, stop=True,
                        R=[self.affSb[gt], self.identF.b], W=[ps.b])
            self.V_cp(A_sb.t[0:Pn, b * 128:(b + 1) * 128], ps.t[0:Pn, 0:128], R=[ps.b], W=[A_sb.b])
        lo, mid, ge = (bis.t[0:Pn, i:i + 1] for i in range(3))
        bb = [bis.b]
        self.memset(self.dve, bis.t[:, 0:1], 0, W=bb)
        self.memset(self.dve, cnt.t[:], 0.0, W=[cnt.b])
        for bit in range(29, -1, -1):
            self.V_ts(mid, lo, 1 << bit, None, ALU.bitwise_or, R=bb, W=bb)
            self.V_ts(junk.t[0:Pn, 0:Fn], A_sb.t[0:Pn, 0:Fn], mid.bitcast(F32), None, ALU.is_ge, ALU.add,
                      R=[A_sb.b] + bb, W=[junk.b, cnt.b], accum_out=cnt.t[0:Pn, 0:1])
            if sample:
                ps = self.ps_get()
                self.mm(ps.t[:, 0:2], self.gsum.t[:], cnt.t[:, 0:2], start=True, stop=True, R=[self.gsum.b, cnt.b], W=[ps.b])
                src, sb_ = ps.t[0:Pn, 0:1], [ps.b]
            else:
                src, sb_ = cnt.t[0:Pn, 0:1], [cnt.b]
            self.V_ts(ge, src, Kcap - 0.5, None, ALU.is_ge, R=sb_, W=bb)
            self.dve.op(lambda e: e.copy_predicated(out=lo, mask=ge, data=mid), R=bb, W=bb)
        self.V_ts(msk.t[0:Pn, 0:Fn], A_sb.t[0:Pn, 0:Fn], lo.bitcast(F32), None, ALU.is_ge, R=[A_sb.b] + bb, W=[msk.b])
        self.memset(self.dve, junk.t[:], 1.0, W=[junk.b])
        self.dve.op(lambda e: e.tensor_tensor_scan(out=csb.t[0:Pn, 0:Fn], data0=junk.t[0:Pn, 0:Fn], data1=msk.t[0:Pn, 0:Fn],
                                                   initial=0.0, op0=ALU.mult, op1=ALU.add), R=[junk.b, msk.b], W=[csb.b])
        if sample:
            self.V_cp(cnt.t[:, 0:1], csb.t[:, Fn - 1:Fn], R=[csb.b], W=[cnt.b])
            ps = self.ps_get()
            self.mm(ps.t[:, 0:2], self.lstrict.t[:], cnt.t[:, 0:2], start=True, stop=True, R=[self.lstrict.b, cnt.b], W=[ps.b])
            self.V_cp(cnt.t[:, 2:3], ps.t[:, 0:1], R=[ps.b], W=[cnt.b])
            self.V_ts(csb.t[:, 0:Fn], csb.t[:, 0:Fn], cnt.t[:, 2:3], None, ALU.add, R=[csb.b, cnt.b], W=[csb.b])
        self.V_stt(junk.t[0:Pn, 0:Fn], csb.t[0:Pn, 0:Fn], Kcap + 0.5, msk.t[0:Pn, 0:Fn], ALU.is_le, ALU.mult, R=[csb.b, msk.b], W=[junk.b])
        self.V_tt(csb.t[0:Pn, 0:Fn], csb.t[0:Pn, 0:Fn], junk.t[0:Pn, 0:Fn], ALU.mult, R=[csb.b, junk.b], W=[csb.b])
        vt = self.vTM[r]
        for b in range(nblk):
            ps = self.ps_get()
            self.tr(ps.t[:, 0:Pn], csb.t[0:Pn, b * 128:(b + 1) * 128], self.identF.t[0:Pn, 0:Pn], R=[csb.b, self.identF.b], W=[ps.b])
            self.V_cp(vt.t[:, b, 0:Pn], ps.t[:, 0:Pn], R=[ps.b], W=[vt.b])

    def moe_phase(self, l):
        self.carve_reset()
        a = self.av
        wg = [a(f"wg{i}", [128, 8, EF], BF16) for i in range(2)]
        wu = [a(f"wu{i}", [128, 8, EF], BF16) for i in range(2)]
        wd = [a(f"wd{i}", [128, 4, D], BF16) for i in range(2)]
        xs = [a(f"xs{i}", [128, ROWW], BF16) for i in range(10)]
        NS = 576
        xsT = a("xsT", [128, 8, NS], BF16)
        hdn = a("hdn", [128, 4, NS], BF16)
        sil = a("sil", [128, NS], F32)
        ye = Ring([a(f"ye{i}", [128, D], F32) for i in range(3)])
        Sr = Ring([a(f"S{i}", [128, 512], BF16) for i in range(3)])
        Sp = Ring([a(f"Sp{i}", [128, 32], BF16) for i in range(2)])
        idxrow = a("idxrow", [2, NS], F32)
        idxI = Ring([a(f"idxI{i}", [128, 8], I32) for i in range(2)])
        idxF = a("idxF", [128, 5, 2], F32)
        zt = a("zt", [128, D], F32)
        self.iota1 = a("iota1", [128, 512], F32)
        self.dma(self.qsp, self.iota1.t[:], self.iota1_d, W=[self.iota1.b])
        self.memset(self.pool, zt.t[:], 0.0, W=[zt.b])
        for gt in range(TTOT // 128):
            self.dma(self.qsp, self.ffn_d[gt * 128:(gt + 1) * 128, :], zt.t[:], R=[zt.b], W=[self.ffn_bt[gt]])
        if self.bcreg is None:
            self.bcreg = self.nc.gpsimd.to_reg(TTOT - 1)

        def load_w(e):
            i = e % 2
            self.dma(self.qpl, wg[i].t[:], self.w_gate_d[l, e].rearrange("(k p) f -> p k f", p=128), W=[wg[i].b])
            self.dma(self.qpl, wu[i].t[:], self.w_up_d[l, e].rearrange("(k p) f -> p k f", p=128), W=[wu[i].b])
            self.dma(self.qpl, wd[i].t[:], self.w_down_d[l, e].rearrange("(k p) d -> p k d", p=128), W=[wd[i].b])

        load_w(0)
        tiles = [(c, 128) for c in range(4)] + [(4, 64)]
        scat = {"prev": [], "cur": []}

        def stage1(e):
            xse = xs[(e % 2) * 5:(e % 2) * 5 + 5]
            psI = self.psacc.items[0][0]
            psI2 = self.psacc.items[0][1]
            vt = self.vTM[0]
            ntile = T_S // 128
            Sl = [None] * ntile

            def onehot(gt):
                seg, b = gt // 4, gt % 4
                S = Sr.get()
                col = vt.t[:, b, e * 8 + seg:e * 8 + seg + 1]
                self.V_ts(S.t[:], self.iota1.t[:], col, None, ALU.is_equal, R=[self.iota1.b, vt.b], W=[S.b])
                Sl[gt] = S

            for gt in range(3):
                onehot(gt)
            yield
            for gt in range(ntile):
                S = Sl[gt]
                self.mm(psI.t[0:2, :], self.tidhl.t[:, gt, :], S.t[:], start=(gt == 0), stop=(gt == ntile - 1),
                        R=[self.tidhl.b, S.b], W=[psI.b], sig=True)
                if gt + 3 < ntile:
                    onehot(gt + 3)
                yield
            self.A(idxrow.t[0:2, 0:512], psI.t[0:2, :], AF.Copy, R=[psI.b], W=[idxrow.b])
            psJ = psI2
            for r in (1, 2):
                vtp = self.vTM[r]
                for b in range(2):
                    gt = RB[r] // 128 + b
                    S = Sp.get()
                    self.V_ts(S.t[:], self.iota1.t[:, 0:32], vtp.t[:, b, e:e + 1], None, ALU.is_equal, R=[self.iota1.b, vtp.b], W=[S.b])
                    self.mm(psJ.t[0:2, (r - 1) * 32:r * 32], self.tidhl.t[:, gt, :], S.t[:], start=(b == 0), stop=(b == 1),
                            R=[self.tidhl.b, S.b], W=[psJ.b], sig=True)
            yield
            self.A(idxrow.t[0:2, 512:NS], psJ.t[0:2, 0:64], AF.Copy, R=[psJ.b], W=[idxrow.b])
            psT = psI
            for (c, nr) in tiles:
                self.tr(psT.t[0:nr, c * 2:c * 2 + 2], idxrow.t[0:2, c * 128:c * 128 + nr], self.identF.t[0:2, 0:2],
                        R=[idxrow.b, self.identF.b], W=[psT.b], sig=(c == 4))
            yield
            self.V_cp(idxF.t[:].rearrange("p a b -> p (a b)"), psT.t[:, 0:10], R=[psT.b], W=[idxF.b])
            ii = idxI.get()
            self.V_stt(ii.t[:, 0:5], idxF.t[:, :, 0], 64.0, idxF.t[:, :, 1], ALU.mult, ALU.add, R=[idxF.b], W=[ii.b])
            for (c, nr) in tiles:
                self.qpl.issue(lambda g, c=c, nr=nr: g.indirect_dma_start(
                    out=xse[c].t[0:nr, :], out_offset=None, in_=self.u2rows_d,
                    in_offset=bass.IndirectOffsetOnAxis(ap=ii.t[0:nr, c:c + 1], axis=0),
                    bounds_check=self.bcreg, oob_is_err=False), R=[ii.b] + self.u2rows_bt, W=[xse[c].b])
            yield

        def stage2(e):
            i = e % 2
            xse = xs[(e % 2) * 5:(e % 2) * 5 + 5]
            for (c, nr) in tiles:
                ps = self.ps_get()
                psb = ps.t[:, :].bitcast(BF16)
                for k in range(8):
                    self.tr(psb[:, k * 128:k * 128 + nr], xse[c].t[0:nr, k * 128:(k + 1) * 128], self.identB.t[0:nr, 0:nr],
                            R=[xse[c].b, self.identB.b], W=[ps.b], sig=(k == 7))
                yield
                src = psb[:, :].rearrange("p (k n) -> p k n", k=8)[:, :, 0:nr]
                if c % 2 == 0:
                    self.A(xsT.t[:, :, c * 128:c * 128 + nr], src, AF.Copy, R=[ps.b], W=[xsT.b])
                else:
                    self.V_cp(xsT.t[:, :, c * 128:c * 128 + nr], src, R=[ps.b], W=[xsT.b])
            for f in range(4):
                for (c0, c1) in ((0, 512), (512, NS)):
                    n = c1 - c0
                    psg, psu = self.ps_get(), self.ps_get()
                    for k in range(8):
                        self.mm(psg.t[:, 0:n], wg[i].t[:, k, f * 128:(f + 1) * 128], xsT.t[:, k, c0:c1], start=(k == 0), stop=(k == 7),
                                R=[wg[i].b, xsT.b], W=[psg.b])
                    for k in range(8):
                        self.mm(psu.t[:, 0:n], wu[i].t[:, k, f * 128:(f + 1) * 128], xsT.t[:, k, c0:c1], start=(k == 0), stop=(k == 7),
                                R=[wu[i].b, xsT.b], W=[psu.b])
                    yield
                    self.A(sil.t[:, c0:c1], psg.t[:, 0:n], AF.Silu, R=[psg.b], W=[sil.b])
                    self.V_tt(hdn.t[:, f, c0:c1], psu.t[:, 0:n], sil.t[:, c0:c1], ALU.mult, R=[psu.b, sil.b], W=[hdn.b])
            for (c, nr) in tiles:
                y = ye.get()
                gcol = xse[c].t[0:nr, D + 2 * e:D + 2 * e + 2].bitcast(F32)
                for half in range(2):
                    ps = self.ps_get()
                    for f in range(4):
                        self.mm(ps.t[0:nr, :], hdn.t[:, f, c * 128:c * 128 + nr], wd[i].t[:, f, half * 512:(half + 1) * 512],
                                start=(f == 0), stop=(f == 3), R=[hdn.b, wd[i].b], W=[ps.b])
                    yield
                    if half == 0:
                        self.A(y.t[0:nr, 0:512], ps.t[0:nr, :], AF.Copy, scale=gcol, R=[ps.b, xse[c].b], W=[y.b])
                    else:
                        self.V_ts(y.t[0:nr, 512:D], ps.t[0:nr, :], gcol, None, ALU.mult, R=[ps.b, xse[c].b], W=[y.b])
                tid = xse[c].t[0:nr, D + 2 * NE:D + 2 * NE + 2].bitcast(I32)
                for t_prev in scat["prev"]:
                    self.pool.wait(t_prev)
                last_e = (e == NE - 1)
                tk = self.qpl.issue(lambda g, y=y, nr=nr, tid=tid: g.indirect_dma_start(
                    out=self.ffn_d, out_offset=bass.IndirectOffsetOnAxis(ap=tid, axis=0), in_=y.t[0:nr, :], in_offset=None,
                    compute_op=ALU.add), R=[y.b, xse[c].b] + (self.ffn_bt if e == 0 else []), W=(self.ffn_bt if last_e else []))
                scat["cur"].append(tk)
            scat["prev"], scat["cur"] = scat["cur"], []

        for _ in stage1(0):
            pass
        for e in range(NE):
            side = None
            if e + 1 < NE:
                load_w(e + 1)
                side = stage1(e + 1)
            if e == 1 and l + 1 < self.depth:
                self.load_layer_weights(l + 1)
            for _ in stage2(e):
                if side is not None:
                    for _k in range(4):
                        if next(side, "done") == "done":
                            side = None
                            break
            if side is not None:
                for _ in side:
                    pass

    def ln2_A(self, l, r, c, X, ft):
        cond = 1 if r == 0 else 0
        g0 = RB[r] + c * CH
        gc = g0 // CH
        x1Tv = self.x1T_d.rearrange("(k p) t -> p k t", p=128)
        xt, xtb = X["xt"], X["xtb"]
        cs = slice(HAL, HAL + CH)
        self.dma(self.qsp, xt.t[:, :, cs], x1Tv[:, :, g0:g0 + CH], R=[self.x1T_b[gc]], W=xtb)
        for jt in range(2):
            gt = g0 // 128 + jt
            self.dma(self.qsp, ft[jt].t[:], self.ffn_d[gt * 128:(gt + 1) * 128, :], R=[self.ffn_bt[gt]], W=[ft[jt].b])
        yield
        for k in range(8):
            ps = self.psS.get()
            for jt in range(2):
                self.tr(ps.t[:, jt * 128:(jt + 1) * 128], ft[jt].t[:, k * 128:(k + 1) * 128], self.identF.t[:],
                        R=[ft[jt].b, self.identF.b], W=[ps.b], sig=(jt == 1))
            self.V_stt(xt.t[:, k, cs], ps.t[:, 0:CH], self.g2a.t[:, k, cond:cond + 1], xt.t[:, k, cs], ALU.mult, ALU.add,
                       R=[ps.b, self.g2a.b, xtb[k]], W=[xtb[k]])
            yield

    def ln2_B(self, l, r, c, X, ot):
        g0 = RB[r] + c * CH
        gc = g0 // CH
        xTv = self.xT_d.rearrange("(k p) t -> p k t", p=128)
        xt, xtb = X["xt"], X["xtb"]
        cs = slice(HAL, HAL + CH)
        yield from self.ln_core(ring=self.psB, gen=True, X=X)
        yield from self.affine_xt(V_L2G, V_L2B, gen=True, X=X)
        if l < self.depth - 1:
            self.dma(self.qsp, xTv[:, :, g0:g0 + CH], xt.t[:, :, cs], R=xtb, W=[self.xT_b[gc]])
        else:
            for jt in range(2):
                o = ot[jt]
                for hf in range(2):
                    ps = self.psB.get()
                    for kk in range(4):
                        k = hf * 4 + kk
                        self.tr(ps.t[:, kk * 128:(kk + 1) * 128], xt.t[:, k, HAL + jt * 128:HAL + (jt + 1) * 128], self.identF.t[:],
                                R=[xtb[k], self.identF.b], W=[ps.b], sig=(kk == 3))
                    if hf == 0:
                        self.A(o.t[:, 0:512], ps.t[:, :], AF.Copy, R=[ps.b], W=[o.b])
                    else:
                        self.V_cp(o.t[:, 512:D], ps.t[:, :], R=[ps.b], W=[o.b])
                    yield
                row = c * CH + jt * 128
                dst = self.y_s[row:row + 128, :] if r == 0 else self.y_p[(r - 1) * T_P + row:(r - 1) * T_P + row + 128, :]
                self.dma(self.qsp, dst, o.t[:], R=[o.b], W=[Buf("o")])

    def ln2_phase(self, l):
        self.carve_reset()
        a = self.av
        ft = [a(f"ft{i}", [128, D], F32) for i in range(4)]
        ot = [a(f"ot{i}", [128, D], F32) for i in range(2)]
        sets = []
        for i in range(2):
            sets.append(dict(xt=a(f"xl{i}", [128, 8, CW], F32), xtb=[Buf(f"xl{i}_{k}") for k in range(8)],
                             st=[a(f"stl{i}_{j}", [128, CH], F32) for j in range(3)]))
        chunks = [(r, c) for r in range(3) for c in range(REQ_T[r] // CH)]
        prevB = None
        for i, (r, c) in enumerate(chunks):
            X = sets[i % 2]
            genA = self.ln2_A(l, r, c, X, ft[(i % 2) * 2:(i % 2) * 2 + 2])
            while genA is not None or prevB is not None:
                if genA is not None and next(genA, "done") == "done":
                    genA = None
                if prevB is not None:
                    for _ in range(3):
                        if next(prevB, "done") == "done":
                            prevB = None
                            break
                if genA is None and prevB is None:
                    break
            prevB = self.ln2_B(l, r, c, X, ot)
        for _ in prevB:
            pass

    def build(self):
        self.setup()
        self.load_consts()
        self.transpose_in()
        self.load_layer_weights(0)
        self.barrier()
        for l in range(self.depth):
            self.mark(f"L{l} mod")
            self.mod_phase(l)
            self.barrier()
            for r in range(3):
                self.carve_ab()
                self.mark(f"L{l} r{r} A")
                if r == 0:
                    self.ctx_phase(l)
                nch = REQ_T[r] // CH
                for c in range(nch):
                    self.phase_a_chunk(l, r, c)
                self.mark(f"L{l} r{r} B")
                self.phase_b(l, r)
                self.barrier()
                self.mark(f"L{l} r{r} route")
                self.routing(r)
                self.barrier()
            self.mark(f"L{l} moe")
            self.moe_phase(l)
            self.barrier()
            self.mark(f"L{l} ln2")
            self.ln2_phase(l)
            self.barrier()
        self.barrier()
        self.mark("end")
        return self.nc


def _rope_tables():
    rows_n = T_S // 64
    r, cl = np.meshgrid(np.arange(rows_n, dtype=np.float32), np.arange(64, dtype=np.float32), indexing="ij")
    inv = (np.float32(10000.0) ** (-np.arange(0, 32, 2, dtype=np.float32) / np.float32(32))).astype(np.float32)
    ang = np.concatenate([r.reshape(-1)[:, None] * inv, cl.reshape(-1)[:, None] * inv], axis=-1).astype(np.float32)
    cos, sin = np.cos(ang).astype(np.float32), np.sin(ang).astype(np.float32)
    cosT = np.concatenate([cos.T, cos.T], axis=0)
    sinT = np.concatenate([-sin.T, sin.T], axis=0)
    return np.ascontiguousarray(cosT), np.ascontiguousarray(sinT)


def _consts():
    c = {}
    c["identF"] = np.eye(128, dtype=np.float32)
    p = np.arange(128)
    c["gsum"] = (p[:, None] // 8 == p[None, :] // 8).astype(np.float32)
    c["lstrict"] = ((p[:, None] // 8 == p[None, :] // 8) & (p[:, None] % 8 < p[None, :] % 8)).astype(np.float32)
    c["iota1"] = np.broadcast_to(np.arange(1, 513, dtype=np.float32)[None, :], (128, 512)).copy()
    rows = np.arange(36)[None, :] * 128 + p[:, None]
    c["tidhl"] = np.stack([rows // 64, rows % 64], axis=-1).astype(np.float32)
    c["tidcol"] = rows.astype(np.float32)
    edges = np.zeros((128, 2, 16), np.float32)
    invw = np.zeros((128, 2), np.float32)
    for m in range(2):
        for half in range(2):
            w = (2, 4, 8, 16)[m * 2 + half]
            sl = slice(half * 64, half * 64 + 64)
            invw[sl, m] = 1.0 / w
            for i in range(8):
                t = i
                cnt = min(t + w // 2, 10 ** 9) - max(0, t - w // 2)
                edges[sl, m, i] = 1.0 / cnt
                j = 7 - i
                cnt = min(w // 2, j + 1) + min(w // 2, 10 ** 9)
                edges[sl, m, 8 + i] = 1.0 / cnt
    c["edges"] = edges
    misc = np.zeros((128, 16), np.float32)
    misc[:, 0] = RMS_EPS
    misc[:, 1] = LN_EPS / (ALPHA * ALPHA)
    misc[:, 2:4] = invw
    c["misc"] = misc
    c["cosT"], c["sinT"] = _rope_tables()
    return c


def _prep_shared(inp):
    f = lambda a: np.ascontiguousarray(np.asarray(a, dtype=np.float32))
    w_in = f(inp["w_in"])
    L = DEPTH
    krs = np.concatenate([np.arange(C_KR + 32, C_KR + 64), np.arange(C_KR, C_KR + 32)])
    w_in_x = np.concatenate([w_in, w_in[:, :, krs]], axis=2)
    sh = {}
    sh["w_in_x"] = np.ascontiguousarray(w_in_x.reshape(L, 8, 128, NCOL).transpose(0, 2, 1, 3))
    w_uq = f(inp["w_uq"])
    sw = np.concatenate([np.concatenate([np.arange(h * 192 + 160, h * 192 + 192), np.arange(h * 192 + 128, h * 192 + 160)]) for h in range(NH)])
    w_uq_x = np.concatenate([w_uq, w_uq[:, :, sw]], axis=2)
    sh["w_uq_x"] = np.ascontiguousarray(w_uq_x.reshape(L, 3, 128, 1024).transpose(0, 2, 1, 3))
    sh["w_uk_x"] = np.ascontiguousarray(f(inp["w_uk"]).reshape(L, 2, 128, 512).transpose(0, 2, 1, 3))
    sh["w_uv_x"] = np.ascontiguousarray(f(inp["w_uv"]).reshape(L, 2, 128, 512).transpose(0, 2, 1, 3))
    pw = f(inp["pool_w"])
    bd = np.zeros((L, 256, 256), np.float32)
    for g in range(4):
        bd[:, g * 64:(g + 1) * 64, g * 64:(g + 1) * 64] = pw[:, g]
    sh["poolw_x"] = np.ascontiguousarray(bd.reshape(L, 2, 128, 256).transpose(0, 2, 1, 3))
    sh["w_out_x"] = np.ascontiguousarray(f(inp["w_out"]).reshape(L, 8, 128, D).transpose(0, 2, 1, 3))
    sh["w_r_x"] = np.ascontiguousarray(f(inp["w_router"]).reshape(L, 8, 128, NE).transpose(0, 2, 1, 3))
    sh["w_ada"] = f(inp["w_ada"])
    vecs = np.zeros((L, 128, NV), np.float32)
    col = lambda v, n: v.reshape(L, n, 128).transpose(0, 2, 1)
    vecs[:, :, V_BADA:V_BADA + 48] = col(f(inp["b_ada"]), 48)
    vecs[:, :, V_L1G:V_L1G + 8] = col(f(inp["ln1_g"]), 8)
    vecs[:, :, V_L1B:V_L1B + 8] = col(f(inp["ln1_b"]), 8)
    vecs[:, :, V_L2G:V_L2G + 8] = col(f(inp["ln2_g"]), 8)
    vecs[:, :, V_L2B:V_L2B + 8] = col(f(inp["ln2_b"]), 8)
    vecs[:, :, V_QN:V_QN + 3] = col(f(inp["q_norm"]), 3)
    vecs[:, :, V_KVN:V_KVN + 2] = col(f(inp["kv_norm"]), 2)
    vecs[:, :, V_PS:V_PS + 2] = col(f(inp["pool_scale"]), 2)
    cw = f(inp["conv_w"])
    for m in range(2):
        for t in range(3):
            vecs[:, :, V_CW + m * 3 + t] = cw[:, t, m * 128:(m + 1) * 128]
    sh["vecs"] = vecs
    sh["w_gate"] = f(inp["w_gate"])
    sh["w_up"] = f(inp["w_up"])
    sh["w_down"] = f(inp["w_down"])
    sh.update(_consts())
    return sh


_NC_CACHE = {}


def kernel(**inp):
    sh = _prep_shared(inp)
    xp = np.asarray(inp["x_prompt"], dtype=np.float32)
    xsmp = np.asarray(inp["x_sample"], dtype=np.float32)
    cckv = np.asarray(inp["cache_ckv"], dtype=np.float32)
    ckr = np.asarray(inp["cache_krope"], dtype=np.float32)
    cvec = np.asarray(inp["c"], dtype=np.float32)
    cctx = np.asarray(inp["c_ctx"], dtype=np.float32)
    in_maps = []
    for core in range(8):
        s = core // 4
        m = dict(sh)
        m["xs_in"] = np.ascontiguousarray(xsmp[s])
        m["xp_in"] = np.ascontiguousarray(xp[2 * core:2 * core + 2].reshape(2 * T_P, D))
        m["cckv"] = np.ascontiguousarray(cckv[s])
        m["ckr"] = np.ascontiguousarray(ckr[s])
        cond = np.stack([cctx, cvec[s]], axis=-1)
        m["condT"] = np.ascontiguousarray(cond.reshape(8, 128, 2).transpose(1, 0, 2))
        in_maps.append(m)
    if "nc" not in _NC_CACHE:
        _NC_CACHE["nc"] = KB().build()
    nc = _NC_CACHE["nc"]
    res = run_bass_kernel_spmd(nc, in_maps, core_ids=list(range(8)))
    rs = res.results
    y_prompt = np.concatenate([np.asarray(rs[c]["y_p"]).reshape(2, T_P, D) for c in range(8)], axis=0)
    y_sample = np.stack([np.asarray(rs[0]["y_s"]), np.asarray(rs[4]["y_s"])], axis=0)
    new_ckv = np.concatenate([np.asarray(rs[c]["nckv"]) for c in range(8)], axis=0)
    new_krope = np.concatenate([np.asarray(rs[c]["nkr"]) for c in range(8)], axis=0)
    return (y_prompt.astype(np.float32), y_sample.astype(np.float32), new_ckv.astype(np.float32), new_krope.astype(np.float32))
```

```python
import numpy as np
import ml_dtypes
from contextlib import ExitStack
import concourse.bass as bass
import concourse.mybir as mybir
from concourse.bass_utils import run_bass_kernel_spmd

F32 = mybir.dt.float32
BF16 = mybir.dt.bfloat16
I32 = mybir.dt.int32
AF = mybir.ActivationFunctionType
ALU = mybir.AluOpType
AX = mybir.AxisListType

D = 1024
DEPTH = 4
T_S = 4096
T_P = 256
PAST = 512
NH = 4
DN = 128
DR = 64
DV = 128
QL = 384
KVL = 256
NE = 16
EF = 512
ALPHA = (2 * DEPTH) ** 0.25
RMS_EPS = 1e-6
LN_EPS = 1e-5
ATTN_SCALE = (DN + DR) ** -0.5
CH = 256
HAL = 8
CW = CH + 2 * HAL
VA = 130
NCOL = 1792
C_Q, C_KV, C_KR, C_P, C_GB, C_GC, C_H, C_KRS = 0, 384, 640, 704, 960, 1216, 1472, 1728
RB = [0, T_S, T_S + T_P]
TTOT = T_S + 2 * T_P
REQ_T = [T_S, T_P, T_P]
REQ_CAP = [2 * T_S // NE, 2 * T_P // NE, 2 * T_P // NE]
ROWW = 1060
NV = 93
V_BADA, V_L1G, V_L1B, V_L2G, V_L2B, V_QN, V_KVN, V_PS, V_CW = 0, 48, 56, 64, 72, 80, 83, 85, 87
BIGIDX = float(1 << 20)


class Tok:
    __slots__ = ("sem", "val")

    def __init__(self, sem=None, val=0):
        self.sem = sem
        self.val = val


class Buf:
    __slots__ = ("name", "w", "r")

    def __init__(self, name):
        self.name = name
        self.w = None
        self.r = []


class Eng:
    def __init__(self, K, eng, name):
        self.K = K
        self.e = eng
        self.name = name
        self.sem = None
        self.cnt = 0
        self.nsem = 0
        self.waited = {}
        self.pending = []
        self.last = None
        self.nins = 0
        self._newsem()

    def _newsem(self):
        self.sem = self.K.es.enter_context(self.K.nc.semaphore(f"p_{self.name}{self.nsem}"))
        self.nsem += 1
        self.cnt = 0

    def wait(self, tok):
        if tok is None:
            return
        assert tok.sem is not None, f"wait on unsignalled token ({self.name})"
        key = id(tok.sem)
        if self.waited.get(key, (None, 0))[1] >= tok.val:
            return
        self.e.wait_ge(tok.sem, tok.val)
        self.waited[key] = (tok.sem, tok.val)

    def begin(self, R, W):
        for b in R:
            if b.w is not None:
                self.wait(b.w[1])
        for b in W:
            if b.w is not None and b.w[0] != self.name:
                self.wait(b.w[1])
            for (en, tk) in b.r:
                if en != self.name:
                    self.wait(tk)

    def end(self, ins, R, W, sig=True):
        tok = Tok()
        self.pending.append(tok)
        for b in R:
            b.r.append((self.name, tok))
        for b in W:
            b.w = (self.name, tok)
            b.r = []
        if sig:
            if self.cnt >= 30000:
                self._newsem()
            self.cnt += 1
            ins.then_inc(self.sem, 1)
            for t in self.pending:
                t.sem = self.sem
                t.val = self.cnt
            self.pending = []
            self.last = tok
        return tok

    def op(self, fn, R=(), W=(), sig=True):
        self.begin(R, W)
        ins = fn(self.e)
        self.nins += 1
        return self.end(ins, R, W, sig)


class DmaQ:
    def __init__(self, K, E, nsem):
        self.K = K
        self.E = E
        self.sems = [K.es.enter_context(K.nc.semaphore(f"d_{E.name}{i}")) for i in range(nsem)]
        self.vals = [0] * nsem
        self.last = [None] * nsem
        self.i = 0

    def issue(self, fn, R=(), W=()):
        E = self.E
        E.begin(R, W)
        j = self.i
        self.i = (self.i + 1) % len(self.sems)
        if self.vals[j] >= 30000:
            E.wait(self.last[j])
            self.sems[j] = self.K.es.enter_context(self.K.nc.semaphore(f"d_{E.name}{j}_{self.K.uid()}"))
            self.vals[j] = 0
            self.last[j] = None
        E.wait(self.last[j])
        ins = fn(E.e)
        self.vals[j] += 16
        ins.then_inc(self.sems[j], 16)
        tok = Tok(self.sems[j], self.vals[j])
        self.last[j] = tok
        for b in R:
            b.r.append(("dma", tok))
        for b in W:
            b.w = ("dma", tok)
            b.r = []
        return tok


class Ring:
    def __init__(self, items):
        self.items = items
        self.i = 0

    def get(self):
        it = self.items[self.i]
        self.i = (self.i + 1) % len(self.items)
        return it


class T:
    __slots__ = ("t", "b")

    def __init__(self, t, b):
        self.t = t
        self.b = b


class KB:
    def __init__(self, depth=DEPTH, debug=None):
        self.depth = depth
        self.debug = debug
        self.es = ExitStack()
        self.nc = bass.Bass("TRN2", target_bir_lowering=False)
        self._uid = 0
        nc = self.nc
        self.pe = Eng(self, nc.tensor, "pe")
        self.act = Eng(self, nc.scalar, "act")
        self.dve = Eng(self, nc.vector, "dve")
        self.pool = Eng(self, nc.gpsimd, "pool")
        self.sp = Eng(self, nc.sync, "sp")
        self.engs = [self.pe, self.act, self.dve, self.pool, self.sp]
        self.qsp = DmaQ(self, self.sp, 12)
        self.qpl = DmaQ(self, self.pool, 12)
        self.dbg_outs = []
        self.marks = []

    def mark(self, label):
        self.marks.append((label, self.pe.nins))
        if self.debug == "marks" and hasattr(self, "identB"):
            ps = self.ps_get()
            self.mm(ps.t[0:3, 0:6], self.identB.t[:, 0:3], self.identB.t[:, 0:6], start=True, stop=True, R=[self.identB.b], W=[ps.b])

    def uid(self):
        self._uid += 1
        return self._uid

    def dram_in(self, name, shape, dt=F32):
        return self.nc.dram_tensor(name, list(shape), dt, kind="ExternalInput").ap()

    def dram_out(self, name, shape, dt=F32):
        return self.nc.dram_tensor(name, list(shape), dt, kind="ExternalOutput").ap()

    def dram_scr(self, name, shape, dt=F32):
        return self.nc.dram_tensor(name, list(shape), dt, kind="Internal").ap()

    def sbt(self, name, shape, dt):
        h = self.es.enter_context(self.nc.sbuf_tensor("s_" + name, list(shape), dt))
        return T(h, Buf(name))

    def carve_reset(self):
        self.aoff = 0

    def av(self, name, shape, dt):
        n = 1
        for s in shape[1:]:
            n *= s
        words = (n * (2 if dt == BF16 else 4) + 3) // 4
        words = (words + 7) // 8 * 8
        assert self.aoff + words <= self.arena_words, f"arena overflow at {name}: {self.aoff + words}"
        v = self.arena[:, self.aoff:self.aoff + words]
        self.aoff += words
        if dt != F32:
            v = v.bitcast(dt)
        v = v[:, 0:n]
        if len(shape) == 3:
            v = v.rearrange("p (a b) -> p a b", a=shape[1])
        elif len(shape) == 4:
            v = v.rearrange("p (a b c) -> p a b c", a=shape[1], b=shape[2])
        return T(v, Buf(name))

    def barrier(self):
        toks = [e.last for e in self.engs if e.last is not None]
        for q in (self.qsp, self.qpl):
            toks += [t for t in q.last if t is not None]
        for e in self.engs:
            for t in toks:
                e.wait(t)

    def dma(self, q, out, in_, R=(), W=(), **kw):
        return q.issue(lambda e: e.dma_start(out=out, in_=in_, **kw), R=R, W=W)

    def ps_get(self):
        return self.psring.get()

    def mm(self, out, lhsT, rhs, start, stop, R, W, sig=None):
        if sig is None:
            sig = stop
        return self.pe.op(lambda e: e.matmul(out, lhsT, rhs, start=start, stop=stop), R=R, W=W, sig=sig)

    def tr(self, out, in_, ident, R, W, sig=True):
        return self.pe.op(lambda e: e.transpose(out, in_, ident), R=R, W=W, sig=sig)

    def A(self, out, in_, func, R, W, **kw):
        return self.act.op(lambda e: e.activation(out=out, in_=in_, func=func, **kw), R=R, W=W)

    def V_tt(self, out, in0, in1, op, R, W, eng=None):
        eng = eng or self.dve
        return eng.op(lambda e: e.tensor_tensor(out=out, in0=in0, in1=in1, op=op), R=R, W=W)

    def V_ts(self, out, in0, s1, s2, op0, op1=None, R=(), W=(), eng=None, accum_out=None):
        eng = eng or self.dve
        kw = {}
        if op1 is not None:
            kw["op1"] = op1
        if accum_out is not None:
            kw["accum_out"] = accum_out
        return eng.op(lambda e: e.tensor_scalar(out=out, in0=in0, scalar1=s1, scalar2=s2, op0=op0, **kw), R=R, W=W)

    def V_stt(self, out, in0, scalar, in1, op0, op1, R, W):
        return self.dve.op(lambda e: e.scalar_tensor_tensor(out=out, in0=in0, scalar=scalar, in1=in1, op0=op0, op1=op1), R=R, W=W)

    def V_cp(self, out, in_, R, W, eng=None):
        eng = eng or self.dve
        return eng.op(lambda e: e.tensor_copy(out=out, in_=in_), R=R, W=W)

    def V_rcp(self, out, in_, R, W):
        return self.dve.op(lambda e: e.reciprocal(out=out, in_=in_), R=R, W=W)

    def memset(self, eng, ap, val, W):
        return eng.op(lambda e: e.memset(ap, val), R=(), W=W)

    def setup(self):
        nc = self.nc
        L = DEPTH
        di = self.dram_in
        self.xs_in = di("xs_in", [T_S, D])
        self.xp_in = di("xp_in", [2 * T_P, D])
        self.cckv = di("cckv", [L, PAST, KVL])
        self.ckr = di("ckr", [L, PAST, DR])
        self.condT = di("condT", [128, 8, 2])
        self.w_in_d = di("w_in_x", [L, 128, 8, NCOL])
        self.w_uq_d = di("w_uq_x", [L, 128, 3, 1024])
        self.w_uk_d = di("w_uk_x", [L, 128, 2, 512])
        self.w_uv_d = di("w_uv_x", [L, 128, 2, 512])
        self.poolw_d = di("poolw_x", [L, 128, 2, 256])
        self.w_out_d = di("w_out_x", [L, 128, 8, D])
        self.w_r_d = di("w_r_x", [L, 128, 8, NE])
        self.w_ada_d = di("w_ada", [L, D, 6 * D])
        self.vecs_d = di("vecs", [L, 128, NV])
        self.w_gate_d = di("w_gate", [L, NE, D, EF])
        self.w_up_d = di("w_up", [L, NE, D, EF])
        self.w_down_d = di("w_down", [L, NE, EF, D])
        self.identF_d = di("identF", [128, 128])
        self.gsum_d = di("gsum", [128, 128])
        self.lstrict_d = di("lstrict", [128, 128])
        self.iota1_d = di("iota1", [128, 512])
        self.tidhl_d = di("tidhl", [128, 36, 2])
        self.edges_d = di("edges", [128, 2, 16])
        self.misc_d = di("misc", [128, 16])
        self.cosT_d = di("cosT", [DR, T_S])
        self.sinT_d = di("sinT", [DR, T_S])
        self.tidcol_d = di("tidcol", [128, 36])
        self.y_s = self.dram_out("y_s", [T_S, D])
        self.y_p = self.dram_out("y_p", [2 * T_P, D])
        self.nckv = self.dram_out("nckv", [2, L, T_P, KVL])
        self.nkr = self.dram_out("nkr", [2, L, T_P, DR])
        self.xT_d = self.dram_scr("xT_d", [D, TTOT])
        self.x1T_d = self.dram_scr("x1T_d", [D, TTOT])
        self.u2rows_d = self.dram_scr("u2rows_d", [TTOT, ROWW], BF16)
        self.ffn_d = self.dram_scr("ffn_d", [TTOT, D])
        self.xT_b = [Buf(f"xT_d{c}") for c in range(TTOT // CH)]
        self.x1T_b = [Buf(f"x1T_d{c}") for c in range(TTOT // CH)]
        self.u2rows_b = Buf("u2rows_d")
        self.ffn_b = Buf("ffn_d")
        self.out_b = Buf("outs")

        s = self.sbt
        self.w_in = s("w_in", [128, 8, NCOL], BF16)
        self.w_uq = s("w_uq", [128, 3, 1024], BF16)
        self.w_uk = s("w_uk", [128, 2, 512], BF16)
        self.w_uv = s("w_uv", [128, 2, 512], BF16)
        self.poolw = s("poolw", [128, 2, 256], BF16)
        self.w_out = s("w_out", [128, 8, D], BF16)
        self.w_r = s("w_r", [128, 8, NE], F32)
        self.vec = s("vec", [128, NV], F32)
        self.modT = s("modT", [128, 48, 2], F32)
        self.s1p = s("s1p", [128, 8, 2], F32)
        self.g1a = s("g1a", [128, 8, 2], F32)
        self.G2 = s("G2", [128, 8, 2], F32)
        self.B2 = s("B2", [128, 8, 2], F32)
        self.g2a = s("g2a", [128, 8, 2], F32)
        self.wr2 = s("wr2", [128, 2, 8, NE], F32)
        self.rconst = s("rconst", [1, 2, NE], F32)
        self.scT = s("scT", [128, 8, 2], F32)
        self.identF = s("identF", [128, 128], F32)
        self.identB = s("identB", [128, 128], BF16)
        self.onesB = s("onesB", [128, 128], BF16)
        self.onesrow = s("onesrow", [1, 128], F32)
        self.tidhl = s("tidhl", [128, 36, 2], BF16)
        self.tidcol = s("tidcol", [128, 36], F32)
        self.edges = s("edges", [128, 2, 16], F32)
        self.misc = s("misc", [128, 16], F32)
        self.affS = s("affS", [128, 36, NE], F32)
        self.xt = s("xt", [128, 8, CW], F32)
        self.ubuf = s("ubuf", [128, 8, CW], BF16)
        self.st = [s(f"st{i}", [128, CH], F32) for i in range(3)]
        self.stf = [s(f"stf{i}", [128, CH], F32) for i in range(2)]
        self.u2b = [Buf(f"u2b{k}") for k in range(8)]
        self.zbk = Ring([s(f"zbk{i}", [128, CH], BF16) for i in range(2)])
        self.zsqk = Ring([s(f"zsqk{i}", [128, CH], BF16) for i in range(2)])
        self.xtb = [Buf(f"xt{k}") for k in range(8)]
        self.ubb = [Buf(f"ub{k}") for k in range(8)]
        self.affSb = [Buf(f"affS{g}") for g in range(TTOT // 128)]
        self.u2rows_bt = [Buf(f"u2r{g}") for g in range(TTOT // 128)]
        self.ffn_bt = [Buf(f"ffn{g}") for g in range(TTOT // 128)]
        self.vTM = [s("vTM0", [128, 4, 128], F32), s("vTM1", [128, 2, NE], F32), s("vTM2", [128, 2, NE], F32)]
        self.arena_words = 118 * 256
        self.arena = self.es.enter_context(nc.sbuf_tensor("arena", [128, self.arena_words], F32))
        banks = [T(self.es.enter_context(nc.psum_tensor(f"ps{i}", [128, 512], F32)), Buf(f"ps{i}")) for i in range(8)]
        self.psring = Ring(banks[:6])
        self.psS = Ring(banks[:3])
        self.psB = Ring(banks[3:6])
        self.psacc = Ring([(banks[6], banks[7])])
        self.bcreg = None

    def load_consts(self):
        q = self.qsp
        for (dst, src) in [(self.identF, self.identF_d), (self.edges, self.edges_d),
                           (self.misc, self.misc_d), (self.scT, self.condT), (self.tidcol, self.tidcol_d)]:
            self.dma(q, dst.t[:], src, W=[dst.b])
        self.V_cp(self.identB.t[:], self.identF.t[:], R=[self.identF.b], W=[self.identB.b])
        self.carve_reset()
        tf = self.av("tidhlF", [128, 36, 2], F32)
        self.dma(q, tf.t[:], self.tidhl_d, W=[tf.b])
        self.V_cp(self.tidhl.t[:], tf.t[:], R=[tf.b], W=[self.tidhl.b])
        self.barrier()
        self.memset(self.dve, self.onesB.t[:], 1.0, W=[self.onesB.b])
        self.memset(self.dve, self.onesrow.t[:], 1.0, W=[self.onesrow.b])
        self.A(self.scT.t[:], self.scT.t[:], AF.Silu, R=[self.scT.b], W=[self.scT.b])

    def transpose_in(self):
        self.carve_reset()
        tin = [self.av(f"tin{i}", [128, D], F32) for i in range(2)]
        tout = [self.av(f"tout{i}", [128, 8, 128], F32) for i in range(2)]
        xTv = self.xT_d.rearrange("(k p) t -> p k t", p=128)
        for g in range(TTOT // 128):
            src = self.xs_in[g * 128:(g + 1) * 128, :] if g < T_S // 128 else self.xp_in[(g - T_S // 128) * 128:(g - T_S // 128 + 1) * 128, :]
            ti = tin[g % 2]
            to = tout[g % 2]
            self.dma(self.qsp, ti.t[:], src, W=[ti.b])
            for hf in range(2):
                ps = self.ps_get()
                for kk in range(4):
                    k = hf * 4 + kk
                    self.tr(ps.t[:, kk * 128:(kk + 1) * 128], ti.t[:, k * 128:(k + 1) * 128], self.identF.t[:],
                            R=[ti.b, self.identF.b], W=[ps.b], sig=(kk == 3))
                if hf == 0:
                    self.A(to.t[:, 0:4, :], ps.t[:, :].rearrange("p (a b) -> p a b", a=4), AF.Copy, R=[ps.b], W=[to.b])
                else:
                    self.V_cp(to.t[:, 4:8, :], ps.t[:, :].rearrange("p (a b) -> p a b", a=4), R=[ps.b], W=[to.b])
            cb = self.xT_b[g // 2]
            self.dma(self.qsp, xTv[:, :, g * 128:(g + 1) * 128], to.t[:], R=[to.b], W=[cb])

    def load_layer_weights(self, l):
        q = self.qpl
        for k in range(8):
            self.dma(q, self.w_in.t[:, k, :], self.w_in_d[l][:, k, :], W=[self.w_in.b], max_dma_last_dim=4096)
        for k in range(3):
            self.dma(q, self.w_uq.t[:, k, :], self.w_uq_d[l][:, k, :], W=[self.w_uq.b], max_dma_last_dim=4096)
        self.dma(q, self.w_uk.t[:], self.w_uk_d[l], W=[self.w_uk.b], max_dma_last_dim=2048)
        self.dma(q, self.w_uv.t[:], self.w_uv_d[l], W=[self.w_uv.b], max_dma_last_dim=2048)
        self.dma(q, self.poolw.t[:], self.poolw_d[l], W=[self.poolw.b], max_dma_last_dim=1024)
        for k in range(8):
            self.dma(q, self.w_out.t[:, k, :], self.w_out_d[l][:, k, :], W=[self.w_out.b], max_dma_last_dim=4096)
        self.dma(self.qsp, self.w_r.t[:], self.w_r_d[l], W=[self.w_r.b])

    def mod_phase(self, l):
        self.carve_reset()
        wst = [self.av(f"wst{i}", [128, 8, 512], F32) for i in range(2)]
        tmp = self.av("modtmp", [128, 8, 2], F32)
        vec, modT = self.vec, self.modT
        self.dma(self.qsp, vec.t[:], self.vecs_d[l], W=[vec.b])
        for blk in range(12):
            w = wst[blk % 2]
            self.dma(self.qsp, w.t[:], self.w_ada_d[l][:, blk * 512:(blk + 1) * 512].rearrange("(k p) n -> p k n", p=128), W=[w.b])
            ps = self.ps_get()
            for j in range(4):
                for k in range(8):
                    self.mm(ps.t[:, j * 2:(j + 1) * 2], w.t[:, k, j * 128:(j + 1) * 128], self.scT.t[:, k, :],
                            start=(k == 0), stop=(k == 7), R=[w.b, self.scT.b], W=[ps.b])
            for j in range(4):
                jj = blk * 4 + j
                self.V_ts(modT.t[:, jj, :], ps.t[:, j * 2:(j + 1) * 2], vec.t[:, V_BADA + jj:V_BADA + jj + 1], None, ALU.add,
                          R=[ps.b, vec.b], W=[modT.b])
        sh1, sc1, gt1 = modT.t[:, 0:8, :], modT.t[:, 8:16, :], modT.t[:, 16:24, :]
        sh2, sc2, gt2 = modT.t[:, 24:32, :], modT.t[:, 32:40, :], modT.t[:, 40:48, :]
        mb = [modT.b]
        self.V_ts(self.s1p.t[:], sc1, 1.0, None, ALU.add, R=mb, W=[self.s1p.b])
        self.V_ts(self.g1a.t[:], gt1, 1.0 / ALPHA, None, ALU.mult, R=mb, W=[self.g1a.b])
        self.V_ts(self.g2a.t[:], gt2, 1.0 / ALPHA, None, ALU.mult, R=mb, W=[self.g2a.b])
        self.V_ts(tmp.t[:], sc2, 1.0, None, ALU.add, R=mb, W=[tmp.b])
        for c in range(2):
            self.V_tt(self.G2.t[:, :, c], tmp.t[:, :, c], vec.t[:, V_L1G:V_L1G + 8], ALU.mult, R=[tmp.b, vec.b], W=[self.G2.b])
            self.V_tt(self.B2.t[:, :, c], tmp.t[:, :, c], vec.t[:, V_L1B:V_L1B + 8], ALU.mult, R=[tmp.b, vec.b], W=[self.B2.b])
        self.V_tt(self.B2.t[:], self.B2.t[:], sh2, ALU.add, R=[self.B2.b] + mb, W=[self.B2.b])
        for c in range(2):
            for k in range(8):
                self.V_ts(self.wr2.t[:, c, k, :], self.w_r.t[:, k, :], self.G2.t[:, k, c:c + 1], None, ALU.mult,
                          R=[self.w_r.b, self.G2.b], W=[self.wr2.b])
            ps = self.ps_get()
            for k in range(8):
                self.mm(ps.t[0:1, 0:NE], self.B2.t[:, k, c:c + 1], self.w_r.t[:, k, :], start=(k == 0), stop=(k == 7),
                        R=[self.B2.b, self.w_r.b], W=[ps.b])
            self.V_cp(self.rconst.t[0:1, c, :], ps.t[0:1, 0:NE], R=[ps.b], W=[self.rconst.b])

    def carve_ab(self):
        self.carve_reset()
        a = self.av
        self.KT = a("KT", [128, NH, PAST + T_S], BF16)
        self.krT = a("krT", [128, PAST + T_S], BF16)
        self.Vt = a("Vt", [128, (PAST + T_S) // 128, NH * VA], BF16)
        nkc = (PAST + T_S) // CH
        self.KTb = [Buf(f"KT{i}") for i in range(nkc)]
        self.krTb = [Buf(f"krT{i}") for i in range(nkc)]
        self.Vb = [Buf(f"V{i}") for i in range(nkc)]
        self.qn = a("qn", [128, 3, CH], BF16)
        self.qnope = a("qnope", [128, NH, CH], BF16)
        self.qrope = a("qrope", [128, NH, CH], BF16)
        self.sqb = a("sqb", [128, 3, CH], BF16)
        self.PT = Ring([a(f"PT{i}", [128, 2, CH], BF16) for i in range(2)])
        pb_off = self.aoff
        self.pb = [a(f"pb{i}", [128, 2, CW], F32) for i in range(4)]
        assert self.aoff - pb_off == 8 * CW
        self.xnext = self.arena[:, pb_off:pb_off + 8 * CW].rearrange("p (a b) -> p a b", a=8)
        self.gbS = a("gbS", [128, 2, CH], F32)
        self.pooledB = a("pooledB", [128, 2, CH], BF16)
        self.ckvB = a("ckvB", [128, 2, CH], BF16)
        self.rowbuf = Ring([a(f"rowbuf{i}", [128, ROWW], BF16) for i in range(1)])
        self.u2buf = a("u2buf", [128, 8, CH], BF16)
        for rw in self.rowbuf.items:
            self.memset(self.pool, rw.t[:, D + 2 * NE:ROWW], 0.0, W=[rw.b])
        self.rt = a("rt", [128, 2, CH], F32)
        self.otm = Ring([a(f"otm{i}", [128, DV], BF16) for i in range(2)])
        self.smx = a("smx", [128, 8], F32)
        self.ex = a("ex", [128, NE], F32)
        self.edt = a("edt", [128, 2, 8], F32)
        save = self.aoff
        self.ckvF = a("ckvF", [128, 2, CH], F32)
        self.krF = a("krF", [128, CH], F32)
        self.outT = a("outT", [128, 2, KVL], F32)
        self.krout = a("krout", [128, 2, DR], F32)
        self.aoff = save
        self.ctxF = [a(f"ctxF{i}", [128, KVL], F32) for i in range(2)]
        self.ctxkr = [a(f"ctxkr{i}", [128, DR], F32) for i in range(2)]
        self.memset(self.pool, self.krT.t[DR:128, :], 0.0, W=self.krTb)
        self.memset(self.pool, self.Vt.t[:], 1.0, W=self.Vb)
        self.memset(self.pool, self.qrope.t[DR:128, :, :], 0.0, W=[self.qrope.b])

    def kv_up(self, kc):
        koff = kc * CH
        for h in range(NH):
            ps = self.ps_get()
            for m in range(2):
                self.mm(ps.t[:, 0:CH], self.w_uk.t[:, m, h * 128:(h + 1) * 128], self.ckvB.t[:, m, :],
                        start=(m == 0), stop=(m == 1), R=[self.w_uk.b, self.ckvB.b], W=[ps.b])
            if h % 2 == 0:
                self.A(self.KT.t[:, h, koff:koff + CH], ps.t[:, 0:CH], AF.Copy, R=[ps.b], W=[self.KTb[kc]])
            else:
                self.V_cp(self.KT.t[:, h, koff:koff + CH], ps.t[:, 0:CH], R=[ps.b], W=[self.KTb[kc]])
        for j in range(2):
            ps = self.ps_get()
            for m in range(2):
                self.mm(ps.t[:, :], self.ckvB.t[:, m, j * 128:(j + 1) * 128], self.w_uv.t[:, m, :],
                        start=(m == 0), stop=(m == 1), R=[self.w_uv.b, self.ckvB.b], W=[ps.b])
            kt = koff // 128 + j
            dst = self.Vt.t[:, kt, :].rearrange("p (h v) -> p h v", h=NH)[:, :, 0:DV]
            src = ps.t[:, :].rearrange("p (h v) -> p h v", h=NH)
            if j == 0:
                self.A(dst, src, AF.Copy, R=[ps.b], W=[self.Vb[kc]])
            else:
                self.V_cp(dst, src, R=[ps.b], W=[self.Vb[kc]])

    def ctx_phase(self, l):
        for jj in range(PAST // CH):
            for j in range(2):
                tile = jj * 2 + j
                cf, ck = self.ctxF[tile % 2], self.ctxkr[tile % 2]
                self.dma(self.qsp, cf.t[:], self.cckv[l][tile * 128:(tile + 1) * 128, :], W=[cf.b])
                self.dma(self.qsp, ck.t[:], self.ckr[l][tile * 128:(tile + 1) * 128, :], W=[ck.b])
                ps = self.ps_get()
                for m in range(2):
                    self.tr(ps.t[:, m * 128:(m + 1) * 128], cf.t[:, m * 128:(m + 1) * 128], self.identF.t[:],
                            R=[cf.b, self.identF.b], W=[ps.b], sig=(m == 1))
                self.V_cp(self.ckvB.t[:, :, j * 128:(j + 1) * 128], ps.t[:, 0:256].rearrange("p (m k) -> p m k", m=2),
                          R=[ps.b], W=[self.ckvB.b])
                ps2 = self.ps_get()
                self.tr(ps2.t[0:DR, 0:128], ck.t[:, :], self.identF.t[:], R=[ck.b, self.identF.b], W=[ps2.b])
                self.A(self.krT.t[0:DR, tile * 128:(tile + 1) * 128], ps2.t[0:DR, 0:128], AF.Copy, R=[ps2.b], W=[self.krTb[jj]])
            self.kv_up(jj)

    def u1_gen(self, cond, c0, c1):
        xt, ub = self.xt, self.ubuf
        for k in range(8):
            sc = self.s1p.t[:, k, cond:cond + 1]
            sh = self.modT.t[:, k, cond:cond + 1]
            if k % 2 == 0:
                self.V_ts(ub.t[:, k, c0:c1], xt.t[:, k, c0:c1], sc, sh, ALU.mult, ALU.add,
                          R=[self.xtb[k], self.s1p.b, self.modT.b], W=[self.ubb[k]])
            else:
                self.A(ub.t[:, k, c0:c1], xt.t[:, k, c0:c1], AF.Identity, scale=sc, bias=sh,
                       R=[self.xtb[k], self.s1p.b, self.modT.b], W=[self.ubb[k]])

    def load_rope(self, t0):
        self.dma(self.qsp, self.rt.t[0:DR, 0, :], self.cosT_d[:, t0:t0 + CH], W=[self.rt.b])
        self.dma(self.qsp, self.rt.t[0:DR, 1, :], self.sinT_d[:, t0:t0 + CH], W=[self.rt.b])

    def phase_a_chunk(self, l, r, c):
        cond = 1 if r == 0 else 0
        t0 = c * CH
        g0 = RB[r] + t0
        gc = g0 // CH
        kc = (PAST // CH if r == 0 else 0) + c
        koff = kc * CH
        xTv = self.xT_d.rearrange("(k p) t -> p k t", p=128)
        xt, ub = self.xt, self.ubuf
        self.dma(self.qsp, xt.t[:, :, HAL:HAL + CH], xTv[:, :, g0:g0 + CH], R=[self.xT_b[gc]], W=self.xtb)
        if r == 0:
            self.load_rope(t0)
        self.u1_gen(cond, HAL, HAL + CH)
        pkv = [self.ps_get(), self.ps_get()]
        for m in range(2):
            for k in range(8):
                self.mm(pkv[m].t[:, 0:CH], self.w_in.t[:, k, C_KV + m * 128:C_KV + (m + 1) * 128], ub.t[:, k, HAL:HAL + CH],
                        start=(k == 0), stop=(k == 7), R=[self.w_in.b, self.ubb[k]], W=[pkv[m].b])
        pkr = self.ps_get()
        for k in range(8):
            self.mm(pkr.t[0:DR, 0:CH], self.w_in.t[:, k, C_KR:C_KR + DR], ub.t[:, k, HAL:HAL + CH],
                    start=(k == 0), stop=(k == 7), R=[self.w_in.b, self.ubb[k]], W=[pkr.b])
        if r == 0:
            for k in range(8):
                self.mm(pkr.t[0:DR, CH:2 * CH], self.w_in.t[:, k, C_KRS:C_KRS + DR], ub.t[:, k, HAL:HAL + CH],
                        start=(k == 0), stop=(k == 7), R=[self.w_in.b, self.ubb[k]], W=[pkr.b])
        for m in range(2):
            self.A(self.sqb.t[:, m, :], pkv[m].t[:, 0:CH], AF.Square, R=[pkv[m].b], W=[self.sqb.b])
        pss = self.ps_get()
        for m in range(2):
            self.mm(pss.t[:, 0:CH], self.onesB.t[:], self.sqb.t[:, m, :], start=(m == 0), stop=(m == 1),
                    R=[self.onesB.b, self.sqb.b], W=[pss.b])
        st0 = self.st[0]
        self.A(st0.t[:], pss.t[:, 0:CH], AF.Ln, scale=1.0 / KVL, bias=self.misc.t[:, 0:1], R=[pss.b, self.misc.b], W=[st0.b])
        self.A(st0.t[:], st0.t[:], AF.Exp, scale=-0.5, R=[st0.b], W=[st0.b])
        for m in range(2):
            nrm = self.vec.t[:, V_KVN + m:V_KVN + m + 1]
            if r == 0:
                self.V_stt(self.ckvB.t[:, m, :], pkv[m].t[:, 0:CH], nrm, st0.t[:], ALU.mult, ALU.mult,
                           R=[pkv[m].b, self.vec.b, st0.b], W=[self.ckvB.b])
            else:
                self.V_stt(self.ckvF.t[:, m, :], pkv[m].t[:, 0:CH], nrm, st0.t[:], ALU.mult, ALU.mult,
                           R=[pkv[m].b, self.vec.b, st0.b], W=[self.ckvF.b])
        if r != 0:
            self.V_cp(self.ckvB.t[:], self.ckvF.t[:], R=[self.ckvF.b], W=[self.ckvB.b])
        if r == 0:
            s1, s2 = self.st[1], self.st[2]
            self.V_tt(s1.t[0:DR, :], pkr.t[0:DR, 0:CH], self.rt.t[0:DR, 0, :], ALU.mult, R=[pkr.b, self.rt.b], W=[s1.b])
            self.V_tt(s2.t[0:DR, :], pkr.t[0:DR, CH:2 * CH], self.rt.t[0:DR, 1, :], ALU.mult, R=[pkr.b, self.rt.b], W=[s2.b])
            self.V_tt(self.krT.t[0:DR, koff:koff + CH], s1.t[0:DR, :], s2.t[0:DR, :], ALU.add, R=[s1.b, s2.b], W=[self.krTb[kc]])
        else:
            self.A(self.krF.t[0:DR, :], pkr.t[0:DR, 0:CH], AF.Copy, R=[pkr.b], W=[self.krF.b])
            self.V_cp(self.krT.t[0:DR, koff:koff + CH], self.krF.t[0:DR, :], R=[self.krF.b], W=[self.krTb[kc]])
        self.kv_up(kc)
        if r != 0:
            for j in range(2):
                ps = self.ps_get()
                for m in range(2):
                    self.tr(ps.t[:, m * 128:(m + 1) * 128], self.ckvF.t[:, m, j * 128:(j + 1) * 128], self.identF.t[:],
                            R=[self.ckvF.b, self.identF.b], W=[ps.b], sig=(m == 1))
                self.A(self.outT.t[:, j, :], ps.t[:, 0:KVL], AF.Copy, R=[ps.b], W=[self.outT.b])
            self.dma(self.qsp, self.nckv[r - 1, l].rearrange("(j p) f -> p j f", p=128), self.outT.t[:], R=[self.outT.b], W=[self.out_b])
            ps = self.ps_get()
            for j in range(2):
                self.tr(ps.t[:, j * DR:(j + 1) * DR], self.krF.t[0:DR, j * 128:(j + 1) * 128], self.identF.t[0:DR, 0:DR],
                        R=[self.krF.b, self.identF.b], W=[ps.b], sig=(j == 1))
            self.V_cp(self.krout.t[:], ps.t[:, 0:2 * DR].rearrange("p (j f) -> p j f", j=2), R=[ps.b], W=[self.krout.b])
            self.dma(self.qsp, self.nkr[r - 1, l].rearrange("(j p) f -> p j f", p=128), self.krout.t[:], R=[self.krout.b], W=[self.out_b])

    def ln_core(self, ring=None, gen=False, X=None):
        g = self._ln_core(ring, X)
        if gen:
            return g
        for _ in g:
            pass

    def _ln_core(self, ring, X=None):
        xt = X["xt"] if X else self.xt
        xtb = X["xtb"] if X else self.xtb
        stl = X["st"] if X else self.st
        ring = ring or self.psring
        psm, psq = ring.get(), ring.get()
        cs = slice(HAL, HAL + CH)
        for j in range(8):
            zb, zs = self.zbk.get(), self.zsqk.get()
            self.A(zb.t[:], xt.t[:, j, cs], AF.Copy, R=[xtb[j]], W=[zb.b])
            if X is not None and X.get("act_sq"):
                self.A(zs.t[:], xt.t[:, j, cs], AF.Square, R=[xtb[j]], W=[zs.b])
            else:
                self.V_tt(zs.t[:], xt.t[:, j, cs], xt.t[:, j, cs], ALU.mult, R=[xtb[j]], W=[zs.b])
            yield
            self.mm(psm.t[:, 0:CH], self.onesB.t[:], zb.t[:], start=(j == 0), stop=(j == 7), R=[self.onesB.b, zb.b], W=[psm.b], sig=True)
            self.mm(psq.t[:, 0:CH], self.onesB.t[:], zs.t[:], start=(j == 0), stop=(j == 7), R=[self.onesB.b, zs.b], W=[psq.b], sig=True)
            yield
        s0, s1, s2 = stl[0], stl[1], stl[2]
        self.A(s0.t[:], psm.t[:, 0:CH], AF.Copy, scale=1.0 / D, R=[psm.b], W=[s0.b])
        self.V_tt(s1.t[:], s0.t[:], s0.t[:], ALU.mult, R=[s0.b], W=[s1.b])
        yield
        self.V_stt(s1.t[:], psq.t[:, 0:CH], 1.0 / D, s1.t[:], ALU.mult, ALU.subtract, R=[psq.b, s1.b], W=[s1.b])
        yield
        self.A(s1.t[:], s1.t[:], AF.Ln, bias=self.misc.t[:, 1:2], scale=1.0, R=[s1.b, self.misc.b], W=[s1.b])
        yield
        self.A(s1.t[:], s1.t[:], AF.Exp, scale=-0.5, R=[s1.b], W=[s1.b])
        self.V_stt(s2.t[:], s0.t[:], -1.0, s1.t[:], ALU.mult, ALU.mult, R=[s0.b, s1.b], W=[s2.b])
        yield
        for j in range(8):
            self.V_tt(xt.t[:, j, cs], xt.t[:, j, cs], s1.t[:], ALU.mult, R=[xtb[j], s1.b], W=[xtb[j]])
            self.V_tt(xt.t[:, j, cs], xt.t[:, j, cs], s2.t[:], ALU.add, R=[xtb[j], s2.b], W=[xtb[j]],
                      eng=(self.dve if (X is not None and X.get("act_sq")) else self.pool))
            yield

    def affine_xt(self, gcol0, bcol0, gen=False, X=None):
        g = self._affine_xt(gcol0, bcol0, X)
        if gen:
            return g
        for _ in g:
            pass

    def _affine_xt(self, gcol0, bcol0, X=None):
        xt = X["xt"] if X else self.xt
        xtb = X["xtb"] if X else self.xtb
        cs = slice(HAL, HAL + CH)
        for k in range(8):
            g = self.vec.t[:, gcol0 + k:gcol0 + k + 1]
            b = self.vec.t[:, bcol0 + k:bcol0 + k + 1]
            if k % 2 == 0 and not (X is not None and X.get("act_sq")):
                self.V_ts(xt.t[:, k, cs], xt.t[:, k, cs], g, b, ALU.mult, ALU.add, R=[xtb[k], self.vec.b], W=[xtb[k]])
            else:
                self.A(xt.t[:, k, cs], xt.t[:, k, cs], AF.Identity, scale=g, bias=b, R=[xtb[k], self.vec.b], W=[xtb[k]])
            yield

    def load_xnext(self, r, c):
        nch = REQ_T[r] // CH
        g0 = RB[r] + c * CH
        gc = g0 // CH
        first, last = (c == 0), (c == nch - 1)
        lo = 0 if first else -HAL
        hi = CH if last else CH + HAL
        xTv = self.xT_d.rearrange("(k p) t -> p k t", p=128)
        rb = [self.xT_b[gc]] + ([] if first else [self.xT_b[gc - 1]]) + ([] if last else [self.xT_b[gc + 1]])
        self.dma(self.qsp, self.xnext[:, :, HAL + lo:HAL + hi], xTv[:, :, g0 + lo:g0 + hi], R=rb, W=[q.b for q in self.pb])

    def pb_ctx(self, r, c):
        Tr = REQ_T[r]
        nch = Tr // CH
        t0 = c * CH
        g0 = RB[r] + t0
        first, last = (c == 0), (c == nch - 1)
        return dict(cond=1 if r == 0 else 0, t0=t0, g0=g0, gc=g0 // CH, first=first, last=last,
                    lo=0 if first else -HAL, hi=CH if last else CH + HAL)

    def pb_front(self, l, r, c):
        X = self.pb_ctx(r, c)
        cond, first, last, lo, hi = X["cond"], X["first"], X["last"], X["lo"], X["hi"]
        ub = self.ubuf
        cs = slice(HAL, HAL + CH)
        pbb = [q.b for q in self.pb]
        xn = self.xnext
        if first:
            self.load_xnext(r, c)
        if r == 0:
            self.load_rope(X["t0"])
        if first:
            self.memset(self.pool, ub.t[:, :, 0:HAL], 0.0, W=self.ubb)
        if last:
            self.memset(self.pool, ub.t[:, :, HAL + CH:CW], 0.0, W=self.ubb)
        c0, c1 = HAL + lo, HAL + hi
        for k in range(8):
            sc = self.s1p.t[:, k, cond:cond + 1]
            sh = self.modT.t[:, k, cond:cond + 1]
            if k % 2 == 0:
                self.V_ts(ub.t[:, k, c0:c1], xn[:, k, c0:c1], sc, sh, ALU.mult, ALU.add,
                          R=[pbb[k // 2], self.s1p.b, self.modT.b], W=[self.ubb[k]])
            else:
                self.A(ub.t[:, k, c0:c1], xn[:, k, c0:c1], AF.Identity, scale=sc, bias=sh,
                       R=[pbb[k // 2], self.s1p.b, self.modT.b], W=[self.ubb[k]])
        wb = self.w_in.b
        ring = self.psS

        def win(ps, col, ncols, c0, c1, prow=128):
            for k in range(8):
                self.mm(ps.t[0:prow, 0:c1 - c0], self.w_in.t[:, k, col:col + ncols], ub.t[:, k, c0:c1],
                        start=(k == 0), stop=(k == 7), R=[wb, self.ubb[k]], W=[ps.b])

        pq = self.psB.items
        for m in range(3):
            win(pq[m], C_Q + m * 128, 128, HAL, HAL + CH)
            self.A(self.sqb.t[:, m, :], pq[m].t[:, 0:CH], AF.Square, R=[pq[m].b], W=[self.sqb.b])
        P, Bq, Cq, Vv = self.pb
        for m in range(2):
            ps = ring.get()
            win(ps, C_P + m * 128, 128, 0, CW)
            self.A(P.t[:, m, :], ps.t[:, 0:CW], AF.Copy, R=[ps.b], W=[P.b])
        for m in range(2):
            psc = ring.get()
            win(psc, C_GC + m * 128, 128, 0, CW)
            self.A(Vv.t[:, m, :], psc.t[:, 0:CW], AF.Copy, R=[psc.b], W=[Vv.b])
            psh = ring.get()
            win(psh, C_H + m * 128, 128, 0, CW)
            self.V_tt(Vv.t[:, m, :], psh.t[:, 0:CW], Vv.t[:, m, :], ALU.mult, R=[psh.b, Vv.b], W=[Vv.b])
        for m in range(2):
            ps = ring.get()
            win(ps, C_GB + m * 128, 128, HAL, HAL + CH)
            self.A(self.gbS.t[:, m, :], ps.t[:, 0:CH], AF.Copy, R=[ps.b], W=[self.gbS.b])
        st0 = self.stf[0]
        pss = ring.get()
        for m in range(3):
            self.mm(pss.t[:, 0:CH], self.onesB.t[:], self.sqb.t[:, m, :], start=(m == 0), stop=(m == 2),
                    R=[self.onesB.b, self.sqb.b], W=[pss.b])
        self.A(st0.t[:], pss.t[:, 0:CH], AF.Ln, scale=1.0 / QL, bias=self.misc.t[:, 0:1], R=[pss.b, self.misc.b], W=[st0.b])
        self.A(st0.t[:], st0.t[:], AF.Exp, scale=-0.5, R=[st0.b], W=[st0.b])
        for m in range(3):
            self.V_stt(self.qn.t[:, m, :], pq[m].t[:, 0:CH], self.vec.t[:, V_QN + m:V_QN + m + 1], st0.t[:], ALU.mult, ALU.mult,
                       R=[pq[m].b, self.vec.b, st0.b], W=[self.qn.b])
        for h in range(NH):
            ps = ring.get()
            for m in range(3):
                self.mm(ps.t[:, 0:CH], self.w_uq.t[:, m, h * 192:h * 192 + 128], self.qn.t[:, m, :], start=(m == 0), stop=(m == 2),
                        R=[self.w_uq.b, self.qn.b], W=[ps.b])
            self.A(self.qnope.t[:, h, :], ps.t[:, 0:CH], AF.Copy, scale=ATTN_SCALE, R=[ps.b], W=[self.qnope.b])
            ps2 = ring.get()
            for m in range(3):
                self.mm(ps2.t[0:DR, 0:CH], self.w_uq.t[:, m, h * 192 + 128:h * 192 + 192], self.qn.t[:, m, :], start=(m == 0), stop=(m == 2),
                        R=[self.w_uq.b, self.qn.b], W=[ps2.b])
            if r == 0:
                for m in range(3):
                    self.mm(ps2.t[0:DR, CH:2 * CH], self.w_uq.t[:, m, 768 + h * DR:768 + (h + 1) * DR], self.qn.t[:, m, :],
                            start=(m == 0), stop=(m == 2), R=[self.w_uq.b, self.qn.b], W=[ps2.b])
                s1 = self.stf[1]
                self.V_tt(s1.t[0:DR, :], ps2.t[0:DR, 0:CH], self.rt.t[0:DR, 0, :], ALU.mult, R=[ps2.b, self.rt.b], W=[s1.b])
                self.V_tt(ps2.t[0:DR, CH:2 * CH], ps2.t[0:DR, CH:2 * CH], self.rt.t[0:DR, 1, :], ALU.mult, R=[ps2.b, self.rt.b], W=[ps2.b])
                self.V_tt(s1.t[0:DR, :], ps2.t[0:DR, CH:2 * CH], s1.t[0:DR, :], ALU.add, R=[s1.b, ps2.b], W=[s1.b])
                self.A(self.qrope.t[0:DR, h, :], s1.t[0:DR, :], AF.Copy, scale=ATTN_SCALE, R=[s1.b], W=[self.qrope.b])
            else:
                self.A(self.qrope.t[0:DR, h, :], ps2.t[0:DR, 0:CH], AF.Copy, scale=ATTN_SCALE, R=[ps2.b], W=[self.qrope.b])
        pl = self.pool
        add = ALU.add
        self.V_tt(Bq.t[:, :, 1:CW], P.t[:, :, 0:CW - 1], P.t[:, :, 1:CW], add, R=[P.b], W=[Bq.b], eng=pl)
        self.V_tt(Cq.t[64:128, 0, 2:CW - 1], Bq.t[64:128, 0, 1:CW - 2], Bq.t[64:128, 0, 3:CW], add, R=[Bq.b], W=[Cq.b], eng=pl)
        self.V_tt(Cq.t[:, 1, 2:CW - 1], Bq.t[:, 1, 1:CW - 2], Bq.t[:, 1, 3:CW], add, R=[Bq.b], W=[Cq.b], eng=pl)
        self.V_tt(Bq.t[:, 1, 4:CW - 3], Cq.t[:, 1, 2:CW - 5], Cq.t[:, 1, 6:CW - 1], add, R=[Cq.b], W=[Bq.b], eng=pl)
        self.V_tt(Cq.t[64:128, 1, 8:CW - 7], Bq.t[64:128, 1, 4:CW - 11], Bq.t[64:128, 1, 12:CW - 3], add, R=[Bq.b], W=[Cq.b], eng=pl)
        for m in range(2):
            for (p0, p1, Wb) in ((0, 64, Bq), (64, 128, Cq)):
                self.V_stt(self.pooledB.t[p0:p1, m, :], Wb.t[p0:p1, m, cs], self.misc.t[p0:p1, 2 + m:3 + m], P.t[p0:p1, m, cs],
                           ALU.mult, ALU.subtract, R=[Wb.b, P.b, self.misc.b], W=[self.pooledB.b])
                for (is_edge, ec0, wc0, oc0) in ((first, 0, HAL, 0), (last, 8, HAL + CH - 8, CH - 8)):
                    if is_edge:
                        self.V_tt(self.edt.t[p0:p1, m, :], Wb.t[p0:p1, m, wc0:wc0 + 8], self.edges.t[p0:p1, m, ec0:ec0 + 8], ALU.mult,
                                  R=[Wb.b, self.edges.b], W=[self.edt.b])
                        self.V_tt(self.pooledB.t[p0:p1, m, oc0:oc0 + 8], self.edt.t[p0:p1, m, :], P.t[p0:p1, m, wc0:wc0 + 8], ALU.subtract,
                                  R=[self.edt.b, P.b], W=[self.pooledB.b])
        cacc = Bq
        for m in range(2):
            w0 = self.vec.t[:, V_CW + m * 3 + 0:V_CW + m * 3 + 1]
            w1 = self.vec.t[:, V_CW + m * 3 + 1:V_CW + m * 3 + 2]
            w2 = self.vec.t[:, V_CW + m * 3 + 2:V_CW + m * 3 + 3]
            self.V_ts(cacc.t[:, m, cs], Vv.t[:, m, HAL - 1:HAL + CH - 1], w0, None, ALU.mult, R=[Vv.b, self.vec.b, self.pooledB.b], W=[cacc.b])
            self.V_stt(cacc.t[:, m, cs], Vv.t[:, m, cs], w1, cacc.t[:, m, cs], ALU.mult, ALU.add, R=[Vv.b, cacc.b, self.vec.b], W=[cacc.b])
            self.V_stt(cacc.t[:, m, cs], Vv.t[:, m, HAL + 1:HAL + CH + 1], w2, cacc.t[:, m, cs], ALU.mult, ALU.add, R=[Vv.b, cacc.b, self.vec.b], W=[cacc.b])
        for mo in range(2):
            ps = ring.get()
            for mi in range(2):
                self.mm(ps.t[:, 0:CH], self.poolw.t[:, mi, mo * 128:(mo + 1) * 128], self.pooledB.t[:, mi, :], start=(mi == 0), stop=(mi == 1),
                        R=[self.poolw.b, self.pooledB.b], W=[ps.b])
            self.A(ub.t[:, 4 + mo, cs], ps.t[:, 0:CH], AF.Copy, scale=self.vec.t[:, V_PS + mo:V_PS + mo + 1], R=[ps.b, self.vec.b], W=[self.ubb[4 + mo]])
            self.V_tt(ub.t[:, 6 + mo, cs], cacc.t[:, mo, cs], self.gbS.t[:, mo, :], ALU.mult, R=[cacc.b, self.gbS.b], W=[self.ubb[6 + mo]])
        if not last:
            self.load_xnext(r, c + 1)

    def pb_attention(self, l, r, c, side):
        ub = self.ubuf
        nkt = (PAST + T_S) // 128 if r == 0 else T_P // 128
        npair = nkt // 2
        ring = self.psS

        def emit_S(h, kp):
            ps = ring.get()
            for j in range(2):
                kt = kp * 2 + j
                self.mm(ps.t[:, j * CH:(j + 1) * CH], self.KT.t[:, h, kt * 128:(kt + 1) * 128], self.qnope.t[:, h, :],
                        start=True, stop=False, R=[self.KTb[kp], self.qnope.b], W=[ps.b], sig=False)
                self.mm(ps.t[:, j * CH:(j + 1) * CH], self.krT.t[:, kt * 128:(kt + 1) * 128], self.qrope.t[:, h, :],
                        start=False, stop=True, R=[self.krTb[kp], self.qrope.b], W=[ps.b], sig=(j == 1))
            return ps

        seq = [(h, kp) for h in range(NH) for kp in range(npair)]
        nside = max(1, (130 + len(seq) - 1) // len(seq))
        ps_next = emit_S(*seq[0])
        acc = self.psacc.items[0]
        for i, (h, kp) in enumerate(seq):
            ps = ps_next
            pt = self.PT.get()
            self.A(pt.t[:].rearrange("p a b -> p (a b)"), ps.t[:, :], AF.Exp, R=[ps.b], W=[pt.b])
            if i + 1 < len(seq):
                ps_next = emit_S(*seq[i + 1])
            for j in range(2):
                kt = kp * 2 + j
                for qt in range(2):
                    self.mm(acc[qt].t[:, 0:VA], pt.t[:, j, qt * 128:(qt + 1) * 128], self.Vt.t[:, kt, h * VA:(h + 1) * VA],
                            start=(kt == 0), stop=(kt == nkt - 1), R=[self.Vb[kp], pt.b], W=[acc[qt].b], sig=(j == 1 and qt == 1))
            if kp == npair - 1:
                sm = self.smx
                for qt in range(2):
                    self.V_rcp(sm.t[:, 4 + qt:5 + qt], acc[qt].t[:, DV:DV + 1], R=[acc[qt].b], W=[sm.b])
                    otm = self.otm.get()
                    self.V_ts(otm.t[:], acc[qt].t[:, 0:DV], sm.t[:, 4 + qt:5 + qt], None, ALU.mult, R=[acc[qt].b, sm.b], W=[otm.b])
                    pst = ring.get()
                    pstb = pst.t[:, :].bitcast(BF16)
                    self.tr(pstb[:, 0:128], otm.t[:], self.identB.t[:], R=[otm.b, self.identB.b], W=[pst.b])
                    self.A(ub.t[:, h, HAL + qt * 128:HAL + (qt + 1) * 128], pstb[:, 0:128], AF.Copy, R=[pst.b], W=[self.ubb[h]])
            if side is not None:
                for _ in range(nside):
                    if next(side, "done") == "done":
                        side = None
                        break
        if side is not None:
            for _ in side:
                pass

    def pb_wout(self, l, r, c):
        X = self.pb_ctx(r, c)
        cond, g0, gc = X["cond"], X["g0"], X["gc"]
        xTv = self.xT_d.rearrange("(k p) t -> p k t", p=128)
        xt, ub = self.xt, self.ubuf
        cs = slice(HAL, HAL + CH)
        self.dma(self.qsp, xt.t[:, :, cs], xTv[:, :, g0:g0 + CH], R=[self.xT_b[gc]], W=self.xtb)
        for j in range(8):
            ps = self.psB.get()
            for k in range(8):
                self.mm(ps.t[:, 0:CH], self.w_out.t[:, k, j * 128:(j + 1) * 128], ub.t[:, k, cs], start=(k == 0), stop=(k == 7),
                        R=[self.w_out.b, self.ubb[k]], W=[ps.b])
            self.V_stt(xt.t[:, j, cs], ps.t[:, 0:CH], self.g1a.t[:, j, cond:cond + 1], xt.t[:, j, cs], ALU.mult, ALU.add,
                       R=[ps.b, self.g1a.b, self.xtb[j]], W=[self.xtb[j]])

    def pb_late(self, l, r, c):
        X = self.pb_ctx(r, c)
        cond, g0, gc = X["cond"], X["g0"], X["gc"]
        x1Tv = self.x1T_d.rearrange("(k p) t -> p k t", p=128)
        xt, u2 = self.xt, self.u2buf
        cs = slice(HAL, HAL + CH)
        yield from self.ln_core(ring=self.psB, gen=True)
        for jt in range(2):
            gt = g0 // 128 + jt
            ps = self.psB.get()
            for k in range(8):
                self.mm(ps.t[:, 0:NE], xt.t[:, k, HAL + jt * 128:HAL + (jt + 1) * 128], self.wr2.t[:, cond, k, :], start=(k == 0), stop=False,
                        R=[self.xtb[k], self.wr2.b], W=[ps.b], sig=False)
            self.mm(ps.t[:, 0:NE], self.onesrow.t[0:1, :], self.rconst.t[0:1, cond, :], start=False, stop=True,
                    R=[self.onesrow.b, self.rconst.b], W=[ps.b], sig=True)
            yield
            sm = self.smx
            self.dve.op(lambda e: e.tensor_reduce(out=sm.t[:, 0:1], in_=ps.t[:, 0:NE], axis=AX.X, op=ALU.max), R=[ps.b], W=[sm.b])
            self.V_ts(sm.t[:, 1:2], sm.t[:, 0:1], -1.0, None, ALU.mult, R=[sm.b], W=[sm.b])
            yield
            self.A(self.ex.t[:], ps.t[:, 0:NE], AF.Exp, bias=sm.t[:, 1:2], scale=1.0, accum_out=sm.t[:, 2:3], R=[ps.b, sm.b], W=[self.ex.b, sm.b])
            yield
            self.V_rcp(sm.t[:, 3:4], sm.t[:, 2:3], R=[sm.b], W=[sm.b])
            self.V_ts(self.affS.t[:, gt, :], self.ex.t[:], sm.t[:, 3:4], None, ALU.mult, R=[self.ex.b, sm.b], W=[self.affSb[gt]])
            yield
        for k in range(8):
            g = self.G2.t[:, k, cond:cond + 1]
            b = self.B2.t[:, k, cond:cond + 1]
            if k % 2 == 1:
                self.V_ts(u2.t[:, k, :], xt.t[:, k, cs], g, b, ALU.mult, ALU.add, R=[self.xtb[k], self.G2.b, self.B2.b], W=[self.u2b[k]])
            else:
                self.A(u2.t[:, k, :], xt.t[:, k, cs], AF.Identity, scale=g, bias=b, R=[self.xtb[k], self.G2.b, self.B2.b], W=[self.u2b[k]])
            yield
        yield from self.affine_xt(V_L1G, V_L1B, gen=True)
        self.dma(self.qsp, x1Tv[:, :, g0:g0 + CH], xt.t[:, :, cs], R=self.xtb, W=[self.x1T_b[gc]])
        for jt in range(2):
            gt = g0 // 128 + jt
            ps = self.psB.get()
            psb = ps.t[:, :].bitcast(BF16)
            for k in range(8):
                self.tr(psb[:, k * 128:(k + 1) * 128], u2.t[:, k, jt * 128:(jt + 1) * 128], self.identB.t[:],
                        R=[self.u2b[k], self.identB.b], W=[ps.b], sig=(k == 7))
                if k % 4 == 3:
                    yield
            rw = self.rowbuf.get()
            self.A(rw.t[:, 0:D], psb[:, :], AF.Copy, R=[ps.b], W=[rw.b])
            self.V_cp(rw.t[:, D:D + 2 * NE].bitcast(F32), self.affS.t[:, gt, :], R=[self.affSb[gt], rw.b], W=[rw.b])
            self.V_cp(rw.t[:, D + 2 * NE:D + 2 * NE + 2].bitcast(I32), self.tidcol.t[:, gt:gt + 1], R=[self.tidcol.b, rw.b], W=[rw.b])
            self.dma(self.qsp, self.u2rows_d[gt * 128:(gt + 1) * 128, :], rw.t[:], R=[rw.b], W=[self.u2rows_bt[gt]])
            yield

    def phase_b(self, l, r):
        nch = REQ_T[r] // CH
        side = None
        for c in range(nch):
            self.pb_front(l, r, c)
            self.pb_attention(l, r, c, side)
            self.pb_wout(l, r, c)
            side = self.pb_late(l, r, c)
        for _ in side:
            pass

    def routing(self, r):
        self.carve_reset()
        a = self.av
        sample = (r == 0)
        Pn = 128 if sample else NE
        Fn = 512 if sample else T_P
        nblk = Fn // 128
        Kcap = float(REQ_CAP[r])
        A_sb = a("A_sb", [128, 512], F32)
        junk = a("junk", [128, 512], F32)
        msk = a("msk", [128, 512], F32)
        csb = a("csb", [128, 512], F32)
        affX = [a(f"affX{i}", [128, NE, 8], F32) for i in range(2)]
        bis = a("bis", [128, 8], I32)
        cnt = a("cnt", [128, 4], F32)
        self.gsum = a("gsum", [128, 128], F32)
        self.lstrict = a("lstrict", [128, 128], F32)
        if sample:
            self.dma(self.qsp, self.gsum.t[:], self.gsum_d, W=[self.gsum.b])
            self.dma(self.qsp, self.lstrict.t[:], self.lstrict_d, W=[self.lstrict.b])
        g0t = RB[r] // 128
        for b in range(nblk):
            ps = self.ps_get()
            if sample:
                for seg in range(8):
                    gt = seg * 4 + b
                    ax = affX[seg % 2]
                    self.memset(self.pool, ax.t[:], 0.0, W=[ax.b])
                    self.V_cp(ax.t[:, :, seg], self.affS.t[:, gt, :], R=[self.affSb[gt], ax.b], W=[ax.b])
                    self.mm(ps.t[:, 0:128], ax.t[:].rearrange("p e s -> p (e s)"), self.identF.t[:], start=(seg == 0), stop=(seg == 7),
                            R=[ax.b, self.identF.b], W=[ps.b], sig=True)
            else:
                gt = g0t + b
                self.mm(ps.t[0:NE, 0:128], self.affS.t[:, gt, :], self.identF.t[:], start=True, stop=True,
                        R=[self.affSb[gt], self.identF.b], W=[ps.b])
            self.V_cp(A_sb.t[0:Pn, b * 128:(b + 1) * 128], ps.t[0:Pn, 0:128], R=[ps.b], W=[A_sb.b])
        lo, mid, ge = (bis.t[0:Pn, i:i + 1] for i in range(3))
        bb = [bis.b]
        self.memset(self.dve, bis.t[:, 0:1], 0, W=bb)
        self.memset(self.dve, cnt.t[:], 0.0, W=[cnt.b])
        for bit in range(29, -1, -1):
            self.V_ts(mid, lo, 1 << bit, None, ALU.bitwise_or, R=bb, W=bb)
            self.V_ts(junk.t[0:Pn, 0:Fn], A_sb.t[0:Pn, 0:Fn], mid.bitcast(F32), None, ALU.is_ge, ALU.add,
                      R=[A_sb.b] + bb, W=[junk.b, cnt.b], accum_out=cnt.t[0:Pn, 0:1])
            if sample:
                ps = self.ps_get()
                self.mm(ps.t[:, 0:2], self.gsum.t[:], cnt.t[:, 0:2], start=True, stop=True, R=[self.gsum.b, cnt.b], W=[ps.b])
                src, sb_ = ps.t[0:Pn, 0:1], [ps.b]
            else:
                src, sb_ = cnt.t[0:Pn, 0:1], [cnt.b]
            self.V_ts(ge, src, Kcap - 0.5, None, ALU.is_ge, R=sb_, W=bb)
            self.dve.op(lambda e: e.copy_predicated(out=lo, mask=ge, data=mid), R=bb, W=bb)
        self.V_ts(msk.t[0:Pn, 0:Fn], A_sb.t[0:Pn, 0:Fn], lo.bitcast(F32), None, ALU.is_ge, R=[A_sb.b] + bb, W=[msk.b])
        self.memset(self.dve, junk.t[:], 1.0, W=[junk.b])
        self.dve.op(lambda e: e.tensor_tensor_scan(out=csb.t[0:Pn, 0:Fn], data0=junk.t[0:Pn, 0:Fn], data1=msk.t[0:Pn, 0:Fn],
                                                   initial=0.0, op0=ALU.mult, op1=ALU.add), R=[junk.b, msk.b], W=[csb.b])
        if sample:
            self.V_cp(cnt.t[:, 0:1], csb.t[:, Fn - 1:Fn], R=[csb.b], W=[cnt.b])
            ps = self.ps_get()
            self.mm(ps.t[:, 0:2], self.lstrict.t[:], cnt.t[:, 0:2], start=True, stop=True, R=[self.lstrict.b, cnt.b], W=[ps.b])
            self.V_cp(cnt.t[:, 2:3], ps.t[:, 0:1], R=[ps.b], W=[cnt.b])
            self.V_ts(csb.t[:, 0:Fn], csb.t[:, 0:Fn], cnt.t[:, 2:3], None, ALU.add, R=[csb.b, cnt.b], W=[csb.b])
        self.V_stt(junk.t[0:Pn, 0:Fn], csb.t[0:Pn, 0:Fn], Kcap + 0.5, msk.t[0:Pn, 0:Fn], ALU.is_le, ALU.mult, R=[csb.b, msk.b], W=[junk.b])
        self.V_tt(csb.t[0:Pn, 0:Fn], csb.t[0:Pn, 0:Fn], junk.t[0:Pn, 0:Fn], ALU.mult, R=[csb.b, junk.b], W=[csb.b])
        vt = self.vTM[r]
        for b in range(nblk):
            ps = self.ps_get()
            self.tr(ps.t[:, 0:Pn], csb.t[0:Pn, b * 128:(b + 1) * 128], self.identF.t[0:Pn, 0:Pn], R=[csb.b, self.identF.b], W=[ps.b])
            self.V_cp(vt.t[:, b, 0:Pn], ps.t[:, 0:Pn], R=[ps.b], W=[vt.b])

    def moe_phase(self, l):
        self.carve_reset()
        a = self.av
        wg = [a(f"wg{i}", [128, 8, EF], BF16) for i in range(2)]
        wu = [a(f"wu{i}", [128, 8, EF], BF16) for i in range(2)]
        wd = [a(f"wd{i}", [128, 4, D], BF16) for i in range(2)]
        xs = [a(f"xs{i}", [128, ROWW], BF16) for i in range(10)]
        NS = 576
        xsT = a("xsT", [128, 8, NS], BF16)
        hdn = a("hdn", [128, 4, NS], BF16)
        sil = a("sil", [128, NS], F32)
        ye = Ring([a(f"ye{i}", [128, D], F32) for i in range(3)])
        Sr = Ring([a(f"S{i}", [128, 512], BF16) for i in range(3)])
        Sp = Ring([a(f"Sp{i}", [128, 32], BF16) for i in range(2)])
        idxrow = a("idxrow", [2, NS], F32)
        idxI = Ring([a(f"idxI{i}", [128, 8], I32) for i in range(2)])
        idxF = a("idxF", [128, 5, 2], F32)
        zt = a("zt", [128, D], F32)
        self.iota1 = a("iota1", [128, 512], F32)
        self.dma(self.qsp, self.iota1.t[:], self.iota1_d, W=[self.iota1.b])
        self.memset(self.pool, zt.t[:], 0.0, W=[zt.b])
        for gt in range(TTOT // 128):
            self.dma(self.qsp, self.ffn_d[gt * 128:(gt + 1) * 128, :], zt.t[:], R=[zt.b], W=[self.ffn_bt[gt]])
        if self.bcreg is None:
            self.bcreg = self.nc.gpsimd.to_reg(TTOT - 1)

        def load_w(e):
            i = e % 2
            self.dma(self.qpl, wg[i].t[:], self.w_gate_d[l, e].rearrange("(k p) f -> p k f", p=128), W=[wg[i].b])
            self.dma(self.qpl, wu[i].t[:], self.w_up_d[l, e].rearrange("(k p) f -> p k f", p=128), W=[wu[i].b])
            self.dma(self.qpl, wd[i].t[:], self.w_down_d[l, e].rearrange("(k p) d -> p k d", p=128), W=[wd[i].b])

        load_w(0)
        tiles = [(c, 128) for c in range(4)] + [(4, 64)]
        scat = {"prev": [], "cur": []}

        def stage1(e):
            xse = xs[(e % 2) * 5:(e % 2) * 5 + 5]
            psI = self.psacc.items[0][0]
            psI2 = self.psacc.items[0][1]
            vt = self.vTM[0]
            ntile = T_S // 128
            Sl = [None] * ntile

            def onehot(gt):
                seg, b = gt // 4, gt % 4
                S = Sr.get()
                col = vt.t[:, b, e * 8 + seg:e * 8 + seg + 1]
                self.V_ts(S.t[:], self.iota1.t[:], col, None, ALU.is_equal, R=[self.iota1.b, vt.b], W=[S.b])
                Sl[gt] = S

            for gt in range(3):
                onehot(gt)
            yield
            for gt in range(ntile):
                S = Sl[gt]
                self.mm(psI.t[0:2, :], self.tidhl.t[:, gt, :], S.t[:], start=(gt == 0), stop=(gt == ntile - 1),
                        R=[self.tidhl.b, S.b], W=[psI.b], sig=True)
                if gt + 3 < ntile:
                    onehot(gt + 3)
                yield
            self.A(idxrow.t[0:2, 0:512], psI.t[0:2, :], AF.Copy, R=[psI.b], W=[idxrow.b])
            psJ = psI2
            for r in (1, 2):
                vtp = self.vTM[r]
                for b in range(2):
                    gt = RB[r] // 128 + b
                    S = Sp.get()
                    self.V_ts(S.t[:], self.iota1.t[:, 0:32], vtp.t[:, b, e:e + 1], None, ALU.is_equal, R=[self.iota1.b, vtp.b], W=[S.b])
                    self.mm(psJ.t[0:2, (r - 1) * 32:r * 32], self.tidhl.t[:, gt, :], S.t[:], start=(b == 0), stop=(b == 1),
                            R=[self.tidhl.b, S.b], W=[psJ.b], sig=True)
            yield
            self.A(idxrow.t[0:2, 512:NS], psJ.t[0:2, 0:64], AF.Copy, R=[psJ.b], W=[idxrow.b])
            psT = psI
            for (c, nr) in tiles:
                self.tr(psT.t[0:nr, c * 2:c * 2 + 2], idxrow.t[0:2, c * 128:c * 128 + nr], self.identF.t[0:2, 0:2],
                        R=[idxrow.b, self.identF.b], W=[psT.b], sig=(c == 4))
            yield
            self.V_cp(idxF.t[:].rearrange("p a b -> p (a b)"), psT.t[:, 0:10], R=[psT.b], W=[idxF.b])
            ii = idxI.get()
            self.V_stt(ii.t[:, 0:5], idxF.t[:, :, 0], 64.0, idxF.t[:, :, 1], ALU.mult, ALU.add, R=[idxF.b], W=[ii.b])
            for (c, nr) in tiles:
                self.qpl.issue(lambda g, c=c, nr=nr: g.indirect_dma_start(
                    out=xse[c].t[0:nr, :], out_offset=None, in_=self.u2rows_d,
                    in_offset=bass.IndirectOffsetOnAxis(ap=ii.t[0:nr, c:c + 1], axis=0),
                    bounds_check=self.bcreg, oob_is_err=False), R=[ii.b] + self.u2rows_bt, W=[xse[c].b])
            yield

        def stage2(e):
            i = e % 2
            xse = xs[(e % 2) * 5:(e % 2) * 5 + 5]
            for (c, nr) in tiles:
                ps = self.ps_get()
                psb = ps.t[:, :].bitcast(BF16)
                for k in range(8):
                    self.tr(psb[:, k * 128:k * 128 + nr], xse[c].t[0:nr, k * 128:(k + 1) * 128], self.identB.t[0:nr, 0:nr],
                            R=[xse[c].b, self.identB.b], W=[ps.b], sig=(k == 7))
                yield
                src = psb[:, :].rearrange("p (k n) -> p k n", k=8)[:, :, 0:nr]
                if c % 2 == 0:
                    self.A(xsT.t[:, :, c * 128:c * 128 + nr], src, AF.Copy, R=[ps.b], W=[xsT.b])
                else:
                    self.V_cp(xsT.t[:, :, c * 128:c * 128 + nr], src, R=[ps.b], W=[xsT.b])
            for f in range(4):
                for (c0, c1) in ((0, 512), (512, NS)):
                    n = c1 - c0
                    psg, psu = self.ps_get(), self.ps_get()
                    for k in range(8):
                        self.mm(psg.t[:, 0:n], wg[i].t[:, k, f * 128:(f + 1) * 128], xsT.t[:, k, c0:c1], start=(k == 0), stop=(k == 7),
                                R=[wg[i].b, xsT.b], W=[psg.b])
                    for k in range(8):
                        self.mm(psu.t[:, 0:n], wu[i].t[:, k, f * 128:(f + 1) * 128], xsT.t[:, k, c0:c1], start=(k == 0), stop=(k == 7),
                                R=[wu[i].b, xsT.b], W=[psu.b])
                    yield
                    self.A(sil.t[:, c0:c1], psg.t[:, 0:n], AF.Silu, R=[psg.b], W=[sil.b])
                    self.V_tt(hdn.t[:, f, c0:c1], psu.t[:, 0:n], sil.t[:, c0:c1], ALU.mult, R=[psu.b, sil.b], W=[hdn.b])
            for (c, nr) in tiles:
                y = ye.get()
                gcol = xse[c].t[0:nr, D + 2 * e:D + 2 * e + 2].bitcast(F32)
                for half in range(2):
                    ps = self.ps_get()
                    for f in range(4):
                        self.mm(ps.t[0:nr, :], hdn.t[:, f, c * 128:c * 128 + nr], wd[i].t[:, f, half * 512:(half + 1) * 512],
                                start=(f == 0), stop=(f == 3), R=[hdn.b, wd[i].b], W=[ps.b])
                    yield
                    if half == 0:
                        self.A(y.t[0:nr, 0:512], ps.t[0:nr, :], AF.Copy, scale=gcol, R=[ps.b, xse[c].b], W=[y.b])
                    else:
                        self.V_ts(y.t[0:nr, 512:D], ps.t[0:nr, :], gcol, None, ALU.mult, R=[ps.b, xse[c].b], W=[y.b])
                tid = xse[c].t[0:nr, D + 2 * NE:D + 2 * NE + 2].bitcast(I32)
                for t_prev in scat["prev"]:
                    self.pool.wait(t_prev)
                last_e = (e == NE - 1)
                tk = self.qpl.issue(lambda g, y=y, nr=nr, tid=tid: g.indirect_dma_start(
                    out=self.ffn_d, out_offset=bass.IndirectOffsetOnAxis(ap=tid, axis=0), in_=y.t[0:nr, :], in_offset=None,
                    compute_op=ALU.add), R=[y.b, xse[c].b] + (self.ffn_bt if e == 0 else []), W=(self.ffn_bt if last_e else []))
                scat["cur"].append(tk)
            scat["prev"], scat["cur"] = scat["cur"], []

        for _ in stage1(0):
            pass
        for e in range(NE):
            side = None
            if e + 1 < NE:
                load_w(e + 1)
                side = stage1(e + 1)
            if e == 1 and l + 1 < self.depth:
                self.load_layer_weights(l + 1)
            for _ in stage2(e):
                if side is not None:
                    for _k in range(4):
                        if next(side, "done") == "done":
                            side = None
                            break
            if side is not None:
                for _ in side:
                    pass

    def ln2_A(self, l, r, c, X, ft):
        cond = 1 if r == 0 else 0
        g0 = RB[r] + c * CH
        gc = g0 // CH
        x1Tv = self.x1T_d.rearrange("(k p) t -> p k t", p=128)
        xt, xtb = X["xt"], X["xtb"]
        cs = slice(HAL, HAL + CH)
        self.dma(self.qsp, xt.t[:, :, cs], x1Tv[:, :, g0:g0 + CH], R=[self.x1T_b[gc]], W=xtb)
        for jt in range(2):
            gt = g0 // 128 + jt
            self.dma(self.qsp, ft[jt].t[:], self.ffn_d[gt * 128:(gt + 1) * 128, :], R=[self.ffn_bt[gt]], W=[ft[jt].b])
        yield
        for k in range(8):
            ps = self.psS.get()
            for jt in range(2):
                self.tr(ps.t[:, jt * 128:(jt + 1) * 128], ft[jt].t[:, k * 128:(k + 1) * 128], self.identF.t[:],
                        R=[ft[jt].b, self.identF.b], W=[ps.b], sig=(jt == 1))
            self.V_stt(xt.t[:, k, cs], ps.t[:, 0:CH], self.g2a.t[:, k, cond:cond + 1], xt.t[:, k, cs], ALU.mult, ALU.add,
                       R=[ps.b, self.g2a.b, xtb[k]], W=[xtb[k]])
            yield

    def ln2_B(self, l, r, c, X, ot):
        g0 = RB[r] + c * CH
        gc = g0 // CH
        xTv = self.xT_d.rearrange("(k p) t -> p k t", p=128)
        xt, xtb = X["xt"], X["xtb"]
        cs = slice(HAL, HAL + CH)
        yield from self.ln_core(ring=self.psB, gen=True, X=X)
        yield from self.affine_xt(V_L2G, V_L2B, gen=True, X=X)
        if l < self.depth - 1:
            self.dma(self.qsp, xTv[:, :, g0:g0 + CH], xt.t[:, :, cs], R=xtb, W=[self.xT_b[gc]])
        else:
            for jt in range(2):
                o = ot[jt]
                for hf in range(2):
                    ps = self.psB.get()
                    for kk in range(4):
                        k = hf * 4 + kk
                        self.tr(ps.t[:, kk * 128:(kk + 1) * 128], xt.t[:, k, HAL + jt * 128:HAL + (jt + 1) * 128], self.identF.t[:],
                                R=[xtb[k], self.identF.b], W=[ps.b], sig=(kk == 3))
                    if hf == 0:
                        self.A(o.t[:, 0:512], ps.t[:, :], AF.Copy, R=[ps.b], W=[o.b])
                    else:
                        self.V_cp(o.t[:, 512:D], ps.t[:, :], R=[ps.b], W=[o.b])
                    yield
                row = c * CH + jt * 128
                dst = self.y_s[row:row + 128, :] if r == 0 else self.y_p[(r - 1) * T_P + row:(r - 1) * T_P + row + 128, :]
                self.dma(self.qsp, dst, o.t[:], R=[o.b], W=[Buf("o")])

    def ln2_phase(self, l):
        self.carve_reset()
        a = self.av
        ft = [a(f"ft{i}", [128, D], F32) for i in range(4)]
        ot = [a(f"ot{i}", [128, D], F32) for i in range(2)]
        sets = []
        for i in range(2):
            sets.append(dict(act_sq=True, xt=a(f"xl{i}", [128, 8, CW], F32), xtb=[Buf(f"xl{i}_{k}") for k in range(8)],
                             st=[a(f"stl{i}_{j}", [128, CH], F32) for j in range(3)]))
        chunks = [(r, c) for r in range(3) for c in range(REQ_T[r] // CH)]
        prevB = None
        for i, (r, c) in enumerate(chunks):
            X = sets[i % 2]
            genA = self.ln2_A(l, r, c, X, ft[(i % 2) * 2:(i % 2) * 2 + 2])
            while genA is not None or prevB is not None:
                if genA is not None and next(genA, "done") == "done":
                    genA = None
                if prevB is not None:
                    for _ in range(3):
                        if next(prevB, "done") == "done":
                            prevB = None
                            break
                if genA is None and prevB is None:
                    break
            prevB = self.ln2_B(l, r, c, X, ot)
        for _ in prevB:
            pass

    def build(self):
        self.setup()
        self.load_consts()
        self.transpose_in()
        self.load_layer_weights(0)
        self.barrier()
        for l in range(self.depth):
            self.mark(f"L{l} mod")
            self.mod_phase(l)
            self.barrier()
            for r in range(3):
                self.carve_ab()
                self.mark(f"L{l} r{r} A")
                if r == 0:
                    self.ctx_phase(l)
                nch = REQ_T[r] // CH
                for c in range(nch):
                    self.phase_a_chunk(l, r, c)
                self.mark(f"L{l} r{r} B")
                self.phase_b(l, r)
                self.barrier()
                self.mark(f"L{l} r{r} route")
                self.routing(r)
                self.barrier()
            self.mark(f"L{l} moe")
            self.moe_phase(l)
            self.barrier()
            self.mark(f"L{l} ln2")
            self.ln2_phase(l)
            self.barrier()
        self.barrier()
        self.mark("end")
        return self.nc


def _rope_tables():
    rows_n = T_S // 64
    r, cl = np.meshgrid(np.arange(rows_n, dtype=np.float32), np.arange(64, dtype=np.float32), indexing="ij")
    inv = (np.float32(10000.0) ** (-np.arange(0, 32, 2, dtype=np.float32) / np.float32(32))).astype(np.float32)
    ang = np.concatenate([r.reshape(-1)[:, None] * inv, cl.reshape(-1)[:, None] * inv], axis=-1).astype(np.float32)
    cos, sin = np.cos(ang).astype(np.float32), np.sin(ang).astype(np.float32)
    cosT = np.concatenate([cos.T, cos.T], axis=0)
    sinT = np.concatenate([-sin.T, sin.T], axis=0)
    return np.ascontiguousarray(cosT), np.ascontiguousarray(sinT)


def _consts():
    c = {}
    c["identF"] = np.eye(128, dtype=np.float32)
    p = np.arange(128)
    c["gsum"] = (p[:, None] // 8 == p[None, :] // 8).astype(np.float32)
    c["lstrict"] = ((p[:, None] // 8 == p[None, :] // 8) & (p[:, None] % 8 < p[None, :] % 8)).astype(np.float32)
    c["iota1"] = np.broadcast_to(np.arange(1, 513, dtype=np.float32)[None, :], (128, 512)).copy()
    rows = np.arange(36)[None, :] * 128 + p[:, None]
    c["tidhl"] = np.stack([rows // 64, rows % 64], axis=-1).astype(np.float32)
    c["tidcol"] = rows.astype(np.float32)
    edges = np.zeros((128, 2, 16), np.float32)
    invw = np.zeros((128, 2), np.float32)
    for m in range(2):
        for half in range(2):
            w = (2, 4, 8, 16)[m * 2 + half]
            sl = slice(half * 64, half * 64 + 64)
            invw[sl, m] = 1.0 / w
            for i in range(8):
                t = i
                cnt = min(t + w // 2, 10 ** 9) - max(0, t - w // 2)
                edges[sl, m, i] = 1.0 / cnt
                j = 7 - i
                cnt = min(w // 2, j + 1) + min(w // 2, 10 ** 9)
                edges[sl, m, 8 + i] = 1.0 / cnt
    c["edges"] = edges
    misc = np.zeros((128, 16), np.float32)
    misc[:, 0] = RMS_EPS
    misc[:, 1] = LN_EPS / (ALPHA * ALPHA)
    misc[:, 2:4] = invw
    c["misc"] = misc
    c["cosT"], c["sinT"] = _rope_tables()
    return c


def _prep_shared(inp):
    f = lambda a: np.ascontiguousarray(np.asarray(a, dtype=np.float32))
    w_in = f(inp["w_in"])
    L = DEPTH
    krs = np.concatenate([np.arange(C_KR + 32, C_KR + 64), np.arange(C_KR, C_KR + 32)])
    w_in_x = np.concatenate([w_in, w_in[:, :, krs]], axis=2)
    sh = {}
    sh["w_in_x"] = np.ascontiguousarray(w_in_x.reshape(L, 8, 128, NCOL).transpose(0, 2, 1, 3))
    w_uq = f(inp["w_uq"])
    sw = np.concatenate([np.concatenate([np.arange(h * 192 + 160, h * 192 + 192), np.arange(h * 192 + 128, h * 192 + 160)]) for h in range(NH)])
    w_uq_x = np.concatenate([w_uq, w_uq[:, :, sw]], axis=2)
    sh["w_uq_x"] = np.ascontiguousarray(w_uq_x.reshape(L, 3, 128, 1024).transpose(0, 2, 1, 3))
    sh["w_uk_x"] = np.ascontiguousarray(f(inp["w_uk"]).reshape(L, 2, 128, 512).transpose(0, 2, 1, 3))
    sh["w_uv_x"] = np.ascontiguousarray(f(inp["w_uv"]).reshape(L, 2, 128, 512).transpose(0, 2, 1, 3))
    pw = f(inp["pool_w"])
    bd = np.zeros((L, 256, 256), np.float32)
    for g in range(4):
        bd[:, g * 64:(g + 1) * 64, g * 64:(g + 1) * 64] = pw[:, g]
    sh["poolw_x"] = np.ascontiguousarray(bd.reshape(L, 2, 128, 256).transpose(0, 2, 1, 3))
    sh["w_out_x"] = np.ascontiguousarray(f(inp["w_out"]).reshape(L, 8, 128, D).transpose(0, 2, 1, 3))
    sh["w_r_x"] = np.ascontiguousarray(f(inp["w_router"]).reshape(L, 8, 128, NE).transpose(0, 2, 1, 3))
    sh["w_ada"] = f(inp["w_ada"])
    vecs = np.zeros((L, 128, NV), np.float32)
    col = lambda v, n: v.reshape(L, n, 128).transpose(0, 2, 1)
    vecs[:, :, V_BADA:V_BADA + 48] = col(f(inp["b_ada"]), 48)
    vecs[:, :, V_L1G:V_L1G + 8] = col(f(inp["ln1_g"]), 8)
    vecs[:, :, V_L1B:V_L1B + 8] = col(f(inp["ln1_b"]), 8)
    vecs[:, :, V_L2G:V_L2G + 8] = col(f(inp["ln2_g"]), 8)
    vecs[:, :, V_L2B:V_L2B + 8] = col(f(inp["ln2_b"]), 8)
    vecs[:, :, V_QN:V_QN + 3] = col(f(inp["q_norm"]), 3)
    vecs[:, :, V_KVN:V_KVN + 2] = col(f(inp["kv_norm"]), 2)
    vecs[:, :, V_PS:V_PS + 2] = col(f(inp["pool_scale"]), 2)
    cw = f(inp["conv_w"])
    for m in range(2):
        for t in range(3):
            vecs[:, :, V_CW + m * 3 + t] = cw[:, t, m * 128:(m + 1) * 128]
    sh["vecs"] = vecs
    sh["w_gate"] = f(inp["w_gate"])
    sh["w_up"] = f(inp["w_up"])
    sh["w_down"] = f(inp["w_down"])
    sh.update(_consts())
    return sh


_NC_CACHE = {}


def kernel(**inp):
    sh = _prep_shared(inp)
    xp = np.asarray(inp["x_prompt"], dtype=np.float32)
    xsmp = np.asarray(inp["x_sample"], dtype=np.float32)
    cckv = np.asarray(inp["cache_ckv"], dtype=np.float32)
    ckr = np.asarray(inp["cache_krope"], dtype=np.float32)
    cvec = np.asarray(inp["c"], dtype=np.float32)
    cctx = np.asarray(inp["c_ctx"], dtype=np.float32)
    in_maps = []
    for core in range(8):
        s = core // 4
        m = dict(sh)
        m["xs_in"] = np.ascontiguousarray(xsmp[s])
        m["xp_in"] = np.ascontiguousarray(xp[2 * core:2 * core + 2].reshape(2 * T_P, D))
        m["cckv"] = np.ascontiguousarray(cckv[s])
        m["ckr"] = np.ascontiguousarray(ckr[s])
        cond = np.stack([cctx, cvec[s]], axis=-1)
        m["condT"] = np.ascontiguousarray(cond.reshape(8, 128, 2).transpose(1, 0, 2))
        in_maps.append(m)
    if "nc" not in _NC_CACHE:
        _NC_CACHE["nc"] = KB().build()
    nc = _NC_CACHE["nc"]
    res = run_bass_kernel_spmd(nc, in_maps, core_ids=list(range(8)))
    rs = res.results
    y_prompt = np.concatenate([np.asarray(rs[c]["y_p"]).reshape(2, T_P, D) for c in range(8)], axis=0)
    y_sample = np.stack([np.asarray(rs[0]["y_s"]), np.asarray(rs[4]["y_s"])], axis=0)
    new_ckv = np.concatenate([np.asarray(rs[c]["nckv"]) for c in range(8)], axis=0)
    new_krope = np.concatenate([np.asarray(rs[c]["nkr"]) for c in range(8)], axis=0)
    return (y_prompt.astype(np.float32), y_sample.astype(np.float32), new_ckv.astype(np.float32), new_krope.astype(np.float32))
```

```python
import numpy as np
import ml_dtypes
from contextlib import ExitStack
import concourse.bass as bass
import concourse.mybir as mybir
from concourse.bass_utils import run_bass_kernel_spmd

F32 = mybir.dt.float32
BF16 = mybir.dt.bfloat16
I32 = mybir.dt.int32
AF = mybir.ActivationFunctionType
ALU = mybir.AluOpType
AX = mybir.AxisListType

D = 1024
DEPTH = 4
T_S = 4096
T_P = 256
PAST = 512
NH = 4
DN = 128
DR = 64
DV = 128
QL = 384
KVL = 256
NE = 16
EF = 512
ALPHA = (2 * DEPTH) ** 0.25
RMS_EPS = 1e-6
LN_EPS = 1e-5
ATTN_SCALE = (DN + DR) ** -0.5
CH = 256
HAL = 8
CW = CH + 2 * HAL
VA = 130
NCOL = 1792
C_Q, C_KV, C_KR, C_P, C_GB, C_GC, C_H, C_KRS = 0, 384, 640, 704, 960, 1216, 1472, 1728
RB = [0, T_S, T_S + T_P]
TTOT = T_S + 2 * T_P
REQ_T = [T_S, T_P, T_P]
REQ_CAP = [2 * T_S // NE, 2 * T_P // NE, 2 * T_P // NE]
ROWW = 1060
NV = 93
V_BADA, V_L1G, V_L1B, V_L2G, V_L2B, V_QN, V_KVN, V_PS, V_CW = 0, 48, 56, 64, 72, 80, 83, 85, 87
BIGIDX = float(1 << 20)


class Tok:
    __slots__ = ("sem", "val")

    def __init__(self, sem=None, val=0):
        self.sem = sem
        self.val = val


class Buf:
    __slots__ = ("name", "w", "r")

    def __init__(self, name):
        self.name = name
        self.w = None
        self.r = []


class Eng:
    def __init__(self, K, eng, name):
        self.K = K
        self.e = eng
        self.name = name
        self.sem = None
        self.cnt = 0
        self.nsem = 0
        self.waited = {}
        self.pending = []
        self.last = None
        self.nins = 0
        self._newsem()

    def _newsem(self):
        self.sem = self.K.es.enter_context(self.K.nc.semaphore(f"p_{self.name}{self.nsem}"))
        self.nsem += 1
        self.cnt = 0

    def wait(self, tok):
        if tok is None:
            return
        assert tok.sem is not None, f"wait on unsignalled token ({self.name})"
        key = id(tok.sem)
        if self.waited.get(key, (None, 0))[1] >= tok.val:
            return
        self.e.wait_ge(tok.sem, tok.val)
        self.waited[key] = (tok.sem, tok.val)

    def begin(self, R, W):
        for b in R:
            if b.w is not None:
                self.wait(b.w[1])
        for b in W:
            if b.w is not None and b.w[0] != self.name:
                self.wait(b.w[1])
            for (en, tk) in b.r:
                if en != self.name:
                    self.wait(tk)

    def end(self, ins, R, W, sig=True):
        tok = Tok()
        self.pending.append(tok)
        for b in R:
            b.r.append((self.name, tok))
        for b in W:
            b.w = (self.name, tok)
            b.r = []
        if sig:
            if self.cnt >= 30000:
                self._newsem()
            self.cnt += 1
            ins.then_inc(self.sem, 1)
            for t in self.pending:
                t.sem = self.sem
                t.val = self.cnt
            self.pending = []
            self.last = tok
        return tok

    def op(self, fn, R=(), W=(), sig=True):
        self.begin(R, W)
        ins = fn(self.e)
        self.nins += 1
        return self.end(ins, R, W, sig)


class DmaQ:
    def __init__(self, K, E, nsem):
        self.K = K
        self.E = E
        self.sems = [K.es.enter_context(K.nc.semaphore(f"d_{E.name}{i}")) for i in range(nsem)]
        self.vals = [0] * nsem
        self.last = [None] * nsem
        self.i = 0

    def issue(self, fn, R=(), W=()):
        E = self.E
        E.begin(R, W)
        j = self.i
        self.i = (self.i + 1) % len(self.sems)
        if self.vals[j] >= 30000:
            E.wait(self.last[j])
            self.sems[j] = self.K.es.enter_context(self.K.nc.semaphore(f"d_{E.name}{j}_{self.K.uid()}"))
            self.vals[j] = 0
            self.last[j] = None
        E.wait(self.last[j])
        ins = fn(E.e)
        self.vals[j] += 16
        ins.then_inc(self.sems[j], 16)
        tok = Tok(self.sems[j], self.vals[j])
        self.last[j] = tok
        for b in R:
            b.r.append(("dma", tok))
        for b in W:
            b.w = ("dma", tok)
            b.r = []
        return tok


class Ring:
    def __init__(self, items):
        self.items = items
        self.i = 0

    def get(self):
        it = self.items[self.i]
        self.i = (self.i + 1) % len(self.items)
        return it


class T:
    __slots__ = ("t", "b")

    def __init__(self, t, b):
        self.t = t
        self.b = b


class KB:
    def __init__(self, depth=DEPTH, debug=None):
        self.depth = depth
        self.debug = debug
        self.es = ExitStack()
        self.nc = bass.Bass("TRN2", target_bir_lowering=False)
        self._uid = 0
        nc = self.nc
        self.pe = Eng(self, nc.tensor, "pe")
        self.act = Eng(self, nc.scalar, "act")
        self.dve = Eng(self, nc.vector, "dve")
        self.pool = Eng(self, nc.gpsimd, "pool")
        self.sp = Eng(self, nc.sync, "sp")
        self.engs = [self.pe, self.act, self.dve, self.pool, self.sp]
        self.qsp = DmaQ(self, self.sp, 12)
        self.qpl = DmaQ(self, self.pool, 12)
        self.dbg_outs = []
        self.marks = []

    def mark(self, label):
        self.marks.append((label, self.pe.nins))
        if self.debug == "marks" and hasattr(self, "identB"):
            ps = self.ps_get()
            self.mm(ps.t[0:3, 0:6], self.identB.t[:, 0:3], self.identB.t[:, 0:6], start=True, stop=True, R=[self.identB.b], W=[ps.b])

    def uid(self):
        self._uid += 1
        return self._uid

    def dram_in(self, name, shape, dt=F32):
        return self.nc.dram_tensor(name, list(shape), dt, kind="ExternalInput").ap()

    def dram_out(self, name, shape, dt=F32):
        return self.nc.dram_tensor(name, list(shape), dt, kind="ExternalOutput").ap()

    def dram_scr(self, name, shape, dt=F32):
        return self.nc.dram_tensor(name, list(shape), dt, kind="Internal").ap()

    def sbt(self, name, shape, dt):
        h = self.es.enter_context(self.nc.sbuf_tensor("s_" + name, list(shape), dt))
        return T(h, Buf(name))

    def carve_reset(self):
        self.aoff = 0

    def av(self, name, shape, dt):
        n = 1
        for s in shape[1:]:
            n *= s
        words = (n * (2 if dt == BF16 else 4) + 3) // 4
        words = (words + 7) // 8 * 8
        assert self.aoff + words <= self.arena_words, f"arena overflow at {name}: {self.aoff + words}"
        v = self.arena[:, self.aoff:self.aoff + words]
        self.aoff += words
        if dt != F32:
            v = v.bitcast(dt)
        v = v[:, 0:n]
        if len(shape) == 3:
            v = v.rearrange("p (a b) -> p a b", a=shape[1])
        elif len(shape) == 4:
            v = v.rearrange("p (a b c) -> p a b c", a=shape[1], b=shape[2])
        return T(v, Buf(name))

    def barrier(self):
        toks = [e.last for e in self.engs if e.last is not None]
        for q in (self.qsp, self.qpl):
            toks += [t for t in q.last if t is not None]
        for e in self.engs:
            for t in toks:
                e.wait(t)

    def dma(self, q, out, in_, R=(), W=(), **kw):
        return q.issue(lambda e: e.dma_start(out=out, in_=in_, **kw), R=R, W=W)

    def ps_get(self):
        return self.psring.get()

    def mm(self, out, lhsT, rhs, start, stop, R, W, sig=None):
        if sig is None:
            sig = stop
        return self.pe.op(lambda e: e.matmul(out, lhsT, rhs, start=start, stop=stop), R=R, W=W, sig=sig)

    def tr(self, out, in_, ident, R, W, sig=True):
        return self.pe.op(lambda e: e.transpose(out, in_, ident), R=R, W=W, sig=sig)

    def A(self, out, in_, func, R, W, **kw):
        return self.act.op(lambda e: e.activation(out=out, in_=in_, func=func, **kw), R=R, W=W)

    def V_tt(self, out, in0, in1, op, R, W, eng=None):
        eng = eng or self.dve
        return eng.op(lambda e: e.tensor_tensor(out=out, in0=in0, in1=in1, op=op), R=R, W=W)

    def V_ts(self, out, in0, s1, s2, op0, op1=None, R=(), W=(), eng=None, accum_out=None):
        eng = eng or self.dve
        kw = {}
        if op1 is not None:
            kw["op1"] = op1
        if accum_out is not None:
            kw["accum_out"] = accum_out
        return eng.op(lambda e: e.tensor_scalar(out=out, in0=in0, scalar1=s1, scalar2=s2, op0=op0, **kw), R=R, W=W)

    def V_stt(self, out, in0, scalar, in1, op0, op1, R, W):
        return self.dve.op(lambda e: e.scalar_tensor_tensor(out=out, in0=in0, scalar=scalar, in1=in1, op0=op0, op1=op1), R=R, W=W)

    def V_cp(self, out, in_, R, W, eng=None):
        eng = eng or self.dve
        return eng.op(lambda e: e.tensor_copy(out=out, in_=in_), R=R, W=W)

    def V_rcp(self, out, in_, R, W):
        return self.dve.op(lambda e: e.reciprocal(out=out, in_=in_), R=R, W=W)

    def memset(self, eng, ap, val, W):
        return eng.op(lambda e: e.memset(ap, val), R=(), W=W)

    def setup(self):
        nc = self.nc
        L = DEPTH
        di = self.dram_in
        self.xs_in = di("xs_in", [T_S, D])
        self.xp_in = di("xp_in", [2 * T_P, D])
        self.cckv = di("cckv", [L, PAST, KVL])
        self.ckr = di("ckr", [L, PAST, DR])
        self.condT = di("condT", [128, 8, 2])
        self.w_in_d = di("w_in_x", [L, 128, 8, NCOL])
        self.w_uq_d = di("w_uq_x", [L, 128, 3, 1024])
        self.w_uk_d = di("w_uk_x", [L, 128, 2, 512])
        self.w_uv_d = di("w_uv_x", [L, 128, 2, 512])
        self.poolw_d = di("poolw_x", [L, 128, 2, 256])
        self.w_out_d = di("w_out_x", [L, 128, 8, D])
        self.w_r_d = di("w_r_x", [L, 128, 8, NE])
        self.w_ada_d = di("w_ada", [L, D, 6 * D])
        self.vecs_d = di("vecs", [L, 128, NV])
        self.w_gate_d = di("w_gate", [L, NE, D, EF])
        self.w_up_d = di("w_up", [L, NE, D, EF])
        self.w_down_d = di("w_down", [L, NE, EF, D])
        self.identF_d = di("identF", [128, 128])
        self.gsum_d = di("gsum", [128, 128])
        self.lstrict_d = di("lstrict", [128, 128])
        self.iota1_d = di("iota1", [128, 512])
        self.tidhl_d = di("tidhl", [128, 36, 2])
        self.edges_d = di("edges", [128, 2, 16])
        self.misc_d = di("misc", [128, 16])
        self.cosT_d = di("cosT", [DR, T_S])
        self.sinT_d = di("sinT", [DR, T_S])
        self.tidcol_d = di("tidcol", [128, 36])
        self.y_s = self.dram_out("y_s", [T_S, D])
        self.y_p = self.dram_out("y_p", [2 * T_P, D])
        self.nckv = self.dram_out("nckv", [2, L, T_P, KVL])
        self.nkr = self.dram_out("nkr", [2, L, T_P, DR])
        self.xT_d = self.dram_scr("xT_d", [D, TTOT])
        self.x1T_d = self.dram_scr("x1T_d", [D, TTOT])
        self.u2rows_d = self.dram_scr("u2rows_d", [TTOT, ROWW], BF16)
        self.ffn_d = self.dram_scr("ffn_d", [TTOT, D])
        self.xT_b = [Buf(f"xT_d{c}") for c in range(TTOT // CH)]
        self.x1T_b = [Buf(f"x1T_d{c}") for c in range(TTOT // CH)]
        self.u2rows_b = Buf("u2rows_d")
        self.ffn_b = Buf("ffn_d")
        self.out_b = Buf("outs")

        s = self.sbt
        self.w_in = s("w_in", [128, 8, NCOL], BF16)
        self.w_uq = s("w_uq", [128, 3, 1024], BF16)
        self.w_uk = s("w_uk", [128, 2, 512], BF16)
        self.w_uv = s("w_uv", [128, 2, 512], BF16)
        self.poolw = s("poolw", [128, 2, 256], BF16)
        self.w_out = s("w_out", [128, 8, D], BF16)
        self.w_r = s("w_r", [128, 8, NE], F32)
        self.vec = s("vec", [128, NV], F32)
        self.modT = s("modT", [128, 48, 2], F32)
        self.s1p = s("s1p", [128, 8, 2], F32)
        self.g1a = s("g1a", [128, 8, 2], F32)
        self.G2 = s("G2", [128, 8, 2], F32)
        self.B2 = s("B2", [128, 8, 2], F32)
        self.g2a = s("g2a", [128, 8, 2], F32)
        self.wr2 = s("wr2", [128, 2, 8, NE], F32)
        self.rconst = s("rconst", [1, 2, NE], F32)
        self.scT = s("scT", [128, 8, 2], F32)
        self.identF = s("identF", [128, 128], F32)
        self.identB = s("identB", [128, 128], BF16)
        self.onesB = s("onesB", [128, 128], BF16)
        self.onesrow = s("onesrow", [1, 128], F32)
        self.tidhl = s("tidhl", [128, 36, 2], BF16)
        self.tidcol = s("tidcol", [128, 36], F32)
        self.edges = s("edges", [128, 2, 16], F32)
        self.misc = s("misc", [128, 16], F32)
        self.affS = s("affS", [128, 36, NE], F32)
        self.xt = s("xt", [128, 8, CW], F32)
        self.ubuf = s("ubuf", [128, 8, CW], BF16)
        self.st = [s(f"st{i}", [128, CH], F32) for i in range(3)]
        self.stf = [s(f"stf{i}", [128, CH], F32) for i in range(2)]
        self.u2b = [Buf(f"u2b{k}") for k in range(8)]
        self.zbk = Ring([s(f"zbk{i}", [128, CH], BF16) for i in range(2)])
        self.zsqk = Ring([s(f"zsqk{i}", [128, CH], BF16) for i in range(2)])
        self.xtb = [Buf(f"xt{k}") for k in range(8)]
        self.ubb = [Buf(f"ub{k}") for k in range(8)]
        self.affSb = [Buf(f"affS{g}") for g in range(TTOT // 128)]
        self.u2rows_bt = [Buf(f"u2r{g}") for g in range(TTOT // 128)]
        self.ffn_bt = [Buf(f"ffn{g}") for g in range(TTOT // 128)]
        self.vTM = [s("vTM0", [128, 4, 128], F32), s("vTM1", [128, 2, NE], F32), s("vTM2", [128, 2, NE], F32)]
        self.arena_words = 118 * 256
        self.arena = self.es.enter_context(nc.sbuf_tensor("arena", [128, self.arena_words], F32))
        banks = [T(self.es.enter_context(nc.psum_tensor(f"ps{i}", [128, 512], F32)), Buf(f"ps{i}")) for i in range(8)]
        self.psring = Ring(banks[:6])
        self.psS = Ring(banks[:3])
        self.psB = Ring(banks[3:6])
        self.psacc = Ring([(banks[6], banks[7])])
        self.bcreg = None

    def load_consts(self):
        q = self.qsp
        for (dst, src) in [(self.identF, self.identF_d), (self.edges, self.edges_d),
                           (self.misc, self.misc_d), (self.scT, self.condT), (self.tidcol, self.tidcol_d)]:
            self.dma(q, dst.t[:], src, W=[dst.b])
        self.V_cp(self.identB.t[:], self.identF.t[:], R=[self.identF.b], W=[self.identB.b])
        self.carve_reset()
        tf = self.av("tidhlF", [128, 36, 2], F32)
        self.dma(q, tf.t[:], self.tidhl_d, W=[tf.b])
        self.V_cp(self.tidhl.t[:], tf.t[:], R=[tf.b], W=[self.tidhl.b])
        self.barrier()
        self.memset(self.dve, self.onesB.t[:], 1.0, W=[self.onesB.b])
        self.memset(self.dve, self.onesrow.t[:], 1.0, W=[self.onesrow.b])
        self.A(self.scT.t[:], self.scT.t[:], AF.Silu, R=[self.scT.b], W=[self.scT.b])

    def transpose_in(self):
        self.carve_reset()
        tin = [self.av(f"tin{i}", [128, D], F32) for i in range(2)]
        tout = [self.av(f"tout{i}", [128, 8, 128], F32) for i in range(2)]
        xTv = self.xT_d.rearrange("(k p) t -> p k t", p=128)
        for g in range(TTOT // 128):
            src = self.xs_in[g * 128:(g + 1) * 128, :] if g < T_S // 128 else self.xp_in[(g - T_S // 128) * 128:(g - T_S // 128 + 1) * 128, :]
            ti = tin[g % 2]
            to = tout[g % 2]
            self.dma(self.qsp, ti.t[:], src, W=[ti.b])
            for hf in range(2):
                ps = self.ps_get()
                for kk in range(4):
                    k = hf * 4 + kk
                    self.tr(ps.t[:, kk * 128:(kk + 1) * 128], ti.t[:, k * 128:(k + 1) * 128], self.identF.t[:],
                            R=[ti.b, self.identF.b], W=[ps.b], sig=(kk == 3))
                if hf == 0:
                    self.A(to.t[:, 0:4, :], ps.t[:, :].rearrange("p (a b) -> p a b", a=4), AF.Copy, R=[ps.b], W=[to.b])
                else:
                    self.V_cp(to.t[:, 4:8, :], ps.t[:, :].rearrange("p (a b) -> p a b", a=4), R=[ps.b], W=[to.b])
            cb = self.xT_b[g // 2]
            self.dma(self.qsp, xTv[:, :, g * 128:(g + 1) * 128], to.t[:], R=[to.b], W=[cb])

    def load_layer_weights(self, l):
        q = self.qpl
        for k in range(8):
            self.dma(q, self.w_in.t[:, k, :], self.w_in_d[l][:, k, :], W=[self.w_in.b], max_dma_last_dim=4096)
        for k in range(3):
            self.dma(q, self.w_uq.t[:, k, :], self.w_uq_d[l][:, k, :], W=[self.w_uq.b], max_dma_last_dim=4096)
        self.dma(q, self.w_uk.t[:], self.w_uk_d[l], W=[self.w_uk.b], max_dma_last_dim=2048)
        self.dma(q, self.w_uv.t[:], self.w_uv_d[l], W=[self.w_uv.b], max_dma_last_dim=2048)
        self.dma(q, self.poolw.t[:], self.poolw_d[l], W=[self.poolw.b], max_dma_last_dim=1024)
        for k in range(8):
            self.dma(q, self.w_out.t[:, k, :], self.w_out_d[l][:, k, :], W=[self.w_out.b], max_dma_last_dim=4096)
        self.dma(self.qsp, self.w_r.t[:], self.w_r_d[l], W=[self.w_r.b])

    def mod_phase(self, l):
        self.carve_reset()
        wst = [self.av(f"wst{i}", [128, 8, 512], F32) for i in range(2)]
        tmp = self.av("modtmp", [128, 8, 2], F32)
        vec, modT = self.vec, self.modT
        self.dma(self.qsp, vec.t[:], self.vecs_d[l], W=[vec.b])
        for blk in range(12):
            w = wst[blk % 2]
            self.dma(self.qsp, w.t[:], self.w_ada_d[l][:, blk * 512:(blk + 1) * 512].rearrange("(k p) n -> p k n", p=128), W=[w.b])
            ps = self.ps_get()
            for j in range(4):
                for k in range(8):
                    self.mm(ps.t[:, j * 2:(j + 1) * 2], w.t[:, k, j * 128:(j + 1) * 128], self.scT.t[:, k, :],
                            start=(k == 0), stop=(k == 7), R=[w.b, self.scT.b], W=[ps.b])
            for j in range(4):
                jj = blk * 4 + j
                self.V_ts(modT.t[:, jj, :], ps.t[:, j * 2:(j + 1) * 2], vec.t[:, V_BADA + jj:V_BADA + jj + 1], None, ALU.add,
                          R=[ps.b, vec.b], W=[modT.b])
        sh1, sc1, gt1 = modT.t[:, 0:8, :], modT.t[:, 8:16, :], modT.t[:, 16:24, :]
        sh2, sc2, gt2 = modT.t[:, 24:32, :], modT.t[:, 32:40, :], modT.t[:, 40:48, :]
        mb = [modT.b]
        self.V_ts(self.s1p.t[:], sc1, 1.0, None, ALU.add, R=mb, W=[self.s1p.b])
        self.V_ts(self.g1a.t[:], gt1, 1.0 / ALPHA, None, ALU.mult, R=mb, W=[self.g1a.b])
        self.V_ts(self.g2a.t[:], gt2, 1.0 / ALPHA, None, ALU.mult, R=mb, W=[self.g2a.b])
        self.V_ts(tmp.t[:], sc2, 1.0, None, ALU.add, R=mb, W=[tmp.b])
        for c in range(2):
            self.V_tt(self.G2.t[:, :, c], tmp.t[:, :, c], vec.t[:, V_L1G:V_L1G + 8], ALU.mult, R=[tmp.b, vec.b], W=[self.G2.b])
            self.V_tt(self.B2.t[:, :, c], tmp.t[:, :, c], vec.t[:, V_L1B:V_L1B + 8], ALU.mult, R=[tmp.b, vec.b], W=[self.B2.b])
        self.V_tt(self.B2.t[:], self.B2.t[:], sh2, ALU.add, R=[self.B2.b] + mb, W=[self.B2.b])
        for c in range(2):
            for k in range(8):
                self.V_ts(self.wr2.t[:, c, k, :], self.w_r.t[:, k, :], self.G2.t[:, k, c:c + 1], None, ALU.mult,
                          R=[self.w_r.b, self.G2.b], W=[self.wr2.b])
            ps = self.ps_get()
            for k in range(8):
                self.mm(ps.t[0:1, 0:NE], self.B2.t[:, k, c:c + 1], self.w_r.t[:, k, :], start=(k == 0), stop=(k == 7),
                        R=[self.B2.b, self.w_r.b], W=[ps.b])
            self.V_cp(self.rconst.t[0:1, c, :], ps.t[0:1, 0:NE], R=[ps.b], W=[self.rconst.b])

    def carve_ab(self):
        self.carve_reset()
        a = self.av
        self.KT = a("KT", [128, NH, PAST + T_S], BF16)
        self.krT = a("krT", [128, PAST + T_S], BF16)
        self.Vt = a("Vt", [128, (PAST + T_S) // 128, NH * VA], BF16)
        nkc = (PAST + T_S) // CH
        self.KTb = [Buf(f"KT{i}") for i in range(nkc)]
        self.krTb = [Buf(f"krT{i}") for i in range(nkc)]
        self.Vb = [Buf(f"V{i}") for i in range(nkc)]
        self.qn = a("qn", [128, 3, CH], BF16)
        self.qnope = a("qnope", [128, NH, CH], BF16)
        self.qrope = a("qrope", [128, NH, CH], BF16)
        self.sqb = a("sqb", [128, 3, CH], BF16)
        self.PT = Ring([a(f"PT{i}", [128, 2, CH], BF16) for i in range(2)])
        pb_off = self.aoff
        self.pb = [a(f"pb{i}", [128, 2, CW], F32) for i in range(4)]
        assert self.aoff - pb_off == 8 * CW
        self.xnext = self.arena[:, pb_off:pb_off + 8 * CW].rearrange("p (a b) -> p a b", a=8)
        self.gbS = a("gbS", [128, 2, CH], F32)
        self.pooledB = a("pooledB", [128, 2, CH], BF16)
        self.ckvB = a("ckvB", [128, 2, CH], BF16)
        self.rowbuf = Ring([a(f"rowbuf{i}", [128, ROWW], BF16) for i in range(1)])
        self.u2buf = a("u2buf", [128, 8, CH], BF16)
        for rw in self.rowbuf.items:
            self.memset(self.pool, rw.t[:, D + 2 * NE:ROWW], 0.0, W=[rw.b])
        self.rt = a("rt", [128, 2, CH], F32)
        self.otm = Ring([a(f"otm{i}", [128, DV], BF16) for i in range(2)])
        self.smx = a("smx", [128, 8], F32)
        self.ex = a("ex", [128, NE], F32)
        self.edt = a("edt", [128, 2, 8], F32)
        save = self.aoff
        self.ckvF = a("ckvF", [128, 2, CH], F32)
        self.krF = a("krF", [128, CH], F32)
        self.outT = a("outT", [128, 2, KVL], F32)
        self.krout = a("krout", [128, 2, DR], F32)
        self.aoff = save
        self.ctxF = [a(f"ctxF{i}", [128, KVL], F32) for i in range(2)]
        self.ctxkr = [a(f"ctxkr{i}", [128, DR], F32) for i in range(2)]
        self.memset(self.pool, self.krT.t[DR:128, :], 0.0, W=self.krTb)
        self.memset(self.pool, self.Vt.t[:], 1.0, W=self.Vb)
        self.memset(self.pool, self.qrope.t[DR:128, :, :], 0.0, W=[self.qrope.b])

    def kv_up(self, kc):
        koff = kc * CH
        for h in range(NH):
            ps = self.ps_get()
            for m in range(2):
                self.mm(ps.t[:, 0:CH], self.w_uk.t[:, m, h * 128:(h + 1) * 128], self.ckvB.t[:, m, :],
                        start=(m == 0), stop=(m == 1), R=[self.w_uk.b, self.ckvB.b], W=[ps.b])
            if h % 2 == 0:
                self.A(self.KT.t[:, h, koff:koff + CH], ps.t[:, 0:CH], AF.Copy, R=[ps.b], W=[self.KTb[kc]])
            else:
                self.V_cp(self.KT.t[:, h, koff:koff + CH], ps.t[:, 0:CH], R=[ps.b], W=[self.KTb[kc]])
        for j in range(2):
            ps = self.ps_get()
            for m in range(2):
                self.mm(ps.t[:, :], self.ckvB.t[:, m, j * 128:(j + 1) * 128], self.w_uv.t[:, m, :],
                        start=(m == 0), stop=(m == 1), R=[self.w_uv.b, self.ckvB.b], W=[ps.b])
            kt = koff // 128 + j
            dst = self.Vt.t[:, kt, :].rearrange("p (h v) -> p h v", h=NH)[:, :, 0:DV]
            src = ps.t[:, :].rearrange("p (h v) -> p h v", h=NH)
            if j == 0:
                self.A(dst, src, AF.Copy, R=[ps.b], W=[self.Vb[kc]])
            else:
                self.V_cp(dst, src, R=[ps.b], W=[self.Vb[kc]])

    def ctx_phase(self, l):
        for jj in range(PAST // CH):
            for j in range(2):
                tile = jj * 2 + j
                cf, ck = self.ctxF[tile % 2], self.ctxkr[tile % 2]
                self.dma(self.qsp, cf.t[:], self.cckv[l][tile * 128:(tile + 1) * 128, :], W=[cf.b])
                self.dma(self.qsp, ck.t[:], self.ckr[l][tile * 128:(tile + 1) * 128, :], W=[ck.b])
                ps = self.ps_get()
                for m in range(2):
                    self.tr(ps.t[:, m * 128:(m + 1) * 128], cf.t[:, m * 128:(m + 1) * 128], self.identF.t[:],
                            R=[cf.b, self.identF.b], W=[ps.b], sig=(m == 1))
                self.V_cp(self.ckvB.t[:, :, j * 128:(j + 1) * 128], ps.t[:, 0:256].rearrange("p (m k) -> p m k", m=2),
                          R=[ps.b], W=[self.ckvB.b])
                ps2 = self.ps_get()
                self.tr(ps2.t[0:DR, 0:128], ck.t[:, :], self.identF.t[:], R=[ck.b, self.identF.b], W=[ps2.b])
                self.A(self.krT.t[0:DR, tile * 128:(tile + 1) * 128], ps2.t[0:DR, 0:128], AF.Copy, R=[ps2.b], W=[self.krTb[jj]])
            self.kv_up(jj)

    def u1_gen(self, cond, c0, c1):
        xt, ub = self.xt, self.ubuf
        for k in range(8):
            sc = self.s1p.t[:, k, cond:cond + 1]
            sh = self.modT.t[:, k, cond:cond + 1]
            if k % 2 == 0:
                self.V_ts(ub.t[:, k, c0:c1], xt.t[:, k, c0:c1], sc, sh, ALU.mult, ALU.add,
                          R=[self.xtb[k], self.s1p.b, self.modT.b], W=[self.ubb[k]])
            else:
                self.A(ub.t[:, k, c0:c1], xt.t[:, k, c0:c1], AF.Identity, scale=sc, bias=sh,
                       R=[self.xtb[k], self.s1p.b, self.modT.b], W=[self.ubb[k]])

    def load_rope(self, t0):
        self.dma(self.qsp, self.rt.t[0:DR, 0, :], self.cosT_d[:, t0:t0 + CH], W=[self.rt.b])
        self.dma(self.qsp, self.rt.t[0:DR, 1, :], self.sinT_d[:, t0:t0 + CH], W=[self.rt.b])

    def phase_a_chunk(self, l, r, c):
        cond = 1 if r == 0 else 0
        t0 = c * CH
        g0 = RB[r] + t0
        gc = g0 // CH
        kc = (PAST // CH if r == 0 else 0) + c
        koff = kc * CH
        xTv = self.xT_d.rearrange("(k p) t -> p k t", p=128)
        xt, ub = self.xt, self.ubuf
        self.dma(self.qsp, xt.t[:, :, HAL:HAL + CH], xTv[:, :, g0:g0 + CH], R=[self.xT_b[gc]], W=self.xtb)
        if r == 0:
            self.load_rope(t0)
        self.u1_gen(cond, HAL, HAL + CH)
        pkv = [self.ps_get(), self.ps_get()]
        for m in range(2):
            for k in range(8):
                self.mm(pkv[m].t[:, 0:CH], self.w_in.t[:, k, C_KV + m * 128:C_KV + (m + 1) * 128], ub.t[:, k, HAL:HAL + CH],
                        start=(k == 0), stop=(k == 7), R=[self.w_in.b, self.ubb[k]], W=[pkv[m].b])
        pkr = self.ps_get()
        for k in range(8):
            self.mm(pkr.t[0:DR, 0:CH], self.w_in.t[:, k, C_KR:C_KR + DR], ub.t[:, k, HAL:HAL + CH],
                    start=(k == 0), stop=(k == 7), R=[self.w_in.b, self.ubb[k]], W=[pkr.b])
        if r == 0:
            for k in range(8):
                self.mm(pkr.t[0:DR, CH:2 * CH], self.w_in.t[:, k, C_KRS:C_KRS + DR], ub.t[:, k, HAL:HAL + CH],
                        start=(k == 0), stop=(k == 7), R=[self.w_in.b, self.ubb[k]], W=[pkr.b])
        for m in range(2):
            self.A(self.sqb.t[:, m, :], pkv[m].t[:, 0:CH], AF.Square, R=[pkv[m].b], W=[self.sqb.b])
        pss = self.ps_get()
        for m in range(2):
            self.mm(pss.t[:, 0:CH], self.onesB.t[:], self.sqb.t[:, m, :], start=(m == 0), stop=(m == 1),
                    R=[self.onesB.b, self.sqb.b], W=[pss.b])
        st0 = self.st[0]
        self.A(st0.t[:], pss.t[:, 0:CH], AF.Ln, scale=1.0 / KVL, bias=self.misc.t[:, 0:1], R=[pss.b, self.misc.b], W=[st0.b])
        self.A(st0.t[:], st0.t[:], AF.Exp, scale=-0.5, R=[st0.b], W=[st0.b])
        for m in range(2):
            nrm = self.vec.t[:, V_KVN + m:V_KVN + m + 1]
            if r == 0:
                self.V_stt(self.ckvB.t[:, m, :], pkv[m].t[:, 0:CH], nrm, st0.t[:], ALU.mult, ALU.mult,
                           R=[pkv[m].b, self.vec.b, st0.b], W=[self.ckvB.b])
            else:
                self.V_stt(self.ckvF.t[:, m, :], pkv[m].t[:, 0:CH], nrm, st0.t[:], ALU.mult, ALU.mult,
                           R=[pkv[m].b, self.vec.b, st0.b], W=[self.ckvF.b])
        if r != 0:
            self.V_cp(self.ckvB.t[:], self.ckvF.t[:], R=[self.ckvF.b], W=[self.ckvB.b])
        if r == 0:
            s1, s2 = self.st[1], self.st[2]
            self.V_tt(s1.t[0:DR, :], pkr.t[0:DR, 0:CH], self.rt.t[0:DR, 0, :], ALU.mult, R=[pkr.b, self.rt.b], W=[s1.b])
            self.V_tt(s2.t[0:DR, :], pkr.t[0:DR, CH:2 * CH], self.rt.t[0:DR, 1, :], ALU.mult, R=[pkr.b, self.rt.b], W=[s2.b])
            self.V_tt(self.krT.t[0:DR, koff:koff + CH], s1.t[0:DR, :], s2.t[0:DR, :], ALU.add, R=[s1.b, s2.b], W=[self.krTb[kc]])
        else:
            self.A(self.krF.t[0:DR, :], pkr.t[0:DR, 0:CH], AF.Copy, R=[pkr.b], W=[self.krF.b])
            self.V_cp(self.krT.t[0:DR, koff:koff + CH], self.krF.t[0:DR, :], R=[self.krF.b], W=[self.krTb[kc]])
        self.kv_up(kc)
        if r != 0:
            for j in range(2):
                ps = self.ps_get()
                for m in range(2):
                    self.tr(ps.t[:, m * 128:(m + 1) * 128], self.ckvF.t[:, m, j * 128:(j + 1) * 128], self.identF.t[:],
                            R=[self.ckvF.b, self.identF.b], W=[ps.b], sig=(m == 1))
                self.A(self.outT.t[:, j, :], ps.t[:, 0:KVL], AF.Copy, R=[ps.b], W=[self.outT.b])
            self.dma(self.qsp, self.nckv[r - 1, l].rearrange("(j p) f -> p j f", p=128), self.outT.t[:], R=[self.outT.b], W=[self.out_b])
            ps = self.ps_get()
            for j in range(2):
                self.tr(ps.t[:, j * DR:(j + 1) * DR], self.krF.t[0:DR, j * 128:(j + 1) * 128], self.identF.t[0:DR, 0:DR],
                        R=[self.krF.b, self.identF.b], W=[ps.b], sig=(j == 1))
            self.V_cp(self.krout.t[:], ps.t[:, 0:2 * DR].rearrange("p (j f) -> p j f", j=2), R=[ps.b], W=[self.krout.b])
            self.dma(self.qsp, self.nkr[r - 1, l].rearrange("(j p) f -> p j f", p=128), self.krout.t[:], R=[self.krout.b], W=[self.out_b])

    def ln_core(self, ring=None, gen=False, X=None):
        g = self._ln_core(ring, X)
        if gen:
            return g
        for _ in g:
            pass

    def _ln_core(self, ring, X=None):
        xt = X["xt"] if X else self.xt
        xtb = X["xtb"] if X else self.xtb
        stl = X["st"] if X else self.st
        ring = ring or self.psring
        psm, psq = ring.get(), ring.get()
        cs = slice(HAL, HAL + CH)
        for j in range(8):
            zb, zs = self.zbk.get(), self.zsqk.get()
            self.A(zb.t[:], xt.t[:, j, cs], AF.Copy, R=[xtb[j]], W=[zb.b])
            if X is not None and X.get("act_sq"):
                self.A(zs.t[:], xt.t[:, j, cs], AF.Square, R=[xtb[j]], W=[zs.b])
            else:
                self.V_tt(zs.t[:], xt.t[:, j, cs], xt.t[:, j, cs], ALU.mult, R=[xtb[j]], W=[zs.b])
            yield
            self.mm(psm.t[:, 0:CH], self.onesB.t[:], zb.t[:], start=(j == 0), stop=(j == 7), R=[self.onesB.b, zb.b], W=[psm.b], sig=True)
            self.mm(psq.t[:, 0:CH], self.onesB.t[:], zs.t[:], start=(j == 0), stop=(j == 7), R=[self.onesB.b, zs.b], W=[psq.b], sig=True)
            yield
        s0, s1, s2 = stl[0], stl[1], stl[2]
        self.A(s0.t[:], psm.t[:, 0:CH], AF.Copy, scale=1.0 / D, R=[psm.b], W=[s0.b])
        self.V_tt(s1.t[:], s0.t[:], s0.t[:], ALU.mult, R=[s0.b], W=[s1.b])
        yield
        self.V_stt(s1.t[:], psq.t[:, 0:CH], 1.0 / D, s1.t[:], ALU.mult, ALU.subtract, R=[psq.b, s1.b], W=[s1.b])
        yield
        self.A(s1.t[:], s1.t[:], AF.Ln, bias=self.misc.t[:, 1:2], scale=1.0, R=[s1.b, self.misc.b], W=[s1.b])
        yield
        self.A(s1.t[:], s1.t[:], AF.Exp, scale=-0.5, R=[s1.b], W=[s1.b])
        self.V_stt(s2.t[:], s0.t[:], -1.0, s1.t[:], ALU.mult, ALU.mult, R=[s0.b, s1.b], W=[s2.b])
        yield
        for j in range(8):
            self.V_tt(xt.t[:, j, cs], xt.t[:, j, cs], s1.t[:], ALU.mult, R=[xtb[j], s1.b], W=[xtb[j]])
            self.V_tt(xt.t[:, j, cs], xt.t[:, j, cs], s2.t[:], ALU.add, R=[xtb[j], s2.b], W=[xtb[j]],
                      eng=(self.dve if (X is not None and X.get("act_sq")) else self.pool))
            yield

    def affine_xt(self, gcol0, bcol0, gen=False, X=None):
        g = self._affine_xt(gcol0, bcol0, X)
        if gen:
            return g
        for _ in g:
            pass

    def _affine_xt(self, gcol0, bcol0, X=None):
        xt = X["xt"] if X else self.xt
        xtb = X["xtb"] if X else self.xtb
        cs = slice(HAL, HAL + CH)
        for k in range(8):
            g = self.vec.t[:, gcol0 + k:gcol0 + k + 1]
            b = self.vec.t[:, bcol0 + k:bcol0 + k + 1]
            if k % 2 == 0 and not (X is not None and X.get("act_sq")):
                self.V_ts(xt.t[:, k, cs], xt.t[:, k, cs], g, b, ALU.mult, ALU.add, R=[xtb[k], self.vec.b], W=[xtb[k]])
            else:
                self.A(xt.t[:, k, cs], xt.t[:, k, cs], AF.Identity, scale=g, bias=b, R=[xtb[k], self.vec.b], W=[xtb[k]])
            yield

    def load_xnext(self, r, c):
        nch = REQ_T[r] // CH
        g0 = RB[r] + c * CH
        gc = g0 // CH
        first, last = (c == 0), (c == nch - 1)
        lo = 0 if first else -HAL
        hi = CH if last else CH + HAL
        xTv = self.xT_d.rearrange("(k p) t -> p k t", p=128)
        rb = [self.xT_b[gc]] + ([] if first else [self.xT_b[gc - 1]]) + ([] if last else [self.xT_b[gc + 1]])
        self.dma(self.qsp, self.xnext[:, :, HAL + lo:HAL + hi], xTv[:, :, g0 + lo:g0 + hi], R=rb, W=[q.b for q in self.pb])

    def pb_ctx(self, r, c):
        Tr = REQ_T[r]
        nch = Tr // CH
        t0 = c * CH
        g0 = RB[r] + t0
        first, last = (c == 0), (c == nch - 1)
        return dict(cond=1 if r == 0 else 0, t0=t0, g0=g0, gc=g0 // CH, first=first, last=last,
                    lo=0 if first else -HAL, hi=CH if last else CH + HAL)

    def pb_front(self, l, r, c):
        X = self.pb_ctx(r, c)
        cond, first, last, lo, hi = X["cond"], X["first"], X["last"], X["lo"], X["hi"]
        ub = self.ubuf
        cs = slice(HAL, HAL + CH)
        pbb = [q.b for q in self.pb]
        xn = self.xnext
        if first:
            self.load_xnext(r, c)
        if r == 0:
            self.load_rope(X["t0"])
        if first:
            self.memset(self.pool, ub.t[:, :, 0:HAL], 0.0, W=self.ubb)
        if last:
            self.memset(self.pool, ub.t[:, :, HAL + CH:CW], 0.0, W=self.ubb)
        c0, c1 = HAL + lo, HAL + hi
        for k in range(8):
            sc = self.s1p.t[:, k, cond:cond + 1]
            sh = self.modT.t[:, k, cond:cond + 1]
            if k % 2 == 0:
                self.V_ts(ub.t[:, k, c0:c1], xn[:, k, c0:c1], sc, sh, ALU.mult, ALU.add,
                          R=[pbb[k // 2], self.s1p.b, self.modT.b], W=[self.ubb[k]])
            else:
                self.A(ub.t[:, k, c0:c1], xn[:, k, c0:c1], AF.Identity, scale=sc, bias=sh,
                       R=[pbb[k // 2], self.s1p.b, self.modT.b], W=[self.ubb[k]])
        wb = self.w_in.b
        ring = self.psS

        def win(ps, col, ncols, c0, c1, prow=128):
            for k in range(8):
                self.mm(ps.t[0:prow, 0:c1 - c0], self.w_in.t[:, k, col:col + ncols], ub.t[:, k, c0:c1],
                        start=(k == 0), stop=(k == 7), R=[wb, self.ubb[k]], W=[ps.b])

        pq = self.psB.items
        for m in range(3):
            win(pq[m], C_Q + m * 128, 128, HAL, HAL + CH)
            self.A(self.sqb.t[:, m, :], pq[m].t[:, 0:CH], AF.Square, R=[pq[m].b], W=[self.sqb.b])
        P, Bq, Cq, Vv = self.pb
        for m in range(2):
            ps = ring.get()
            win(ps, C_P + m * 128, 128, 0, CW)
            self.A(P.t[:, m, :], ps.t[:, 0:CW], AF.Copy, R=[ps.b], W=[P.b])
        for m in range(2):
            psc = ring.get()
            win(psc, C_GC + m * 128, 128, 0, CW)
            self.A(Vv.t[:, m, :], psc.t[:, 0:CW], AF.Copy, R=[psc.b], W=[Vv.b])
            psh = ring.get()
            win(psh, C_H + m * 128, 128, 0, CW)
            self.V_tt(Vv.t[:, m, :], psh.t[:, 0:CW], Vv.t[:, m, :], ALU.mult, R=[psh.b, Vv.b], W=[Vv.b])
        for m in range(2):
            ps = ring.get()
            win(ps, C_GB + m * 128, 128, HAL, HAL + CH)
            self.A(self.gbS.t[:, m, :], ps.t[:, 0:CH], AF.Copy, R=[ps.b], W=[self.gbS.b])
        st0 = self.stf[0]
        pss = ring.get()
        for m in range(3):
            self.mm(pss.t[:, 0:CH], self.onesB.t[:], self.sqb.t[:, m, :], start=(m == 0), stop=(m == 2),
                    R=[self.onesB.b, self.sqb.b], W=[pss.b])
        self.A(st0.t[:], pss.t[:, 0:CH], AF.Ln, scale=1.0 / QL, bias=self.misc.t[:, 0:1], R=[pss.b, self.misc.b], W=[st0.b])
        self.A(st0.t[:], st0.t[:], AF.Exp, scale=-0.5, R=[st0.b], W=[st0.b])
        for m in range(3):
            self.V_stt(self.qn.t[:, m, :], pq[m].t[:, 0:CH], self.vec.t[:, V_QN + m:V_QN + m + 1], st0.t[:], ALU.mult, ALU.mult,
                       R=[pq[m].b, self.vec.b, st0.b], W=[self.qn.b])
        for h in range(NH):
            ps = ring.get()
            for m in range(3):
                self.mm(ps.t[:, 0:CH], self.w_uq.t[:, m, h * 192:h * 192 + 128], self.qn.t[:, m, :], start=(m == 0), stop=(m == 2),
                        R=[self.w_uq.b, self.qn.b], W=[ps.b])
            self.A(self.qnope.t[:, h, :], ps.t[:, 0:CH], AF.Copy, scale=ATTN_SCALE, R=[ps.b], W=[self.qnope.b])
            ps2 = ring.get()
            for m in range(3):
                self.mm(ps2.t[0:DR, 0:CH], self.w_uq.t[:, m, h * 192 + 128:h * 192 + 192], self.qn.t[:, m, :], start=(m == 0), stop=(m == 2),
                        R=[self.w_uq.b, self.qn.b], W=[ps2.b])
            if r == 0:
                for m in range(3):
                    self.mm(ps2.t[0:DR, CH:2 * CH], self.w_uq.t[:, m, 768 + h * DR:768 + (h + 1) * DR], self.qn.t[:, m, :],
                            start=(m == 0), stop=(m == 2), R=[self.w_uq.b, self.qn.b], W=[ps2.b])
                s1 = self.stf[1]
                self.V_tt(s1.t[0:DR, :], ps2.t[0:DR, 0:CH], self.rt.t[0:DR, 0, :], ALU.mult, R=[ps2.b, self.rt.b], W=[s1.b])
                self.V_tt(ps2.t[0:DR, CH:2 * CH], ps2.t[0:DR, CH:2 * CH], self.rt.t[0:DR, 1, :], ALU.mult, R=[ps2.b, self.rt.b], W=[ps2.b])
                self.V_tt(s1.t[0:DR, :], ps2.t[0:DR, CH:2 * CH], s1.t[0:DR, :], ALU.add, R=[s1.b, ps2.b], W=[s1.b])
                self.A(self.qrope.t[0:DR, h, :], s1.t[0:DR, :], AF.Copy, scale=ATTN_SCALE, R=[s1.b], W=[self.qrope.b])
            else:
                self.A(self.qrope.t[0:DR, h, :], ps2.t[0:DR, 0:CH], AF.Copy, scale=ATTN_SCALE, R=[ps2.b], W=[self.qrope.b])
        pl = self.pool
        add = ALU.add
        self.V_tt(Bq.t[:, :, 1:CW], P.t[:, :, 0:CW - 1], P.t[:, :, 1:CW], add, R=[P.b], W=[Bq.b], eng=pl)
        self.V_tt(Cq.t[64:128, 0, 2:CW - 1], Bq.t[64:128, 0, 1:CW - 2], Bq.t[64:128, 0, 3:CW], add, R=[Bq.b], W=[Cq.b], eng=pl)
        self.V_tt(Cq.t[:, 1, 2:CW - 1], Bq.t[:, 1, 1:CW - 2], Bq.t[:, 1, 3:CW], add, R=[Bq.b], W=[Cq.b], eng=pl)
        self.V_tt(Bq.t[:, 1, 4:CW - 3], Cq.t[:, 1, 2:CW - 5], Cq.t[:, 1, 6:CW - 1], add, R=[Cq.b], W=[Bq.b], eng=pl)
        self.V_tt(Cq.t[64:128, 1, 8:CW - 7], Bq.t[64:128, 1, 4:CW - 11], Bq.t[64:128, 1, 12:CW - 3], add, R=[Bq.b], W=[Cq.b], eng=pl)
        for m in range(2):
            for (p0, p1, Wb) in ((0, 64, Bq), (64, 128, Cq)):
                self.V_stt(self.pooledB.t[p0:p1, m, :], Wb.t[p0:p1, m, cs], self.misc.t[p0:p1, 2 + m:3 + m], P.t[p0:p1, m, cs],
                           ALU.mult, ALU.subtract, R=[Wb.b, P.b, self.misc.b], W=[self.pooledB.b])
                for (is_edge, ec0, wc0, oc0) in ((first, 0, HAL, 0), (last, 8, HAL + CH - 8, CH - 8)):
                    if is_edge:
                        self.V_tt(self.edt.t[p0:p1, m, :], Wb.t[p0:p1, m, wc0:wc0 + 8], self.edges.t[p0:p1, m, ec0:ec0 + 8], ALU.mult,
                                  R=[Wb.b, self.edges.b], W=[self.edt.b])
                        self.V_tt(self.pooledB.t[p0:p1, m, oc0:oc0 + 8], self.edt.t[p0:p1, m, :], P.t[p0:p1, m, wc0:wc0 + 8], ALU.subtract,
                                  R=[self.edt.b, P.b], W=[self.pooledB.b])
        cacc = Bq
        for m in range(2):
            w0 = self.vec.t[:, V_CW + m * 3 + 0:V_CW + m * 3 + 1]
            w1 = self.vec.t[:, V_CW + m * 3 + 1:V_CW + m * 3 + 2]
            w2 = self.vec.t[:, V_CW + m * 3 + 2:V_CW + m * 3 + 3]
            self.V_ts(cacc.t[:, m, cs], Vv.t[:, m, HAL - 1:HAL + CH - 1], w0, None, ALU.mult, R=[Vv.b, self.vec.b, self.pooledB.b], W=[cacc.b])
            self.V_stt(cacc.t[:, m, cs], Vv.t[:, m, cs], w1, cacc.t[:, m, cs], ALU.mult, ALU.add, R=[Vv.b, cacc.b, self.vec.b], W=[cacc.b])
            self.V_stt(cacc.t[:, m, cs], Vv.t[:, m, HAL + 1:HAL + CH + 1], w2, cacc.t[:, m, cs], ALU.mult, ALU.add, R=[Vv.b, cacc.b, self.vec.b], W=[cacc.b])
        for mo in range(2):
            ps = ring.get()
            for mi in range(2):
                self.mm(ps.t[:, 0:CH], self.poolw.t[:, mi, mo * 128:(mo + 1) * 128], self.pooledB.t[:, mi, :], start=(mi == 0), stop=(mi == 1),
                        R=[self.poolw.b, self.pooledB.b], W=[ps.b])
            self.A(ub.t[:, 4 + mo, cs], ps.t[:, 0:CH], AF.Copy, scale=self.vec.t[:, V_PS + mo:V_PS + mo + 1], R=[ps.b, self.vec.b], W=[self.ubb[4 + mo]])
            self.V_tt(ub.t[:, 6 + mo, cs], cacc.t[:, mo, cs], self.gbS.t[:, mo, :], ALU.mult, R=[cacc.b, self.gbS.b], W=[self.ubb[6 + mo]])
        if not last:
            self.load_xnext(r, c + 1)

    def pb_attention(self, l, r, c, side):
        ub = self.ubuf
        nkt = (PAST + T_S) // 128 if r == 0 else T_P // 128
        npair = nkt // 2
        ring = self.psS

        def emit_S(h, kp):
            ps = ring.get()
            for j in range(2):
                kt = kp * 2 + j
                self.mm(ps.t[:, j * CH:(j + 1) * CH], self.KT.t[:, h, kt * 128:(kt + 1) * 128], self.qnope.t[:, h, :],
                        start=True, stop=False, R=[self.KTb[kp], self.qnope.b], W=[ps.b], sig=False)
                self.mm(ps.t[:, j * CH:(j + 1) * CH], self.krT.t[:, kt * 128:(kt + 1) * 128], self.qrope.t[:, h, :],
                        start=False, stop=True, R=[self.krTb[kp], self.qrope.b], W=[ps.b], sig=(j == 1))
            return ps

        seq = [(h, kp) for h in range(NH) for kp in range(npair)]
        nside = max(1, (130 + len(seq) - 1) // len(seq))
        ps_next = emit_S(*seq[0])
        acc = self.psacc.items[0]
        for i, (h, kp) in enumerate(seq):
            ps = ps_next
            pt = self.PT.get()
            self.A(pt.t[:].rearrange("p a b -> p (a b)"), ps.t[:, :], AF.Exp, R=[ps.b], W=[pt.b])
            if i + 1 < len(seq):
                ps_next = emit_S(*seq[i + 1])
            for j in range(2):
                kt = kp * 2 + j
                for qt in range(2):
                    self.mm(acc[qt].t[:, 0:VA], pt.t[:, j, qt * 128:(qt + 1) * 128], self.Vt.t[:, kt, h * VA:(h + 1) * VA],
                            start=(kt == 0), stop=(kt == nkt - 1), R=[self.Vb[kp], pt.b], W=[acc[qt].b], sig=(j == 1 and qt == 1))
            if kp == npair - 1:
                sm = self.smx
                for qt in range(2):
                    self.V_rcp(sm.t[:, 4 + qt:5 + qt], acc[qt].t[:, DV:DV + 1], R=[acc[qt].b], W=[sm.b])
                    otm = self.otm.get()
                    self.V_ts(otm.t[:], acc[qt].t[:, 0:DV], sm.t[:, 4 + qt:5 + qt], None, ALU.mult, R=[acc[qt].b, sm.b], W=[otm.b])
                    pst = ring.get()
                    pstb = pst.t[:, :].bitcast(BF16)
                    self.tr(pstb[:, 0:128], otm.t[:], self.identB.t[:], R=[otm.b, self.identB.b], W=[pst.b])
                    self.A(ub.t[:, h, HAL + qt * 128:HAL + (qt + 1) * 128], pstb[:, 0:128], AF.Copy, R=[pst.b], W=[self.ubb[h]])
            if side is not None:
                for _ in range(nside):
                    if next(side, "done") == "done":
                        side = None
                        break
        if side is not None:
            for _ in side:
                pass

    def pb_wout(self, l, r, c):
        X = self.pb_ctx(r, c)
        cond, g0, gc = X["cond"], X["g0"], X["gc"]
        xTv = self.xT_d.rearrange("(k p) t -> p k t", p=128)
        xt, ub = self.xt, self.ubuf
        cs = slice(HAL, HAL + CH)
        self.dma(self.qsp, xt.t[:, :, cs], xTv[:, :, g0:g0 + CH], R=[self.xT_b[gc]], W=self.xtb)
        for j in range(8):
            ps = self.psB.get()
            for k in range(8):
                self.mm(ps.t[:, 0:CH], self.w_out.t[:, k, j * 128:(j + 1) * 128], ub.t[:, k, cs], start=(k == 0), stop=(k == 7),
                        R=[self.w_out.b, self.ubb[k]], W=[ps.b])
            self.V_stt(xt.t[:, j, cs], ps.t[:, 0:CH], self.g1a.t[:, j, cond:cond + 1], xt.t[:, j, cs], ALU.mult, ALU.add,
                       R=[ps.b, self.g1a.b, self.xtb[j]], W=[self.xtb[j]])

    def pb_late(self, l, r, c):
        X = self.pb_ctx(r, c)
        cond, g0, gc = X["cond"], X["g0"], X["gc"]
        x1Tv = self.x1T_d.rearrange("(k p) t -> p k t", p=128)
        xt, u2 = self.xt, self.u2buf
        cs = slice(HAL, HAL + CH)
        yield from self.ln_core(ring=self.psB, gen=True)
        for jt in range(2):
            gt = g0 // 128 + jt
            ps = self.psB.get()
            for k in range(8):
                self.mm(ps.t[:, 0:NE], xt.t[:, k, HAL + jt * 128:HAL + (jt + 1) * 128], self.wr2.t[:, cond, k, :], start=(k == 0), stop=False,
                        R=[self.xtb[k], self.wr2.b], W=[ps.b], sig=False)
            self.mm(ps.t[:, 0:NE], self.onesrow.t[0:1, :], self.rconst.t[0:1, cond, :], start=False, stop=True,
                    R=[self.onesrow.b, self.rconst.b], W=[ps.b], sig=True)
            yield
            sm = self.smx
            self.dve.op(lambda e: e.tensor_reduce(out=sm.t[:, 0:1], in_=ps.t[:, 0:NE], axis=AX.X, op=ALU.max), R=[ps.b], W=[sm.b])
            self.V_ts(sm.t[:, 1:2], sm.t[:, 0:1], -1.0, None, ALU.mult, R=[sm.b], W=[sm.b])
            yield
            self.A(self.ex.t[:], ps.t[:, 0:NE], AF.Exp, bias=sm.t[:, 1:2], scale=1.0, accum_out=sm.t[:, 2:3], R=[ps.b, sm.b], W=[self.ex.b, sm.b])
            yield
            self.V_rcp(sm.t[:, 3:4], sm.t[:, 2:3], R=[sm.b], W=[sm.b])
            self.V_ts(self.affS.t[:, gt, :], self.ex.t[:], sm.t[:, 3:4], None, ALU.mult, R=[self.ex.b, sm.b], W=[self.affSb[gt]])
            yield
        for k in range(8):
            g = self.G2.t[:, k, cond:cond + 1]
            b = self.B2.t[:, k, cond:cond + 1]
            if k % 2 == 1:
                self.V_ts(u2.t[:, k, :], xt.t[:, k, cs], g, b, ALU.mult, ALU.add, R=[self.xtb[k], self.G2.b, self.B2.b], W=[self.u2b[k]])
            else:
                self.A(u2.t[:, k, :], xt.t[:, k, cs], AF.Identity, scale=g, bias=b, R=[self.xtb[k], self.G2.b, self.B2.b], W=[self.u2b[k]])
            yield
        yield from self.affine_xt(V_L1G, V_L1B, gen=True)
        self.dma(self.qsp, x1Tv[:, :, g0:g0 + CH], xt.t[:, :, cs], R=self.xtb, W=[self.x1T_b[gc]])
        for jt in range(2):
            gt = g0 // 128 + jt
            ps = self.psB.get()
            psb = ps.t[:, :].bitcast(BF16)
            for k in range(8):
                self.tr(psb[:, k * 128:(k + 1) * 128], u2.t[:, k, jt * 128:(jt + 1) * 128], self.identB.t[:],
                        R=[self.u2b[k], self.identB.b], W=[ps.b], sig=(k == 7))
                if k % 4 == 3:
                    yield
            rw = self.rowbuf.get()
            self.A(rw.t[:, 0:D], psb[:, :], AF.Copy, R=[ps.b], W=[rw.b])
            self.V_cp(rw.t[:, D:D + 2 * NE].bitcast(F32), self.affS.t[:, gt, :], R=[self.affSb[gt], rw.b], W=[rw.b])
            self.V_cp(rw.t[:, D + 2 * NE:D + 2 * NE + 2].bitcast(I32), self.tidcol.t[:, gt:gt + 1], R=[self.tidcol.b, rw.b], W=[rw.b])
            self.dma(self.qsp, self.u2rows_d[gt * 128:(gt + 1) * 128, :], rw.t[:], R=[rw.b], W=[self.u2rows_bt[gt]])
            yield

    def phase_b(self, l, r):
        nch = REQ_T[r] // CH
        side = None
        for c in range(nch):
            self.pb_front(l, r, c)
            self.pb_attention(l, r, c, side)
            self.pb_wout(l, r, c)
            side = self.pb_late(l, r, c)
        for _ in side:
            pass

    def routing(self, r):
        self.carve_reset()
        a = self.av
        sample = (r == 0)
        Pn = 128 if sample else NE
        Fn = 512 if sample else T_P
        nblk = Fn // 128
        Kcap = float(REQ_CAP[r])
        A_sb = a("A_sb", [128, 512], F32)
        junk = a("junk", [128, 512], F32)
        msk = a("msk", [128, 512], F32)
        csb = a("csb", [128, 512], F32)
        affX = [a(f"affX{i}", [128, NE, 8], F32) for i in range(2)]
        bis = a("bis", [128, 8], I32)
        cnt = a("cnt", [128, 4], F32)
        self.gsum = a("gsum", [128, 128], F32)
        self.lstrict = a("lstrict", [128, 128], F32)
        if sample:
            self.dma(self.qsp, self.gsum.t[:], self.gsum_d, W=[self.gsum.b])
            self.dma(self.qsp, self.lstrict.t[:], self.lstrict_d, W=[self.lstrict.b])
        g0t = RB[r] // 128
        for b in range(nblk):
            ps = self.ps_get()
            if sample:
                for seg in range(8):
                    gt = seg * 4 + b
                    ax = affX[seg % 2]
                    self.memset(self.pool, ax.t[:], 0.0, W=[ax.b])
                    self.V_cp(ax.t[:, :, seg], self.affS.t[:, gt, :], R=[self.affSb[gt], ax.b], W=[ax.b])
                    self.mm(ps.t[:, 0:128], ax.t[:].rearrange("p e s -> p (e s)"), self.identF.t[:], start=(seg == 0), stop=(seg == 7),
                            R=[ax.b, self.identF.b], W=[ps.b], sig=True)
            else:
                gt = g0t + b
                self.mm(ps.t[0:NE, 0:128], self.affS.t[:, gt, :], self.identF.t[:], start=True, stop=True,
                        R=[self.affSb[gt], self.identF.b], W=[ps.b])
            self.V_cp(A_sb.t[0:Pn, b * 128:(b + 1) * 128], ps.t[0:Pn, 0:128], R=[ps.b], W=[A_sb.b])
        lo, mid, ge = (bis.t[0:Pn, i:i + 1] for i in range(3))
        bb = [bis.b]
        self.memset(self.dve, bis.t[:, 0:1], 0, W=bb)
        self.memset(self.dve, cnt.t[:], 0.0, W=[cnt.b])
        for bit in range(29, -1, -1):
            self.V_ts(mid, lo, 1 << bit, None, ALU.bitwise_or, R=bb, W=bb)
            self.V_ts(junk.t[0:Pn, 0:Fn], A_sb.t[0:Pn, 0:Fn], mid.bitcast(F32), None, ALU.is_ge, ALU.add,
                      R=[A_sb.b] + bb, W=[junk.b, cnt.b], accum_out=cnt.t[0:Pn, 0:1])
            if sample:
                ps = self.ps_get()
                self.mm(ps.t[:, 0:2], self.gsum.t[:], cnt.t[:, 0:2], start=True, stop=True, R=[self.gsum.b, cnt.b], W=[ps.b])
                src, sb_ = ps.t[0:Pn, 0:1], [ps.b]
            else:
                src, sb_ = cnt.t[0:Pn, 0:1], [cnt.b]
            self.V_ts(ge, src, Kcap - 0.5, None, ALU.is_ge, R=sb_, W=bb)
            self.dve.op(lambda e: e.copy_predicated(out=lo, mask=ge, data=mid), R=bb, W=bb)
        self.V_ts(msk.t[0:Pn, 0:Fn], A_sb.t[0:Pn, 0:Fn], lo.bitcast(F32), None, ALU.is_ge, R=[A_sb.b] + bb, W=[msk.b])
        self.memset(self.dve, junk.t[:], 1.0, W=[junk.b])
        self.dve.op(lambda e: e.tensor_tensor_scan(out=csb.t[0:Pn, 0:Fn], data0=junk.t[0:Pn, 0:Fn], data1=msk.t[0:Pn, 0:Fn],
                                                   initial=0.0, op0=ALU.mult, op1=ALU.add), R=[junk.b, msk.b], W=[csb.b])
        if sample:
            self.V_cp(cnt.t[:, 0:1], csb.t[:, Fn - 1:Fn], R=[csb.b], W=[cnt.b])
            ps = self.ps_get()
            self.mm(ps.t[:, 0:2], self.lstrict.t[:], cnt.t[:, 0:2], start=True, stop=True, R=[self.lstrict.b, cnt.b], W=[ps.b])
            self.V_cp(cnt.t[:, 2:3], ps.t[:, 0:1], R=[ps.b], W=[cnt.b])
            self.V_ts(csb.t[:, 0:Fn], csb.t[:, 0:Fn], cnt.t[:, 2:3], None, ALU.add, R=[csb.b, cnt.b], W=[csb.b])
        self.V_stt(junk.t[0:Pn, 0:Fn], csb.t[0:Pn, 0:Fn], Kcap + 0.5, msk.t[0:Pn, 0:Fn], ALU.is_le, ALU.mult, R=[csb.b, msk.b], W=[junk.b])
        self.V_tt(csb.t[0:Pn, 0:Fn], csb.t[0:Pn, 0:Fn], junk.t[0:Pn, 0:Fn], ALU.mult, R=[csb.b, junk.b], W=[csb.b])
        vt = self.vTM[r]
        for b in range(nblk):
            ps = self.ps_get()
            self.tr(ps.t[:, 0:Pn], csb.t[0:Pn, b * 128:(b + 1) * 128], self.identF.t[0:Pn, 0:Pn], R=[csb.b, self.identF.b], W=[ps.b])
            self.V_cp(vt.t[:, b, 0:Pn], ps.t[:, 0:Pn], R=[ps.b], W=[vt.b])

    def moe_phase(self, l):
        self.carve_reset()
        a = self.av
        wg = [a(f"wg{i}", [128, 8, EF], BF16) for i in range(2)]
        wu = [a(f"wu{i}", [128, 8, EF], BF16) for i in range(2)]
        wd = [a(f"wd{i}", [128, 4, D], BF16) for i in range(2)]
        xs = [a(f"xs{i}", [128, ROWW], BF16) for i in range(10)]
        NS = 576
        xsT = a("xsT", [128, 8, NS], BF16)
        hdn = a("hdn", [128, 4, NS], BF16)
        sil = a("sil", [128, NS], F32)
        ye = Ring([a(f"ye{i}", [128, D], F32) for i in range(3)])
        Sr = Ring([a(f"S{i}", [128, 512], BF16) for i in range(3)])
        Sp = Ring([a(f"Sp{i}", [128, 32], BF16) for i in range(2)])
        idxrow = a("idxrow", [2, NS], F32)
        idxI = Ring([a(f"idxI{i}", [128, 8], I32) for i in range(2)])
        idxF = a("idxF", [128, 5, 2], F32)
        zt = a("zt", [128, D], F32)
        self.iota1 = a("iota1", [128, 512], F32)
        self.dma(self.qsp, self.iota1.t[:], self.iota1_d, W=[self.iota1.b])
        self.memset(self.pool, zt.t[:], 0.0, W=[zt.b])
        for gt in range(TTOT // 128):
            self.dma(self.qsp, self.ffn_d[gt * 128:(gt + 1) * 128, :], zt.t[:], R=[zt.b], W=[self.ffn_bt[gt]])
        if self.bcreg is None:
            self.bcreg = self.nc.gpsimd.to_reg(TTOT - 1)

        def load_w(e):
            i = e % 2
            self.dma(self.qpl, wg[i].t[:], self.w_gate_d[l, e].rearrange("(k p) f -> p k f", p=128), W=[wg[i].b])
            self.dma(self.qpl, wu[i].t[:], self.w_up_d[l, e].rearrange("(k p) f -> p k f", p=128), W=[wu[i].b])
            self.dma(self.qpl, wd[i].t[:], self.w_down_d[l, e].rearrange("(k p) d -> p k d", p=128), W=[wd[i].b])

        load_w(0)
        tiles = [(c, 128) for c in range(4)] + [(4, 64)]
        scat = {"prev": [], "cur": []}

        def stage1(e):
            xse = xs[(e % 2) * 5:(e % 2) * 5 + 5]
            psI = self.psacc.items[0][0]
            psI2 = self.psacc.items[0][1]
            vt = self.vTM[0]
            ntile = T_S // 128
            Sl = [None] * ntile

            def onehot(gt):
                seg, b = gt // 4, gt % 4
                S = Sr.get()
                col = vt.t[:, b, e * 8 + seg:e * 8 + seg + 1]
                self.V_ts(S.t[:], self.iota1.t[:], col, None, ALU.is_equal, R=[self.iota1.b, vt.b], W=[S.b])
                Sl[gt] = S

            for gt in range(3):
                onehot(gt)
            yield
            for gt in range(ntile):
                S = Sl[gt]
                self.mm(psI.t[0:2, :], self.tidhl.t[:, gt, :], S.t[:], start=(gt == 0), stop=(gt == ntile - 1),
                        R=[self.tidhl.b, S.b], W=[psI.b], sig=True)
                if gt + 3 < ntile:
                    onehot(gt + 3)
                yield
            self.A(idxrow.t[0:2, 0:512], psI.t[0:2, :], AF.Copy, R=[psI.b], W=[idxrow.b])
            psJ = psI2
            for r in (1, 2):
                vtp = self.vTM[r]
                for b in range(2):
                    gt = RB[r] // 128 + b
                    S = Sp.get()
                    self.V_ts(S.t[:], self.iota1.t[:, 0:32], vtp.t[:, b, e:e + 1], None, ALU.is_equal, R=[self.iota1.b, vtp.b], W=[S.b])
                    self.mm(psJ.t[0:2, (r - 1) * 32:r * 32], self.tidhl.t[:, gt, :], S.t[:], start=(b == 0), stop=(b == 1),
                            R=[self.tidhl.b, S.b], W=[psJ.b], sig=True)
            yield
            self.A(idxrow.t[0:2, 512:NS], psJ.t[0:2, 0:64], AF.Copy, R=[psJ.b], W=[idxrow.b])
            psT = psI
            for (c, nr) in tiles:
                self.tr(psT.t[0:nr, c * 2:c * 2 + 2], idxrow.t[0:2, c * 128:c * 128 + nr], self.identF.t[0:2, 0:2],
                        R=[idxrow.b, self.identF.b], W=[psT.b], sig=(c == 4))
            yield
            self.V_cp(idxF.t[:].rearrange("p a b -> p (a b)"), psT.t[:, 0:10], R=[psT.b], W=[idxF.b])
            ii = idxI.get()
            self.V_stt(ii.t[:, 0:5], idxF.t[:, :, 0], 64.0, idxF.t[:, :, 1], ALU.mult, ALU.add, R=[idxF.b], W=[ii.b])
            for (c, nr) in tiles:
                self.qpl.issue(lambda g, c=c, nr=nr: g.indirect_dma_start(
                    out=xse[c].t[0:nr, :], out_offset=None, in_=self.u2rows_d,
                    in_offset=bass.IndirectOffsetOnAxis(ap=ii.t[0:nr, c:c + 1], axis=0),
                    bounds_check=self.bcreg, oob_is_err=False), R=[ii.b] + self.u2rows_bt, W=[xse[c].b])
            yield

        def stage2(e):
            i = e % 2
            xse = xs[(e % 2) * 5:(e % 2) * 5 + 5]
            for (c, nr) in tiles:
                ps = self.ps_get()
                psb = ps.t[:, :].bitcast(BF16)
                for k in range(8):
                    self.tr(psb[:, k * 128:k * 128 + nr], xse[c].t[0:nr, k * 128:(k + 1) * 128], self.identB.t[0:nr, 0:nr],
                            R=[xse[c].b, self.identB.b], W=[ps.b], sig=(k == 7))
                yield
                src = psb[:, :].rearrange("p (k n) -> p k n", k=8)[:, :, 0:nr]
                if c % 2 == 0:
                    self.A(xsT.t[:, :, c * 128:c * 128 + nr], src, AF.Copy, R=[ps.b], W=[xsT.b])
                else:
                    self.V_cp(xsT.t[:, :, c * 128:c * 128 + nr], src, R=[ps.b], W=[xsT.b])
            for f in range(4):
                for (c0, c1) in ((0, 512), (512, NS)):
                    n = c1 - c0
                    psg, psu = self.ps_get(), self.ps_get()
                    for k in range(8):
                        self.mm(psg.t[:, 0:n], wg[i].t[:, k, f * 128:(f + 1) * 128], xsT.t[:, k, c0:c1], start=(k == 0), stop=(k == 7),
                                R=[wg[i].b, xsT.b], W=[psg.b])
                    for k in range(8):
                        self.mm(psu.t[:, 0:n], wu[i].t[:, k, f * 128:(f + 1) * 128], xsT.t[:, k, c0:c1], start=(k == 0), stop=(k == 7),
                                R=[wu[i].b, xsT.b], W=[psu.b])
                    yield
                    self.A(sil.t[:, c0:c1], psg.t[:, 0:n], AF.Silu, R=[psg.b], W=[sil.b])
                    self.V_tt(hdn.t[:, f, c0:c1], psu.t[:, 0:n], sil.t[:, c0:c1], ALU.mult, R=[psu.b, sil.b], W=[hdn.b])
            for (c, nr) in tiles:
                y = ye.get()
                gcol = xse[c].t[0:nr, D + 2 * e:D + 2 * e + 2].bitcast(F32)
                for half in range(2):
                    ps = self.ps_get()
                    for f in range(4):
                        self.mm(ps.t[0:nr, :], hdn.t[:, f, c * 128:c * 128 + nr], wd[i].t[:, f, half * 512:(half + 1) * 512],
                                start=(f == 0), stop=(f == 3), R=[hdn.b, wd[i].b], W=[ps.b])
                    yield
                    if half == 0:
                        self.A(y.t[0:nr, 0:512], ps.t[0:nr, :], AF.Copy, scale=gcol, R=[ps.b, xse[c].b], W=[y.b])
                    else:
                        self.V_ts(y.t[0:nr, 512:D], ps.t[0:nr, :], gcol, None, ALU.mult, R=[ps.b, xse[c].b], W=[y.b])
                tid = xse[c].t[0:nr, D + 2 * NE:D + 2 * NE + 2].bitcast(I32)
                for t_prev in scat["prev"]:
                    self.pool.wait(t_prev)
                last_e = (e == NE - 1)
                tk = self.qpl.issue(lambda g, y=y, nr=nr, tid=tid: g.indirect_dma_start(
                    out=self.ffn_d, out_offset=bass.IndirectOffsetOnAxis(ap=tid, axis=0), in_=y.t[0:nr, :], in_offset=None,
                    compute_op=ALU.add), R=[y.b, xse[c].b] + (self.ffn_bt if e == 0 else []), W=(self.ffn_bt if last_e else []))
                scat["cur"].append(tk)
            scat["prev"], scat["cur"] = scat["cur"], []

        for _ in stage1(0):
            pass
        for e in range(NE):
            side = None
            if e + 1 < NE:
                load_w(e + 1)
                side = stage1(e + 1)
            if e == 1 and l + 1 < self.depth:
                self.load_layer_weights(l + 1)
            for _ in stage2(e):
                if side is not None:
                    for _k in range(4):
                        if next(side, "done") == "done":
                            side = None
                            break
            if side is not None:
                for _ in side:
                    pass

    def ln2_A(self, l, r, c, X, ft):
        cond = 1 if r == 0 else 0
        g0 = RB[r] + c * CH
        gc = g0 // CH
        x1Tv = self.x1T_d.rearrange("(k p) t -> p k t", p=128)
        xt, xtb = X["xt"], X["xtb"]
        cs = slice(HAL, HAL + CH)
        self.dma(self.qsp, xt.t[:, :, cs], x1Tv[:, :, g0:g0 + CH], R=[self.x1T_b[gc]], W=xtb)
        for jt in range(2):
            gt = g0 // 128 + jt
            self.dma(self.qsp, ft[jt].t[:], self.ffn_d[gt * 128:(gt + 1) * 128, :], R=[self.ffn_bt[gt]], W=[ft[jt].b])
        yield
        for k in range(8):
            ps = self.psS.get()
            for jt in range(2):
                self.tr(ps.t[:, jt * 128:(jt + 1) * 128], ft[jt].t[:, k * 128:(k + 1) * 128], self.identF.t[:],
                        R=[ft[jt].b, self.identF.b], W=[ps.b], sig=(jt == 1))
            self.V_stt(xt.t[:, k, cs], ps.t[:, 0:CH], self.g2a.t[:, k, cond:cond + 1], xt.t[:, k, cs], ALU.mult, ALU.add,
                       R=[ps.b, self.g2a.b, xtb[k]], W=[xtb[k]])
            yield

    def ln2_B(self, l, r, c, X, ot):
        g0 = RB[r] + c * CH
        gc = g0 // CH
        xTv = self.xT_d.rearrange("(k p) t -> p k t", p=128)
        xt, xtb = X["xt"], X["xtb"]
        cs = slice(HAL, HAL + CH)
        yield from self.ln_core(ring=self.psB, gen=True, X=X)
        yield from self.affine_xt(V_L2G, V_L2B, gen=True, X=X)
        if l < self.depth - 1:
            self.dma(self.qsp, xTv[:, :, g0:g0 + CH], xt.t[:, :, cs], R=xtb, W=[self.xT_b[gc]])
        else:
            for jt in range(2):
                o = ot[jt]
                for hf in range(2):
                    ps = self.psB.get()
                    for kk in range(4):
                        k = hf * 4 + kk
                        self.tr(ps.t[:, kk * 128:(kk + 1) * 128], xt.t[:, k, HAL + jt * 128:HAL + (jt + 1) * 128], self.identF.t[:],
                                R=[xtb[k], self.identF.b], W=[ps.b], sig=(kk == 3))
                    if hf == 0:
                        self.A(o.t[:, 0:512], ps.t[:, :], AF.Copy, R=[ps.b], W=[o.b])
                    else:
                        self.V_cp(o.t[:, 512:D], ps.t[:, :], R=[ps.b], W=[o.b])
                    yield
                row = c * CH + jt * 128
                dst = self.y_s[row:row + 128, :] if r == 0 else self.y_p[(r - 1) * T_P + row:(r - 1) * T_P + row + 128, :]
                self.dma(self.qsp, dst, o.t[:], R=[o.b], W=[Buf("o")])

    def ln2_phase(self, l):
        self.carve_reset()
        a = self.av
        NSET = 3
        ft = [a(f"ft{i}", [128, D], F32) for i in range(2 * NSET)]
        ot = [a(f"ot{i}", [128, D], F32) for i in range(2)]
        sets = []
        for i in range(NSET):
            sets.append(dict(act_sq=True, xt=a(f"xl{i}", [128, 8, CW], F32), xtb=[Buf(f"xl{i}_{k}") for k in range(8)],
                             st=[a(f"stl{i}_{j}", [128, CH], F32) for j in range(3)]))
        chunks = [(r, c) for r in range(3) for c in range(REQ_T[r] // CH)]
        prevB = None
        for i, (r, c) in enumerate(chunks):
            X = sets[i % NSET]
            genA = self.ln2_A(l, r, c, X, ft[(i % NSET) * 2:(i % NSET) * 2 + 2])
            while genA is not None or prevB is not None:
                if genA is not None and next(genA, "done") == "done":
                    genA = None
                if prevB is not None:
                    for _ in range(3):
                        if next(prevB, "done") == "done":
                            prevB = None
                            break
                if genA is None and prevB is None:
                    break
            prevB = self.ln2_B(l, r, c, X, ot)
        for _ in prevB:
            pass

    def build(self):
        self.setup()
        self.load_consts()
        self.transpose_in()
        self.load_layer_weights(0)
        self.barrier()
        for l in range(self.depth):
            self.mark(f"L{l} mod")
            self.mod_phase(l)
            self.barrier()
            for r in range(3):
                self.carve_ab()
                self.mark(f"L{l} r{r} A")
                if r == 0:
                    self.ctx_phase(l)
                nch = REQ_T[r] // CH
                for c in range(nch):
                    self.phase_a_chunk(l, r, c)
                self.mark(f"L{l} r{r} B")
                self.phase_b(l, r)
                self.barrier()
                self.mark(f"L{l} r{r} route")
                self.routing(r)
                self.barrier()
            self.mark(f"L{l} moe")
            self.moe_phase(l)
            self.barrier()
            self.mark(f"L{l} ln2")
            self.ln2_phase(l)
            self.barrier()
        self.barrier()
        self.mark("end")
        return self.nc


def _rope_tables():
    rows_n = T_S // 64
    r, cl = np.meshgrid(np.arange(rows_n, dtype=np.float32), np.arange(64, dtype=np.float32), indexing="ij")
    inv = (np.float32(10000.0) ** (-np.arange(0, 32, 2, dtype=np.float32) / np.float32(32))).astype(np.float32)
    ang = np.concatenate([r.reshape(-1)[:, None] * inv, cl.reshape(-1)[:, None] * inv], axis=-1).astype(np.float32)
    cos, sin = np.cos(ang).astype(np.float32), np.sin(ang).astype(np.float32)
    cosT = np.concatenate([cos.T, cos.T], axis=0)
    sinT = np.concatenate([-sin.T, sin.T], axis=0)
    return np.ascontiguousarray(cosT), np.ascontiguousarray(sinT)


def _consts():
    c = {}
    c["identF"] = np.eye(128, dtype=np.float32)
    p = np.arange(128)
    c["gsum"] = (p[:, None] // 8 == p[None, :] // 8).astype(np.float32)
    c["lstrict"] = ((p[:, None] // 8 == p[None, :] // 8) & (p[:, None] % 8 < p[None, :] % 8)).astype(np.float32)
    c["iota1"] = np.broadcast_to(np.arange(1, 513, dtype=np.float32)[None, :], (128, 512)).copy()
    rows = np.arange(36)[None, :] * 128 + p[:, None]
    c["tidhl"] = np.stack([rows // 64, rows % 64], axis=-1).astype(np.float32)
    c["tidcol"] = rows.astype(np.float32)
    edges = np.zeros((128, 2, 16), np.float32)
    invw = np.zeros((128, 2), np.float32)
    for m in range(2):
        for half in range(2):
            w = (2, 4, 8, 16)[m * 2 + half]
            sl = slice(half * 64, half * 64 + 64)
            invw[sl, m] = 1.0 / w
            for i in range(8):
                t = i
                cnt = min(t + w // 2, 10 ** 9) - max(0, t - w // 2)
                edges[sl, m, i] = 1.0 / cnt
                j = 7 - i
                cnt = min(w // 2, j + 1) + min(w // 2, 10 ** 9)
                edges[sl, m, 8 + i] = 1.0 / cnt
    c["edges"] = edges
    misc = np.zeros((128, 16), np.float32)
    misc[:, 0] = RMS_EPS
    misc[:, 1] = LN_EPS / (ALPHA * ALPHA)
    misc[:, 2:4] = invw
    c["misc"] = misc
    c["cosT"], c["sinT"] = _rope_tables()
    return c


def _prep_shared(inp):
    f = lambda a: np.ascontiguousarray(np.asarray(a, dtype=np.float32))
    w_in = f(inp["w_in"])
    L = DEPTH
    krs = np.concatenate([np.arange(C_KR + 32, C_KR + 64), np.arange(C_KR, C_KR + 32)])
    w_in_x = np.concatenate([w_in, w_in[:, :, krs]], axis=2)
    sh = {}
    sh["w_in_x"] = np.ascontiguousarray(w_in_x.reshape(L, 8, 128, NCOL).transpose(0, 2, 1, 3))
    w_uq = f(inp["w_uq"])
    sw = np.concatenate([np.concatenate([np.arange(h * 192 + 160, h * 192 + 192), np.arange(h * 192 + 128, h * 192 + 160)]) for h in range(NH)])
    w_uq_x = np.concatenate([w_uq, w_uq[:, :, sw]], axis=2)
    sh["w_uq_x"] = np.ascontiguousarray(w_uq_x.reshape(L, 3, 128, 1024).transpose(0, 2, 1, 3))
    sh["w_uk_x"] = np.ascontiguousarray(f(inp["w_uk"]).reshape(L, 2, 128, 512).transpose(0, 2, 1, 3))
    sh["w_uv_x"] = np.ascontiguousarray(f(inp["w_uv"]).reshape(L, 2, 128, 512).transpose(0, 2, 1, 3))
    pw = f(inp["pool_w"])
    bd = np.zeros((L, 256, 256), np.float32)
    for g in range(4):
        bd[:, g * 64:(g + 1) * 64, g * 64:(g + 1) * 64] = pw[:, g]
    sh["poolw_x"] = np.ascontiguousarray(bd.reshape(L, 2, 128, 256).transpose(0, 2, 1, 3))
    sh["w_out_x"] = np.ascontiguousarray(f(inp["w_out"]).reshape(L, 8, 128, D).transpose(0, 2, 1, 3))
    sh["w_r_x"] = np.ascontiguousarray(f(inp["w_router"]).reshape(L, 8, 128, NE).transpose(0, 2, 1, 3))
    sh["w_ada"] = f(inp["w_ada"])
    vecs = np.zeros((L, 128, NV), np.float32)
    col = lambda v, n: v.reshape(L, n, 128).transpose(0, 2, 1)
    vecs[:, :, V_BADA:V_BADA + 48] = col(f(inp["b_ada"]), 48)
    vecs[:, :, V_L1G:V_L1G + 8] = col(f(inp["ln1_g"]), 8)
    vecs[:, :, V_L1B:V_L1B + 8] = col(f(inp["ln1_b"]), 8)
    vecs[:, :, V_L2G:V_L2G + 8] = col(f(inp["ln2_g"]), 8)
    vecs[:, :, V_L2B:V_L2B + 8] = col(f(inp["ln2_b"]), 8)
    vecs[:, :, V_QN:V_QN + 3] = col(f(inp["q_norm"]), 3)
    vecs[:, :, V_KVN:V_KVN + 2] = col(f(inp["kv_norm"]), 2)
    vecs[:, :, V_PS:V_PS + 2] = col(f(inp["pool_scale"]), 2)
    cw = f(inp["conv_w"])
    for m in range(2):
        for t in range(3):
            vecs[:, :, V_CW + m * 3 + t] = cw[:, t, m * 128:(m + 1) * 128]
    sh["vecs"] = vecs
    sh["w_gate"] = f(inp["w_gate"])
    sh["w_up"] = f(inp["w_up"])
    sh["w_down"] = f(inp["w_down"])
    sh.update(_consts())
    return sh


_NC_CACHE = {}


def kernel(**inp):
    sh = _prep_shared(inp)
    xp = np.asarray(inp["x_prompt"], dtype=np.float32)
    xsmp = np.asarray(inp["x_sample"], dtype=np.float32)
    cckv = np.asarray(inp["cache_ckv"], dtype=np.float32)
    ckr = np.asarray(inp["cache_krope"], dtype=np.float32)
    cvec = np.asarray(inp["c"], dtype=np.float32)
    cctx = np.asarray(inp["c_ctx"], dtype=np.float32)
    in_maps = []
    for core in range(8):
        s = core // 4
        m = dict(sh)
        m["xs_in"] = np.ascontiguousarray(xsmp[s])
        m["xp_in"] = np.ascontiguousarray(xp[2 * core:2 * core + 2].reshape(2 * T_P, D))
        m["cckv"] = np.ascontiguousarray(cckv[s])
        m["ckr"] = np.ascontiguousarray(ckr[s])
        cond = np.stack([cctx, cvec[s]], axis=-1)
        m["condT"] = np.ascontiguousarray(cond.reshape(8, 128, 2).transpose(1, 0, 2))
        in_maps.append(m)
    if "nc" not in _NC_CACHE:
        _NC_CACHE["nc"] = KB().build()
    nc = _NC_CACHE["nc"]
    res = run_bass_kernel_spmd(nc, in_maps, core_ids=list(range(8)))
    rs = res.results
    y_prompt = np.concatenate([np.asarray(rs[c]["y_p"]).reshape(2, T_P, D) for c in range(8)], axis=0)
    y_sample = np.stack([np.asarray(rs[0]["y_s"]), np.asarray(rs[4]["y_s"])], axis=0)
    new_ckv = np.concatenate([np.asarray(rs[c]["nckv"]) for c in range(8)], axis=0)
    new_krope = np.concatenate([np.asarray(rs[c]["nkr"]) for c in range(8)], axis=0)
    return (y_prompt.astype(np.float32), y_sample.astype(np.float32), new_ckv.astype(np.float32), new_krope.astype(np.float32))
```

```python
import numpy as np
import ml_dtypes
from contextlib import ExitStack
import concourse.bass as bass
import concourse.mybir as mybir
from concourse.bass_utils import run_bass_kernel_spmd

F32 = mybir.dt.float32
BF16 = mybir.dt.bfloat16
I32 = mybir.dt.int32
AF = mybir.ActivationFunctionType
ALU = mybir.AluOpType
AX = mybir.AxisListType

D = 1024
DEPTH = 4
T_S = 4096
T_P = 256
PAST = 512
NH = 4
DN = 128
DR = 64
DV = 128
QL = 384
KVL = 256
NE = 16
EF = 512
ALPHA = (2 * DEPTH) ** 0.25
RMS_EPS = 1e-6
LN_EPS = 1e-5
ATTN_SCALE = (DN + DR) ** -0.5
CH = 256
HAL = 8
CW = CH + 2 * HAL
VA = 130
NCOL = 1792
C_Q, C_KV, C_KR, C_P, C_GB, C_GC, C_H, C_KRS = 0, 384, 640, 704, 960, 1216, 1472, 1728
RB = [0, T_S, T_S + T_P]
TTOT = T_S + 2 * T_P
REQ_T = [T_S, T_P, T_P]
REQ_CAP = [2 * T_S // NE, 2 * T_P // NE, 2 * T_P // NE]
ROWW = 1060
NV = 93
V_BADA, V_L1G, V_L1B, V_L2G, V_L2B, V_QN, V_KVN, V_PS, V_CW = 0, 48, 56, 64, 72, 80, 83, 85, 87
BIGIDX = float(1 << 20)


class Tok:
    __slots__ = ("sem", "val")

    def __init__(self, sem=None, val=0):
        self.sem = sem
        self.val = val


class Buf:
    __slots__ = ("name", "w", "r")

    def __init__(self, name):
        self.name = name
        self.w = None
        self.r = []


class Eng:
    def __init__(self, K, eng, name):
        self.K = K
        self.e = eng
        self.name = name
        self.sem = None
        self.cnt = 0
        self.nsem = 0
        self.waited = {}
        self.pending = []
        self.last = None
        self.nins = 0
        self._newsem()

    def _newsem(self):
        self.sem = self.K.es.enter_context(self.K.nc.semaphore(f"p_{self.name}{self.nsem}"))
        self.nsem += 1
        self.cnt = 0

    def wait(self, tok):
        if tok is None:
            return
        assert tok.sem is not None, f"wait on unsignalled token ({self.name})"
        key = id(tok.sem)
        if self.waited.get(key, (None, 0))[1] >= tok.val:
            return
        self.e.wait_ge(tok.sem, tok.val)
        self.waited[key] = (tok.sem, tok.val)

    def begin(self, R, W):
        for b in R:
            if b.w is not None:
                self.wait(b.w[1])
        for b in W:
            if b.w is not None and b.w[0] != self.name:
                self.wait(b.w[1])
            for (en, tk) in b.r:
                if en != self.name:
                    self.wait(tk)

    def end(self, ins, R, W, sig=True):
        tok = Tok()
        self.pending.append(tok)
        for b in R:
            b.r.append((self.name, tok))
        for b in W:
            b.w = (self.name, tok)
            b.r = []
        if sig:
            if self.cnt >= 30000:
                self._newsem()
            self.cnt += 1
            ins.then_inc(self.sem, 1)
            for t in self.pending:
                t.sem = self.sem
                t.val = self.cnt
            self.pending = []
            self.last = tok
        return tok

    def op(self, fn, R=(), W=(), sig=True):
        self.begin(R, W)
        ins = fn(self.e)
        self.nins += 1
        return self.end(ins, R, W, sig)


class DmaQ:
    def __init__(self, K, E, nsem):
        self.K = K
        self.E = E
        self.sems = [K.es.enter_context(K.nc.semaphore(f"d_{E.name}{i}")) for i in range(nsem)]
        self.vals = [0] * nsem
        self.last = [None] * nsem
        self.i = 0

    def issue(self, fn, R=(), W=()):
        E = self.E
        E.begin(R, W)
        j = self.i
        self.i = (self.i + 1) % len(self.sems)
        if self.vals[j] >= 30000:
            E.wait(self.last[j])
            self.sems[j] = self.K.es.enter_context(self.K.nc.semaphore(f"d_{E.name}{j}_{self.K.uid()}"))
            self.vals[j] = 0
            self.last[j] = None
        E.wait(self.last[j])
        ins = fn(E.e)
        self.vals[j] += 16
        ins.then_inc(self.sems[j], 16)
        tok = Tok(self.sems[j], self.vals[j])
        self.last[j] = tok
        for b in R:
            b.r.append(("dma", tok))
        for b in W:
            b.w = ("dma", tok)
            b.r = []
        return tok


class Ring:
    def __init__(self, items):
        self.items = items
        self.i = 0

    def get(self):
        it = self.items[self.i]
        self.i = (self.i + 1) % len(self.items)
        return it


class T:
    __slots__ = ("t", "b")

    def __init__(self, t, b):
        self.t = t
        self.b = b


class KB:
    def __init__(self, depth=DEPTH, debug=None):
        self.depth = depth
        self.debug = debug
        self.es = ExitStack()
        self.nc = bass.Bass("TRN2", target_bir_lowering=False)
        self._uid = 0
        nc = self.nc
        self.pe = Eng(self, nc.tensor, "pe")
        self.act = Eng(self, nc.scalar, "act")
        self.dve = Eng(self, nc.vector, "dve")
        self.pool = Eng(self, nc.gpsimd, "pool")
        self.sp = Eng(self, nc.sync, "sp")
        self.engs = [self.pe, self.act, self.dve, self.pool, self.sp]
        self.qsp = DmaQ(self, self.sp, 12)
        self.qpl = DmaQ(self, self.pool, 12)
        self.dbg_outs = []
        self.marks = []

    def mark(self, label):
        self.marks.append((label, self.pe.nins))
        if self.debug == "marks" and hasattr(self, "identB"):
            ps = self.ps_get()
            self.mm(ps.t[0:3, 0:6], self.identB.t[:, 0:3], self.identB.t[:, 0:6], start=True, stop=True, R=[self.identB.b], W=[ps.b])

    def uid(self):
        self._uid += 1
        return self._uid

    def dram_in(self, name, shape, dt=F32):
        return self.nc.dram_tensor(name, list(shape), dt, kind="ExternalInput").ap()

    def dram_out(self, name, shape, dt=F32):
        return self.nc.dram_tensor(name, list(shape), dt, kind="ExternalOutput").ap()

    def dram_scr(self, name, shape, dt=F32):
        return self.nc.dram_tensor(name, list(shape), dt, kind="Internal").ap()

    def sbt(self, name, shape, dt):
        h = self.es.enter_context(self.nc.sbuf_tensor("s_" + name, list(shape), dt))
        return T(h, Buf(name))

    def carve_reset(self):
        self.aoff = 0

    def av(self, name, shape, dt):
        n = 1
        for s in shape[1:]:
            n *= s
        words = (n * (2 if dt == BF16 else 4) + 3) // 4
        words = (words + 7) // 8 * 8
        assert self.aoff + words <= self.arena_words, f"arena overflow at {name}: {self.aoff + words}"
        v = self.arena[:, self.aoff:self.aoff + words]
        self.aoff += words
        if dt != F32:
            v = v.bitcast(dt)
        v = v[:, 0:n]
        if len(shape) == 3:
            v = v.rearrange("p (a b) -> p a b", a=shape[1])
        elif len(shape) == 4:
            v = v.rearrange("p (a b c) -> p a b c", a=shape[1], b=shape[2])
        return T(v, Buf(name))

    def barrier(self):
        toks = [e.last for e in self.engs if e.last is not None]
        for q in (self.qsp, self.qpl):
            toks += [t for t in q.last if t is not None]
        for e in self.engs:
            for t in toks:
                e.wait(t)

    def dma(self, q, out, in_, R=(), W=(), **kw):
        return q.issue(lambda e: e.dma_start(out=out, in_=in_, **kw), R=R, W=W)

    def ps_get(self):
        return self.psring.get()

    def mm(self, out, lhsT, rhs, start, stop, R, W, sig=None):
        if sig is None:
            sig = stop
        return self.pe.op(lambda e: e.matmul(out, lhsT, rhs, start=start, stop=stop), R=R, W=W, sig=sig)

    def tr(self, out, in_, ident, R, W, sig=True):
        return self.pe.op(lambda e: e.transpose(out, in_, ident), R=R, W=W, sig=sig)

    def A(self, out, in_, func, R, W, **kw):
        return self.act.op(lambda e: e.activation(out=out, in_=in_, func=func, **kw), R=R, W=W)

    def V_tt(self, out, in0, in1, op, R, W, eng=None):
        eng = eng or self.dve
        return eng.op(lambda e: e.tensor_tensor(out=out, in0=in0, in1=in1, op=op), R=R, W=W)

    def V_ts(self, out, in0, s1, s2, op0, op1=None, R=(), W=(), eng=None, accum_out=None):
        eng = eng or self.dve
        kw = {}
        if op1 is not None:
            kw["op1"] = op1
        if accum_out is not None:
            kw["accum_out"] = accum_out
        return eng.op(lambda e: e.tensor_scalar(out=out, in0=in0, scalar1=s1, scalar2=s2, op0=op0, **kw), R=R, W=W)

    def V_stt(self, out, in0, scalar, in1, op0, op1, R, W):
        return self.dve.op(lambda e: e.scalar_tensor_tensor(out=out, in0=in0, scalar=scalar, in1=in1, op0=op0, op1=op1), R=R, W=W)

    def V_cp(self, out, in_, R, W, eng=None):
        eng = eng or self.dve
        return eng.op(lambda e: e.tensor_copy(out=out, in_=in_), R=R, W=W)

    def V_rcp(self, out, in_, R, W):
        return self.dve.op(lambda e: e.reciprocal(out=out, in_=in_), R=R, W=W)

    def memset(self, eng, ap, val, W):
        return eng.op(lambda e: e.memset(ap, val), R=(), W=W)

    def setup(self):
        nc = self.nc
        L = DEPTH
        di = self.dram_in
        self.xs_in = di("xs_in", [T_S, D])
        self.xp_in = di("xp_in", [2 * T_P, D])
        self.cckv = di("cckv", [L, PAST, KVL])
        self.ckr = di("ckr", [L, PAST, DR])
        self.condT = di("condT", [128, 8, 2])
        self.w_in_d = di("w_in_x", [L, 128, 8, NCOL])
        self.w_uq_d = di("w_uq_x", [L, 128, 3, 1024])
        self.w_uk_d = di("w_uk_x", [L, 128, 2, 512])
        self.w_uv_d = di("w_uv_x", [L, 128, 2, 512])
        self.poolw_d = di("poolw_x", [L, 128, 2, 256])
        self.w_out_d = di("w_out_x", [L, 128, 8, D])
        self.w_r_d = di("w_r_x", [L, 128, 8, NE])
        self.w_ada_d = di("w_ada", [L, D, 6 * D])
        self.vecs_d = di("vecs", [L, 128, NV])
        self.w_gate_d = di("w_gate", [L, NE, D, EF])
        self.w_up_d = di("w_up", [L, NE, D, EF])
        self.w_down_d = di("w_down", [L, NE, EF, D])
        self.identF_d = di("identF", [128, 128])
        self.gsum_d = di("gsum", [128, 128])
        self.lstrict_d = di("lstrict", [128, 128])
        self.iota1_d = di("iota1", [128, 512])
        self.tidhl_d = di("tidhl", [128, 36, 2])
        self.edges_d = di("edges", [128, 2, 16])
        self.misc_d = di("misc", [128, 16])
        self.cosT_d = di("cosT", [DR, T_S])
        self.sinT_d = di("sinT", [DR, T_S])
        self.tidcol_d = di("tidcol", [128, 36])
        self.y_s = self.dram_out("y_s", [T_S, D])
        self.y_p = self.dram_out("y_p", [2 * T_P, D])
        self.nckv = self.dram_out("nckv", [2, L, T_P, KVL])
        self.nkr = self.dram_out("nkr", [2, L, T_P, DR])
        self.xT_d = self.dram_scr("xT_d", [D, TTOT])
        self.x1T_d = self.dram_scr("x1T_d", [D, TTOT])
        self.u2rows_d = self.dram_scr("u2rows_d", [TTOT, ROWW], BF16)
        self.ffn_d = self.dram_scr("ffn_d", [TTOT, D])
        self.xT_b = [Buf(f"xT_d{c}") for c in range(TTOT // CH)]
        self.x1T_b = [Buf(f"x1T_d{c}") for c in range(TTOT // CH)]
        self.u2rows_b = Buf("u2rows_d")
        self.ffn_b = Buf("ffn_d")
        self.out_b = Buf("outs")

        s = self.sbt
        self.w_in = s("w_in", [128, 8, NCOL], BF16)
        self.w_uq = s("w_uq", [128, 3, 1024], BF16)
        self.w_uk = s("w_uk", [128, 2, 512], BF16)
        self.w_uv = s("w_uv", [128, 2, 512], BF16)
        self.poolw = s("poolw", [128, 2, 256], BF16)
        self.w_out = s("w_out", [128, 8, D], BF16)
        self.w_r = s("w_r", [128, 8, NE], F32)
        self.vec = s("vec", [128, NV], F32)
        self.modT = s("modT", [128, 48, 2], F32)
        self.s1p = s("s1p", [128, 8, 2], F32)
        self.g1a = s("g1a", [128, 8, 2], F32)
        self.G2 = s("G2", [128, 8, 2], F32)
        self.B2 = s("B2", [128, 8, 2], F32)
        self.g2a = s("g2a", [128, 8, 2], F32)
        self.wr2 = s("wr2", [128, 2, 8, NE], F32)
        self.rconst = s("rconst", [1, 2, NE], F32)
        self.scT = s("scT", [128, 8, 2], F32)
        self.identF = s("identF", [128, 128], F32)
        self.identB = s("identB", [128, 128], BF16)
        self.onesB = s("onesB", [128, 128], BF16)
        self.onesrow = s("onesrow", [1, 128], F32)
        self.tidhl = s("tidhl", [128, 36, 2], BF16)
        self.tidcol = s("tidcol", [128, 36], F32)
        self.edges = s("edges", [128, 2, 16], F32)
        self.misc = s("misc", [128, 16], F32)
        self.affS = s("affS", [128, 36, NE], F32)
        self.xt = s("xt", [128, 8, CW], F32)
        self.ubuf = s("ubuf", [128, 8, CW], BF16)
        self.st = [s(f"st{i}", [128, CH], F32) for i in range(3)]
        self.stf = [s(f"stf{i}", [128, CH], F32) for i in range(2)]
        self.u2b = [Buf(f"u2b{k}") for k in range(8)]
        self.zbk = Ring([s(f"zbk{i}", [128, CH], BF16) for i in range(2)])
        self.zsqk = Ring([s(f"zsqk{i}", [128, CH], BF16) for i in range(2)])
        self.xtb = [Buf(f"xt{k}") for k in range(8)]
        self.ubb = [Buf(f"ub{k}") for k in range(8)]
        self.affSb = [Buf(f"affS{g}") for g in range(TTOT // 128)]
        self.u2rows_bt = [Buf(f"u2r{g}") for g in range(TTOT // 128)]
        self.ffn_bt = [Buf(f"ffn{g}") for g in range(TTOT // 128)]
        self.vTM = [s("vTM0", [128, 4, 128], F32), s("vTM1", [128, 2, NE], F32), s("vTM2", [128, 2, NE], F32)]
        self.arena_words = 118 * 256
        self.arena = self.es.enter_context(nc.sbuf_tensor("arena", [128, self.arena_words], F32))
        banks = [T(self.es.enter_context(nc.psum_tensor(f"ps{i}", [128, 512], F32)), Buf(f"ps{i}")) for i in range(8)]
        self.psring = Ring(banks[:6])
        self.psS = Ring(banks[:3])
        self.psB = Ring(banks[3:6])
        self.psacc = Ring([(banks[6], banks[7])])
        self.bcreg = None

    def load_consts(self):
        q = self.qsp
        for (dst, src) in [(self.identF, self.identF_d), (self.edges, self.edges_d),
                           (self.misc, self.misc_d), (self.scT, self.condT), (self.tidcol, self.tidcol_d)]:
            self.dma(q, dst.t[:], src, W=[dst.b])
        self.V_cp(self.identB.t[:], self.identF.t[:], R=[self.identF.b], W=[self.identB.b])
        self.carve_reset()
        tf = self.av("tidhlF", [128, 36, 2], F32)
        self.dma(q, tf.t[:], self.tidhl_d, W=[tf.b])
        self.V_cp(self.tidhl.t[:], tf.t[:], R=[tf.b], W=[self.tidhl.b])
        self.barrier()
        self.memset(self.dve, self.onesB.t[:], 1.0, W=[self.onesB.b])
        self.memset(self.dve, self.onesrow.t[:], 1.0, W=[self.onesrow.b])
        self.A(self.scT.t[:], self.scT.t[:], AF.Silu, R=[self.scT.b], W=[self.scT.b])

    def transpose_in(self):
        self.carve_reset()
        tin = [self.av(f"tin{i}", [128, D], F32) for i in range(2)]
        tout = [self.av(f"tout{i}", [128, 8, 128], F32) for i in range(2)]
        xTv = self.xT_d.rearrange("(k p) t -> p k t", p=128)
        for g in range(TTOT // 128):
            src = self.xs_in[g * 128:(g + 1) * 128, :] if g < T_S // 128 else self.xp_in[(g - T_S // 128) * 128:(g - T_S // 128 + 1) * 128, :]
            ti = tin[g % 2]
            to = tout[g % 2]
            self.dma(self.qsp, ti.t[:], src, W=[ti.b])
            for hf in range(2):
                ps = self.ps_get()
                for kk in range(4):
                    k = hf * 4 + kk
                    self.tr(ps.t[:, kk * 128:(kk + 1) * 128], ti.t[:, k * 128:(k + 1) * 128], self.identF.t[:],
                            R=[ti.b, self.identF.b], W=[ps.b], sig=(kk == 3))
                if hf == 0:
                    self.A(to.t[:, 0:4, :], ps.t[:, :].rearrange("p (a b) -> p a b", a=4), AF.Copy, R=[ps.b], W=[to.b])
                else:
                    self.V_cp(to.t[:, 4:8, :], ps.t[:, :].rearrange("p (a b) -> p a b", a=4), R=[ps.b], W=[to.b])
            cb = self.xT_b[g // 2]
            self.dma(self.qsp, xTv[:, :, g * 128:(g + 1) * 128], to.t[:], R=[to.b], W=[cb])

    def load_layer_weights(self, l):
        q = self.qpl
        for k in range(8):
            self.dma(q, self.w_in.t[:, k, :], self.w_in_d[l][:, k, :], W=[self.w_in.b], max_dma_last_dim=4096)
        for k in range(3):
            self.dma(q, self.w_uq.t[:, k, :], self.w_uq_d[l][:, k, :], W=[self.w_uq.b], max_dma_last_dim=4096)
        self.dma(q, self.w_uk.t[:], self.w_uk_d[l], W=[self.w_uk.b], max_dma_last_dim=2048)
        self.dma(q, self.w_uv.t[:], self.w_uv_d[l], W=[self.w_uv.b], max_dma_last_dim=2048)
        self.dma(q, self.poolw.t[:], self.poolw_d[l], W=[self.poolw.b], max_dma_last_dim=1024)
        for k in range(8):
            self.dma(q, self.w_out.t[:, k, :], self.w_out_d[l][:, k, :], W=[self.w_out.b], max_dma_last_dim=4096)
        self.dma(self.qsp, self.w_r.t[:], self.w_r_d[l], W=[self.w_r.b])

    def mod_phase(self, l):
        self.carve_reset()
        wst = [self.av(f"wst{i}", [128, 8, 512], F32) for i in range(2)]
        tmp = self.av("modtmp", [128, 8, 2], F32)
        vec, modT = self.vec, self.modT
        self.dma(self.qsp, vec.t[:], self.vecs_d[l], W=[vec.b])
        for blk in range(12):
            w = wst[blk % 2]
            self.dma(self.qsp, w.t[:], self.w_ada_d[l][:, blk * 512:(blk + 1) * 512].rearrange("(k p) n -> p k n", p=128), W=[w.b])
            ps = self.ps_get()
            for j in range(4):
                for k in range(8):
                    self.mm(ps.t[:, j * 2:(j + 1) * 2], w.t[:, k, j * 128:(j + 1) * 128], self.scT.t[:, k, :],
                            start=(k == 0), stop=(k == 7), R=[w.b, self.scT.b], W=[ps.b])
            for j in range(4):
                jj = blk * 4 + j
                self.V_ts(modT.t[:, jj, :], ps.t[:, j * 2:(j + 1) * 2], vec.t[:, V_BADA + jj:V_BADA + jj + 1], None, ALU.add,
                          R=[ps.b, vec.b], W=[modT.b])
        sh1, sc1, gt1 = modT.t[:, 0:8, :], modT.t[:, 8:16, :], modT.t[:, 16:24, :]
        sh2, sc2, gt2 = modT.t[:, 24:32, :], modT.t[:, 32:40, :], modT.t[:, 40:48, :]
        mb = [modT.b]
        self.V_ts(self.s1p.t[:], sc1, 1.0, None, ALU.add, R=mb, W=[self.s1p.b])
        self.V_ts(self.g1a.t[:], gt1, 1.0 / ALPHA, None, ALU.mult, R=mb, W=[self.g1a.b])
        self.V_ts(self.g2a.t[:], gt2, 1.0 / ALPHA, None, ALU.mult, R=mb, W=[self.g2a.b])
        self.V_ts(tmp.t[:], sc2, 1.0, None, ALU.add, R=mb, W=[tmp.b])
        for c in range(2):
            self.V_tt(self.G2.t[:, :, c], tmp.t[:, :, c], vec.t[:, V_L1G:V_L1G + 8], ALU.mult, R=[tmp.b, vec.b], W=[self.G2.b])
            self.V_tt(self.B2.t[:, :, c], tmp.t[:, :, c], vec.t[:, V_L1B:V_L1B + 8], ALU.mult, R=[tmp.b, vec.b], W=[self.B2.b])
        self.V_tt(self.B2.t[:], self.B2.t[:], sh2, ALU.add, R=[self.B2.b] + mb, W=[self.B2.b])
        for c in range(2):
            for k in range(8):
                self.V_ts(self.wr2.t[:, c, k, :], self.w_r.t[:, k, :], self.G2.t[:, k, c:c + 1], None, ALU.mult,
                          R=[self.w_r.b, self.G2.b], W=[self.wr2.b])
            ps = self.ps_get()
            for k in range(8):
                self.mm(ps.t[0:1, 0:NE], self.B2.t[:, k, c:c + 1], self.w_r.t[:, k, :], start=(k == 0), stop=(k == 7),
                        R=[self.B2.b, self.w_r.b], W=[ps.b])
            self.V_cp(self.rconst.t[0:1, c, :], ps.t[0:1, 0:NE], R=[ps.b], W=[self.rconst.b])

    def carve_ab(self):
        self.carve_reset()
        a = self.av
        self.KT = a("KT", [128, NH, PAST + T_S], BF16)
        self.krT = a("krT", [128, PAST + T_S], BF16)
        self.Vt = a("Vt", [128, (PAST + T_S) // 128, NH * VA], BF16)
        nkc = (PAST + T_S) // CH
        self.KTb = [Buf(f"KT{i}") for i in range(nkc)]
        self.krTb = [Buf(f"krT{i}") for i in range(nkc)]
        self.Vb = [Buf(f"V{i}") for i in range(nkc)]
        self.qn = a("qn", [128, 3, CH], BF16)
        self.qnope = a("qnope", [128, NH, CH], BF16)
        self.qrope = a("qrope", [128, NH, CH], BF16)
        self.sqb = a("sqb", [128, 3, CH], BF16)
        self.PT = Ring([a(f"PT{i}", [128, 2, CH], BF16) for i in range(2)])
        pb_off = self.aoff
        self.pb = [a(f"pb{i}", [128, 2, CW], F32) for i in range(4)]
        assert self.aoff - pb_off == 8 * CW
        self.xnext = self.arena[:, pb_off:pb_off + 8 * CW].rearrange("p (a b) -> p a b", a=8)
        self.gbS = a("gbS", [128, 2, CH], F32)
        self.pooledB = a("pooledB", [128, 2, CH], BF16)
        self.ckvB = a("ckvB", [128, 2, CH], BF16)
        self.rowbuf = Ring([a(f"rowbuf{i}", [128, ROWW], BF16) for i in range(1)])
        self.u2buf = a("u2buf", [128, 8, CH], BF16)
        for rw in self.rowbuf.items:
            self.memset(self.pool, rw.t[:, D + 2 * NE:ROWW], 0.0, W=[rw.b])
        self.rt = a("rt", [128, 2, CH], F32)
        self.otm = Ring([a(f"otm{i}", [128, DV], BF16) for i in range(2)])
        self.smx = a("smx", [128, 8], F32)
        self.ex = a("ex", [128, NE], F32)
        self.edt = a("edt", [128, 2, 8], F32)
        save = self.aoff
        self.ckvF = a("ckvF", [128, 2, CH], F32)
        self.krF = a("krF", [128, CH], F32)
        self.outT = a("outT", [128, 2, KVL], F32)
        self.krout = a("krout", [128, 2, DR], F32)
        self.aoff = save
        self.ctxF = [a(f"ctxF{i}", [128, KVL], F32) for i in range(2)]
        self.ctxkr = [a(f"ctxkr{i}", [128, DR], F32) for i in range(2)]
        self.memset(self.pool, self.krT.t[DR:128, :], 0.0, W=self.krTb)
        self.memset(self.pool, self.Vt.t[:], 1.0, W=self.Vb)
        self.memset(self.pool, self.qrope.t[DR:128, :, :], 0.0, W=[self.qrope.b])

    def kv_up(self, kc):
        koff = kc * CH
        for h in range(NH):
            ps = self.ps_get()
            for m in range(2):
                self.mm(ps.t[:, 0:CH], self.w_uk.t[:, m, h * 128:(h + 1) * 128], self.ckvB.t[:, m, :],
                        start=(m == 0), stop=(m == 1), R=[self.w_uk.b, self.ckvB.b], W=[ps.b])
            if h % 2 == 0:
                self.A(self.KT.t[:, h, koff:koff + CH], ps.t[:, 0:CH], AF.Copy, R=[ps.b], W=[self.KTb[kc]])
            else:
                self.V_cp(self.KT.t[:, h, koff:koff + CH], ps.t[:, 0:CH], R=[ps.b], W=[self.KTb[kc]])
        for j in range(2):
            ps = self.ps_get()
            for m in range(2):
                self.mm(ps.t[:, :], self.ckvB.t[:, m, j * 128:(j + 1) * 128], self.w_uv.t[:, m, :],
                        start=(m == 0), stop=(m == 1), R=[self.w_uv.b, self.ckvB.b], W=[ps.b])
            kt = koff // 128 + j
            dst = self.Vt.t[:, kt, :].rearrange("p (h v) -> p h v", h=NH)[:, :, 0:DV]
            src = ps.t[:, :].rearrange("p (h v) -> p h v", h=NH)
            if j == 0:
                self.A(dst, src, AF.Copy, R=[ps.b], W=[self.Vb[kc]])
            else:
                self.V_cp(dst, src, R=[ps.b], W=[self.Vb[kc]])

    def ctx_phase(self, l):
        for jj in range(PAST // CH):
            for j in range(2):
                tile = jj * 2 + j
                cf, ck = self.ctxF[tile % 2], self.ctxkr[tile % 2]
                self.dma(self.qsp, cf.t[:], self.cckv[l][tile * 128:(tile + 1) * 128, :], W=[cf.b])
                self.dma(self.qsp, ck.t[:], self.ckr[l][tile * 128:(tile + 1) * 128, :], W=[ck.b])
                ps = self.ps_get()
                for m in range(2):
                    self.tr(ps.t[:, m * 128:(m + 1) * 128], cf.t[:, m * 128:(m + 1) * 128], self.identF.t[:],
                            R=[cf.b, self.identF.b], W=[ps.b], sig=(m == 1))
                self.V_cp(self.ckvB.t[:, :, j * 128:(j + 1) * 128], ps.t[:, 0:256].rearrange("p (m k) -> p m k", m=2),
                          R=[ps.b], W=[self.ckvB.b])
                ps2 = self.ps_get()
                self.tr(ps2.t[0:DR, 0:128], ck.t[:, :], self.identF.t[:], R=[ck.b, self.identF.b], W=[ps2.b])
                self.A(self.krT.t[0:DR, tile * 128:(tile + 1) * 128], ps2.t[0:DR, 0:128], AF.Copy, R=[ps2.b], W=[self.krTb[jj]])
            self.kv_up(jj)

    def u1_gen(self, cond, c0, c1):
        xt, ub = self.xt, self.ubuf
        for k in range(8):
            sc = self.s1p.t[:, k, cond:cond + 1]
            sh = self.modT.t[:, k, cond:cond + 1]
            if k % 2 == 0:
                self.V_ts(ub.t[:, k, c0:c1], xt.t[:, k, c0:c1], sc, sh, ALU.mult, ALU.add,
                          R=[self.xtb[k], self.s1p.b, self.modT.b], W=[self.ubb[k]])
            else:
                self.A(ub.t[:, k, c0:c1], xt.t[:, k, c0:c1], AF.Identity, scale=sc, bias=sh,
                       R=[self.xtb[k], self.s1p.b, self.modT.b], W=[self.ubb[k]])

    def load_rope(self, t0):
        self.dma(self.qsp, self.rt.t[0:DR, 0, :], self.cosT_d[:, t0:t0 + CH], W=[self.rt.b])
        self.dma(self.qsp, self.rt.t[0:DR, 1, :], self.sinT_d[:, t0:t0 + CH], W=[self.rt.b])

    def phase_a_chunk(self, l, r, c):
        cond = 1 if r == 0 else 0
        t0 = c * CH
        g0 = RB[r] + t0
        gc = g0 // CH
        kc = (PAST // CH if r == 0 else 0) + c
        koff = kc * CH
        xTv = self.xT_d.rearrange("(k p) t -> p k t", p=128)
        xt, ub = self.xt, self.ubuf
        self.dma(self.qsp, xt.t[:, :, HAL:HAL + CH], xTv[:, :, g0:g0 + CH], R=[self.xT_b[gc]], W=self.xtb)
        if r == 0:
            self.load_rope(t0)
        self.u1_gen(cond, HAL, HAL + CH)
        pkv = [self.ps_get(), self.ps_get()]
        for m in range(2):
            for k in range(8):
                self.mm(pkv[m].t[:, 0:CH], self.w_in.t[:, k, C_KV + m * 128:C_KV + (m + 1) * 128], ub.t[:, k, HAL:HAL + CH],
                        start=(k == 0), stop=(k == 7), R=[self.w_in.b, self.ubb[k]], W=[pkv[m].b])
        pkr = self.ps_get()
        for k in range(8):
            self.mm(pkr.t[0:DR, 0:CH], self.w_in.t[:, k, C_KR:C_KR + DR], ub.t[:, k, HAL:HAL + CH],
                    start=(k == 0), stop=(k == 7), R=[self.w_in.b, self.ubb[k]], W=[pkr.b])
        if r == 0:
            for k in range(8):
                self.mm(pkr.t[0:DR, CH:2 * CH], self.w_in.t[:, k, C_KRS:C_KRS + DR], ub.t[:, k, HAL:HAL + CH],
                        start=(k == 0), stop=(k == 7), R=[self.w_in.b, self.ubb[k]], W=[pkr.b])
        for m in range(2):
            self.A(self.sqb.t[:, m, :], pkv[m].t[:, 0:CH], AF.Square, R=[pkv[m].b], W=[self.sqb.b])
        pss = self.ps_get()
        for m in range(2):
            self.mm(pss.t[:, 0:CH], self.onesB.t[:], self.sqb.t[:, m, :], start=(m == 0), stop=(m == 1),
                    R=[self.onesB.b, self.sqb.b], W=[pss.b])
        st0 = self.st[0]
        self.A(st0.t[:], pss.t[:, 0:CH], AF.Ln, scale=1.0 / KVL, bias=self.misc.t[:, 0:1], R=[pss.b, self.misc.b], W=[st0.b])
        self.A(st0.t[:], st0.t[:], AF.Exp, scale=-0.5, R=[st0.b], W=[st0.b])
        for m in range(2):
            nrm = self.vec.t[:, V_KVN + m:V_KVN + m + 1]
            if r == 0:
                self.V_stt(self.ckvB.t[:, m, :], pkv[m].t[:, 0:CH], nrm, st0.t[:], ALU.mult, ALU.mult,
                           R=[pkv[m].b, self.vec.b, st0.b], W=[self.ckvB.b])
            else:
                self.V_stt(self.ckvF.t[:, m, :], pkv[m].t[:, 0:CH], nrm, st0.t[:], ALU.mult, ALU.mult,
                           R=[pkv[m].b, self.vec.b, st0.b], W=[self.ckvF.b])
        if r != 0:
            self.V_cp(self.ckvB.t[:], self.ckvF.t[:], R=[self.ckvF.b], W=[self.ckvB.b])
        if r == 0:
            s1, s2 = self.st[1], self.st[2]
            self.V_tt(s1.t[0:DR, :], pkr.t[0:DR, 0:CH], self.rt.t[0:DR, 0, :], ALU.mult, R=[pkr.b, self.rt.b], W=[s1.b])
            self.V_tt(s2.t[0:DR, :], pkr.t[0:DR, CH:2 * CH], self.rt.t[0:DR, 1, :], ALU.mult, R=[pkr.b, self.rt.b], W=[s2.b])
            self.V_tt(self.krT.t[0:DR, koff:koff + CH], s1.t[0:DR, :], s2.t[0:DR, :], ALU.add, R=[s1.b, s2.b], W=[self.krTb[kc]])
        else:
            self.A(self.krF.t[0:DR, :], pkr.t[0:DR, 0:CH], AF.Copy, R=[pkr.b], W=[self.krF.b])
            self.V_cp(self.krT.t[0:DR, koff:koff + CH], self.krF.t[0:DR, :], R=[self.krF.b], W=[self.krTb[kc]])
        self.kv_up(kc)
        if r != 0:
            for j in range(2):
                ps = self.ps_get()
                for m in range(2):
                    self.tr(ps.t[:, m * 128:(m + 1) * 128], self.ckvF.t[:, m, j * 128:(j + 1) * 128], self.identF.t[:],
                            R=[self.ckvF.b, self.identF.b], W=[ps.b], sig=(m == 1))
                self.A(self.outT.t[:, j, :], ps.t[:, 0:KVL], AF.Copy, R=[ps.b], W=[self.outT.b])
            self.dma(self.qsp, self.nckv[r - 1, l].rearrange("(j p) f -> p j f", p=128), self.outT.t[:], R=[self.outT.b], W=[self.out_b])
            ps = self.ps_get()
            for j in range(2):
                self.tr(ps.t[:, j * DR:(j + 1) * DR], self.krF.t[0:DR, j * 128:(j + 1) * 128], self.identF.t[0:DR, 0:DR],
                        R=[self.krF.b, self.identF.b], W=[ps.b], sig=(j == 1))
            self.V_cp(self.krout.t[:], ps.t[:, 0:2 * DR].rearrange("p (j f) -> p j f", j=2), R=[ps.b], W=[self.krout.b])
            self.dma(self.qsp, self.nkr[r - 1, l].rearrange("(j p) f -> p j f", p=128), self.krout.t[:], R=[self.krout.b], W=[self.out_b])

    def ln_core(self, ring=None, gen=False, X=None):
        g = self._ln_core(ring, X)
        if gen:
            return g
        for _ in g:
            pass

    def _ln_core(self, ring, X=None):
        xt = X["xt"] if X else self.xt
        xtb = X["xtb"] if X else self.xtb
        stl = X["st"] if X else self.st
        ring = ring or self.psring
        psm, psq = ring.get(), ring.get()
        cs = slice(HAL, HAL + CH)
        for j in range(8):
            zb, zs = self.zbk.get(), self.zsqk.get()
            self.A(zb.t[:], xt.t[:, j, cs], AF.Copy, R=[xtb[j]], W=[zb.b])
            if X is not None and X.get("act_sq"):
                self.A(zs.t[:], xt.t[:, j, cs], AF.Square, R=[xtb[j]], W=[zs.b])
            else:
                self.V_tt(zs.t[:], xt.t[:, j, cs], xt.t[:, j, cs], ALU.mult, R=[xtb[j]], W=[zs.b])
            yield
            self.mm(psm.t[:, 0:CH], self.onesB.t[:], zb.t[:], start=(j == 0), stop=(j == 7), R=[self.onesB.b, zb.b], W=[psm.b], sig=True)
            self.mm(psq.t[:, 0:CH], self.onesB.t[:], zs.t[:], start=(j == 0), stop=(j == 7), R=[self.onesB.b, zs.b], W=[psq.b], sig=True)
            yield
        s0, s1, s2 = stl[0], stl[1], stl[2]
        self.A(s0.t[:], psm.t[:, 0:CH], AF.Copy, scale=1.0 / D, R=[psm.b], W=[s0.b])
        self.V_tt(s1.t[:], s0.t[:], s0.t[:], ALU.mult, R=[s0.b], W=[s1.b])
        yield
        self.V_stt(s1.t[:], psq.t[:, 0:CH], 1.0 / D, s1.t[:], ALU.mult, ALU.subtract, R=[psq.b, s1.b], W=[s1.b])
        yield
        self.A(s1.t[:], s1.t[:], AF.Ln, bias=self.misc.t[:, 1:2], scale=1.0, R=[s1.b, self.misc.b], W=[s1.b])
        yield
        self.A(s1.t[:], s1.t[:], AF.Exp, scale=-0.5, R=[s1.b], W=[s1.b])
        self.V_stt(s2.t[:], s0.t[:], -1.0, s1.t[:], ALU.mult, ALU.mult, R=[s0.b, s1.b], W=[s2.b])
        yield
        for j in range(8):
            self.V_tt(xt.t[:, j, cs], xt.t[:, j, cs], s1.t[:], ALU.mult, R=[xtb[j], s1.b], W=[xtb[j]])
            self.V_tt(xt.t[:, j, cs], xt.t[:, j, cs], s2.t[:], ALU.add, R=[xtb[j], s2.b], W=[xtb[j]],
                      eng=(self.dve if (X is not None and X.get("act_sq")) else self.pool))
            yield

    def affine_xt(self, gcol0, bcol0, gen=False, X=None):
        g = self._affine_xt(gcol0, bcol0, X)
        if gen:
            return g
        for _ in g:
            pass

    def _affine_xt(self, gcol0, bcol0, X=None):
        xt = X["xt"] if X else self.xt
        xtb = X["xtb"] if X else self.xtb
        cs = slice(HAL, HAL + CH)
        for k in range(8):
            g = self.vec.t[:, gcol0 + k:gcol0 + k + 1]
            b = self.vec.t[:, bcol0 + k:bcol0 + k + 1]
            if k % 2 == 0 and not (X is not None and X.get("act_sq")):
                self.V_ts(xt.t[:, k, cs], xt.t[:, k, cs], g, b, ALU.mult, ALU.add, R=[xtb[k], self.vec.b], W=[xtb[k]])
            else:
                self.A(xt.t[:, k, cs], xt.t[:, k, cs], AF.Identity, scale=g, bias=b, R=[xtb[k], self.vec.b], W=[xtb[k]])
            yield

    def load_xnext(self, r, c):
        nch = REQ_T[r] // CH
        g0 = RB[r] + c * CH
        gc = g0 // CH
        first, last = (c == 0), (c == nch - 1)
        lo = 0 if first else -HAL
        hi = CH if last else CH + HAL
        xTv = self.xT_d.rearrange("(k p) t -> p k t", p=128)
        rb = [self.xT_b[gc]] + ([] if first else [self.xT_b[gc - 1]]) + ([] if last else [self.xT_b[gc + 1]])
        self.dma(self.qsp, self.xnext[:, :, HAL + lo:HAL + hi], xTv[:, :, g0 + lo:g0 + hi], R=rb, W=[q.b for q in self.pb])

    def pb_ctx(self, r, c):
        Tr = REQ_T[r]
        nch = Tr // CH
        t0 = c * CH
        g0 = RB[r] + t0
        first, last = (c == 0), (c == nch - 1)
        return dict(cond=1 if r == 0 else 0, t0=t0, g0=g0, gc=g0 // CH, first=first, last=last,
                    lo=0 if first else -HAL, hi=CH if last else CH + HAL)

    def pb_front(self, l, r, c):
        X = self.pb_ctx(r, c)
        cond, first, last, lo, hi = X["cond"], X["first"], X["last"], X["lo"], X["hi"]
        ub = self.ubuf
        cs = slice(HAL, HAL + CH)
        pbb = [q.b for q in self.pb]
        xn = self.xnext
        if first:
            self.load_xnext(r, c)
        if r == 0:
            self.load_rope(X["t0"])
        if first:
            self.memset(self.pool, ub.t[:, :, 0:HAL], 0.0, W=self.ubb)
        if last:
            self.memset(self.pool, ub.t[:, :, HAL + CH:CW], 0.0, W=self.ubb)
        c0, c1 = HAL + lo, HAL + hi
        for k in range(8):
            sc = self.s1p.t[:, k, cond:cond + 1]
            sh = self.modT.t[:, k, cond:cond + 1]
            if k % 2 == 0:
                self.V_ts(ub.t[:, k, c0:c1], xn[:, k, c0:c1], sc, sh, ALU.mult, ALU.add,
                          R=[pbb[k // 2], self.s1p.b, self.modT.b], W=[self.ubb[k]])
            else:
                self.A(ub.t[:, k, c0:c1], xn[:, k, c0:c1], AF.Identity, scale=sc, bias=sh,
                       R=[pbb[k // 2], self.s1p.b, self.modT.b], W=[self.ubb[k]])
        wb = self.w_in.b
        ring = self.psS

        def win(ps, col, ncols, c0, c1, prow=128):
            for k in range(8):
                self.mm(ps.t[0:prow, 0:c1 - c0], self.w_in.t[:, k, col:col + ncols], ub.t[:, k, c0:c1],
                        start=(k == 0), stop=(k == 7), R=[wb, self.ubb[k]], W=[ps.b])

        pq = self.psB.items
        for m in range(3):
            win(pq[m], C_Q + m * 128, 128, HAL, HAL + CH)
            self.A(self.sqb.t[:, m, :], pq[m].t[:, 0:CH], AF.Square, R=[pq[m].b], W=[self.sqb.b])
        P, Bq, Cq, Vv = self.pb
        for m in range(2):
            ps = ring.get()
            win(ps, C_P + m * 128, 128, 0, CW)
            self.A(P.t[:, m, :], ps.t[:, 0:CW], AF.Copy, R=[ps.b], W=[P.b])
        for m in range(2):
            psc = ring.get()
            win(psc, C_GC + m * 128, 128, 0, CW)
            self.A(Vv.t[:, m, :], psc.t[:, 0:CW], AF.Copy, R=[psc.b], W=[Vv.b])
            psh = ring.get()
            win(psh, C_H + m * 128, 128, 0, CW)
            self.V_tt(Vv.t[:, m, :], psh.t[:, 0:CW], Vv.t[:, m, :], ALU.mult, R=[psh.b, Vv.b], W=[Vv.b])
        for m in range(2):
            ps = ring.get()
            win(ps, C_GB + m * 128, 128, HAL, HAL + CH)
            self.A(self.gbS.t[:, m, :], ps.t[:, 0:CH], AF.Copy, R=[ps.b], W=[self.gbS.b])
        st0 = self.stf[0]
        pss = ring.get()
        for m in range(3):
            self.mm(pss.t[:, 0:CH], self.onesB.t[:], self.sqb.t[:, m, :], start=(m == 0), stop=(m == 2),
                    R=[self.onesB.b, self.sqb.b], W=[pss.b])
        self.A(st0.t[:], pss.t[:, 0:CH], AF.Ln, scale=1.0 / QL, bias=self.misc.t[:, 0:1], R=[pss.b, self.misc.b], W=[st0.b])
        self.A(st0.t[:], st0.t[:], AF.Exp, scale=-0.5, R=[st0.b], W=[st0.b])
        for m in range(3):
            self.V_stt(self.qn.t[:, m, :], pq[m].t[:, 0:CH], self.vec.t[:, V_QN + m:V_QN + m + 1], st0.t[:], ALU.mult, ALU.mult,
                       R=[pq[m].b, self.vec.b, st0.b], W=[self.qn.b])
        for h in range(NH):
            ps = ring.get()
            for m in range(3):
                self.mm(ps.t[:, 0:CH], self.w_uq.t[:, m, h * 192:h * 192 + 128], self.qn.t[:, m, :], start=(m == 0), stop=(m == 2),
                        R=[self.w_uq.b, self.qn.b], W=[ps.b])
            self.A(self.qnope.t[:, h, :], ps.t[:, 0:CH], AF.Copy, scale=ATTN_SCALE, R=[ps.b], W=[self.qnope.b])
            ps2 = ring.get()
            for m in range(3):
                self.mm(ps2.t[0:DR, 0:CH], self.w_uq.t[:, m, h * 192 + 128:h * 192 + 192], self.qn.t[:, m, :], start=(m == 0), stop=(m == 2),
                        R=[self.w_uq.b, self.qn.b], W=[ps2.b])
            if r == 0:
                for m in range(3):
                    self.mm(ps2.t[0:DR, CH:2 * CH], self.w_uq.t[:, m, 768 + h * DR:768 + (h + 1) * DR], self.qn.t[:, m, :],
                            start=(m == 0), stop=(m == 2), R=[self.w_uq.b, self.qn.b], W=[ps2.b])
                s1 = self.stf[1]
                self.V_tt(s1.t[0:DR, :], ps2.t[0:DR, 0:CH], self.rt.t[0:DR, 0, :], ALU.mult, R=[ps2.b, self.rt.b], W=[s1.b])
                self.V_tt(ps2.t[0:DR, CH:2 * CH], ps2.t[0:DR, CH:2 * CH], self.rt.t[0:DR, 1, :], ALU.mult, R=[ps2.b, self.rt.b], W=[ps2.b])
                self.V_tt(s1.t[0:DR, :], ps2.t[0:DR, CH:2 * CH], s1.t[0:DR, :], ALU.add, R=[s1.b, ps2.b], W=[s1.b])
                self.A(self.qrope.t[0:DR, h, :], s1.t[0:DR, :], AF.Copy, scale=ATTN_SCALE, R=[s1.b], W=[self.qrope.b])
            else:
                self.A(self.qrope.t[0:DR, h, :], ps2.t[0:DR, 0:CH], AF.Copy, scale=ATTN_SCALE, R=[ps2.b], W=[self.qrope.b])
        pl = self.pool
        add = ALU.add
        self.V_tt(Bq.t[:, :, 1:CW], P.t[:, :, 0:CW - 1], P.t[:, :, 1:CW], add, R=[P.b], W=[Bq.b], eng=pl)
        self.V_tt(Cq.t[64:128, 0, 2:CW - 1], Bq.t[64:128, 0, 1:CW - 2], Bq.t[64:128, 0, 3:CW], add, R=[Bq.b], W=[Cq.b], eng=pl)
        self.V_tt(Cq.t[:, 1, 2:CW - 1], Bq.t[:, 1, 1:CW - 2], Bq.t[:, 1, 3:CW], add, R=[Bq.b], W=[Cq.b], eng=pl)
        self.V_tt(Bq.t[:, 1, 4:CW - 3], Cq.t[:, 1, 2:CW - 5], Cq.t[:, 1, 6:CW - 1], add, R=[Cq.b], W=[Bq.b], eng=pl)
        self.V_tt(Cq.t[64:128, 1, 8:CW - 7], Bq.t[64:128, 1, 4:CW - 11], Bq.t[64:128, 1, 12:CW - 3], add, R=[Bq.b], W=[Cq.b], eng=pl)
        for m in range(2):
            for (p0, p1, Wb) in ((0, 64, Bq), (64, 128, Cq)):
                self.V_stt(self.pooledB.t[p0:p1, m, :], Wb.t[p0:p1, m, cs], self.misc.t[p0:p1, 2 + m:3 + m], P.t[p0:p1, m, cs],
                           ALU.mult, ALU.subtract, R=[Wb.b, P.b, self.misc.b], W=[self.pooledB.b])
                for (is_edge, ec0, wc0, oc0) in ((first, 0, HAL, 0), (last, 8, HAL + CH - 8, CH - 8)):
                    if is_edge:
                        self.V_tt(self.edt.t[p0:p1, m, :], Wb.t[p0:p1, m, wc0:wc0 + 8], self.edges.t[p0:p1, m, ec0:ec0 + 8], ALU.mult,
                                  R=[Wb.b, self.edges.b], W=[self.edt.b])
                        self.V_tt(self.pooledB.t[p0:p1, m, oc0:oc0 + 8], self.edt.t[p0:p1, m, :], P.t[p0:p1, m, wc0:wc0 + 8], ALU.subtract,
                                  R=[self.edt.b, P.b], W=[self.pooledB.b])
        cacc = Bq
        for m in range(2):
            w0 = self.vec.t[:, V_CW + m * 3 + 0:V_CW + m * 3 + 1]
            w1 = self.vec.t[:, V_CW + m * 3 + 1:V_CW + m * 3 + 2]
            w2 = self.vec.t[:, V_CW + m * 3 + 2:V_CW + m * 3 + 3]
            self.V_ts(cacc.t[:, m, cs], Vv.t[:, m, HAL - 1:HAL + CH - 1], w0, None, ALU.mult, R=[Vv.b, self.vec.b, self.pooledB.b], W=[cacc.b])
            self.V_stt(cacc.t[:, m, cs], Vv.t[:, m, cs], w1, cacc.t[:, m, cs], ALU.mult, ALU.add, R=[Vv.b, cacc.b, self.vec.b], W=[cacc.b])
            self.V_stt(cacc.t[:, m, cs], Vv.t[:, m, HAL + 1:HAL + CH + 1], w2, cacc.t[:, m, cs], ALU.mult, ALU.add, R=[Vv.b, cacc.b, self.vec.b], W=[cacc.b])
        for mo in range(2):
            ps = ring.get()
            for mi in range(2):
                self.mm(ps.t[:, 0:CH], self.poolw.t[:, mi, mo * 128:(mo + 1) * 128], self.pooledB.t[:, mi, :], start=(mi == 0), stop=(mi == 1),
                        R=[self.poolw.b, self.pooledB.b], W=[ps.b])
            self.A(ub.t[:, 4 + mo, cs], ps.t[:, 0:CH], AF.Copy, scale=self.vec.t[:, V_PS + mo:V_PS + mo + 1], R=[ps.b, self.vec.b], W=[self.ubb[4 + mo]])
            self.V_tt(ub.t[:, 6 + mo, cs], cacc.t[:, mo, cs], self.gbS.t[:, mo, :], ALU.mult, R=[cacc.b, self.gbS.b], W=[self.ubb[6 + mo]])
        if not last:
            self.load_xnext(r, c + 1)

    def pb_attention(self, l, r, c, side):
        ub = self.ubuf
        nkt = (PAST + T_S) // 128 if r == 0 else T_P // 128
        npair = nkt // 2
        ring = self.psS

        def emit_S(h, kp):
            ps = ring.get()
            for j in range(2):
                kt = kp * 2 + j
                self.mm(ps.t[:, j * CH:(j + 1) * CH], self.KT.t[:, h, kt * 128:(kt + 1) * 128], self.qnope.t[:, h, :],
                        start=True, stop=False, R=[self.KTb[kp], self.qnope.b], W=[ps.b], sig=False)
                self.mm(ps.t[:, j * CH:(j + 1) * CH], self.krT.t[:, kt * 128:(kt + 1) * 128], self.qrope.t[:, h, :],
                        start=False, stop=True, R=[self.krTb[kp], self.qrope.b], W=[ps.b], sig=(j == 1))
            return ps

        seq = [(h, kp) for h in range(NH) for kp in range(npair)]
        nside = max(1, (130 + len(seq) - 1) // len(seq))
        ps_next = emit_S(*seq[0])
        acc = self.psacc.items[0]
        for i, (h, kp) in enumerate(seq):
            ps = ps_next
            pt = self.PT.get()
            self.A(pt.t[:].rearrange("p a b -> p (a b)"), ps.t[:, :], AF.Exp, R=[ps.b], W=[pt.b])
            if i + 1 < len(seq):
                ps_next = emit_S(*seq[i + 1])
            for j in range(2):
                kt = kp * 2 + j
                for qt in range(2):
                    self.mm(acc[qt].t[:, 0:VA], pt.t[:, j, qt * 128:(qt + 1) * 128], self.Vt.t[:, kt, h * VA:(h + 1) * VA],
                            start=(kt == 0), stop=(kt == nkt - 1), R=[self.Vb[kp], pt.b], W=[acc[qt].b], sig=(j == 1 and qt == 1))
            if kp == npair - 1:
                sm = self.smx
                for qt in range(2):
                    self.V_rcp(sm.t[:, 4 + qt:5 + qt], acc[qt].t[:, DV:DV + 1], R=[acc[qt].b], W=[sm.b])
                    otm = self.otm.get()
                    self.V_ts(otm.t[:], acc[qt].t[:, 0:DV], sm.t[:, 4 + qt:5 + qt], None, ALU.mult, R=[acc[qt].b, sm.b], W=[otm.b])
                    pst = ring.get()
                    pstb = pst.t[:, :].bitcast(BF16)
                    self.tr(pstb[:, 0:128], otm.t[:], self.identB.t[:], R=[otm.b, self.identB.b], W=[pst.b])
                    self.A(ub.t[:, h, HAL + qt * 128:HAL + (qt + 1) * 128], pstb[:, 0:128], AF.Copy, R=[pst.b], W=[self.ubb[h]])
            if side is not None:
                for _ in range(nside):
                    if next(side, "done") == "done":
                        side = None
                        break
        if side is not None:
            for _ in side:
                pass

    def pb_wout(self, l, r, c):
        X = self.pb_ctx(r, c)
        cond, g0, gc = X["cond"], X["g0"], X["gc"]
        xTv = self.xT_d.rearrange("(k p) t -> p k t", p=128)
        xt, ub = self.xt, self.ubuf
        cs = slice(HAL, HAL + CH)
        self.dma(self.qsp, xt.t[:, :, cs], xTv[:, :, g0:g0 + CH], R=[self.xT_b[gc]], W=self.xtb)
        for j in range(8):
            ps = self.psB.get()
            for k in range(8):
                self.mm(ps.t[:, 0:CH], self.w_out.t[:, k, j * 128:(j + 1) * 128], ub.t[:, k, cs], start=(k == 0), stop=(k == 7),
                        R=[self.w_out.b, self.ubb[k]], W=[ps.b])
            self.V_stt(xt.t[:, j, cs], ps.t[:, 0:CH], self.g1a.t[:, j, cond:cond + 1], xt.t[:, j, cs], ALU.mult, ALU.add,
                       R=[ps.b, self.g1a.b, self.xtb[j]], W=[self.xtb[j]])

    def pb_late(self, l, r, c):
        X = self.pb_ctx(r, c)
        cond, g0, gc = X["cond"], X["g0"], X["gc"]
        x1Tv = self.x1T_d.rearrange("(k p) t -> p k t", p=128)
        xt, u2 = self.xt, self.u2buf
        cs = slice(HAL, HAL + CH)
        yield from self.ln_core(ring=self.psB, gen=True)
        for jt in range(2):
            gt = g0 // 128 + jt
            ps = self.psB.get()
            for k in range(8):
                self.mm(ps.t[:, 0:NE], xt.t[:, k, HAL + jt * 128:HAL + (jt + 1) * 128], self.wr2.t[:, cond, k, :], start=(k == 0), stop=False,
                        R=[self.xtb[k], self.wr2.b], W=[ps.b], sig=False)
            self.mm(ps.t[:, 0:NE], self.onesrow.t[0:1, :], self.rconst.t[0:1, cond, :], start=False, stop=True,
                    R=[self.onesrow.b, self.rconst.b], W=[ps.b], sig=True)
            yield
            sm = self.smx
            self.dve.op(lambda e: e.tensor_reduce(out=sm.t[:, 0:1], in_=ps.t[:, 0:NE], axis=AX.X, op=ALU.max), R=[ps.b], W=[sm.b])
            self.V_ts(sm.t[:, 1:2], sm.t[:, 0:1], -1.0, None, ALU.mult, R=[sm.b], W=[sm.b])
            yield
            self.A(self.ex.t[:], ps.t[:, 0:NE], AF.Exp, bias=sm.t[:, 1:2], scale=1.0, accum_out=sm.t[:, 2:3], R=[ps.b, sm.b], W=[self.ex.b, sm.b])
            yield
            self.V_rcp(sm.t[:, 3:4], sm.t[:, 2:3], R=[sm.b], W=[sm.b])
            self.V_ts(self.affS.t[:, gt, :], self.ex.t[:], sm.t[:, 3:4], None, ALU.mult, R=[self.ex.b, sm.b], W=[self.affSb[gt]])
            yield
        for k in range(8):
            g = self.G2.t[:, k, cond:cond + 1]
            b = self.B2.t[:, k, cond:cond + 1]
            if k % 2 == 1:
                self.V_ts(u2.t[:, k, :], xt.t[:, k, cs], g, b, ALU.mult, ALU.add, R=[self.xtb[k], self.G2.b, self.B2.b], W=[self.u2b[k]])
            else:
                self.A(u2.t[:, k, :], xt.t[:, k, cs], AF.Identity, scale=g, bias=b, R=[self.xtb[k], self.G2.b, self.B2.b], W=[self.u2b[k]])
            yield
        yield from self.affine_xt(V_L1G, V_L1B, gen=True)
        self.dma(self.qsp, x1Tv[:, :, g0:g0 + CH], xt.t[:, :, cs], R=self.xtb, W=[self.x1T_b[gc]])
        for jt in range(2):
            gt = g0 // 128 + jt
            ps = self.psB.get()
            psb = ps.t[:, :].bitcast(BF16)
            for k in range(8):
                self.tr(psb[:, k * 128:(k + 1) * 128], u2.t[:, k, jt * 128:(jt + 1) * 128], self.identB.t[:],
                        R=[self.u2b[k], self.identB.b], W=[ps.b], sig=(k == 7))
                if k % 4 == 3:
                    yield
            rw = self.rowbuf.get()
            self.A(rw.t[:, 0:D], psb[:, :], AF.Copy, R=[ps.b], W=[rw.b])
            self.V_cp(rw.t[:, D:D + 2 * NE].bitcast(F32), self.affS.t[:, gt, :], R=[self.affSb[gt], rw.b], W=[rw.b])
            self.V_cp(rw.t[:, D + 2 * NE:D + 2 * NE + 2].bitcast(I32), self.tidcol.t[:, gt:gt + 1], R=[self.tidcol.b, rw.b], W=[rw.b])
            self.dma(self.qsp, self.u2rows_d[gt * 128:(gt + 1) * 128, :], rw.t[:], R=[rw.b], W=[self.u2rows_bt[gt]])
            yield

    def phase_b(self, l, r):
        nch = REQ_T[r] // CH
        side = None
        for c in range(nch):
            self.pb_front(l, r, c)
            self.pb_attention(l, r, c, side)
            self.pb_wout(l, r, c)
            side = self.pb_late(l, r, c)
        for _ in side:
            pass

    def routing(self, r):
        self.carve_reset()
        a = self.av
        sample = (r == 0)
        Pn = 128 if sample else NE
        Fn = 512 if sample else T_P
        nblk = Fn // 128
        Kcap = float(REQ_CAP[r])
        A_sb = a("A_sb", [128, 512], F32)
        junk = a("junk", [128, 512], F32)
        msk = a("msk", [128, 512], F32)
        csb = a("csb", [128, 512], F32)
        affX = [a(f"affX{i}", [128, NE, 8], F32) for i in range(2)]
        bis = a("bis", [128, 8], I32)
        cnt = a("cnt", [128, 4], F32)
        self.gsum = a("gsum", [128, 128], F32)
        self.lstrict = a("lstrict", [128, 128], F32)
        if sample:
            self.dma(self.qsp, self.gsum.t[:], self.gsum_d, W=[self.gsum.b])
            self.dma(self.qsp, self.lstrict.t[:], self.lstrict_d, W=[self.lstrict.b])
        g0t = RB[r] // 128
        for b in range(nblk):
            ps = self.ps_get()
            if sample:
                for seg in range(8):
                    gt = seg * 4 + b
                    ax = affX[seg % 2]
                    self.memset(self.pool, ax.t[:], 0.0, W=[ax.b])
                    self.V_cp(ax.t[:, :, seg], self.affS.t[:, gt, :], R=[self.affSb[gt], ax.b], W=[ax.b])
                    self.mm(ps.t[:, 0:128], ax.t[:].rearrange("p e s -> p (e s)"), self.identF.t[:], start=(seg == 0), stop=(seg == 7),
                            R=[ax.b, self.identF.b], W=[ps.b], sig=True)
            else:
                gt = g0t + b
                self.mm(ps.t[0:NE, 0:128], self.affS.t[:, gt, :], self.identF.t[:], start=True, stop=True,
                        R=[self.affSb[gt], self.identF.b], W=[ps.b])
            self.V_cp(A_sb.t[0:Pn, b * 128:(b + 1) * 128], ps.t[0:Pn, 0:128], R=[ps.b], W=[A_sb.b])
        lo, mid, ge = (bis.t[0:Pn, i:i + 1] for i in range(3))
        bb = [bis.b]
        self.memset(self.dve, bis.t[:, 0:1], 0, W=bb)
        self.memset(self.dve, cnt.t[:], 0.0, W=[cnt.b])
        for bit in range(29, -1, -1):
            self.V_ts(mid, lo, 1 << bit, None, ALU.bitwise_or, R=bb, W=bb)
            self.V_ts(junk.t[0:Pn, 0:Fn], A_sb.t[0:Pn, 0:Fn], mid.bitcast(F32), None, ALU.is_ge, ALU.add,
                      R=[A_sb.b] + bb, W=[junk.b, cnt.b], accum_out=cnt.t[0:Pn, 0:1])
            if sample:
                ps = self.ps_get()
                self.mm(ps.t[:, 0:2], self.gsum.t[:], cnt.t[:, 0:2], start=True, stop=True, R=[self.gsum.b, cnt.b], W=[ps.b])
                src, sb_ = ps.t[0:Pn, 0:1], [ps.b]
            else:
                src, sb_ = cnt.t[0:Pn, 0:1], [cnt.b]
            self.V_ts(ge, src, Kcap - 0.5, None, ALU.is_ge, R=sb_, W=bb)
            self.dve.op(lambda e: e.copy_predicated(out=lo, mask=ge, data=mid), R=bb, W=bb)
        self.V_ts(msk.t[0:Pn, 0:Fn], A_sb.t[0:Pn, 0:Fn], lo.bitcast(F32), None, ALU.is_ge, R=[A_sb.b] + bb, W=[msk.b])
        self.memset(self.dve, junk.t[:], 1.0, W=[junk.b])
        self.dve.op(lambda e: e.tensor_tensor_scan(out=csb.t[0:Pn, 0:Fn], data0=junk.t[0:Pn, 0:Fn], data1=msk.t[0:Pn, 0:Fn],
                                                   initial=0.0, op0=ALU.mult, op1=ALU.add), R=[junk.b, msk.b], W=[csb.b])
        if sample:
            self.V_cp(cnt.t[:, 0:1], csb.t[:, Fn - 1:Fn], R=[csb.b], W=[cnt.b])
            ps = self.ps_get()
            self.mm(ps.t[:, 0:2], self.lstrict.t[:], cnt.t[:, 0:2], start=True, stop=True, R=[self.lstrict.b, cnt.b], W=[ps.b])
            self.V_cp(cnt.t[:, 2:3], ps.t[:, 0:1], R=[ps.b], W=[cnt.b])
            self.V_ts(csb.t[:, 0:Fn], csb.t[:, 0:Fn], cnt.t[:, 2:3], None, ALU.add, R=[csb.b, cnt.b], W=[csb.b])
        self.V_stt(junk.t[0:Pn, 0:Fn], csb.t[0:Pn, 0:Fn], Kcap + 0.5, msk.t[0:Pn, 0:Fn], ALU.is_le, ALU.mult, R=[csb.b, msk.b], W=[junk.b])
        self.V_tt(csb.t[0:Pn, 0:Fn], csb.t[0:Pn, 0:Fn], junk.t[0:Pn, 0:Fn], ALU.mult, R=[csb.b, junk.b], W=[csb.b])
        vt = self.vTM[r]
        for b in range(nblk):
            ps = self.ps_get()
            self.tr(ps.t[:, 0:Pn], csb.t[0:Pn, b * 128:(b + 1) * 128], self.identF.t[0:Pn, 0:Pn], R=[csb.b, self.identF.b], W=[ps.b])
            self.V_cp(vt.t[:, b, 0:Pn], ps.t[:, 0:Pn], R=[ps.b], W=[vt.b])

    def moe_phase(self, l):
        self.carve_reset()
        a = self.av
        wg = [a(f"wg{i}", [128, 8, EF], BF16) for i in range(2)]
        wu = [a(f"wu{i}", [128, 8, EF], BF16) for i in range(2)]
        wd = [a(f"wd{i}", [128, 4, D], BF16) for i in range(2)]
        xs = [a(f"xs{i}", [128, ROWW], BF16) for i in range(10)]
        NS = 576
        xsT = a("xsT", [128, 8, NS], BF16)
        hdn = a("hdn", [128, 4, NS], BF16)
        sil = a("sil", [128, NS], F32)
        ye = Ring([a(f"ye{i}", [128, D], F32) for i in range(3)])
        Sr = Ring([a(f"S{i}", [128, 512], BF16) for i in range(3)])
        Sp = Ring([a(f"Sp{i}", [128, 32], BF16) for i in range(2)])
        idxrow = a("idxrow", [2, NS], F32)
        idxI = Ring([a(f"idxI{i}", [128, 8], I32) for i in range(2)])
        idxF = a("idxF", [128, 5, 2], F32)
        zt = a("zt", [128, D], F32)
        self.iota1 = a("iota1", [128, 512], F32)
        self.dma(self.qsp, self.iota1.t[:], self.iota1_d, W=[self.iota1.b])
        self.memset(self.pool, zt.t[:], 0.0, W=[zt.b])
        for gt in range(TTOT // 128):
            self.dma(self.qsp, self.ffn_d[gt * 128:(gt + 1) * 128, :], zt.t[:], R=[zt.b], W=[self.ffn_bt[gt]])
        if self.bcreg is None:
            self.bcreg = self.nc.gpsimd.to_reg(TTOT - 1)

        def load_w(e):
            i = e % 2
            self.dma(self.qpl, wg[i].t[:], self.w_gate_d[l, e].rearrange("(k p) f -> p k f", p=128), W=[wg[i].b])
            self.dma(self.qpl, wu[i].t[:], self.w_up_d[l, e].rearrange("(k p) f -> p k f", p=128), W=[wu[i].b])
            self.dma(self.qpl, wd[i].t[:], self.w_down_d[l, e].rearrange("(k p) d -> p k d", p=128), W=[wd[i].b])

        load_w(0)
        tiles = [(c, 128) for c in range(4)] + [(4, 64)]
        scat = {"prev": [], "cur": []}

        def stage1(e):
            xse = xs[(e % 2) * 5:(e % 2) * 5 + 5]
            psI = self.psacc.items[0][0]
            psI2 = self.psacc.items[0][1]
            vt = self.vTM[0]
            ntile = T_S // 128
            Sl = [None] * ntile

            def onehot(gt):
                seg, b = gt // 4, gt % 4
                S = Sr.get()
                col = vt.t[:, b, e * 8 + seg:e * 8 + seg + 1]
                self.V_ts(S.t[:], self.iota1.t[:], col, None, ALU.is_equal, R=[self.iota1.b, vt.b], W=[S.b])
                Sl[gt] = S

            for gt in range(3):
                onehot(gt)
            yield
            for gt in range(ntile):
                S = Sl[gt]
                self.mm(psI.t[0:2, :], self.tidhl.t[:, gt, :], S.t[:], start=(gt == 0), stop=(gt == ntile - 1),
                        R=[self.tidhl.b, S.b], W=[psI.b], sig=True)
                if gt + 3 < ntile:
                    onehot(gt + 3)
                yield
            self.A(idxrow.t[0:2, 0:512], psI.t[0:2, :], AF.Copy, R=[psI.b], W=[idxrow.b])
            psJ = psI2
            for r in (1, 2):
                vtp = self.vTM[r]
                for b in range(2):
                    gt = RB[r] // 128 + b
                    S = Sp.get()
                    self.V_ts(S.t[:], self.iota1.t[:, 0:32], vtp.t[:, b, e:e + 1], None, ALU.is_equal, R=[self.iota1.b, vtp.b], W=[S.b])
                    self.mm(psJ.t[0:2, (r - 1) * 32:r * 32], self.tidhl.t[:, gt, :], S.t[:], start=(b == 0), stop=(b == 1),
                            R=[self.tidhl.b, S.b], W=[psJ.b], sig=True)
            yield
            self.A(idxrow.t[0:2, 512:NS], psJ.t[0:2, 0:64], AF.Copy, R=[psJ.b], W=[idxrow.b])
            psT = psI
            for (c, nr) in tiles:
                self.tr(psT.t[0:nr, c * 2:c * 2 + 2], idxrow.t[0:2, c * 128:c * 128 + nr], self.identF.t[0:2, 0:2],
                        R=[idxrow.b, self.identF.b], W=[psT.b], sig=(c == 4))
            yield
            self.V_cp(idxF.t[:].rearrange("p a b -> p (a b)"), psT.t[:, 0:10], R=[psT.b], W=[idxF.b])
            ii = idxI.get()
            self.V_stt(ii.t[:, 0:5], idxF.t[:, :, 0], 64.0, idxF.t[:, :, 1], ALU.mult, ALU.add, R=[idxF.b], W=[ii.b])
            for (c, nr) in tiles:
                self.qpl.issue(lambda g, c=c, nr=nr: g.indirect_dma_start(
                    out=xse[c].t[0:nr, :], out_offset=None, in_=self.u2rows_d,
                    in_offset=bass.IndirectOffsetOnAxis(ap=ii.t[0:nr, c:c + 1], axis=0),
                    bounds_check=self.bcreg, oob_is_err=False), R=[ii.b] + self.u2rows_bt, W=[xse[c].b])
            yield

        def stage2(e):
            i = e % 2
            xse = xs[(e % 2) * 5:(e % 2) * 5 + 5]
            for (c, nr) in tiles:
                ps = self.ps_get()
                psb = ps.t[:, :].bitcast(BF16)
                for k in range(8):
                    self.tr(psb[:, k * 128:k * 128 + nr], xse[c].t[0:nr, k * 128:(k + 1) * 128], self.identB.t[0:nr, 0:nr],
                            R=[xse[c].b, self.identB.b], W=[ps.b], sig=(k == 7))
                yield
                src = psb[:, :].rearrange("p (k n) -> p k n", k=8)[:, :, 0:nr]
                if c % 2 == 0:
                    self.A(xsT.t[:, :, c * 128:c * 128 + nr], src, AF.Copy, R=[ps.b], W=[xsT.b])
                else:
                    self.V_cp(xsT.t[:, :, c * 128:c * 128 + nr], src, R=[ps.b], W=[xsT.b])
            for f in range(4):
                for (c0, c1) in ((0, 512), (512, NS)):
                    n = c1 - c0
                    psg, psu = self.ps_get(), self.ps_get()
                    for k in range(8):
                        self.mm(psg.t[:, 0:n], wg[i].t[:, k, f * 128:(f + 1) * 128], xsT.t[:, k, c0:c1], start=(k == 0), stop=(k == 7),
                                R=[wg[i].b, xsT.b], W=[psg.b])
                    for k in range(8):
                        self.mm(psu.t[:, 0:n], wu[i].t[:, k, f * 128:(f + 1) * 128], xsT.t[:, k, c0:c1], start=(k == 0), stop=(k == 7),
                                R=[wu[i].b, xsT.b], W=[psu.b])
                    yield
                    self.A(sil.t[:, c0:c1], psg.t[:, 0:n], AF.Silu, R=[psg.b], W=[sil.b])
                    self.V_tt(hdn.t[:, f, c0:c1], psu.t[:, 0:n], sil.t[:, c0:c1], ALU.mult, R=[psu.b, sil.b], W=[hdn.b])
            for (c, nr) in tiles:
                y = ye.get()
                gcol = xse[c].t[0:nr, D + 2 * e:D + 2 * e + 2].bitcast(F32)
                for half in range(2):
                    ps = self.ps_get()
                    for f in range(4):
                        self.mm(ps.t[0:nr, :], hdn.t[:, f, c * 128:c * 128 + nr], wd[i].t[:, f, half * 512:(half + 1) * 512],
                                start=(f == 0), stop=(f == 3), R=[hdn.b, wd[i].b], W=[ps.b])
                    yield
                    if half == 0:
                        self.A(y.t[0:nr, 0:512], ps.t[0:nr, :], AF.Copy, scale=gcol, R=[ps.b, xse[c].b], W=[y.b])
                    else:
                        self.V_ts(y.t[0:nr, 512:D], ps.t[0:nr, :], gcol, None, ALU.mult, R=[ps.b, xse[c].b], W=[y.b])
                tid = xse[c].t[0:nr, D + 2 * NE:D + 2 * NE + 2].bitcast(I32)
                for t_prev in scat["prev"]:
                    self.pool.wait(t_prev)
                last_e = (e == NE - 1)
                tk = self.qpl.issue(lambda g, y=y, nr=nr, tid=tid: g.indirect_dma_start(
                    out=self.ffn_d, out_offset=bass.IndirectOffsetOnAxis(ap=tid, axis=0), in_=y.t[0:nr, :], in_offset=None,
                    compute_op=ALU.add), R=[y.b, xse[c].b] + (self.ffn_bt if e == 0 else []), W=(self.ffn_bt if last_e else []))
                scat["cur"].append(tk)
            scat["prev"], scat["cur"] = scat["cur"], []

        for _ in stage1(0):
            pass
        for e in range(NE):
            side = None
            if e + 1 < NE:
                load_w(e + 1)
                side = stage1(e + 1)
            if e == 1 and l + 1 < self.depth:
                self.load_layer_weights(l + 1)
            for _ in stage2(e):
                if side is not None:
                    for _k in range(4):
                        if next(side, "done") == "done":
                            side = None
                            break
            if side is not None:
                for _ in side:
                    pass

    def ln2_A(self, l, r, c, X, ft):
        cond = 1 if r == 0 else 0
        g0 = RB[r] + c * CH
        gc = g0 // CH
        x1Tv = self.x1T_d.rearrange("(k p) t -> p k t", p=128)
        xt, xtb = X["xt"], X["xtb"]
        cs = slice(HAL, HAL + CH)
        if X.get("loaded") != (r, c):
            self.ln2_load(r, c, X, ft)
        yield
        for k in range(8):
            ps = self.psS.get()
            for jt in range(2):
                self.tr(ps.t[:, jt * 128:(jt + 1) * 128], ft[jt].t[:, k * 128:(k + 1) * 128], self.identF.t[:],
                        R=[ft[jt].b, self.identF.b], W=[ps.b], sig=(jt == 1))
            self.V_stt(xt.t[:, k, cs], ps.t[:, 0:CH], self.g2a.t[:, k, cond:cond + 1], xt.t[:, k, cs], ALU.mult, ALU.add,
                       R=[ps.b, self.g2a.b, xtb[k]], W=[xtb[k]])
            yield

    def ln2_load(self, r, c, X, ft):
        g0 = RB[r] + c * CH
        gc = g0 // CH
        x1Tv = self.x1T_d.rearrange("(k p) t -> p k t", p=128)
        cs = slice(HAL, HAL + CH)
        self.dma(self.qsp, X["xt"].t[:, :, cs], x1Tv[:, :, g0:g0 + CH], R=[self.x1T_b[gc]], W=X["xtb"])
        for jt in range(2):
            gt = g0 // 128 + jt
            self.dma(self.qsp, ft[jt].t[:], self.ffn_d[gt * 128:(gt + 1) * 128, :], R=[self.ffn_bt[gt]], W=[ft[jt].b])
        X["loaded"] = (r, c)

    def ln2_B(self, l, r, c, X, ot):
        g0 = RB[r] + c * CH
        gc = g0 // CH
        xTv = self.xT_d.rearrange("(k p) t -> p k t", p=128)
        xt, xtb = X["xt"], X["xtb"]
        cs = slice(HAL, HAL + CH)
        yield from self.ln_core(ring=self.psB, gen=True, X=X)
        yield from self.affine_xt(V_L2G, V_L2B, gen=True, X=X)
        if l < self.depth - 1:
            self.dma(self.qsp, xTv[:, :, g0:g0 + CH], xt.t[:, :, cs], R=xtb, W=[self.xT_b[gc]])
        else:
            for jt in range(2):
                o = ot[jt]
                for hf in range(2):
                    ps = self.psB.get()
                    for kk in range(4):
                        k = hf * 4 + kk
                        self.tr(ps.t[:, kk * 128:(kk + 1) * 128], xt.t[:, k, HAL + jt * 128:HAL + (jt + 1) * 128], self.identF.t[:],
                                R=[xtb[k], self.identF.b], W=[ps.b], sig=(kk == 3))
                    if hf == 0:
                        self.A(o.t[:, 0:512], ps.t[:, :], AF.Copy, R=[ps.b], W=[o.b])
                    else:
                        self.V_cp(o.t[:, 512:D], ps.t[:, :], R=[ps.b], W=[o.b])
                    yield
                row = c * CH + jt * 128
                dst = self.y_s[row:row + 128, :] if r == 0 else self.y_p[(r - 1) * T_P + row:(r - 1) * T_P + row + 128, :]
                self.dma(self.qsp, dst, o.t[:], R=[o.b], W=[Buf("o")])

    def ln2_phase(self, l):
        self.carve_reset()
        a = self.av
        NSET = 3
        ft = [a(f"ft{i}", [128, D], F32) for i in range(2 * NSET)]
        ot = [a(f"ot{i}", [128, D], F32) for i in range(2)]
        sets = []
        for i in range(NSET):
            sets.append(dict(act_sq=True, xt=a(f"xl{i}", [128, 8, CW], F32), xtb=[Buf(f"xl{i}_{k}") for k in range(8)],
                             st=[a(f"stl{i}_{j}", [128, CH], F32) for j in range(3)]))
        chunks = [(r, c) for r in range(3) for c in range(REQ_T[r] // CH)]
        prevB = None
        for i, (r, c) in enumerate(chunks):
            X = sets[i % NSET]
            genA = self.ln2_A(l, r, c, X, ft[(i % NSET) * 2:(i % NSET) * 2 + 2])
            next(genA)
            if i + 1 < len(chunks):
                rn, cn = chunks[i + 1]
                self.ln2_load(rn, cn, sets[(i + 1) % NSET], ft[((i + 1) % NSET) * 2:((i + 1) % NSET) * 2 + 2])
            while genA is not None or prevB is not None:
                if genA is not None and next(genA, "done") == "done":
                    genA = None
                if prevB is not None:
                    for _ in range(3):
                        if next(prevB, "done") == "done":
                            prevB = None
                            break
                if genA is None and prevB is None:
                    break
            prevB = self.ln2_B(l, r, c, X, ot)
        for _ in prevB:
            pass

    def build(self):
        self.setup()
        self.load_consts()
        self.transpose_in()
        self.load_layer_weights(0)
        self.barrier()
        for l in range(self.depth):
            self.mark(f"L{l} mod")
            self.mod_phase(l)
            self.barrier()
            for r in range(3):
                self.carve_ab()
                self.mark(f"L{l} r{r} A")
                if r == 0:
                    self.ctx_phase(l)
                nch = REQ_T[r] // CH
                for c in range(nch):
                    self.phase_a_chunk(l, r, c)
                self.mark(f"L{l} r{r} B")
                self.phase_b(l, r)
                self.barrier()
                self.mark(f"L{l} r{r} route")
                self.routing(r)
                self.barrier()
            self.mark(f"L{l} moe")
            self.moe_phase(l)
            self.barrier()
            self.mark(f"L{l} ln2")
            self.ln2_phase(l)
            self.barrier()
        self.barrier()
        self.mark("end")
        return self.nc


def _rope_tables():
    rows_n = T_S // 64
    r, cl = np.meshgrid(np.arange(rows_n, dtype=np.float32), np.arange(64, dtype=np.float32), indexing="ij")
    inv = (np.float32(10000.0) ** (-np.arange(0, 32, 2, dtype=np.float32) / np.float32(32))).astype(np.float32)
    ang = np.concatenate([r.reshape(-1)[:, None] * inv, cl.reshape(-1)[:, None] * inv], axis=-1).astype(np.float32)
    cos, sin = np.cos(ang).astype(np.float32), np.sin(ang).astype(np.float32)
    cosT = np.concatenate([cos.T, cos.T], axis=0)
    sinT = np.concatenate([-sin.T, sin.T], axis=0)
    return np.ascontiguousarray(cosT), np.ascontiguousarray(sinT)


def _consts():
    c = {}
    c["identF"] = np.eye(128, dtype=np.float32)
    p = np.arange(128)
    c["gsum"] = (p[:, None] // 8 == p[None, :] // 8).astype(np.float32)
    c["lstrict"] = ((p[:, None] // 8 == p[None, :] // 8) & (p[:, None] % 8 < p[None, :] % 8)).astype(np.float32)
    c["iota1"] = np.broadcast_to(np.arange(1, 513, dtype=np.float32)[None, :], (128, 512)).copy()
    rows = np.arange(36)[None, :] * 128 + p[:, None]
    c["tidhl"] = np.stack([rows // 64, rows % 64], axis=-1).astype(np.float32)
    c["tidcol"] = rows.astype(np.float32)
    edges = np.zeros((128, 2, 16), np.float32)
    invw = np.zeros((128, 2), np.float32)
    for m in range(2):
        for half in range(2):
            w = (2, 4, 8, 16)[m * 2 + half]
            sl = slice(half * 64, half * 64 + 64)
            invw[sl, m] = 1.0 / w
            for i in range(8):
                t = i
                cnt = min(t + w // 2, 10 ** 9) - max(0, t - w // 2)
                edges[sl, m, i] = 1.0 / cnt
                j = 7 - i
                cnt = min(w // 2, j + 1) + min(w // 2, 10 ** 9)
                edges[sl, m, 8 + i] = 1.0 / cnt
    c["edges"] = edges
    misc = np.zeros((128, 16), np.float32)
    misc[:, 0] = RMS_EPS
    misc[:, 1] = LN_EPS / (ALPHA * ALPHA)
    misc[:, 2:4] = invw
    c["misc"] = misc
    c["cosT"], c["sinT"] = _rope_tables()
    return c


def _prep_shared(inp):
    f = lambda a: np.ascontiguousarray(np.asarray(a, dtype=np.float32))
    w_in = f(inp["w_in"])
    L = DEPTH
    krs = np.concatenate([np.arange(C_KR + 32, C_KR + 64), np.arange(C_KR, C_KR + 32)])
    w_in_x = np.concatenate([w_in, w_in[:, :, krs]], axis=2)
    sh = {}
    sh["w_in_x"] = np.ascontiguousarray(w_in_x.reshape(L, 8, 128, NCOL).transpose(0, 2, 1, 3))
    w_uq = f(inp["w_uq"])
    sw = np.concatenate([np.concatenate([np.arange(h * 192 + 160, h * 192 + 192), np.arange(h * 192 + 128, h * 192 + 160)]) for h in range(NH)])
    w_uq_x = np.concatenate([w_uq, w_uq[:, :, sw]], axis=2)
    sh["w_uq_x"] = np.ascontiguousarray(w_uq_x.reshape(L, 3, 128, 1024).transpose(0, 2, 1, 3))
    sh["w_uk_x"] = np.ascontiguousarray(f(inp["w_uk"]).reshape(L, 2, 128, 512).transpose(0, 2, 1, 3))
    sh["w_uv_x"] = np.ascontiguousarray(f(inp["w_uv"]).reshape(L, 2, 128, 512).transpose(0, 2, 1, 3))
    pw = f(inp["pool_w"])
    bd = np.zeros((L, 256, 256), np.float32)
    for g in range(4):
        bd[:, g * 64:(g + 1) * 64, g * 64:(g + 1) * 64] = pw[:, g]
    sh["poolw_x"] = np.ascontiguousarray(bd.reshape(L, 2, 128, 256).transpose(0, 2, 1, 3))
    sh["w_out_x"] = np.ascontiguousarray(f(inp["w_out"]).reshape(L, 8, 128, D).transpose(0, 2, 1, 3))
    sh["w_r_x"] = np.ascontiguousarray(f(inp["w_router"]).reshape(L, 8, 128, NE).transpose(0, 2, 1, 3))
    sh["w_ada"] = f(inp["w_ada"])
    vecs = np.zeros((L, 128, NV), np.float32)
    col = lambda v, n: v.reshape(L, n, 128).transpose(0, 2, 1)
    vecs[:, :, V_BADA:V_BADA + 48] = col(f(inp["b_ada"]), 48)
    vecs[:, :, V_L1G:V_L1G + 8] = col(f(inp["ln1_g"]), 8)
    vecs[:, :, V_L1B:V_L1B + 8] = col(f(inp["ln1_b"]), 8)
    vecs[:, :, V_L2G:V_L2G + 8] = col(f(inp["ln2_g"]), 8)
    vecs[:, :, V_L2B:V_L2B + 8] = col(f(inp["ln2_b"]), 8)
    vecs[:, :, V_QN:V_QN + 3] = col(f(inp["q_norm"]), 3)
    vecs[:, :, V_KVN:V_KVN + 2] = col(f(inp["kv_norm"]), 2)
    vecs[:, :, V_PS:V_PS + 2] = col(f(inp["pool_scale"]), 2)
    cw = f(inp["conv_w"])
    for m in range(2):
        for t in range(3):
            vecs[:, :, V_CW + m * 3 + t] = cw[:, t, m * 128:(m + 1) * 128]
    sh["vecs"] = vecs
    sh["w_gate"] = f(inp["w_gate"])
    sh["w_up"] = f(inp["w_up"])
    sh["w_down"] = f(inp["w_down"])
    sh.update(_consts())
    return sh


_NC_CACHE = {}


def kernel(**inp):
    sh = _prep_shared(inp)
    xp = np.asarray(inp["x_prompt"], dtype=np.float32)
    xsmp = np.asarray(inp["x_sample"], dtype=np.float32)
    cckv = np.asarray(inp["cache_ckv"], dtype=np.float32)
    ckr = np.asarray(inp["cache_krope"], dtype=np.float32)
    cvec = np.asarray(inp["c"], dtype=np.float32)
    cctx = np.asarray(inp["c_ctx"], dtype=np.float32)
    in_maps = []
    for core in range(8):
        s = core // 4
        m = dict(sh)
        m["xs_in"] = np.ascontiguousarray(xsmp[s])
        m["xp_in"] = np.ascontiguousarray(xp[2 * core:2 * core + 2].reshape(2 * T_P, D))
        m["cckv"] = np.ascontiguousarray(cckv[s])
        m["ckr"] = np.ascontiguousarray(ckr[s])
        cond = np.stack([cctx, cvec[s]], axis=-1)
        m["condT"] = np.ascontiguousarray(cond.reshape(8, 128, 2).transpose(1, 0, 2))
        in_maps.append(m)
    if "nc" not in _NC_CACHE:
        _NC_CACHE["nc"] = KB().build()
    nc = _NC_CACHE["nc"]
    res = run_bass_kernel_spmd(nc, in_maps, core_ids=list(range(8)))
    rs = res.results
    y_prompt = np.concatenate([np.asarray(rs[c]["y_p"]).reshape(2, T_P, D) for c in range(8)], axis=0)
    y_sample = np.stack([np.asarray(rs[0]["y_s"]), np.asarray(rs[4]["y_s"])], axis=0)
    new_ckv = np.concatenate([np.asarray(rs[c]["nckv"]) for c in range(8)], axis=0)
    new_krope = np.concatenate([np.asarray(rs[c]["nkr"]) for c in range(8)], axis=0)
    return (y_prompt.astype(np.float32), y_sample.astype(np.float32), new_ckv.astype(np.float32), new_krope.astype(np.float32))
```
